# Optimizing a Trainium2 kernel written in Bass

```python
import math
import jax, jax.numpy as jnp
from jax import lax
import numpy as np

D_MODEL = 2048
BATCH = 4
SEQ = 2048
DEPTH = 4
DEC_BATCH = 128
DEC_SEQ = 8
PAST_LEN = 16384
PAGE_SIZE = 128

D_LRU = D_MODEL // 2
LRU_BLOCKS = 16
LRU_BLOCK = D_LRU // LRU_BLOCKS
LRU_CONV = 4
LRU_C = 8.0
D_RW = D_MODEL // 2
RW_HEAD = 64
RW_HEADS = D_RW // RW_HEAD
LORA_W = max(32, int(round(math.sqrt(D_RW) * 1.8 / 32)) * 32)
LORA_A = max(32, int(round(math.sqrt(D_RW) * 1.8 / 32)) * 32)
LORA_G = max(32, int(round(D_RW ** 0.8 * 0.6 / 32)) * 32)
N_RW_PROJ = 3 * D_RW + LORA_W + LORA_A + LORA_G
N_IN = 2 * D_LRU + N_RW_PROJ + 2 * D_MODEL
D_FF = 5632
FFN_CONV = 3
RMS_EPS = 1e-6
GN_EPS = 64e-5
N_MOD = 6

kernel_name = "hybrid_rglru_rwkv7_adaln_convffn_step"


def rms_norm(x, g):
    xf = x.astype(jnp.float32)
    y = xf * lax.rsqrt(jnp.mean(xf * xf, axis=-1, keepdims=True) + RMS_EPS)
    return (y * g.astype(jnp.float32)).astype(x.dtype)


def causal_dwconv(x, buf, w, b):
    width = w.shape[0]
    t_len = x.shape[1]
    xx = jnp.concatenate([buf.astype(x.dtype), x], axis=1)
    out = b
    for j in range(width):
        out = out + w[j] * xx[:, j:j + t_len]
    return out, xx[:, xx.shape[1] - (width - 1):]


def token_shift(p, buf):
    prev = jnp.concatenate([buf[:, None].astype(p.dtype), p[:, :-1]], axis=1)
    return prev, p[:, -1]


def rg_lru(x, h0, w_a, b_a, w_i, b_i, lam):
    bsz, t_len, _ = x.shape
    xb = x.reshape(bsz, t_len, LRU_BLOCKS, LRU_BLOCK)
    r = jax.nn.sigmoid(jnp.einsum('btnd,nde->btne', xb, w_a).reshape(bsz, t_len, D_LRU) + b_a)
    i = jax.nn.sigmoid(jnp.einsum('btnd,nde->btne', xb, w_i).reshape(bsz, t_len, D_LRU) + b_i)
    log_a = -LRU_C * r.astype(jnp.float32) * jax.nn.softplus(-lam.astype(jnp.float32))
    a = jnp.exp(log_a)
    inp = jnp.sqrt(-jnp.expm1(2.0 * log_a)) * (i * x).astype(jnp.float32)
    inp = inp.at[:, 0].add(a[:, 0] * h0.astype(jnp.float32))

    def combine(left, right):
        a_l, b_l = left
        a_r, b_r = right
        return a_l * a_r, a_r * b_l + b_r

    _, h = lax.associative_scan(combine, (a, inp), axis=1)
    return h.astype(x.dtype), h[:, -1].astype(h0.dtype)


def rwkv7_recurrence(r, w, k, v, kk, a, s0):
    def step(s, inp):
        r_t, w_t, k_t, v_t, kk_t, a_t = inp
        s_kk = jnp.einsum('bhvk,bhk->bhv', s, kk_t)
        s = (s * w_t[:, :, None, :] - s_kk[..., None] * (kk_t * a_t)[:, :, None, :]
             + v_t[..., None] * k_t[:, :, None, :])
        y = jnp.einsum('bhvk,bhk->bhv', s, r_t)
        return s, y

    xs = (jnp.moveaxis(r, 1, 0), jnp.moveaxis(w, 1, 0), jnp.moveaxis(k, 1, 0),
          jnp.moveaxis(v, 1, 0), jnp.moveaxis(kk, 1, 0), jnp.moveaxis(a, 1, 0))
    s_fin, ys = lax.scan(step, s0.astype(jnp.float32), xs)
    return jnp.moveaxis(ys, 0, 1), s_fin.astype(s0.dtype)


def layer(x, c, st, l, P):
    lru_buf, lru_h0, shift_buf, s0, ffn_buf = st
    bsz, t_len, _ = x.shape
    mod = jnp.dot(jax.nn.silu(c), P['w_ada'][l]) + P['b_ada'][l]
    sh1, sc1, gt1, sh2, sc2, gt2 = jnp.split(mod[:, None, :], N_MOD, axis=-1)

    h = rms_norm(x, P['norm_mix'][l]) * (1.0 + sc1) + sh1
    p = jnp.dot(h, P['w_in'][l])
    lru_x, lru_gate, rwp, gates = jnp.split(
        p, [D_LRU, 2 * D_LRU, 2 * D_LRU + N_RW_PROJ], axis=-1)

    xc, new_lru_buf = causal_dwconv(lru_x, lru_buf, P['lru_conv_w'][l], P['lru_conv_b'][l])
    hl, new_lru_h = rg_lru(xc, lru_h0, P['lru_wa'][l], P['lru_ba'][l],
                           P['lru_wi'][l], P['lru_bi'][l], P['lru_lambda'][l])
    y_a = jnp.dot(jax.nn.gelu(lru_gate) * hl, P['w_pa'][l])

    prev, new_shift = token_shift(rwp, shift_buf)
    xs = rwp + (prev - rwp) * P['rw_mu'][l]
    r_, k_, v_, lw, la, lg = jnp.split(
        xs, [D_RW, 2 * D_RW, 3 * D_RW, 3 * D_RW + LORA_W, 3 * D_RW + LORA_W + LORA_A], axis=-1)
    w_log = -jax.nn.softplus(-(P['rw_w0'][l] + jnp.dot(jnp.tanh(lw), P['rw_w2'][l])).astype(jnp.float32)) - 0.5
    decay = jnp.exp(-jnp.exp(w_log))
    a_icl = jax.nn.sigmoid(P['rw_a0'][l] + jnp.dot(la, P['rw_a2'][l]))
    gg = jnp.dot(jax.nn.sigmoid(lg), P['rw_g2'][l])
    heads = lambda t: t.astype(jnp.float32).reshape(bsz, t_len, RW_HEADS, RW_HEAD)
    kk = heads(k_ * P['rw_kk'][l])
    kk = kk / jnp.maximum(jnp.sqrt(jnp.sum(kk * kk, axis=-1, keepdims=True)), 1e-12)
    k_mod = heads(k_ * (1.0 + (a_icl - 1.0) * P['rw_ka'][l]))
    r_h, v_h, a_h, w_h = heads(r_), heads(v_), heads(a_icl), heads(decay)
    y_rw, new_s = rwkv7_recurrence(r_h, w_h, k_mod, v_h, kk, a_h, s0)
    mu = jnp.mean(y_rw, axis=-1, keepdims=True)
    var = jnp.mean(jnp.square(y_rw - mu), axis=-1, keepdims=True)
    gn = ((y_rw - mu) * lax.rsqrt(var + GN_EPS) * P['rw_gn_g'][l].astype(jnp.float32).reshape(RW_HEADS, RW_HEAD)
          + P['rw_gn_b'][l].astype(jnp.float32).reshape(RW_HEADS, RW_HEAD))
    bonus = jnp.sum(r_h * k_mod * P['rw_rk'][l].astype(jnp.float32), axis=-1, keepdims=True) * v_h
    o_rw = (gn + bonus).reshape(bsz, t_len, D_RW).astype(x.dtype) * gg
    y_b = jnp.dot(o_rw, P['w_pb'][l])

    g_a, g_b = jnp.split(gates, 2, axis=-1)
    merged = jax.nn.sigmoid(g_a) * y_a + jax.nn.sigmoid(g_b) * y_b
    x = x + gt1 * jnp.dot(merged, P['w_o'][l])

    h2 = rms_norm(x, P['norm_ffn'][l]) * (1.0 + sc2) + sh2
    u = jnp.dot(h2, P['w_up'][l])
    uc, new_ffn_buf = causal_dwconv(u, ffn_buf, P['ffn_conv_w'][l], P['ffn_conv_b'][l])
    u_g, u_v = jnp.split(uc, 2, axis=-1)
    x = x + gt2 * jnp.dot(jax.nn.silu(u_g) * u_v, P['w_down'][l])
    return x, (new_lru_buf, new_lru_h, new_shift, new_s, new_ffn_buf)


def trunk(x, c, states, P):
    bsz = x.shape[0]
    if states is None:
        states = (jnp.zeros((DEPTH, bsz, LRU_CONV - 1, D_LRU), x.dtype),
                  jnp.zeros((DEPTH, bsz, D_LRU), x.dtype),
                  jnp.zeros((DEPTH, bsz, N_RW_PROJ), x.dtype),
                  jnp.zeros((DEPTH, bsz, RW_HEADS, RW_HEAD, RW_HEAD), x.dtype),
                  jnp.zeros((DEPTH, bsz, FFN_CONV - 1, 2 * D_FF), x.dtype))
    outs = ([], [], [], [], [])
    for l in range(DEPTH):
        st = tuple(s[l] for s in states)
        x, new_st = layer(x, c, st, l, P)
        for lst, ns in zip(outs, new_st):
            lst.append(ns)
    y = rms_norm(x, P['norm_final'])
    return y, tuple(jnp.stack(lst, axis=0) for lst in outs)


def setup_inputs(seed: int = 0) -> dict:
    key = jax.random.key(seed)
    ks = iter(jax.random.split(key, 64))
    f32 = jnp.float32
    nrm = lambda shape, scale: jax.random.normal(next(ks), shape, f32) * scale
    uni = lambda shape, lo, hi: jax.random.uniform(next(ks), shape, f32, lo, hi)
    u_lam = uni((DEPTH, D_LRU), 0.9, 0.999) ** (1.0 / LRU_C)
    return {
        'x_prompt': nrm((BATCH, SEQ, D_MODEL), 1.0),
        'x_sample': nrm((DEC_BATCH, DEC_SEQ, D_MODEL), 1.0),
        'c_prompt': nrm((BATCH, D_MODEL), 1.0),
        'c_sample': nrm((DEC_BATCH, D_MODEL), 1.0),
        'state_lru_conv': nrm((DEPTH, DEC_BATCH, LRU_CONV - 1, D_LRU), 1.0),
        'state_lru_h': nrm((DEPTH, DEC_BATCH, D_LRU), 0.5),
        'state_rwkv_shift': nrm((DEPTH, DEC_BATCH, N_RW_PROJ), 1.0),
        'state_rwkv_S': nrm((DEPTH, DEC_BATCH, RW_HEADS, RW_HEAD, RW_HEAD), 0.5),
        'state_ffn_conv': nrm((DEPTH, DEC_BATCH, FFN_CONV - 1, 2 * D_FF), 1.0),
        'w_ada': nrm((DEPTH, D_MODEL, N_MOD * D_MODEL), 0.5 * D_MODEL ** -0.5),
        'b_ada': nrm((DEPTH, N_MOD * D_MODEL), 0.01),
        'norm_mix': 1.0 + nrm((DEPTH, D_MODEL), 0.02),
        'norm_ffn': 1.0 + nrm((DEPTH, D_MODEL), 0.02),
        'w_in': nrm((DEPTH, D_MODEL, N_IN), D_MODEL ** -0.5),
        'lru_conv_w': nrm((DEPTH, LRU_CONV, D_LRU), LRU_CONV ** -0.5),
        'lru_conv_b': nrm((DEPTH, D_LRU), 0.01),
        'lru_wa': nrm((DEPTH, LRU_BLOCKS, LRU_BLOCK, LRU_BLOCK), LRU_BLOCK ** -0.5),
        'lru_ba': nrm((DEPTH, D_LRU), 0.01),
        'lru_wi': nrm((DEPTH, LRU_BLOCKS, LRU_BLOCK, LRU_BLOCK), LRU_BLOCK ** -0.5),
        'lru_bi': nrm((DEPTH, D_LRU), 0.01),
        'lru_lambda': jnp.log(u_lam) - jnp.log1p(-u_lam),
        'w_pa': nrm((DEPTH, D_LRU, D_MODEL), D_LRU ** -0.5),
        'rw_mu': uni((DEPTH, N_RW_PROJ), 0.0, 1.0),
        'rw_w0': uni((DEPTH, D_RW), -6.0, -1.0),
        'rw_w2': nrm((DEPTH, LORA_W, D_RW), 0.5 * LORA_W ** -0.5),
        'rw_a0': nrm((DEPTH, D_RW), 0.1),
        'rw_a2': nrm((DEPTH, LORA_A, D_RW), LORA_A ** -0.5),
        'rw_g2': nrm((DEPTH, LORA_G, D_RW), LORA_G ** -0.5),
        'rw_kk': 0.85 + nrm((DEPTH, D_RW), 0.02),
        'rw_ka': 1.0 + nrm((DEPTH, D_RW), 0.02),
        'rw_rk': nrm((DEPTH, RW_HEADS, RW_HEAD), 0.1),
        'rw_gn_g': 1.0 + nrm((DEPTH, D_RW), 0.02),
        'rw_gn_b': nrm((DEPTH, D_RW), 0.01),
        'w_pb': nrm((DEPTH, D_RW, D_MODEL), D_RW ** -0.5),
        'w_o': nrm((DEPTH, D_MODEL, D_MODEL), D_MODEL ** -0.5),
        'w_up': nrm((DEPTH, D_MODEL, 2 * D_FF), D_MODEL ** -0.5),
        'ffn_conv_w': nrm((DEPTH, FFN_CONV, 2 * D_FF), FFN_CONV ** -0.5),
        'ffn_conv_b': nrm((DEPTH, 2 * D_FF), 0.01),
        'w_down': nrm((DEPTH, D_FF, D_MODEL), D_FF ** -0.5),
        'norm_final': 1.0 + nrm((D_MODEL,), 0.02),
    }


def reference(x_prompt, x_sample, c_prompt, c_sample, state_lru_conv, state_lru_h,
              state_rwkv_shift, state_rwkv_S, state_ffn_conv, w_ada, b_ada, norm_mix, norm_ffn,
              w_in, lru_conv_w, lru_conv_b, lru_wa, lru_ba, lru_wi, lru_bi, lru_lambda, w_pa,
              rw_mu, rw_w0, rw_w2, rw_a0, rw_a2, rw_g2, rw_kk, rw_ka, rw_rk, rw_gn_g, rw_gn_b,
              w_pb, w_o, w_up, ffn_conv_w, ffn_conv_b, w_down, norm_final):
    P = dict(w_ada=w_ada, b_ada=b_ada, norm_mix=norm_mix, norm_ffn=norm_ffn, w_in=w_in,
             lru_conv_w=lru_conv_w, lru_conv_b=lru_conv_b, lru_wa=lru_wa, lru_ba=lru_ba,
             lru_wi=lru_wi, lru_bi=lru_bi, lru_lambda=lru_lambda, w_pa=w_pa, rw_mu=rw_mu,
             rw_w0=rw_w0, rw_w2=rw_w2, rw_a0=rw_a0, rw_a2=rw_a2, rw_g2=rw_g2, rw_kk=rw_kk,
             rw_ka=rw_ka, rw_rk=rw_rk, rw_gn_g=rw_gn_g, rw_gn_b=rw_gn_b, w_pb=w_pb, w_o=w_o,
             w_up=w_up, ffn_conv_w=ffn_conv_w, ffn_conv_b=ffn_conv_b, w_down=w_down,
             norm_final=norm_final)
    y_prompt, (p_lru_conv, p_lru_h, p_rwkv_shift, p_rwkv_S, p_ffn_conv) = trunk(
        x_prompt, c_prompt, None, P)
    y_sample, (s_lru_conv, s_lru_h, s_rwkv_shift, s_rwkv_S, s_ffn_conv) = trunk(
        x_sample, c_sample,
        (state_lru_conv, state_lru_h, state_rwkv_shift, state_rwkv_S, state_ffn_conv), P)
    return (y_prompt, y_sample, p_lru_conv, p_lru_h, p_rwkv_shift, p_rwkv_S, p_ffn_conv,
            s_lru_conv, s_lru_h, s_rwkv_shift, s_rwkv_S, s_ffn_conv)
```

```python
import math
import numpy as np
from contextlib import ExitStack
import concourse.bass as bass
import concourse.mybir as mybir
from concourse.bass_utils import run_bass_kernel_spmd

F32 = mybir.dt.float32
BF16 = mybir.dt.bfloat16
ALU = mybir.AluOpType
AF = mybir.ActivationFunctionType
AX = mybir.AxisListType

ENGS = ("pe", "act", "dve", "pool", "sp")

D = 2048
TP = 2048
NS = 16
TS = 128
T = TP + TS
NSEQ = 17
DL = 1024
NRW = 3360
NIN = 9504
DFF = 5632
DEPTH = 4
TB = [(0, 512), (512, 512), (1024, 512), (1536, 512), (2048, 128)]
SLOT = 2240
NLAYERS = DEPTH


class Buf:
    def __init__(self, prog, name):
        self.prog = prog
        self.name = name
        self.w = {}
        self.r = {}
        self.dsem = None
        self.dcnt = 0

    def dma_sem(self):
        if self.dsem is None:
            self.dsem = self.prog.new_sem("d_" + self.name)
        return self.dsem


class Prog:
    def __init__(self, nc, stack):
        self.nc = nc
        self.stack = stack
        self.streams = {e: [] for e in ENGS}
        self.esem = {e: self.new_sem("e_" + e) for e in ENGS}
        self.ecnt = {e: 0 for e in ENGS}
        self.seen = {e: {} for e in ENGS}
        self.nbuf = 0

    def new_sem(self, name):
        return self.stack.enter_context(self.nc.semaphore(name))

    def buf(self, name=None):
        self.nbuf += 1
        return Buf(self, name or f"b{self.nbuf}")

    def sbuf(self, name, shape, dtype):
        return self.stack.enter_context(self.nc.sbuf_tensor(name, list(shape), dtype))

    def psum(self, name, shape, dtype=F32):
        return self.stack.enter_context(self.nc.psum_tensor(name, list(shape), dtype))

    def _deps(self, eng, reads, writes):
        need = {}

        def add(tok):
            k = id(tok[0])
            if k not in need or need[k][1] < tok[1]:
                need[k] = tok

        for b in reads:
            for tok in b.w.values():
                add(tok)
        for b in writes:
            for tok in b.w.values():
                add(tok)
            for tok in b.r.values():
                add(tok)
        waits = []
        seen = self.seen[eng]
        for k, (sem, val) in need.items():
            if seen.get(k, 0) < val:
                seen[k] = val
                waits.append((sem, val))
        return waits

    def _mark(self, tok, reads, writes):
        k = id(tok[0])
        for b in reads:
            if b in writes:
                continue
            b.r[k] = tok
        for b in writes:
            b.w = {k: tok}
            b.r = {}

    def op(self, eng, fn, reads=(), writes=()):
        waits = self._deps(eng, reads, writes)
        self.ecnt[eng] += 1
        sem = self.esem[eng]
        self._mark((sem, self.ecnt[eng]), reads, writes)

        def run(e, waits=waits, fn=fn, sem=sem):
            for s, v in waits:
                e.wait_ge(s, v)
            fn(e).then_inc(sem, 1)

        self.streams[eng].append(run)

    def dma(self, q, out, in_, reads=(), writes=(), **kw):
        waits = self._deps(q, reads, writes)
        wb = writes[0]
        sem = wb.dma_sem()
        wb.dcnt += 16
        self._mark((sem, wb.dcnt), reads, writes)

        def run(e, waits=waits, sem=sem, out=out, in_=in_, kw=kw):
            for s, v in waits:
                e.wait_ge(s, v)
            e.dma_start(out=out, in_=in_, **kw).then_inc(sem, 16)

        self.streams[q].append(run)

    def final_wait(self, eng, bufs):
        waits = self._deps(eng, bufs, bufs)

        def run(e, waits=waits):
            for s, v in waits:
                e.wait_ge(s, v)

        self.streams[eng].append(run)

    def emit(self):
        nc = self.nc
        with nc.Block() as block:
            @block.tensor
            def _(e):
                for f in self.streams["pe"]:
                    f(e)

            @block.scalar
            def _(e):
                for f in self.streams["act"]:
                    f(e)

            @block.vector
            def _(e):
                for f in self.streams["dve"]:
                    f(e)

            @block.gpsimd
            def _(e):
                for f in self.streams["pool"]:
                    f(e)

            @block.sync
            def _(e):
                for f in self.streams["sp"]:
                    f(e)


W_SHAPES = {
    "w_ada": [DEPTH, D, 6 * D], "b_ada": [DEPTH, 6 * D], "norm_mix": [DEPTH, D], "norm_ffn": [DEPTH, D],
    "w_in": [DEPTH, D, NIN], "lru_conv_w": [DEPTH, 4, DL], "lru_conv_b": [DEPTH, DL],
    "lru_wa": [DEPTH, 16, 64, 64], "lru_ba": [DEPTH, DL], "lru_wi": [DEPTH, 16, 64, 64], "lru_bi": [DEPTH, DL],
    "lru_lambda": [DEPTH, DL], "w_pa": [DEPTH, DL, D], "rw_mu": [DEPTH, NRW], "rw_w0": [DEPTH, DL],
    "rw_w2": [DEPTH, 64, DL], "rw_a0": [DEPTH, DL], "rw_a2": [DEPTH, 64, DL], "rw_g2": [DEPTH, 160, DL],
    "rw_kk": [DEPTH, DL], "rw_ka": [DEPTH, DL], "rw_rk": [DEPTH, 16, 64], "rw_gn_g": [DEPTH, DL],
    "rw_gn_b": [DEPTH, DL], "w_pb": [DEPTH, DL, D], "w_o": [DEPTH, D, D], "w_up": [DEPTH, D, 2 * DFF],
    "ffn_conv_w": [DEPTH, 3, 2 * DFF], "ffn_conv_b": [DEPTH, 2 * DFF], "w_down": [DEPTH, DFF, D],
    "norm_final": [D],
}
IN_SHAPES = {
    "xp": [TP, D], "xs": [TS, D], "cc": [NSEQ, D], "st_lc": [DEPTH, NS, 3, DL], "st_lh": [DEPTH, NS, DL],
    "st_sh": [DEPTH, NS, NRW], "st_S": [DEPTH, NS, 16, 64, 64], "st_fc": [DEPTH, NS, 2, 2 * DFF],
}
OUT_SHAPES = {
    "y_p": [TP, D], "y_s": [TS, D], "o_lc": [DEPTH, NSEQ, 3, DL], "o_lh": [DEPTH, NSEQ, DL],
    "o_sh": [DEPTH, NSEQ, NRW], "o_S": [DEPTH, NSEQ, 16, 64, 64], "o_fc": [DEPTH, NSEQ, 2, 2 * DFF],
}

PROWS = [("b_ada", 96), ("norm_mix", 16), ("norm_ffn", 16), ("lru_conv_w", 32), ("lru_conv_b", 8),
         ("lru_ba", 8), ("lru_bi", 8), ("lru_lambda", 8), ("rw_mu", 27), ("rw_w0", 8), ("rw_a0", 8),
         ("ffn_conv_w", 264), ("ffn_conv_b", 88)]
PCOL = {}
_c = 0
for _n, _r in PROWS:
    PCOL[_n] = _c
    _c += _r
NPROW = _c
NPG = (NPROW + 127) // 128


def build():
    nc = bass.Bass("TRN2", target_bir_lowering=False)
    I = {k: nc.dram_tensor(k, s, F32, kind="ExternalInput") for k, s in IN_SHAPES.items()}
    Wt = {k: nc.dram_tensor(k, s, F32, kind="ExternalInput") for k, s in W_SHAPES.items()}
    O = {k: nc.dram_tensor(k, s, F32, kind="ExternalOutput") for k, s in OUT_SHAPES.items()}
    xT_h = nc.dram_tensor("xT_scr", [D, T], F32, kind="Internal")
    rq_h = nc.dram_tensor("rq_scr", [5, 16, T, 64], F32, kind="Internal")
    tk_h = nc.dram_tensor("tk_scr", [8, T, DL], F32, kind="Internal")
    sg_h = nc.dram_tensor("sg_scr", [2 * D, T], BF16, kind="Internal")
    ga_h = nc.dram_tensor("ga_scr", [DL, T], BF16, kind="Internal")
    oT_h = nc.dram_tensor("oT_scr", [DL, T], BF16, kind="Internal")
    aT_h = nc.dram_tensor("aT_scr", [DFF, T], BF16, kind="Internal")
    xT = xT_h.ap()
    tk = tk_h.ap()
    sg = sg_h.ap()
    ga_d = ga_h.ap()
    oT_d = oT_h.ap()
    aT_d = aT_h.ap()

    with ExitStack() as st:
        P = Prog(nc, st)
        actT = P.sbuf("actT", [128, 16, T], BF16); b_act = P.buf("actT")
        arena = P.sbuf("arena", [128, 5, SLOT], F32)
        b_S = [P.buf(f"S{i}") for i in range(5)]
        wsl = [P.sbuf(f"wsl{i}", [128, 16, 256], BF16) for i in range(2)]
        b_w = [P.buf(f"w{i}") for i in range(2)]
        blk = P.sbuf("blk", [128, 16384], BF16); b_blk = P.buf("blk")
        blkF = blk[:, :].bitcast(F32)
        ident = P.sbuf("ident", [128, 128], F32); b_id = P.buf("ident")
        ones = P.sbuf("ones", [128, 128], F32); b_ones = P.buf("ones")
        scT = P.sbuf("scT", [128, 16, NSEQ], BF16); b_scT = P.buf("scT")
        modT = P.sbuf("modT", [128, 96, NSEQ], F32); b_mod = P.buf("modT")
        A1 = P.sbuf("A1", [128, 16, NSEQ], F32); b_A1 = P.buf("A1")
        PT = P.sbuf("PT", [128, NPG * 128], F32); b_PT = P.buf("PT")
        pst = P.sbuf("pst", [128, 128], F32); b_pst = P.buf("pst")
        cA = P.sbuf("cA", [128, 16], F32); b_cA = P.buf("cA")
        bda = P.sbuf("bda", [128, 8, 128], BF16); b_bda = P.buf("bda")
        bdi = P.sbuf("bdi", [128, 8, 128], BF16); b_bdi = P.buf("bdi")
        lw2 = P.sbuf("lw2", [128, DL], BF16); b_lw2 = P.buf("lw2")
        g2a = P.sbuf("g2a", [128, DL], BF16); b_g2a = P.buf("g2a")
        g2b = P.sbuf("g2b", [32, DL], BF16); b_g2b = P.buf("g2b")
        stlc = P.sbuf("stlc", [128, 8, 48], F32); b_stlc = P.buf("stlc")
        stlh = P.sbuf("stlh", [128, 8, 16], F32); b_stlh = P.buf("stlh")
        stsh = P.sbuf("stsh", [128, 27, 16], F32); b_stsh = P.buf("stsh")
        olc = P.sbuf("olc", [128, 8, 51], F32); b_olc = P.buf("olc")
        olh = P.sbuf("olh", [128, 8, NSEQ], F32); b_olh = P.buf("olh")
        osh = P.sbuf("osh", [128, 27, NSEQ], F32); b_osh = P.buf("osh")
        ofc = P.sbuf("ofc", [128, 8, 34], F32); b_ofc = P.buf("ofc")
        stfc = P.sbuf("stfc", [128, 8, 32], F32); b_stfc = P.buf("stfc")
        tmpb = P.sbuf("tmpb", [128, 512], F32); b_tmpb = P.buf("tmpb")
        tmpc = P.sbuf("tmpc", [128, 512], F32); b_tmpc = P.buf("tmpc")
        rowst = blkF[:, 7168:8192]; b_rowst = b_blk
        Sst = blkF[:, 0:2048]; b_Sst = b_blk
        Stmp = blkF[:, 2048:4096]; b_Stmp = b_blk
        skk = P.sbuf("skk", [128, 32], F32); b_skk = P.buf("skk")
        vbuf = P.sbuf("vbuf", [128, 32 * 8], F32); b_vbuf = P.buf("vbuf")
        ybuf = P.sbuf("ybuf", [128, 32 * 8], F32); b_ybuf = P.buf("ybuf")
        sm = P.sbuf("sm", [128, 64], F32); b_sm = P.buf("sm")
        ct = [blkF[:, 4096 + i * 1024:4096 + (i + 1) * 1024] for i in range(3)]
        b_ct = [b_blk, b_blk, b_blk]
        PS = [P.psum(f"ps{i}", [128, 512]) for i in range(8)]
        b_ps = [P.buf(f"ps{i}") for i in range(8)]
        psc = [0]

        def nps():
            i = psc[0] % 8
            psc[0] += 1
            return PS[i], b_ps[i]

        b_xT = [P.buf(f"xT{c}") for c in range(16)]
        b_rq = P.buf("rq"); b_tk = [P.buf(f"tk{i}") for i in range(8)]
        b_sg = P.buf("sg"); b_ga = P.buf("ga"); b_oT = P.buf("oT"); b_aT = P.buf("aT")
        b_out = {k: P.buf("o_" + k) for k in OUT_SHAPES}

        def slot(i, n=SLOT, off=0):
            return arena[:, i, off:off + n]

        wcnt = [0]

        def wslot():
            i = wcnt[0] % 2
            wcnt[0] += 1
            return wsl[i], b_w[i]

        def act(out, in_, func, reads, writes, bias=None, scale=None):
            kw = {}
            if bias is not None:
                kw["bias"] = bias
            if scale is not None:
                kw["scale"] = scale
            P.op("act", lambda e: e.activation(out=out, in_=in_, func=func, **kw), reads=reads, writes=writes)

        def tt(out, in0, in1, op, reads, writes, eng="dve"):
            P.op(eng, lambda e: e.tensor_tensor(out=out, in0=in0, in1=in1, op=op), reads=reads, writes=writes)

        def ts(out, in0, s1, s2, op0, op1, reads, writes, eng="dve"):
            if s2 is None:
                P.op(eng, lambda e: e.tensor_scalar(out=out, in0=in0, scalar1=s1, scalar2=None, op0=op0), reads=reads, writes=writes)
            else:
                P.op(eng, lambda e: e.tensor_scalar(out=out, in0=in0, scalar1=s1, scalar2=s2, op0=op0, op1=op1), reads=reads, writes=writes)

        def stt(out, in0, scalar, in1, op0, op1, reads, writes):
            P.op("dve", lambda e: e.scalar_tensor_tensor(out=out, in0=in0, scalar=scalar, in1=in1, op0=op0, op1=op1),
                 reads=reads, writes=writes)

        def cp(out, in_, reads, writes, eng="dve"):
            P.op(eng, lambda e: e.tensor_copy(out=out, in_=in_), reads=reads, writes=writes)

        def mm(ps_ap, lhsT, rhs, start, stop, reads, pb):
            P.op("pe", lambda e: e.matmul(ps_ap, lhsT=lhsT, rhs=rhs, start=start, stop=stop), reads=reads, writes=[pb])

        def tr(ps_ap, in_, n_in_part, reads, pb):
            P.op("pe", lambda e: e.transpose(ps_ap, in_, ident[0:n_in_part, 0:n_in_part]), reads=list(reads) + [b_id], writes=[pb])

        def memset(ap, val, writes, eng="pool"):
            P.op(eng, lambda e: e.memset(ap, val), writes=writes)

        memset(ident[:], 0.0, [b_id])
        P.op("pool", lambda e: e.affine_select(out=ident[:], in_=ident[:], compare_op=ALU.not_equal, fill=1.0,
                                               base=0, pattern=[[-1, 128]], channel_multiplier=1),
             reads=[b_id], writes=[b_id])
        memset(ones[:], 1.0, [b_ones])

        P.dma("sp", rowst[0:NSEQ, :], I["cc"].ap()[:, 0:1024], writes=[b_rowst])
        for half in range(2):
            if half == 1:
                P.dma("sp", rowst[0:NSEQ, :], I["cc"].ap()[:, 1024:2048], writes=[b_rowst])
            act(rowst[0:NSEQ, :], rowst[0:NSEQ, :], AF.Silu, [b_rowst], [b_rowst])
            ps, pb = nps()
            for j in range(8):
                tr(ps[:, j * NSEQ:(j + 1) * NSEQ], rowst[0:NSEQ, j * 128:(j + 1) * 128], NSEQ, [b_rowst], pb)
            cp(scT[:, half * 8:(half + 1) * 8, :].rearrange("p a b -> p (a b)"), ps[:, 0:8 * NSEQ], [pb], [b_scT])

        xT_v = xT.rearrange("(kc p) t -> p kc t", p=128)
        for i in range(17):
            src = I["xp"].ap()[i * 128:(i + 1) * 128, :] if i < 16 else I["xs"].ap()[:, :]
            P.dma("sp", slot(0, 2048), src, writes=[b_S[0]])
            for g in range(4):
                ps, pb = nps()
                for j in range(4):
                    kc = g * 4 + j
                    tr(ps[:, j * 128:(j + 1) * 128], slot(0, 128, kc * 128), 128, [b_S[0]], pb)
                P.op("act", lambda e, ps=ps, g=g: e.activation(out=slot(1, 512, g * 512), in_=ps[:, :], func=AF.Identity),
                     reads=[pb], writes=[b_S[1]])
            P.dma("sp", xT_v[:, :, i * 128:(i + 1) * 128], slot(1, 2048).rearrange("p (kc t) -> p kc t", kc=16),
                  reads=[b_S[1]], writes=b_xT)

        def load_params(l):
            for g in range(NPG):
                memset(pst[:], 0.0, [b_pst])
                r = 0
                for name, nrows in PROWS:
                    for rr in range(nrows):
                        grow = PCOL[name] + rr
                        if grow // 128 != g:
                            continue
                    lo = max(PCOL[name], g * 128)
                    hi = min(PCOL[name] + nrows, (g + 1) * 128)
                    if lo >= hi:
                        continue
                    r0 = lo - PCOL[name]
                    n = hi - lo
                    flat = Wt[name].ap()[l]
                    if name in ("lru_conv_w", "ffn_conv_w"):
                        flat = flat.rearrange("j c -> (j c)")
                    if name == "rw_mu":
                        nfull = min(n, max(0, 26 - r0))
                        if nfull > 0:
                            P.dma("act", pst[lo - g * 128:lo - g * 128 + nfull, :],
                                  flat[r0 * 128:(r0 + nfull) * 128].rearrange("(r c) -> r c", c=128), writes=[b_pst])
                        if r0 + n == 27:
                            P.dma("act", pst[hi - 1 - g * 128:hi - g * 128, 0:32],
                                  flat[26 * 128:26 * 128 + 32].rearrange("(r c) -> r c", c=32), writes=[b_pst])
                    else:
                        P.dma("act", pst[lo - g * 128:hi - g * 128, :],
                              flat[r0 * 128:(r0 + n) * 128].rearrange("(r c) -> r c", c=128), writes=[b_pst])
                ps, pb = nps()
                tr(ps[:, 0:128], pst[:, :], 128, [b_pst], pb)
                cp(PT[:, g * 128:(g + 1) * 128], ps[:, 0:128], [pb], [b_PT])

        def pcol(name, row):
            c = PCOL[name] + row
            return PT[:, c:c + 1]

        def loadT(dst3, bdst, src_rows_fn, nrows, nchunks, last_cols=128):
            c = 0
            while c < nchunks:
                g = min(8, nchunks - c)
                width = (g - 1) * 128 + (last_cols if c + g == nchunks else 128)
                P.dma("sp", rowst[0:nrows, 0:width], src_rows_fn(c * 128, width), writes=[b_rowst])
                per = 512 // nrows
                j = 0
                while j < g:
                    gg = min(per, g - j)
                    ps, pb = nps()
                    for k in range(gg):
                        cc_ = c + j + k
                        ncol = last_cols if cc_ == nchunks - 1 else 128
                        tr(ps[0:ncol, k * nrows:(k + 1) * nrows], rowst[0:nrows, (j + k) * 128:(j + k) * 128 + ncol], nrows, [b_rowst], pb)
                    cp(dst3[:, c + j:c + j + gg, :].rearrange("p a b -> p (a b)"), ps[:, 0:gg * nrows], [pb], [bdst])
                    j += gg
                c += g

        def flushT(src3, bsrc, nchunks, n, dst_fn, bout, last_cols=128):
            c = 0
            while c < nchunks:
                g = min(4, nchunks - c)
                ps, pb = nps()
                width = 0
                for k in range(g):
                    ncol = last_cols if c + k == nchunks - 1 else 128
                    tr(ps[0:n, k * 128:k * 128 + ncol], src3[0:ncol, c + k, :], ncol, [bsrc], pb)
                    width += ncol
                cp(rowst[0:n, 0:width], ps[0:n, 0:width], [pb], [b_rowst])
                P.dma("sp", dst_fn(c * 128, width), rowst[0:n, 0:width], reads=[b_rowst], writes=[bout])
                c += g

        def wload(dst, src, bw):
            P.dma("pool", dst, src, writes=[bw])

        def norm_stats(t0, n):
            xb = arena[:, 0:4, :].rearrange("p a b -> p (a b)")[:, 0:16 * n].rearrange("p (kc t) -> p kc t", kc=16)
            P.dma("sp", xb, xT_v[:, :, t0:t0 + n], reads=b_xT, writes=b_S[0:4])
            ps, pb = nps()
            for kc in range(16):
                sqb, bsq = (tmpb, b_tmpb) if kc % 2 == 0 else (tmpc, b_tmpc)
                act(sqb[:, 0:n], xb[:, kc, :], AF.Square, b_S[0:4], [bsq])
                mm(ps[:, 0:n], ones[:], sqb[:, 0:n], kc == 0, kc == 15, [b_ones, bsq], pb)
            rstd = slot(4, n)
            act(rstd, ps[:, 0:n], AF.Sqrt, [pb], [b_S[4]], bias=1e-6, scale=1.0 / D)
            P.op("dve", lambda e: e.reciprocal(out=rstd, in_=rstd), reads=[b_S[4]], writes=[b_S[4]])
            return xb, rstd

        def norm_mod(Aap, Bap, bA, bB):
            for (t0, n) in TB:
                xb, rstd = norm_stats(t0, n)
                for kc in range(16):
                    tt(xb[:, kc, :], xb[:, kc, :], rstd, ALU.mult, b_S[0:5], b_S[0:4])
                    if t0 < TP:
                        ts(actT[:, kc, t0:t0 + n], xb[:, kc, :], Aap[:, kc, 0:1], Bap[:, kc, 0:1], ALU.mult, ALU.add,
                           b_S[0:4] + [bA, bB], [b_act])
                    else:
                        x3 = xb[:, kc, :].rearrange("p (s t) -> p s t", s=NS)
                        tt(x3, x3, Aap[:, kc, 1:NSEQ].unsqueeze(2).to_broadcast([128, NS, 8]), ALU.mult, b_S[0:4] + [bA], b_S[0:4])
                        tt(actT[:, kc, t0:t0 + n].rearrange("p (s t) -> p s t", s=NS), x3,
                           Bap[:, kc, 1:NSEQ].unsqueeze(2).to_broadcast([128, NS, 8]), ALU.add, b_S[0:4] + [bB], [b_act])

        def resid_update(c, t0, n, ps, pb, gcol0):
            xs_ = tmpb[:, 0:n]
            P.dma("sp", xs_, xT_v[:, c, t0:t0 + n], reads=[b_xT[c]], writes=[b_tmpb])
            if t0 < TP:
                stt(xs_, ps[:, 0:n], modT[:, gcol0 + c, 0:1], xs_, ALU.mult, ALU.add, [pb, b_mod, b_tmpb], [b_tmpb])
            else:
                g3 = modT[:, gcol0 + c, 1:NSEQ].unsqueeze(2).to_broadcast([128, NS, 8])
                t3 = tmpc[:, 0:n].rearrange("p (s t) -> p s t", s=NS)
                tt(t3, ps[:, 0:n].rearrange("p (s t) -> p s t", s=NS), g3, ALU.mult, [pb, b_mod], [b_tmpc])
                tt(xs_, xs_, tmpc[:, 0:n], ALU.add, [b_tmpb, b_tmpc], [b_tmpb])
            P.dma("sp", xT_v[:, c, t0:t0 + n], xs_, reads=[b_tmpb], writes=[b_xT[c]])

        def to_tokmajor(src_slot_ap, bsrc, qi, c0):
            tkv = tk[qi].rearrange("(tt p) c -> p tt c", p=128)
            for g0 in range(0, 17, 4):
                g = min(4, 17 - g0)
                ps, pb = nps()
                for k in range(g):
                    tr(ps[:, k * 128:(k + 1) * 128], src_slot_ap[:, (g0 + k) * 128:(g0 + k + 1) * 128], 128, [bsrc], pb)
                act(rowst[:, 0:g * 128], ps[:, 0:g * 128], AF.Identity, [pb], [b_rowst])
                P.dma("sp", tkv[:, g0:g0 + g, c0:c0 + 128], rowst[:, 0:g * 128].rearrange("p (a b) -> p a b", a=g),
                      reads=[b_rowst], writes=[b_tk[qi]])

        for l in range(NLAYERS):
            load_params(l)
            memset(bda[:], 0.0, [b_bda]); memset(bdi[:], 0.0, [b_bdi])
            for (dst, bd_, nm) in ((bda, b_bda, "lru_wa"), (bdi, b_bdi, "lru_wi")):
                wv = Wt[nm].ap()[l].rearrange("(c two) d e -> two d c e", two=2)
                P.dma("pool", dst[0:64, :, 0:64], wv[0], writes=[bd_])
                P.dma("pool", dst[64:128, :, 64:128], wv[1], writes=[bd_])
            P.dma("pool", lw2[0:64, :], Wt["rw_w2"].ap()[l], writes=[b_lw2])
            P.dma("pool", lw2[64:128, :], Wt["rw_a2"].ap()[l], writes=[b_lw2])
            P.dma("pool", g2a[:, :], Wt["rw_g2"].ap()[l, 0:128, :], writes=[b_g2a])
            P.dma("pool", g2b[:, :], Wt["rw_g2"].ap()[l, 128:160, :], writes=[b_g2b])
            lam = PT[:, PCOL["lru_lambda"]:PCOL["lru_lambda"] + 8]
            act(cA[:, 0:8], lam, AF.Exp, [b_PT], [b_cA], scale=-1.0)
            act(cA[:, 0:8], cA[:, 0:8], AF.Ln, [b_cA], [b_cA], bias=1.0)
            ts(cA[:, 8:16], cA[:, 0:8], -16.0, None, ALU.mult, None, [b_cA], [b_cA])
            ts(cA[:, 0:8], cA[:, 0:8], -8.0, None, ALU.mult, None, [b_cA], [b_cA])
            loadT(stlc, b_stlc, lambda c0, w: I["st_lc"].ap()[l].rearrange("s j c -> (s j) c")[:, c0:c0 + w], 48, 8)
            loadT(stlh, b_stlh, lambda c0, w: I["st_lh"].ap()[l][:, c0:c0 + w], 16, 8)
            loadT(stsh, b_stsh, lambda c0, w: I["st_sh"].ap()[l][:, c0:c0 + w], 16, 27, last_cols=32)

            for blk_i in range(48):
                ws, bw = wslot()
                wload(ws[:, :, 0:256], Wt["w_ada"].ap()[l].rearrange("(kc p) c -> p kc c", p=128)[:, :, blk_i * 256:(blk_i + 1) * 256], bw)
                ps, pb = nps()
                for j2 in range(2):
                    for kc in range(16):
                        mm(ps[:, j2 * 32:j2 * 32 + NSEQ], ws[:, kc, j2 * 128:(j2 + 1) * 128], scT[:, kc, :], kc == 0, kc == 15,
                           [bw, b_scT], pb)
                for j2 in range(2):
                    j = blk_i * 2 + j2
                    act(modT[:, j, :], ps[:, j2 * 32:j2 * 32 + NSEQ], AF.Identity, [pb, b_PT], [b_mod], bias=pcol("b_ada", j))

            def make_A(sc0, nm_name):
                nmb = PT[:, PCOL[nm_name]:PCOL[nm_name] + 16].unsqueeze(2).to_broadcast([128, 16, NSEQ])
                stt(A1[:, :, :], modT[:, sc0:sc0 + 16, :], 1.0, nmb, ALU.add, ALU.mult, [b_mod, b_PT], [b_A1])

            make_A(16, "norm_mix")
            norm_mod(A1, modT[:, 0:16, :], b_A1, b_mod)

            win = Wt["w_in"].ap()[l].rearrange("(kc p) c -> p kc c", p=128)

            def proj_block(ws, wc0, m, t0, n, bw):
                ps, pb = nps()
                for kc in range(16):
                    mm(ps[0:m, 0:n], ws[:, kc, wc0:wc0 + m], actT[:, kc, t0:t0 + n], kc == 0, kc == 15, [bw, b_act], pb)
                return ps, pb

            for c in range(8):
                ws, bw = wslot()
                wload(ws[:, :, 0:128], win[:, :, c * 128:(c + 1) * 128], bw)
                wload(ws[:, :, 128:256], win[:, :, DL + c * 128:DL + (c + 1) * 128], bw)
                LX = slot(0); XC = slot(1); AA = slot(3); INP = slot(4)
                LXs = LX[:, 2051:2051 + 176].rearrange("p (s j) -> p s j", s=NS)
                memset(LX[:, 0:3], 0.0, [b_S[0]])
                cp(LXs[:, :, 0:3], stlc[:, c, :].rearrange("p (s j) -> p s j", s=NS), [b_stlc], [b_S[0]], eng="pool")
                for (t0, n) in TB:
                    ps, pb = proj_block(ws, 0, 128, t0, n, bw)
                    if t0 < TP:
                        act(LX[:, 3 + t0:3 + t0 + n], ps[:, 0:n], AF.Identity, [pb], [b_S[0]])
                    else:
                        act(LXs[:, :, 3:11], ps[:, 0:n].rearrange("p (s t) -> p s t", s=NS), AF.Identity, [pb], [b_S[0]])
                cp(olc[:, c, 0:3], LX[:, 2048:2051], [b_S[0]], [b_olc], eng="pool")
                cp(olc[:, c, 3:51].rearrange("p (s j) -> p s j", s=NS), LXs[:, :, 8:11], [b_S[0]], [b_olc], eng="pool")
                XCs = XC[:, TP:T].rearrange("p (s t) -> p s t", s=NS)
                ts(XC[:, 0:TP], LX[:, 0:TP], pcol("lru_conv_w", 0 * 8 + c), pcol("lru_conv_b", c), ALU.mult, ALU.add,
                   [b_S[0], b_PT], [b_S[1]])
                ts(XCs, LXs[:, :, 0:8], pcol("lru_conv_w", 0 * 8 + c), pcol("lru_conv_b", c), ALU.mult, ALU.add,
                   [b_S[0], b_PT], [b_S[1]])
                for j in range(1, 4):
                    stt(XC[:, 0:TP], LX[:, j:j + TP], pcol("lru_conv_w", j * 8 + c), XC[:, 0:TP], ALU.mult, ALU.add,
                        [b_S[0], b_S[1], b_PT], [b_S[1]])
                    stt(XCs, LXs[:, :, j:j + 8], pcol("lru_conv_w", j * 8 + c), XCs, ALU.mult, ALU.add,
                        [b_S[0], b_S[1], b_PT], [b_S[1]])
                XCb = slot(2).bitcast(BF16)[:, 0:T]
                act(XCb, XC[:, 0:T], AF.Identity, [b_S[1]], [b_S[2]])
                for (t0, n) in TB:
                    psr, pbr = nps()
                    mm(psr[:, 0:n], bda[:, c, :], XCb[:, t0:t0 + n], True, True, [b_bda, b_S[2]], pbr)
                    psi, pbi = nps()
                    mm(psi[:, 0:n], bdi[:, c, :], XCb[:, t0:t0 + n], True, True, [b_bdi, b_S[2]], pbi)
                    rr = INP[:, t0:t0 + n]
                    act(rr, psr[:, 0:n], AF.Sigmoid, [pbr, b_PT], [b_S[4]], bias=pcol("lru_ba", c))
                    act(AA[:, t0:t0 + n], rr, AF.Exp, [b_S[4], b_cA], [b_S[3]], scale=cA[:, c:c + 1])
                    act(rr, rr, AF.Exp, [b_S[4], b_cA], [b_S[4]], scale=cA[:, 8 + c:9 + c])
                    act(rr, rr, AF.Sqrt, [b_S[4]], [b_S[4]], scale=-1.0, bias=1.0)
                    act(tmpb[:, 0:n], psi[:, 0:n], AF.Sigmoid, [pbi, b_PT], [b_tmpb], bias=pcol("lru_bi", c))
                    tt(rr, rr, tmpb[:, 0:n], ALU.mult, [b_S[4], b_tmpb], [b_S[4]])
                    tt(rr, rr, XC[:, t0:t0 + n], ALU.mult, [b_S[4], b_S[1]], [b_S[4]])
                AAs = AA[:, TP:T].rearrange("p (s t) -> p s t", s=NS)
                INs = INP[:, TP:T].rearrange("p (s t) -> p s t", s=NS)
                tt(sm[:, 0:NS], AAs[:, :, 0], stlh[:, c, :], ALU.mult, [b_S[3], b_stlh], [b_sm])
                tt(INs[:, :, 0], INs[:, :, 0], sm[:, 0:NS], ALU.add, [b_S[4], b_sm], [b_S[4]])
                memset(AAs[:, :, 0], 0.0, [b_S[3]], eng="dve")
                HL = slot(1)
                P.op("dve", lambda e, HL=HL, AA=AA, INP=INP: e.tensor_tensor_scan(out=HL[:, 0:TP], data0=AA[:, 0:TP], data1=INP[:, 0:TP],
                                                                           initial=0.0, op0=ALU.mult, op1=ALU.add),
                     reads=[b_S[3], b_S[4]], writes=[b_S[1]])
                P.op("dve", lambda e, HL=HL, AA=AA, INP=INP: e.tensor_tensor_scan(out=HL[:, TP:T], data0=AA[:, TP:T], data1=INP[:, TP:T],
                                                                           initial=0.0, op0=ALU.mult, op1=ALU.add),
                     reads=[b_S[3], b_S[4]], writes=[b_S[1]])
                cp(olh[:, c, 0:1], HL[:, TP - 1:TP], [b_S[1]], [b_olh], eng="pool")
                cp(olh[:, c, 1:NSEQ], HL[:, TP:T].rearrange("p (s t) -> p s t", s=NS)[:, :, 7], [b_S[1]], [b_olh], eng="pool")
                GG = slot(0)
                GAb = slot(2).bitcast(BF16)[:, 0:T]
                for (t0, n) in TB:
                    ps, pb = proj_block(ws, 128, 128, t0, n, bw)
                    u = GG[:, t0:t0 + n]
                    act(u, ps[:, 0:n], AF.Identity, [pb], [b_S[0]])
                    tt(tmpb[:, 0:n], u, u, ALU.mult, [b_S[0]], [b_tmpb])
                    ts(tmpb[:, 0:n], tmpb[:, 0:n], 0.044715, 1.0, ALU.mult, ALU.add, [b_tmpb], [b_tmpb])
                    tt(tmpb[:, 0:n], tmpb[:, 0:n], u, ALU.mult, [b_tmpb, b_S[0]], [b_tmpb])
                    act(tmpb[:, 0:n], tmpb[:, 0:n], AF.Sigmoid, [b_tmpb], [b_tmpb], scale=1.5957691216057308)
                    tt(u, u, tmpb[:, 0:n], ALU.mult, [b_S[0], b_tmpb], [b_S[0]])
                    tt(GAb[:, t0:t0 + n], u, HL[:, t0:t0 + n], ALU.mult, [b_S[0], b_S[1]], [b_S[2]])
                P.dma("sp", ga_d[c * 128:(c + 1) * 128, :], GAb, reads=[b_S[2]], writes=[b_ga])
            flushT(olc, b_olc, 8, 51, lambda c0, w: O["o_lc"].ap()[l].rearrange("s j c -> (s j) c")[:, c0:c0 + w], b_out["o_lc"])
            flushT(olh, b_olh, 8, NSEQ, lambda c0, w: O["o_lh"].ap()[l][:, c0:c0 + w], b_out["o_lh"])

            TL = arena[:, 3:5, :].rearrange("p a b -> p (a b)").bitcast(BF16)
            TLa = TL[:, 0:T]
            SG1 = TL[:, T:2 * T]
            SG2 = TL[:, 2 * T:3 * T]
            b_TL = b_S[3:5]
            for q in [24, 25, 26] + list(range(24)):
                m = 32 if q == 26 else 128
                ws, bw = wslot()
                wload(ws[:, :, 0:m], win[:, :, 2 * DL + q * 128:2 * DL + q * 128 + m], bw)
                RW = slot(0); XS = slot(1)
                RWs = RW[:, 2049:2049 + 144].rearrange("p (s j) -> p s j", s=NS)
                memset(RW[0:m, 0:1], 0.0, [b_S[0]])
                cp(RWs[0:m, :, 0], stsh[0:m, q, :], [b_stsh], [b_S[0]], eng="pool")
                for (t0, n) in TB:
                    ps, pb = proj_block(ws, 0, m, t0, n, bw)
                    if t0 < TP:
                        act(RW[0:m, 1 + t0:1 + t0 + n], ps[0:m, 0:n], AF.Identity, [pb], [b_S[0]])
                    else:
                        act(RWs[0:m, :, 1:9], ps[0:m, 0:n].rearrange("p (s t) -> p s t", s=NS), AF.Identity, [pb], [b_S[0]])
                cp(osh[0:m, q, 0:1], RW[0:m, TP:TP + 1], [b_S[0]], [b_osh], eng="pool")
                cp(osh[0:m, q, 1:NSEQ], RWs[0:m, :, 8], [b_S[0]], [b_osh], eng="pool")
                XSs = XS[:, TP:T].rearrange("p (s t) -> p s t", s=NS)
                mu = pcol("rw_mu", q)
                tt(XS[0:m, 0:TP], RW[0:m, 0:TP], RW[0:m, 1:TP + 1], ALU.subtract, [b_S[0]], [b_S[1]])
                tt(XSs[0:m], RWs[0:m, :, 0:8], RWs[0:m, :, 1:9], ALU.subtract, [b_S[0]], [b_S[1]])
                stt(XS[0:m, 0:TP], XS[0:m, 0:TP], mu[0:m], RW[0:m, 1:TP + 1], ALU.mult, ALU.add, [b_S[0], b_S[1], b_PT], [b_S[1]])
                stt(XSs[0:m], XSs[0:m], mu[0:m], RWs[0:m, :, 1:9], ALU.mult, ALU.add, [b_S[0], b_S[1], b_PT], [b_S[1]])
                if q == 24:
                    act(TLa[0:64, :], XS[0:64, 0:T], AF.Tanh, [b_S[1]], b_TL)
                    act(TLa[64:128, :], XS[64:128, 0:T], AF.Identity, [b_S[1]], b_TL)
                elif q == 25:
                    act(SG1, XS[:, 0:T], AF.Sigmoid, [b_S[1]], b_TL)
                elif q == 26:
                    act(SG2[0:32, :], XS[0:32, 0:T], AF.Sigmoid, [b_S[1]], b_TL)
                else:
                    to_tokmajor(XS, b_S[1], q // 8, (q % 8) * 128)
            flushT(osh, b_osh, 27, NSEQ, lambda c0, w: O["o_sh"].ap()[l][:, c0:c0 + w], b_out["o_sh"], last_cols=32)
            for c in range(8):
                DC = slot(0); AC = slot(1); GC = slot(2)
                for (t0, n) in TB:
                    ps, pb = nps()
                    mm(ps[:, 0:n], lw2[0:64, c * 128:(c + 1) * 128], TLa[0:64, t0:t0 + n], True, True, [b_lw2] + b_TL, pb)
                    act(DC[:, t0:t0 + n], ps[:, 0:n], AF.Sigmoid, [pb, b_PT], [b_S[0]], bias=pcol("rw_w0", c))
                    act(DC[:, t0:t0 + n], DC[:, t0:t0 + n], AF.Exp, [b_S[0]], [b_S[0]], scale=-math.exp(-0.5))
                    ps, pb = nps()
                    mm(ps[:, 0:n], lw2[64:128, c * 128:(c + 1) * 128], TLa[64:128, t0:t0 + n], True, True, [b_lw2] + b_TL, pb)
                    act(AC[:, t0:t0 + n], ps[:, 0:n], AF.Sigmoid, [pb, b_PT], [b_S[1]], bias=pcol("rw_a0", c))
                    ps, pb = nps()
                    mm(ps[:, 0:n], g2a[:, c * 128:(c + 1) * 128], SG1[:, t0:t0 + n], True, False, [b_g2a] + b_TL, pb)
                    mm(ps[:, 0:n], g2b[0:32, c * 128:(c + 1) * 128], SG2[0:32, t0:t0 + n], False, True, [b_g2b] + b_TL, pb)
                    act(GC[:, t0:t0 + n], ps[:, 0:n], AF.Identity, [pb], [b_S[2]])
                to_tokmajor(DC, b_S[0], 3, c * 128)
                to_tokmajor(AC, b_S[1], 4, c * 128)
                to_tokmajor(GC, b_S[2], 5, c * 128)
            for gc in range(32):
                if gc % 2 == 0:
                    ws, bw = wslot()
                    wload(ws[:, :, 0:256], win[:, :, 2 * DL + NRW + gc * 128:2 * DL + NRW + (gc + 2) * 128], bw)
                SGb = slot(gc % 2).bitcast(BF16)[:, 0:T]
                bsl = b_S[gc % 2]
                for (t0, n) in TB:
                    ps, pb = proj_block(ws, (gc % 2) * 128, 128, t0, n, bw)
                    act(SGb[:, t0:t0 + n], ps[:, 0:n], AF.Sigmoid, [pb], [bsl])
                P.dma("sp", sg[gc * 128:(gc + 1) * 128, :], SGb, reads=[bsl], writes=[b_sg])

            for i, nm in enumerate(("rw_kk", "rw_ka", "rw_rk")):
                src = Wt[nm].ap()[l]
                if nm == "rw_rk":
                    src = src.rearrange("h k -> (h k)")
                P.dma("act", ct[i][:, :], src.partition_broadcast(128), writes=[b_ct[i]])
            rqv = rq_h.ap()

            def half(i, h):
                return arena[:, i, h * 1024:(h + 1) * 1024]

            def h3(ap):
                return ap.rearrange("p (h k) -> p h k", h=16)

            for i in range(17):
                rows = slice(i * 128, (i + 1) * 128)
                Rt, Kt, Vt, Dt, At = half(0, 0), half(0, 1), half(1, 0), half(1, 1), half(2, 0)
                KKt, TMt, KMt, NBt, BVt = half(2, 1), half(3, 0), half(3, 1), half(4, 0), half(4, 1)
                for (dst, qi, bb) in ((Rt, 0, b_S[0]), (Kt, 1, b_S[0]), (Vt, 2, b_S[1]), (Dt, 3, b_S[1]), (At, 4, b_S[2])):
                    P.dma("sp", dst, tk[qi][rows, :], reads=[b_tk[qi]], writes=[bb])
                tt(KKt, Kt, ct[0][:, :], ALU.mult, [b_S[0], b_ct[0]], [b_S[2]])
                tt(TMt, KKt, KKt, ALU.mult, [b_S[2]], [b_S[3]])
                P.op("dve", lambda e, TMt=TMt: e.tensor_reduce(out=sm[:, 0:16], in_=h3(TMt), axis=AX.X, op=ALU.add),
                     reads=[b_S[3]], writes=[b_sm])
                act(sm[:, 0:16], sm[:, 0:16], AF.Sqrt, [b_sm], [b_sm])
                ts(sm[:, 0:16], sm[:, 0:16], 1e-12, None, ALU.max, None, [b_sm], [b_sm])
                P.op("dve", lambda e: e.reciprocal(out=sm[:, 0:16], in_=sm[:, 0:16]), reads=[b_sm], writes=[b_sm])
                tt(h3(KKt), h3(KKt), sm[:, 0:16].unsqueeze(2).to_broadcast([128, 16, 64]), ALU.mult, [b_S[2], b_sm], [b_S[2]])
                stt(TMt, At, -1.0, ct[1][:, :], ALU.add, ALU.mult, [b_S[2], b_ct[1]], [b_S[3]])
                stt(KMt, TMt, 1.0, Kt, ALU.add, ALU.mult, [b_S[3], b_S[0]], [b_S[3]])
                stt(NBt, KKt, -1.0, At, ALU.mult, ALU.mult, [b_S[2]], [b_S[4]])
                tt(TMt, Rt, KMt, ALU.mult, [b_S[0], b_S[3]], [b_S[3]])
                tt(TMt, TMt, ct[2][:, :], ALU.mult, [b_S[3], b_ct[2]], [b_S[3]])
                P.op("dve", lambda e, TMt=TMt: e.tensor_reduce(out=sm[:, 16:32], in_=h3(TMt), axis=AX.X, op=ALU.add),
                     reads=[b_S[3]], writes=[b_sm])
                tt(h3(BVt), h3(Vt), sm[:, 16:32].unsqueeze(2).to_broadcast([128, 16, 64]), ALU.mult, [b_S[1], b_sm], [b_S[4]])
                for (srct, qi, bb) in ((Rt, 0, b_S[0]), (Dt, 1, b_S[1]), (KMt, 2, b_S[3]), (KKt, 3, b_S[2]), (NBt, 4, b_S[4])):
                    P.dma("sp", rqv[qi].rearrange("h t k -> t h k")[rows], h3(srct), reads=[bb], writes=[b_rq])
                P.dma("sp", tk[6][rows, :], BVt, reads=[b_S[4]], writes=[b_tk[6]])

            def rec_steps(nsq, nsteps, xr_step, v_step, y_step):
                S3 = Sst[:, 0:nsq * 512].rearrange("p (s v k) -> p s v k", s=nsq, v=8)
                T3 = Stmp[:, 0:nsq * 512].rearrange("p (s v k) -> p s v k", s=nsq, v=8)
                K4 = [128, nsq, 8, 64]
                rdx = b_S[0:5]
                for t in range(nsteps):
                    r_, w_, k_, kk_, nb_ = [xr_step(j, t).unsqueeze(2).to_broadcast(K4) for j in range(5)]
                    sk = skk[:, 0:nsq * 8].rearrange("p (s v) -> p s v", s=nsq)
                    tt(T3, S3, kk_, ALU.mult, [b_Sst] + rdx, [b_Stmp])
                    P.op("dve", lambda e, sk=sk, T3=T3: e.tensor_reduce(out=sk, in_=T3, axis=AX.X, op=ALU.add), reads=[b_Stmp], writes=[b_skk])
                    tt(S3, S3, w_, ALU.mult, [b_Sst] + rdx, [b_Sst])
                    tt(T3, nb_, sk.unsqueeze(3).to_broadcast(K4), ALU.mult, [b_skk] + rdx, [b_Stmp])
                    tt(S3, S3, T3, ALU.add, [b_Sst, b_Stmp], [b_Sst])
                    tt(T3, k_, v_step(t).unsqueeze(3).to_broadcast(K4), ALU.mult, [b_vbuf] + rdx, [b_Stmp])
                    tt(S3, S3, T3, ALU.add, [b_Sst, b_Stmp], [b_Sst])
                    tt(T3, S3, r_, ALU.mult, [b_Sst] + rdx, [b_Stmp])
                    yo = y_step(t)
                    P.op("dve", lambda e, yo=yo, T3=T3: e.tensor_reduce(out=yo, in_=T3, axis=AX.X, op=ALU.add), reads=[b_Stmp], writes=[b_ybuf])

            oS = O["o_S"].ap()[l]
            memset(Sst[:, 0:512], 0.0, [b_Sst])
            TC = 32
            for ci in range(TP // TC):
                t0 = ci * TC
                for j in range(5):
                    src = bass.AP(rq_h, j * 16 * T * 64 + t0 * 64, [[T * 64, 16], [0, 8], [1, TC * 64]])
                    P.dma("sp" if j % 2 == 0 else "act", slot(j, TC * 64), src, reads=[b_rq], writes=[b_S[j]])
                P.dma("act", vbuf[:, 0:TC * 8].rearrange("p (t v) -> p t v", t=TC),
                      bass.AP(tk_h, 2 * T * DL + t0 * DL, [[8, 128], [DL, TC], [1, 8]]), reads=[b_tk[2]], writes=[b_vbuf])
                rec_steps(1, TC,
                          lambda j, t: slot(j, 64, t * 64).unsqueeze(1),
                          lambda t: vbuf[:, t * 8:(t + 1) * 8].unsqueeze(1),
                          lambda t: ybuf[:, t * 8:(t + 1) * 8].unsqueeze(1))
                P.dma("act", bass.AP(tk_h, 7 * T * DL + t0 * DL, [[8, 128], [DL, TC], [1, 8]]),
                      ybuf[:, 0:TC * 8].rearrange("p (t v) -> p t v", t=TC), reads=[b_ybuf], writes=[b_tk[7]])
            P.dma("sp", oS[0].rearrange("h (vb v8) k -> (h vb) (v8 k)", vb=8), Sst[:, 0:512], reads=[b_Sst], writes=[b_out["o_S"]])
            for sb in range(4):
                s0 = sb * 4
                P.dma("sp", Sst[:, 0:2048].rearrange("p (s f) -> p s f", s=4),
                      bass.AP(I["st_S"], (l * NS + s0) * 65536, [[512, 128], [65536, 4], [1, 512]]), writes=[b_Sst])
                for j in range(5):
                    for s in range(4):
                        tok0 = TP + (s0 + s) * 8
                        src = bass.AP(rq_h, j * 16 * T * 64 + tok0 * 64, [[T * 64, 16], [0, 8], [1, 8 * 64]])
                        P.dma("sp" if s % 2 == 0 else "act", slot(j, 512, s * 512), src, reads=[b_rq], writes=[b_S[j]])
                P.dma("act", vbuf[:, 0:256].rearrange("p (t v) -> p t v", t=32),
                      bass.AP(tk_h, 2 * T * DL + (TP + s0 * 8) * DL, [[8, 128], [DL, 32], [1, 8]]), reads=[b_tk[2]], writes=[b_vbuf])
                rec_steps(4, 8,
                          lambda j, t: slot(j, 2048).rearrange("p (s t k) -> p s t k", s=4, t=8)[:, :, t, :],
                          lambda t: vbuf[:, 0:256].rearrange("p (s t v) -> p s t v", s=4, t=8)[:, :, t, :],
                          lambda t: ybuf[:, 0:256].rearrange("p (s t v) -> p s t v", s=4, t=8)[:, :, t, :])
                P.dma("act", bass.AP(tk_h, 7 * T * DL + (TP + s0 * 8) * DL, [[8, 128], [DL, 32], [1, 8]]),
                      ybuf[:, 0:256].rearrange("p (t v) -> p t v", t=32), reads=[b_ybuf], writes=[b_tk[7]])
                P.dma("sp", bass.AP(O["o_S"], (l * NSEQ + 1 + s0) * 65536, [[512, 128], [65536, 4], [1, 512]]),
                      Sst[:, 0:2048].rearrange("p (s f) -> p s f", s=4), reads=[b_Sst], writes=[b_out["o_S"]])

            for i, nm in enumerate(("rw_gn_g", "rw_gn_b")):
                P.dma("act", ct[i][:, :], Wt[nm].ap()[l].partition_broadcast(128), writes=[b_ct[i]])
            for i in range(17):
                rows = slice(i * 128, (i + 1) * 128)
                Yt, BVt, GGt, TMt = half(0, 0), half(0, 1), half(1, 0), half(1, 1)
                P.dma("sp", Yt, tk[7][rows, :], reads=[b_tk[7]], writes=[b_S[0]])
                P.dma("sp", BVt, tk[6][rows, :], reads=[b_tk[6]], writes=[b_S[0]])
                P.dma("sp", GGt, tk[5][rows, :], reads=[b_tk[5]], writes=[b_S[1]])
                P.op("dve", lambda e, Yt=Yt: e.tensor_reduce(out=sm[:, 0:16], in_=h3(Yt), axis=AX.X, op=ALU.add), reads=[b_S[0]], writes=[b_sm])
                ts(sm[:, 0:16], sm[:, 0:16], 1.0 / 64, None, ALU.mult, None, [b_sm], [b_sm])
                tt(h3(Yt), h3(Yt), sm[:, 0:16].unsqueeze(2).to_broadcast([128, 16, 64]), ALU.subtract, [b_S[0], b_sm], [b_S[0]])
                tt(TMt, Yt, Yt, ALU.mult, [b_S[0]], [b_S[1]])
                P.op("dve", lambda e, TMt=TMt: e.tensor_reduce(out=sm[:, 16:32], in_=h3(TMt), axis=AX.X, op=ALU.add), reads=[b_S[1]], writes=[b_sm])
                act(sm[:, 16:32], sm[:, 16:32], AF.Sqrt, [b_sm], [b_sm], scale=1.0 / 64, bias=64e-5)
                P.op("dve", lambda e: e.reciprocal(out=sm[:, 16:32], in_=sm[:, 16:32]), reads=[b_sm], writes=[b_sm])
                tt(h3(Yt), h3(Yt), sm[:, 16:32].unsqueeze(2).to_broadcast([128, 16, 64]), ALU.mult, [b_S[0], b_sm], [b_S[0]])
                tt(Yt, Yt, ct[0][:, :], ALU.mult, [b_S[0], b_ct[0]], [b_S[0]])
                tt(Yt, Yt, ct[1][:, :], ALU.add, [b_S[0], b_ct[1]], [b_S[0]])
                tt(Yt, Yt, BVt, ALU.add, [b_S[0]], [b_S[0]])
                tt(Yt, Yt, GGt, ALU.mult, [b_S[0], b_S[1]], [b_S[0]])
                OTb = slot(2).bitcast(BF16)[:, 0:1024].rearrange("p (c t) -> p c t", c=8)
                for g in range(2):
                    ps, pb = nps()
                    for j in range(4):
                        tr(ps[:, j * 128:(j + 1) * 128], Yt[:, (g * 4 + j) * 128:(g * 4 + j + 1) * 128], 128, [b_S[0]], pb)
                    act(OTb[:, g * 4:(g + 1) * 4, :].rearrange("p a b -> p (a b)"), ps[:, :], AF.Identity, [pb], [b_S[2]])
                P.dma("sp", oT_d.rearrange("(c p) t -> p c t", p=128)[:, :, rows], OTb, reads=[b_S[2]], writes=[b_oT])

            wpa = Wt["w_pa"].ap()[l].rearrange("(kc p) c -> p kc c", p=128)
            wpb = Wt["w_pb"].ap()[l].rearrange("(kc p) c -> p kc c", p=128)
            gav = ga_d.rearrange("(kc p) t -> p kc t", p=128)
            otv = oT_d.rearrange("(kc p) t -> p kc t", p=128)
            for c in range(16):
                ws, bw = wslot()
                wload(ws[:, 0:8, 0:128], wpa[:, :, c * 128:(c + 1) * 128], bw)
                wload(ws[:, 8:16, 0:128], wpb[:, :, c * 128:(c + 1) * 128], bw)
                for (t0, n) in TB:
                    gb = blk[:, 0:16 * 512].rearrange("p (kc t) -> p kc t", kc=16)
                    P.dma("sp", gb[:, 0:8, 0:n], gav[:, :, t0:t0 + n], reads=[b_ga], writes=[b_blk])
                    P.dma("act", gb[:, 8:16, 0:n], otv[:, :, t0:t0 + n], reads=[b_oT], writes=[b_blk])
                    sgt = blk[:, 16 * 512:18 * 512]
                    P.dma("sp", sgt[:, 0:n], sg[c * 128:(c + 1) * 128, t0:t0 + n], reads=[b_sg], writes=[b_blk])
                    P.dma("act", sgt[:, 512:512 + n], sg[D + c * 128:D + (c + 1) * 128, t0:t0 + n], reads=[b_sg], writes=[b_blk])
                    psa, pba = nps()
                    for kc in range(8):
                        mm(psa[:, 0:n], ws[:, kc, 0:128], gb[:, kc, 0:n], kc == 0, kc == 7, [bw, b_blk], pba)
                    psb, pbb = nps()
                    for kc in range(8):
                        mm(psb[:, 0:n], ws[:, 8 + kc, 0:128], gb[:, 8 + kc, 0:n], kc == 0, kc == 7, [bw, b_blk], pbb)
                    tt(tmpb[:, 0:n], psa[:, 0:n], sgt[:, 0:n], ALU.mult, [pba, b_blk], [b_tmpb])
                    tt(tmpc[:, 0:n], psb[:, 0:n], sgt[:, 512:512 + n], ALU.mult, [pbb, b_blk], [b_tmpc])
                    tt(actT[:, c, t0:t0 + n], tmpb[:, 0:n], tmpc[:, 0:n], ALU.add, [b_tmpb, b_tmpc], [b_act])
            wo = Wt["w_o"].ap()[l].rearrange("(kc p) c -> p kc c", p=128)
            for c in range(16):
                if c % 2 == 0:
                    ws, bw = wslot()
                    wload(ws[:, :, 0:256], wo[:, :, c * 128:(c + 2) * 128], bw)
                for (t0, n) in TB:
                    ps, pb = proj_block(ws, (c % 2) * 128, 128, t0, n, bw)
                    resid_update(c, t0, n, ps, pb, 32)

            make_A(64, "norm_ffn")
            norm_mod(A1, modT[:, 48:64, :], b_A1, b_mod)

            wup = Wt["w_up"].ap()[l].rearrange("(kc p) c -> p kc c", p=128)
            for j in range(44):
                if j % 8 == 0:
                    for hh in range(2):
                        pass
                ws, bw = wslot()
                wload(ws[:, :, 0:128], wup[:, :, j * 128:(j + 1) * 128], bw)
                wload(ws[:, :, 128:256], wup[:, :, DFF + j * 128:DFF + (j + 1) * 128], bw)
                outs = []
                for hh in range(2):
                    ch = hh * 44 + j
                    UU = slot(0 + hh * 2); UC = slot(1 + hh * 2)
                    bU = b_S[0 + hh * 2]; bC = b_S[1 + hh * 2]
                    UUs = UU[:, 2050:2050 + 160].rearrange("p (s j) -> p s j", s=NS)
                    memset(UU[:, 0:2], 0.0, [bU])
                    loadT(stfc[:, hh * 4:hh * 4 + 1, :], b_stfc,
                          lambda c0, w, ch=ch: I["st_fc"].ap()[l].rearrange("s j c -> (s j) c")[:, ch * 128 + c0:ch * 128 + c0 + w], 32, 1)
                    cp(UUs[:, :, 0:2], stfc[:, hh * 4, :].rearrange("p (s j) -> p s j", s=NS), [b_stfc], [bU], eng="pool")
                    for (t0, n) in TB:
                        ps, pb = proj_block(ws, hh * 128, 128, t0, n, bw)
                        if t0 < TP:
                            act(UU[:, 2 + t0:2 + t0 + n], ps[:, 0:n], AF.Identity, [pb], [bU])
                        else:
                            act(UUs[:, :, 2:10], ps[:, 0:n].rearrange("p (s t) -> p s t", s=NS), AF.Identity, [pb], [bU])
                    k8 = j % 8
                    if hh == 0 and k8 == 0:
                        pass
                    cp(ofc[:, hh, 0:2], UU[:, 2048:2050], [bU], [b_ofc], eng="pool")
                    cp(ofc[:, hh, 2:34].rearrange("p (s j) -> p s j", s=NS), UUs[:, :, 8:10], [bU], [b_ofc], eng="pool")
                    UCs = UC[:, TP:T].rearrange("p (s t) -> p s t", s=NS)
                    ts(UC[:, 0:TP], UU[:, 0:TP], pcol("ffn_conv_w", 0 * 88 + ch), pcol("ffn_conv_b", ch), ALU.mult, ALU.add, [bU, b_PT], [bC])
                    ts(UCs, UUs[:, :, 0:8], pcol("ffn_conv_w", 0 * 88 + ch), pcol("ffn_conv_b", ch), ALU.mult, ALU.add, [bU, b_PT], [bC])
                    for jj in range(1, 3):
                        stt(UC[:, 0:TP], UU[:, jj:jj + TP], pcol("ffn_conv_w", jj * 88 + ch), UC[:, 0:TP], ALU.mult, ALU.add, [bU, bC, b_PT], [bC])
                        stt(UCs, UUs[:, :, jj:jj + 8], pcol("ffn_conv_w", jj * 88 + ch), UCs, ALU.mult, ALU.add, [bU, bC, b_PT], [bC])
                    outs.append((UC, bC))
                    ofv = O["o_fc"].ap()[l].rearrange("s j c -> (s j) c")
                    flushT(ofc[:, hh:hh + 1, :], b_ofc, 1, 34, lambda c0, w, ch=ch: ofv[:, ch * 128 + c0:ch * 128 + c0 + w], b_out["o_fc"])
                (UG, bG), (UV, bV) = outs
                act(UG[:, 0:T], UG[:, 0:T], AF.Silu, [bG], [bG])
                ATb = slot(4).bitcast(BF16)[:, 0:T]
                tt(ATb, UG[:, 0:T], UV[:, 0:T], ALU.mult, [bG, bV], [b_S[4]])
                P.dma("sp", aT_d[j * 128:(j + 1) * 128, :], ATb, reads=[b_S[4]], writes=[b_aT])

            wdn = Wt["w_down"].ap()[l].rearrange("(kc p) c -> p kc c", p=128)
            atv = aT_d.rearrange("(kc p) t -> p kc t", p=128)
            for (t0, n) in TB:
                ab = blk[:, 0:22 * 512].rearrange("p (kc t) -> p kc t", kc=22)
                for cg in range(4):
                    banks = [nps() for _ in range(4)]
                    for hf in range(2):
                        P.dma("sp", ab[:, 0:11, 0:n], atv[:, hf * 22:hf * 22 + 11, t0:t0 + n], reads=[b_aT], writes=[b_blk])
                        P.dma("act", ab[:, 11:22, 0:n], atv[:, hf * 22 + 11:hf * 22 + 22, t0:t0 + n], reads=[b_aT], writes=[b_blk])
                        for ci in range(4):
                            c = cg * 4 + ci
                            ws, bw = wslot()
                            wload(ws[:, :, 0:128], wdn[:, hf * 22:hf * 22 + 16, c * 128:(c + 1) * 128], bw)
                            wload(ws[:, 0:6, 128:256], wdn[:, hf * 22 + 16:hf * 22 + 22, c * 128:(c + 1) * 128], bw)
                            ps, pb = banks[ci]
                            for k2 in range(22):
                                lhs = ws[:, k2, 0:128] if k2 < 16 else ws[:, k2 - 16, 128:256]
                                mm(ps[:, 0:n], lhs, ab[:, k2, 0:n], hf == 0 and k2 == 0, hf == 1 and k2 == 21, [bw, b_blk], pb)
                    for ci in range(4):
                        ps, pb = banks[ci]
                        resid_update(cg * 4 + ci, t0, n, ps, pb, 80)

        nf = Wt["norm_final"].ap().rearrange("(r c) -> r c", c=128)
        memset(pst[:], 0.0, [b_pst])
        P.dma("act", pst[0:16, :], nf, writes=[b_pst])
        ps, pb = nps()
        tr(ps[:, 0:128], pst[:, :], 128, [b_pst], pb)
        cp(PT[:, 0:128], ps[:, 0:128], [pb], [b_PT])
        for (t0, n) in TB:
            xb, rstd = norm_stats(t0, n)
            for kc in range(16):
                stt(xb[:, kc, :], xb[:, kc, :], PT[:, kc:kc + 1], rstd, ALU.mult, ALU.mult, b_S[0:5] + [b_PT], b_S[0:4])
            for tt_i in range(n // 128):
                for g in range(4):
                    ps, pb = nps()
                    for j in range(4):
                        kc = g * 4 + j
                        tr(ps[:, j * 128:(j + 1) * 128], xb[:, kc, tt_i * 128:(tt_i + 1) * 128], 128, b_S[0:4], pb)
                    act(rowst[:, 0:512], ps[:, :], AF.Identity, [pb], [b_rowst])
                    r0 = t0 + tt_i * 128
                    if t0 < TP:
                        P.dma("sp", O["y_p"].ap()[r0:r0 + 128, g * 512:(g + 1) * 512], rowst[:, 0:512], reads=[b_rowst], writes=[b_out["y_p"]])
                    else:
                        P.dma("sp", O["y_s"].ap()[:, g * 512:(g + 1) * 512], rowst[:, 0:512], reads=[b_rowst], writes=[b_out["y_s"]])

        P.final_wait("sp", list(b_out.values()))
        P.emit()
    return nc


_NC_CACHE = {}


def kernel(x_prompt, x_sample, c_prompt, c_sample, state_lru_conv, state_lru_h, state_rwkv_shift, state_rwkv_S,
           state_ffn_conv, **weights):
    f = lambda a: np.ascontiguousarray(np.asarray(a, dtype=np.float32))
    if "nc" not in _NC_CACHE:
        _NC_CACHE["nc"] = build()
    nc = _NC_CACHE["nc"]
    wmap = {k: f(weights[k]) for k in W_SHAPES}
    in_maps = []
    for core in range(8):
        b = core % 4
        ss = slice(core * NS, (core + 1) * NS)
        m = dict(wmap)
        m["xp"] = f(x_prompt[b])
        m["xs"] = f(np.asarray(x_sample)[ss].reshape(TS, D))
        m["cc"] = f(np.concatenate([np.asarray(c_prompt)[b:b + 1], np.asarray(c_sample)[ss]], axis=0))
        m["st_lc"] = f(np.asarray(state_lru_conv)[:, ss])
        m["st_lh"] = f(np.asarray(state_lru_h)[:, ss])
        m["st_sh"] = f(np.asarray(state_rwkv_shift)[:, ss])
        m["st_S"] = f(np.asarray(state_rwkv_S)[:, ss])
        m["st_fc"] = f(np.asarray(state_ffn_conv)[:, ss])
        in_maps.append(m)
    res = run_bass_kernel_spmd(nc, in_maps, core_ids=list(range(8)))
    R = res.results
    y_prompt = np.stack([R[b]["y_p"] for b in range(4)], axis=0)
    y_sample = np.concatenate([R[c]["y_s"].reshape(NS, 8, D) for c in range(8)], axis=0)
    outs = [y_prompt, y_sample]
    names = ["o_lc", "o_lh", "o_sh", "o_S", "o_fc"]
    for nm in names:
        outs.append(np.stack([R[b][nm][:, 0] for b in range(4)], axis=1))
    for nm in names:
        outs.append(np.concatenate([R[c][nm][:, 1:] for c in range(8)], axis=1))
    return tuple(np.ascontiguousarray(o.astype(np.float32, copy=False)) for o in outs)
```

```python
import math
import numpy as np
from contextlib import ExitStack
import concourse.bass as bass
import concourse.mybir as mybir
from concourse.bass_utils import run_bass_kernel_spmd

F32 = mybir.dt.float32
BF16 = mybir.dt.bfloat16
ALU = mybir.AluOpType
AF = mybir.ActivationFunctionType
AX = mybir.AxisListType

ENGS = ("pe", "act", "dve", "pool", "sp")

D = 2048
TP = 2048
NS = 16
TS = 128
T = TP + TS
NSEQ = 17
DL = 1024
NRW = 3360
NIN = 9504
DFF = 5632
DEPTH = 4
TB = [(0, 512), (512, 512), (1024, 512), (1536, 512), (2048, 128)]
SLOT = 2240
NLAYERS = DEPTH


class Buf:
    def __init__(self, prog, name):
        self.prog = prog
        self.name = name
        self.w = {}
        self.r = {}
        self.dsem = None
        self.dcnt = 0

    def dma_sem(self):
        if self.dsem is None:
            self.dsem = self.prog.new_sem("d_" + self.name)
        return self.dsem


class Prog:
    def __init__(self, nc, stack):
        self.nc = nc
        self.stack = stack
        self.streams = {e: [] for e in ENGS}
        self.esem = {e: self.new_sem("e_" + e) for e in ENGS}
        self.ecnt = {e: 0 for e in ENGS}
        self.seen = {e: {} for e in ENGS}
        self.nbuf = 0

    def new_sem(self, name):
        return self.stack.enter_context(self.nc.semaphore(name))

    def buf(self, name=None):
        self.nbuf += 1
        return Buf(self, name or f"b{self.nbuf}")

    def sbuf(self, name, shape, dtype):
        return self.stack.enter_context(self.nc.sbuf_tensor(name, list(shape), dtype))

    def psum(self, name, shape, dtype=F32):
        return self.stack.enter_context(self.nc.psum_tensor(name, list(shape), dtype))

    def _deps(self, eng, reads, writes):
        need = {}

        def add(tok):
            k = id(tok[0])
            if k not in need or need[k][1] < tok[1]:
                need[k] = tok

        for b in reads:
            for tok in b.w.values():
                add(tok)
        for b in writes:
            for tok in b.w.values():
                add(tok)
            for tok in b.r.values():
                add(tok)
        waits = []
        seen = self.seen[eng]
        for k, (sem, val) in need.items():
            if seen.get(k, 0) < val:
                seen[k] = val
                waits.append((sem, val))
        return waits

    def _mark(self, tok, reads, writes):
        k = id(tok[0])
        for b in reads:
            if b in writes:
                continue
            b.r[k] = tok
        for b in writes:
            b.w = {k: tok}
            b.r = {}

    def op(self, eng, fn, reads=(), writes=()):
        waits = self._deps(eng, reads, writes)
        self.ecnt[eng] += 1
        sem = self.esem[eng]
        self._mark((sem, self.ecnt[eng]), reads, writes)

        def run(e, waits=waits, fn=fn, sem=sem):
            for s, v in waits:
                e.wait_ge(s, v)
            fn(e).then_inc(sem, 1)

        self.streams[eng].append(run)

    def dma(self, q, out, in_, reads=(), writes=(), **kw):
        waits = self._deps(q, reads, writes)
        wb = writes[0]
        sem = wb.dma_sem()
        wb.dcnt += 16
        self._mark((sem, wb.dcnt), reads, writes)

        def run(e, waits=waits, sem=sem, out=out, in_=in_, kw=kw):
            for s, v in waits:
                e.wait_ge(s, v)
            e.dma_start(out=out, in_=in_, **kw).then_inc(sem, 16)

        self.streams[q].append(run)

    def final_wait(self, eng, bufs):
        waits = self._deps(eng, bufs, bufs)

        def run(e, waits=waits):
            for s, v in waits:
                e.wait_ge(s, v)

        self.streams[eng].append(run)

    def emit(self):
        nc = self.nc
        with nc.Block() as block:
            @block.tensor
            def _(e):
                for f in self.streams["pe"]:
                    f(e)

            @block.scalar
            def _(e):
                for f in self.streams["act"]:
                    f(e)

            @block.vector
            def _(e):
                for f in self.streams["dve"]:
                    f(e)

            @block.gpsimd
            def _(e):
                for f in self.streams["pool"]:
                    f(e)

            @block.sync
            def _(e):
                for f in self.streams["sp"]:
                    f(e)


W_SHAPES = {
    "w_ada": [DEPTH, D, 6 * D], "b_ada": [DEPTH, 6 * D], "norm_mix": [DEPTH, D], "norm_ffn": [DEPTH, D],
    "w_in": [DEPTH, D, NIN], "lru_conv_w": [DEPTH, 4, DL], "lru_conv_b": [DEPTH, DL],
    "lru_wa": [DEPTH, 16, 64, 64], "lru_ba": [DEPTH, DL], "lru_wi": [DEPTH, 16, 64, 64], "lru_bi": [DEPTH, DL],
    "lru_lambda": [DEPTH, DL], "w_pa": [DEPTH, DL, D], "rw_mu": [DEPTH, NRW], "rw_w0": [DEPTH, DL],
    "rw_w2": [DEPTH, 64, DL], "rw_a0": [DEPTH, DL], "rw_a2": [DEPTH, 64, DL], "rw_g2": [DEPTH, 160, DL],
    "rw_kk": [DEPTH, DL], "rw_ka": [DEPTH, DL], "rw_rk": [DEPTH, 16, 64], "rw_gn_g": [DEPTH, DL],
    "rw_gn_b": [DEPTH, DL], "w_pb": [DEPTH, DL, D], "w_o": [DEPTH, D, D], "w_up": [DEPTH, D, 2 * DFF],
    "ffn_conv_w": [DEPTH, 3, 2 * DFF], "ffn_conv_b": [DEPTH, 2 * DFF], "w_down": [DEPTH, DFF, D],
    "norm_final": [D],
}
IN_SHAPES = {
    "xp": [TP, D], "xs": [TS, D], "cc": [NSEQ, D], "st_lc": [DEPTH, NS, 3, DL], "st_lh": [DEPTH, NS, DL],
    "st_sh": [DEPTH, NS, NRW], "st_S": [DEPTH, NS, 16, 64, 64], "st_fc": [DEPTH, NS, 2, 2 * DFF],
}
OUT_SHAPES = {
    "y_p": [TP, D], "y_s": [TS, D], "o_lc": [DEPTH, NSEQ, 3, DL], "o_lh": [DEPTH, NSEQ, DL],
    "o_sh": [DEPTH, NSEQ, NRW], "o_S": [DEPTH, NSEQ, 16, 64, 64], "o_fc": [DEPTH, NSEQ, 2, 2 * DFF],
}

PROWS = [("b_ada", 96), ("norm_mix", 16), ("norm_ffn", 16), ("lru_conv_w", 32), ("lru_conv_b", 8),
         ("lru_ba", 8), ("lru_bi", 8), ("lru_lambda", 8), ("rw_mu", 27), ("rw_w0", 8), ("rw_a0", 8),
         ("ffn_conv_w", 264), ("ffn_conv_b", 88)]
PCOL = {}
_c = 0
for _n, _r in PROWS:
    PCOL[_n] = _c
    _c += _r
NPROW = _c
NPG = (NPROW + 127) // 128


def build():
    nc = bass.Bass("TRN2", target_bir_lowering=False)
    I = {k: nc.dram_tensor(k, s, F32, kind="ExternalInput") for k, s in IN_SHAPES.items()}
    Wt = {k: nc.dram_tensor(k, s, F32, kind="ExternalInput") for k, s in W_SHAPES.items()}
    O = {k: nc.dram_tensor(k, s, F32, kind="ExternalOutput") for k, s in OUT_SHAPES.items()}
    xT_h = nc.dram_tensor("xT_scr", [D, T], F32, kind="Internal")
    rq_h = nc.dram_tensor("rq_scr", [6, 16, T, 64], F32, kind="Internal")
    tk_h = nc.dram_tensor("tk_scr", [8, T, DL], F32, kind="Internal")
    cc_h = nc.dram_tensor("cc_scr", [32, T], F32, kind="Internal")
    sg_h = nc.dram_tensor("sg_scr", [2 * D, T], BF16, kind="Internal")
    ga_h = nc.dram_tensor("ga_scr", [DL, T], BF16, kind="Internal")
    oT_h = nc.dram_tensor("oT_scr", [DL, T], BF16, kind="Internal")
    aT_h = nc.dram_tensor("aT_scr", [DFF, T], BF16, kind="Internal")
    xT = xT_h.ap()
    tk = tk_h.ap()
    sg = sg_h.ap()
    ga_d = ga_h.ap()
    oT_d = oT_h.ap()
    aT_d = aT_h.ap()

    with ExitStack() as st:
        P = Prog(nc, st)
        actT = P.sbuf("actT", [128, 16, T], BF16); b_act = P.buf("actT")
        arena = P.sbuf("arena", [128, 5, SLOT], F32)
        b_S = [P.buf(f"S{i}") for i in range(5)]
        wsl = [P.sbuf(f"wsl{i}", [128, 16, 256], BF16) for i in range(2)]
        b_w = [P.buf(f"w{i}") for i in range(2)]
        blk = P.sbuf("blk", [128, 16384], BF16); b_blk = P.buf("blk")
        blkF = blk[:, :].bitcast(F32)
        ident = P.sbuf("ident", [128, 128], F32); b_id = P.buf("ident")
        ones = P.sbuf("ones", [128, 128], F32); b_ones = P.buf("ones")
        scT = P.sbuf("scT", [128, 16, NSEQ], BF16); b_scT = P.buf("scT")
        modT = P.sbuf("modT", [128, 96, NSEQ], F32); b_mod = P.buf("modT")
        A1 = P.sbuf("A1", [128, 16, NSEQ], F32); b_A1 = P.buf("A1")
        PT = P.sbuf("PT", [128, NPG * 128], F32); b_PT = P.buf("PT")
        pst = P.sbuf("pst", [128, 128], F32); b_pst = P.buf("pst")
        cA = P.sbuf("cA", [128, 16], F32); b_cA = P.buf("cA")
        bda = P.sbuf("bda", [128, 8, 128], BF16); b_bda = P.buf("bda")
        bdi = P.sbuf("bdi", [128, 8, 128], BF16); b_bdi = P.buf("bdi")
        lw2 = P.sbuf("lw2", [128, DL], BF16); b_lw2 = P.buf("lw2")
        g2a = P.sbuf("g2a", [128, DL], BF16); b_g2a = P.buf("g2a")
        g2b = P.sbuf("g2b", [32, DL], BF16); b_g2b = P.buf("g2b")
        stlc = P.sbuf("stlc", [128, 8, 48], F32); b_stlc = P.buf("stlc")
        stlh = P.sbuf("stlh", [128, 8, 16], F32); b_stlh = P.buf("stlh")
        stsh = P.sbuf("stsh", [128, 27, 16], F32); b_stsh = P.buf("stsh")
        olc = P.sbuf("olc", [128, 8, 51], F32); b_olc = P.buf("olc")
        olh = P.sbuf("olh", [128, 8, NSEQ], F32); b_olh = P.buf("olh")
        osh = P.sbuf("osh", [128, 27, NSEQ], F32); b_osh = P.buf("osh")
        ofc = P.sbuf("ofc", [128, 8, 34], F32); b_ofc = P.buf("ofc")
        stfc = P.sbuf("stfc", [128, 8, 32], F32); b_stfc = P.buf("stfc")
        tmpb = P.sbuf("tmpb", [128, 512], F32); b_tmpb = P.buf("tmpb")
        tmpc = P.sbuf("tmpc", [128, 512], F32); b_tmpc = P.buf("tmpc")
        rowst = blkF[:, 7168:8192]; b_rowst = P.buf("rowst")
        Sst = blkF[:, 0:2048]; b_Sst = b_blk
        Stmp = blkF[:, 2048:4096]; b_Stmp = b_blk
        skk = P.sbuf("skk", [128, 32], F32); b_skk = P.buf("skk")
        sgt = P.sbuf("sgt", [128, 2, 1024], BF16); b_sgt = [P.buf("sgt0"), P.buf("sgt1")]
        skys = P.sbuf("skys", [128, 2 * 2 * 16 * 8], F32); b_skys = [P.buf("skys0"), P.buf("skys1")]
        ccb = P.sbuf("ccb", [128, 2 * 32], F32); b_ccb = [P.buf("ccb0"), P.buf("ccb1")]
        dummy = P.sbuf("fence_dummy", [128, 8], F32)
        b_cc = P.buf("cc")
        vbuf = P.sbuf("vbuf", [128, 32 * 8], F32); b_vbuf = P.buf("vbuf")
        ybuf = P.sbuf("ybuf", [128, 32 * 8], F32); b_ybuf = P.buf("ybuf")
        sm = P.sbuf("sm", [128, 64], F32); b_sm = P.buf("sm")
        ct = [blkF[:, 4096 + i * 1024:4096 + (i + 1) * 1024] for i in range(3)]
        b_ct = [b_blk, b_blk, b_blk]
        PS = [P.psum(f"ps{i}", [128, 512]) for i in range(8)]
        b_ps = [P.buf(f"ps{i}") for i in range(8)]
        psc = [0]

        def nps():
            i = psc[0] % 8
            psc[0] += 1
            return PS[i], b_ps[i]

        b_xT = [P.buf(f"xT{c}") for c in range(16)]
        b_rq = P.buf("rq"); b_tk = [P.buf(f"tk{i}") for i in range(8)]
        b_sg = P.buf("sg"); b_ga = P.buf("ga"); b_oT = P.buf("oT"); b_aT = P.buf("aT")
        b_out = {k: P.buf("o_" + k) for k in OUT_SHAPES}

        def slot(i, n=SLOT, off=0):
            return arena[:, i, off:off + n]

        wcnt = [0]

        def wslot():
            i = wcnt[0] % 2
            wcnt[0] += 1
            return wsl[i], b_w[i]

        def act(out, in_, func, reads, writes, bias=None, scale=None):
            kw = {}
            if bias is not None:
                kw["bias"] = bias
            if scale is not None:
                kw["scale"] = scale
            P.op("act", lambda e: e.activation(out=out, in_=in_, func=func, **kw), reads=reads, writes=writes)

        def tt(out, in0, in1, op, reads, writes, eng="dve"):
            P.op(eng, lambda e: e.tensor_tensor(out=out, in0=in0, in1=in1, op=op), reads=reads, writes=writes)

        def ts(out, in0, s1, s2, op0, op1, reads, writes, eng="dve"):
            if s2 is None:
                P.op(eng, lambda e: e.tensor_scalar(out=out, in0=in0, scalar1=s1, scalar2=None, op0=op0), reads=reads, writes=writes)
            else:
                P.op(eng, lambda e: e.tensor_scalar(out=out, in0=in0, scalar1=s1, scalar2=s2, op0=op0, op1=op1), reads=reads, writes=writes)

        def stt(out, in0, scalar, in1, op0, op1, reads, writes):
            P.op("dve", lambda e: e.scalar_tensor_tensor(out=out, in0=in0, scalar=scalar, in1=in1, op0=op0, op1=op1),
                 reads=reads, writes=writes)

        def cp(out, in_, reads, writes, eng="dve"):
            P.op(eng, lambda e: e.tensor_copy(out=out, in_=in_), reads=reads, writes=writes)

        def mm(ps_ap, lhsT, rhs, start, stop, reads, pb):
            P.op("pe", lambda e: e.matmul(ps_ap, lhsT=lhsT, rhs=rhs, start=start, stop=stop), reads=reads, writes=[pb])

        def tr(ps_ap, in_, n_in_part, reads, pb):
            P.op("pe", lambda e: e.transpose(ps_ap, in_, ident[0:n_in_part, 0:n_in_part]), reads=list(reads) + [b_id], writes=[pb])

        def memset(ap, val, writes, eng="pool"):
            P.op(eng, lambda e: e.memset(ap, val), writes=writes)

        def fence(bufs, eng="pool"):
            P.op(eng, lambda e: e.memset(dummy[:, 0:1], 0.0), writes=list(bufs))

        memset(ident[:], 0.0, [b_id])
        P.op("pool", lambda e: e.affine_select(out=ident[:], in_=ident[:], compare_op=ALU.not_equal, fill=1.0,
                                               base=0, pattern=[[-1, 128]], channel_multiplier=1),
             reads=[b_id], writes=[b_id])
        memset(ones[:], 1.0, [b_ones])

        P.dma("sp", rowst[0:NSEQ, :], I["cc"].ap()[:, 0:1024], writes=[b_rowst])
        for half in range(2):
            if half == 1:
                P.dma("sp", rowst[0:NSEQ, :], I["cc"].ap()[:, 1024:2048], writes=[b_rowst])
            act(rowst[0:NSEQ, :], rowst[0:NSEQ, :], AF.Silu, [b_rowst], [b_rowst])
            ps, pb = nps()
            for j in range(8):
                tr(ps[:, j * NSEQ:(j + 1) * NSEQ], rowst[0:NSEQ, j * 128:(j + 1) * 128], NSEQ, [b_rowst], pb)
            cp(scT[:, half * 8:(half + 1) * 8, :].rearrange("p a b -> p (a b)"), ps[:, 0:8 * NSEQ], [pb], [b_scT])

        xT_v = xT.rearrange("(kc p) t -> p kc t", p=128)
        for i in range(17):
            src = I["xp"].ap()[i * 128:(i + 1) * 128, :] if i < 16 else I["xs"].ap()[:, :]
            P.dma("sp", slot(0, 2048), src, writes=[b_S[0]])
            for g in range(4):
                ps, pb = nps()
                for j in range(4):
                    kc = g * 4 + j
                    tr(ps[:, j * 128:(j + 1) * 128], slot(0, 128, kc * 128), 128, [b_S[0]], pb)
                P.op("act", lambda e, ps=ps, g=g: e.activation(out=slot(1, 512, g * 512), in_=ps[:, :], func=AF.Identity),
                     reads=[pb], writes=[b_S[1]])
            P.dma("sp", xT_v[:, :, i * 128:(i + 1) * 128], slot(1, 2048).rearrange("p (kc t) -> p kc t", kc=16),
                  reads=[b_S[1]], writes=b_xT)

        def load_params(l):
            for g in range(NPG):
                memset(pst[:], 0.0, [b_pst])
                r = 0
                for name, nrows in PROWS:
                    for rr in range(nrows):
                        grow = PCOL[name] + rr
                        if grow // 128 != g:
                            continue
                    lo = max(PCOL[name], g * 128)
                    hi = min(PCOL[name] + nrows, (g + 1) * 128)
                    if lo >= hi:
                        continue
                    r0 = lo - PCOL[name]
                    n = hi - lo
                    flat = Wt[name].ap()[l]
                    if name in ("lru_conv_w", "ffn_conv_w"):
                        flat = flat.rearrange("j c -> (j c)")
                    if name == "rw_mu":
                        nfull = min(n, max(0, 26 - r0))
                        if nfull > 0:
                            P.dma("act", pst[lo - g * 128:lo - g * 128 + nfull, :],
                                  flat[r0 * 128:(r0 + nfull) * 128].rearrange("(r c) -> r c", c=128), writes=[b_pst])
                        if r0 + n == 27:
                            P.dma("act", pst[hi - 1 - g * 128:hi - g * 128, 0:32],
                                  flat[26 * 128:26 * 128 + 32].rearrange("(r c) -> r c", c=32), writes=[b_pst])
                    else:
                        P.dma("act", pst[lo - g * 128:hi - g * 128, :],
                              flat[r0 * 128:(r0 + n) * 128].rearrange("(r c) -> r c", c=128), writes=[b_pst])
                ps, pb = nps()
                tr(ps[:, 0:128], pst[:, :], 128, [b_pst], pb)
                cp(PT[:, g * 128:(g + 1) * 128], ps[:, 0:128], [pb], [b_PT])

        def pcol(name, row):
            c = PCOL[name] + row
            return PT[:, c:c + 1]

        def loadT(dst3, bdst, src_rows_fn, nrows, nchunks, last_cols=128):
            c = 0
            while c < nchunks:
                g = min(8, nchunks - c)
                width = (g - 1) * 128 + (last_cols if c + g == nchunks else 128)
                P.dma("sp", rowst[0:nrows, 0:width], src_rows_fn(c * 128, width), writes=[b_rowst])
                per = 512 // nrows
                j = 0
                while j < g:
                    gg = min(per, g - j)
                    ps, pb = nps()
                    for k in range(gg):
                        cc_ = c + j + k
                        ncol = last_cols if cc_ == nchunks - 1 else 128
                        tr(ps[0:ncol, k * nrows:(k + 1) * nrows], rowst[0:nrows, (j + k) * 128:(j + k) * 128 + ncol], nrows, [b_rowst], pb)
                    cp(dst3[:, c + j:c + j + gg, :].rearrange("p a b -> p (a b)"), ps[:, 0:gg * nrows], [pb], [bdst])
                    j += gg
                c += g

        def flushT(src3, bsrc, nchunks, n, dst_fn, bout, last_cols=128):
            c = 0
            while c < nchunks:
                g = min(4, nchunks - c)
                ps, pb = nps()
                width = 0
                for k in range(g):
                    ncol = last_cols if c + k == nchunks - 1 else 128
                    tr(ps[0:n, k * 128:k * 128 + ncol], src3[0:ncol, c + k, :], ncol, [bsrc], pb)
                    width += ncol
                cp(rowst[0:n, 0:width], ps[0:n, 0:width], [pb], [b_rowst])
                P.dma("sp", dst_fn(c * 128, width), rowst[0:n, 0:width], reads=[b_rowst], writes=[bout])
                c += g

        def wload(dst, src, bw):
            P.dma("pool", dst, src, writes=[bw])

        def norm_stats(t0, n):
            xb = arena[:, 0:4, :].rearrange("p a b -> p (a b)")[:, 0:16 * n].rearrange("p (kc t) -> p kc t", kc=16)
            P.dma("sp", xb, xT_v[:, :, t0:t0 + n], reads=b_xT, writes=b_S[0:4])
            ps, pb = nps()
            for kc in range(16):
                sqb, bsq = (tmpb, b_tmpb) if kc % 2 == 0 else (tmpc, b_tmpc)
                act(sqb[:, 0:n], xb[:, kc, :], AF.Square, b_S[0:4], [bsq])
                mm(ps[:, 0:n], ones[:], sqb[:, 0:n], kc == 0, kc == 15, [b_ones, bsq], pb)
            rstd = slot(4, n)
            act(rstd, ps[:, 0:n], AF.Sqrt, [pb], [b_S[4]], bias=1e-6, scale=1.0 / D)
            P.op("dve", lambda e: e.reciprocal(out=rstd, in_=rstd), reads=[b_S[4]], writes=[b_S[4]])
            return xb, rstd

        def norm_mod(Aap, Bap, bA, bB):
            for (t0, n) in TB:
                xb, rstd = norm_stats(t0, n)
                for kc in range(16):
                    tt(xb[:, kc, :], xb[:, kc, :], rstd, ALU.mult, b_S[0:5], b_S[0:4])
                    if t0 < TP:
                        ts(actT[:, kc, t0:t0 + n], xb[:, kc, :], Aap[:, kc, 0:1], Bap[:, kc, 0:1], ALU.mult, ALU.add,
                           b_S[0:4] + [bA, bB], [b_act])
                    else:
                        x3 = xb[:, kc, :].rearrange("p (s t) -> p s t", s=NS)
                        tt(x3, x3, Aap[:, kc, 1:NSEQ].unsqueeze(2).to_broadcast([128, NS, 8]), ALU.mult, b_S[0:4] + [bA], b_S[0:4])
                        tt(actT[:, kc, t0:t0 + n].rearrange("p (s t) -> p s t", s=NS), x3,
                           Bap[:, kc, 1:NSEQ].unsqueeze(2).to_broadcast([128, NS, 8]), ALU.add, b_S[0:4] + [bB], [b_act])

        def resid_update(c, t0, n, ps, pb, gcol0):
            xs_ = tmpb[:, 0:n]
            P.dma("sp", xs_, xT_v[:, c, t0:t0 + n], reads=[b_xT[c]], writes=[b_tmpb])
            if t0 < TP:
                stt(xs_, ps[:, 0:n], modT[:, gcol0 + c, 0:1], xs_, ALU.mult, ALU.add, [pb, b_mod, b_tmpb], [b_tmpb])
            else:
                g3 = modT[:, gcol0 + c, 1:NSEQ].unsqueeze(2).to_broadcast([128, NS, 8])
                t3 = tmpc[:, 0:n].rearrange("p (s t) -> p s t", s=NS)
                tt(t3, ps[:, 0:n].rearrange("p (s t) -> p s t", s=NS), g3, ALU.mult, [pb, b_mod], [b_tmpc])
                tt(xs_, xs_, tmpc[:, 0:n], ALU.add, [b_tmpb, b_tmpc], [b_tmpb])
            P.dma("sp", xT_v[:, c, t0:t0 + n], xs_, reads=[b_tmpb], writes=[b_xT[c]])

        def to_tokmajor(src_slot_ap, bsrc, qi, c0):
            tkv = tk[qi].rearrange("(tt p) c -> p tt c", p=128)
            for g0 in range(0, 17, 4):
                g = min(4, 17 - g0)
                ps, pb = nps()
                for k in range(g):
                    tr(ps[:, k * 128:(k + 1) * 128], src_slot_ap[:, (g0 + k) * 128:(g0 + k + 1) * 128], 128, [bsrc], pb)
                act(rowst[:, 0:g * 128], ps[:, 0:g * 128], AF.Identity, [pb], [b_rowst])
                P.dma("sp", tkv[:, g0:g0 + g, c0:c0 + 128], rowst[:, 0:g * 128].rearrange("p (a b) -> p a b", a=g),
                      reads=[b_rowst], writes=[b_tk[qi]])

        bx = [[P.buf(f"x_{j}_{p}") for p in range(2)] for j in range(5)]
        bv = [P.buf(f"v_{p}") for p in range(2)]
        by = [P.buf(f"y_{p}") for p in range(2)]
        bSS = [P.buf("SA"), P.buf("SB")]
        bT2 = P.buf("T2"); bTt = P.buf("Tt")
        bKV = [P.buf(f"KV_{i}") for i in range(4)]
        b_ab1 = P.buf("ab1")
        for l in range(NLAYERS):
            load_params(l)
            memset(bda[:], 0.0, [b_bda]); memset(bdi[:], 0.0, [b_bdi])
            for (dst, bd_, nm) in ((bda, b_bda, "lru_wa"), (bdi, b_bdi, "lru_wi")):
                wv = Wt[nm].ap()[l].rearrange("(c two) d e -> two d c e", two=2)
                P.dma("pool", dst[0:64, :, 0:64], wv[0], writes=[bd_])
                P.dma("pool", dst[64:128, :, 64:128], wv[1], writes=[bd_])
            P.dma("pool", lw2[0:64, :], Wt["rw_w2"].ap()[l], writes=[b_lw2])
            P.dma("pool", lw2[64:128, :], Wt["rw_a2"].ap()[l], writes=[b_lw2])
            P.dma("pool", g2a[:, :], Wt["rw_g2"].ap()[l, 0:128, :], writes=[b_g2a])
            P.dma("pool", g2b[:, :], Wt["rw_g2"].ap()[l, 128:160, :], writes=[b_g2b])
            lam = PT[:, PCOL["lru_lambda"]:PCOL["lru_lambda"] + 8]
            act(cA[:, 0:8], lam, AF.Exp, [b_PT], [b_cA], scale=-1.0)
            act(cA[:, 0:8], cA[:, 0:8], AF.Ln, [b_cA], [b_cA], bias=1.0)
            ts(cA[:, 8:16], cA[:, 0:8], -16.0, None, ALU.mult, None, [b_cA], [b_cA])
            ts(cA[:, 0:8], cA[:, 0:8], -8.0, None, ALU.mult, None, [b_cA], [b_cA])
            loadT(stlc, b_stlc, lambda c0, w: I["st_lc"].ap()[l].rearrange("s j c -> (s j) c")[:, c0:c0 + w], 48, 8)
            loadT(stlh, b_stlh, lambda c0, w: I["st_lh"].ap()[l][:, c0:c0 + w], 16, 8)
            loadT(stsh, b_stsh, lambda c0, w: I["st_sh"].ap()[l][:, c0:c0 + w], 16, 27, last_cols=32)

            for blk_i in range(48):
                ws, bw = wslot()
                wload(ws[:, :, 0:256], Wt["w_ada"].ap()[l].rearrange("(kc p) c -> p kc c", p=128)[:, :, blk_i * 256:(blk_i + 1) * 256], bw)
                ps, pb = nps()
                for j2 in range(2):
                    for kc in range(16):
                        mm(ps[:, j2 * 32:j2 * 32 + NSEQ], ws[:, kc, j2 * 128:(j2 + 1) * 128], scT[:, kc, :], kc == 0, kc == 15,
                           [bw, b_scT], pb)
                for j2 in range(2):
                    j = blk_i * 2 + j2
                    act(modT[:, j, :], ps[:, j2 * 32:j2 * 32 + NSEQ], AF.Identity, [pb, b_PT], [b_mod], bias=pcol("b_ada", j))

            def make_A(sc0, nm_name):
                nmb = PT[:, PCOL[nm_name]:PCOL[nm_name] + 16].unsqueeze(2).to_broadcast([128, 16, NSEQ])
                stt(A1[:, :, :], modT[:, sc0:sc0 + 16, :], 1.0, nmb, ALU.add, ALU.mult, [b_mod, b_PT], [b_A1])

            make_A(16, "norm_mix")
            norm_mod(A1, modT[:, 0:16, :], b_A1, b_mod)

            win = Wt["w_in"].ap()[l].rearrange("(kc p) c -> p kc c", p=128)

            def proj_block(ws, wc0, m, t0, n, bw):
                ps, pb = nps()
                for kc in range(16):
                    mm(ps[0:m, 0:n], ws[:, kc, wc0:wc0 + m], actT[:, kc, t0:t0 + n], kc == 0, kc == 15, [bw, b_act], pb)
                return ps, pb

            for c in range(8):
                ws, bw = wslot()
                wload(ws[:, :, 0:128], win[:, :, c * 128:(c + 1) * 128], bw)
                wload(ws[:, :, 128:256], win[:, :, DL + c * 128:DL + (c + 1) * 128], bw)
                LX = slot(0); XC = slot(1); AA = slot(3); INP = slot(4)
                LXs = LX[:, 2051:2051 + 176].rearrange("p (s j) -> p s j", s=NS)
                memset(LX[:, 0:3], 0.0, [b_S[0]])
                cp(LXs[:, :, 0:3], stlc[:, c, :].rearrange("p (s j) -> p s j", s=NS), [b_stlc], [b_S[0]], eng="pool")
                for (t0, n) in TB:
                    ps, pb = proj_block(ws, 0, 128, t0, n, bw)
                    if t0 < TP:
                        act(LX[:, 3 + t0:3 + t0 + n], ps[:, 0:n], AF.Identity, [pb], [b_S[0]])
                    else:
                        act(LXs[:, :, 3:11], ps[:, 0:n].rearrange("p (s t) -> p s t", s=NS), AF.Identity, [pb], [b_S[0]])
                cp(olc[:, c, 0:3], LX[:, 2048:2051], [b_S[0]], [b_olc], eng="pool")
                cp(olc[:, c, 3:51].rearrange("p (s j) -> p s j", s=NS), LXs[:, :, 8:11], [b_S[0]], [b_olc], eng="pool")
                XCs = XC[:, TP:T].rearrange("p (s t) -> p s t", s=NS)
                ts(XC[:, 0:TP], LX[:, 0:TP], pcol("lru_conv_w", 0 * 8 + c), pcol("lru_conv_b", c), ALU.mult, ALU.add,
                   [b_S[0], b_PT], [b_S[1]])
                ts(XCs, LXs[:, :, 0:8], pcol("lru_conv_w", 0 * 8 + c), pcol("lru_conv_b", c), ALU.mult, ALU.add,
                   [b_S[0], b_PT], [b_S[1]])
                for j in range(1, 4):
                    stt(XC[:, 0:TP], LX[:, j:j + TP], pcol("lru_conv_w", j * 8 + c), XC[:, 0:TP], ALU.mult, ALU.add,
                        [b_S[0], b_S[1], b_PT], [b_S[1]])
                    stt(XCs, LXs[:, :, j:j + 8], pcol("lru_conv_w", j * 8 + c), XCs, ALU.mult, ALU.add,
                        [b_S[0], b_S[1], b_PT], [b_S[1]])
                XCb = slot(2).bitcast(BF16)[:, 0:T]
                act(XCb, XC[:, 0:T], AF.Identity, [b_S[1]], [b_S[2]])
                for (t0, n) in TB:
                    psr, pbr = nps()
                    mm(psr[:, 0:n], bda[:, c, :], XCb[:, t0:t0 + n], True, True, [b_bda, b_S[2]], pbr)
                    psi, pbi = nps()
                    mm(psi[:, 0:n], bdi[:, c, :], XCb[:, t0:t0 + n], True, True, [b_bdi, b_S[2]], pbi)
                    rr = INP[:, t0:t0 + n]
                    act(rr, psr[:, 0:n], AF.Sigmoid, [pbr, b_PT], [b_S[4]], bias=pcol("lru_ba", c))
                    act(AA[:, t0:t0 + n], rr, AF.Exp, [b_S[4], b_cA], [b_S[3]], scale=cA[:, c:c + 1])
                    act(rr, rr, AF.Exp, [b_S[4], b_cA], [b_S[4]], scale=cA[:, 8 + c:9 + c])
                    act(rr, rr, AF.Sqrt, [b_S[4]], [b_S[4]], scale=-1.0, bias=1.0)
                    act(tmpb[:, 0:n], psi[:, 0:n], AF.Sigmoid, [pbi, b_PT], [b_tmpb], bias=pcol("lru_bi", c))
                    tt(rr, rr, tmpb[:, 0:n], ALU.mult, [b_S[4], b_tmpb], [b_S[4]])
                    tt(rr, rr, XC[:, t0:t0 + n], ALU.mult, [b_S[4], b_S[1]], [b_S[4]])
                AAs = AA[:, TP:T].rearrange("p (s t) -> p s t", s=NS)
                INs = INP[:, TP:T].rearrange("p (s t) -> p s t", s=NS)
                tt(sm[:, 0:NS], AAs[:, :, 0], stlh[:, c, :], ALU.mult, [b_S[3], b_stlh], [b_sm])
                tt(INs[:, :, 0], INs[:, :, 0], sm[:, 0:NS], ALU.add, [b_S[4], b_sm], [b_S[4]])
                memset(AAs[:, :, 0], 0.0, [b_S[3]], eng="dve")
                HL = slot(1)
                P.op("dve", lambda e, HL=HL, AA=AA, INP=INP: e.tensor_tensor_scan(out=HL[:, 0:TP], data0=AA[:, 0:TP], data1=INP[:, 0:TP],
                                                                           initial=0.0, op0=ALU.mult, op1=ALU.add),
                     reads=[b_S[3], b_S[4]], writes=[b_S[1]])
                P.op("dve", lambda e, HL=HL, AA=AA, INP=INP: e.tensor_tensor_scan(out=HL[:, TP:T], data0=AA[:, TP:T], data1=INP[:, TP:T],
                                                                           initial=0.0, op0=ALU.mult, op1=ALU.add),
                     reads=[b_S[3], b_S[4]], writes=[b_S[1]])
                cp(olh[:, c, 0:1], HL[:, TP - 1:TP], [b_S[1]], [b_olh], eng="pool")
                cp(olh[:, c, 1:NSEQ], HL[:, TP:T].rearrange("p (s t) -> p s t", s=NS)[:, :, 7], [b_S[1]], [b_olh], eng="pool")
                GG = slot(0)
                GAb = slot(2).bitcast(BF16)[:, 0:T]
                for (t0, n) in TB:
                    ps, pb = proj_block(ws, 128, 128, t0, n, bw)
                    u = GG[:, t0:t0 + n]
                    act(u, ps[:, 0:n], AF.Identity, [pb], [b_S[0]])
                    tt(tmpb[:, 0:n], u, u, ALU.mult, [b_S[0]], [b_tmpb])
                    ts(tmpb[:, 0:n], tmpb[:, 0:n], 0.044715, 1.0, ALU.mult, ALU.add, [b_tmpb], [b_tmpb])
                    tt(tmpb[:, 0:n], tmpb[:, 0:n], u, ALU.mult, [b_tmpb, b_S[0]], [b_tmpb])
                    act(tmpb[:, 0:n], tmpb[:, 0:n], AF.Sigmoid, [b_tmpb], [b_tmpb], scale=1.5957691216057308)
                    tt(u, u, tmpb[:, 0:n], ALU.mult, [b_S[0], b_tmpb], [b_S[0]])
                    tt(GAb[:, t0:t0 + n], u, HL[:, t0:t0 + n], ALU.mult, [b_S[0], b_S[1]], [b_S[2]])
                P.dma("sp", ga_d[c * 128:(c + 1) * 128, :], GAb, reads=[b_S[2]], writes=[b_ga])
            flushT(olc, b_olc, 8, 51, lambda c0, w: O["o_lc"].ap()[l].rearrange("s j c -> (s j) c")[:, c0:c0 + w], b_out["o_lc"])
            flushT(olh, b_olh, 8, NSEQ, lambda c0, w: O["o_lh"].ap()[l][:, c0:c0 + w], b_out["o_lh"])

            TL = blk[:, :]
            TLa = TL[:, 0:T]
            SG1 = TL[:, T:2 * T]
            SG2 = TL[:, 2 * T:3 * T]
            b_TL = [b_blk]
            def rw_A(q, par):
                m = 32 if q == 26 else 128
                ws, bw = wslot()
                wload(ws[:, :, 0:m], win[:, :, 2 * DL + q * 128:2 * DL + q * 128 + m], bw)
                RW = slot(2 * par); bRW = b_S[2 * par]
                RWs = RW[:, 2049:2049 + 144].rearrange("p (s j) -> p s j", s=NS)
                memset(RW[0:m, 0:1], 0.0, [bRW])
                cp(RWs[0:m, :, 0], stsh[0:m, q, :], [b_stsh], [bRW], eng="pool")
                for (t0, n) in TB:
                    ps, pb = proj_block(ws, 0, m, t0, n, bw)
                    if t0 < TP:
                        act(RW[0:m, 1 + t0:1 + t0 + n], ps[0:m, 0:n], AF.Identity, [pb], [bRW])
                    else:
                        act(RWs[0:m, :, 1:9], ps[0:m, 0:n].rearrange("p (s t) -> p s t", s=NS), AF.Identity, [pb], [bRW])

            def rw_B(q, par):
                m = 32 if q == 26 else 128
                RW = slot(2 * par); bRW = b_S[2 * par]
                XS = slot(2 * par + 1); bXS = b_S[2 * par + 1]
                RWs = RW[:, 2049:2049 + 144].rearrange("p (s j) -> p s j", s=NS)
                cp(osh[0:m, q, 0:1], RW[0:m, TP:TP + 1], [bRW], [b_osh], eng="pool")
                cp(osh[0:m, q, 1:NSEQ], RWs[0:m, :, 8], [bRW], [b_osh], eng="pool")
                XSs = XS[:, TP:T].rearrange("p (s t) -> p s t", s=NS)
                mu = pcol("rw_mu", q)
                tt(XS[0:m, 0:TP], RW[0:m, 0:TP], RW[0:m, 1:TP + 1], ALU.subtract, [bRW], [bXS])
                tt(XSs[0:m], RWs[0:m, :, 0:8], RWs[0:m, :, 1:9], ALU.subtract, [bRW], [bXS])
                stt(XS[0:m, 0:TP], XS[0:m, 0:TP], mu[0:m], RW[0:m, 1:TP + 1], ALU.mult, ALU.add, [bRW, bXS, b_PT], [bXS])
                stt(XSs[0:m], XSs[0:m], mu[0:m], RWs[0:m, :, 1:9], ALU.mult, ALU.add, [bRW, bXS, b_PT], [bXS])
                if q == 24:
                    act(TLa[0:64, :], XS[0:64, 0:T], AF.Tanh, [bXS], b_TL)
                    act(TLa[64:128, :], XS[64:128, 0:T], AF.Identity, [bXS], b_TL)
                elif q == 25:
                    act(SG1, XS[:, 0:T], AF.Sigmoid, [bXS], b_TL)
                elif q == 26:
                    act(SG2[0:32, :], XS[0:32, 0:T], AF.Sigmoid, [bXS], b_TL)
                else:
                    to_tokmajor(XS, bXS, q // 8, (q % 8) * 128)

            qorder = [24, 25, 26] + list(range(24))
            rw_A(qorder[0], 0)
            for qi_, q in enumerate(qorder):
                if qi_ + 1 < len(qorder):
                    rw_A(qorder[qi_ + 1], (qi_ + 1) % 2)
                rw_B(q, qi_ % 2)
            flushT(osh, b_osh, 27, NSEQ, lambda c0, w: O["o_sh"].ap()[l][:, c0:c0 + w], b_out["o_sh"], last_cols=32)
            for c in range(8):
                DC = slot(0); AC = slot(1); GC = slot(2)
                for (t0, n) in TB:
                    ps, pb = nps()
                    mm(ps[:, 0:n], lw2[0:64, c * 128:(c + 1) * 128], TLa[0:64, t0:t0 + n], True, True, [b_lw2] + b_TL, pb)
                    act(DC[:, t0:t0 + n], ps[:, 0:n], AF.Sigmoid, [pb, b_PT], [b_S[0]], bias=pcol("rw_w0", c))
                    act(DC[:, t0:t0 + n], DC[:, t0:t0 + n], AF.Exp, [b_S[0]], [b_S[0]], scale=-math.exp(-0.5))
                    ps, pb = nps()
                    mm(ps[:, 0:n], lw2[64:128, c * 128:(c + 1) * 128], TLa[64:128, t0:t0 + n], True, True, [b_lw2] + b_TL, pb)
                    act(AC[:, t0:t0 + n], ps[:, 0:n], AF.Sigmoid, [pb, b_PT], [b_S[1]], bias=pcol("rw_a0", c))
                    ps, pb = nps()
                    mm(ps[:, 0:n], g2a[:, c * 128:(c + 1) * 128], SG1[:, t0:t0 + n], True, False, [b_g2a] + b_TL, pb)
                    mm(ps[:, 0:n], g2b[0:32, c * 128:(c + 1) * 128], SG2[0:32, t0:t0 + n], False, True, [b_g2b] + b_TL, pb)
                    act(GC[:, t0:t0 + n], ps[:, 0:n], AF.Identity, [pb], [b_S[2]])
                to_tokmajor(DC, b_S[0], 3, c * 128)
                to_tokmajor(AC, b_S[1], 4, c * 128)
                to_tokmajor(GC, b_S[2], 5, c * 128)
            for gc in range(32):
                if gc % 2 == 0:
                    ws, bw = wslot()
                    wload(ws[:, :, 0:256], win[:, :, 2 * DL + NRW + gc * 128:2 * DL + NRW + (gc + 2) * 128], bw)
                SGb = slot(gc % 2).bitcast(BF16)[:, 0:T]
                bsl = b_S[gc % 2]
                for (t0, n) in TB:
                    ps, pb = proj_block(ws, (gc % 2) * 128, 128, t0, n, bw)
                    act(SGb[:, t0:t0 + n], ps[:, 0:n], AF.Sigmoid, [pb], [bsl])
                P.dma("sp", sg[gc * 128:(gc + 1) * 128, :], SGb, reads=[bsl], writes=[b_sg])

            for i, nm in enumerate(("rw_kk", "rw_ka", "rw_rk")):
                src = Wt[nm].ap()[l]
                if nm == "rw_rk":
                    src = src.rearrange("h k -> (h k)")
                P.dma("act", ct[i][:, :], src.partition_broadcast(128), writes=[b_ct[i]])
            rqv = rq_h.ap()

            def half(i, h):
                return arena[:, i, h * 1024:(h + 1) * 1024]

            def h3(ap):
                return ap.rearrange("p (h k) -> p h k", h=16)

            for i in range(17):
                rows = slice(i * 128, (i + 1) * 128)
                Rt, Kt, Vt, Dt, At = half(0, 0), half(0, 1), half(1, 0), half(1, 1), half(2, 0)
                KKt, TMt, KMt, NBt, BVt = half(2, 1), half(3, 0), half(3, 1), half(4, 0), half(4, 1)
                for (dst, qi, bb) in ((Rt, 0, b_S[0]), (Kt, 1, b_S[0]), (Vt, 2, b_S[1]), (Dt, 3, b_S[1]), (At, 4, b_S[2])):
                    P.dma("sp", dst, tk[qi][rows, :], reads=[b_tk[qi]], writes=[bb])
                tt(KKt, Kt, ct[0][:, :], ALU.mult, [b_S[0], b_ct[0]], [b_S[2]])
                tt(TMt, KKt, KKt, ALU.mult, [b_S[2]], [b_S[3]])
                P.op("dve", lambda e, TMt=TMt: e.tensor_reduce(out=sm[:, 0:16], in_=h3(TMt), axis=AX.X, op=ALU.add),
                     reads=[b_S[3]], writes=[b_sm])
                act(sm[:, 0:16], sm[:, 0:16], AF.Sqrt, [b_sm], [b_sm])
                ts(sm[:, 0:16], sm[:, 0:16], 1e-12, None, ALU.max, None, [b_sm], [b_sm])
                P.op("dve", lambda e: e.reciprocal(out=sm[:, 0:16], in_=sm[:, 0:16]), reads=[b_sm], writes=[b_sm])
                tt(h3(KKt), h3(KKt), sm[:, 0:16].unsqueeze(2).to_broadcast([128, 16, 64]), ALU.mult, [b_S[2], b_sm], [b_S[2]])
                stt(TMt, At, -1.0, ct[1][:, :], ALU.add, ALU.mult, [b_S[2], b_ct[1]], [b_S[3]])
                stt(KMt, TMt, 1.0, Kt, ALU.add, ALU.mult, [b_S[3], b_S[0]], [b_S[3]])
                stt(NBt, KKt, -1.0, At, ALU.mult, ALU.mult, [b_S[2]], [b_S[4]])
                ccv = sm[:, 32:64].rearrange("p (h j) -> p h j", j=2)
                tt(TMt, Rt, KMt, ALU.mult, [b_S[0], b_S[3]], [b_S[3]])
                P.op("dve", lambda e, TMt=TMt, ccv=ccv: e.tensor_reduce(out=ccv[:, :, 1], in_=h3(TMt), axis=AX.X, op=ALU.add),
                     reads=[b_S[3]], writes=[b_sm])
                tt(TMt, TMt, ct[2][:, :], ALU.mult, [b_S[3], b_ct[2]], [b_S[3]])
                P.op("dve", lambda e, TMt=TMt: e.tensor_reduce(out=sm[:, 16:32], in_=h3(TMt), axis=AX.X, op=ALU.add),
                     reads=[b_S[3]], writes=[b_sm])
                tt(h3(BVt), h3(Vt), sm[:, 16:32].unsqueeze(2).to_broadcast([128, 16, 64]), ALU.mult, [b_S[1], b_sm], [b_S[4]])
                for (srct, qi, bb) in ((Rt, 0, b_S[0]), (Dt, 1, b_S[1]), (KMt, 2, b_S[3]), (KKt, 3, b_S[2]), (NBt, 4, b_S[4])):
                    P.dma("sp", rqv[qi].rearrange("h t k -> t h k")[rows], h3(srct), reads=[bb], writes=[b_rq])
                tt(Kt, NBt, Rt, ALU.mult, [b_S[4], b_S[0]], [b_S[0]])
                P.op("dve", lambda e, Kt=Kt, ccv=ccv: e.tensor_reduce(out=ccv[:, :, 0], in_=h3(Kt), axis=AX.X, op=ALU.add),
                     reads=[b_S[0]], writes=[b_sm])
                tt(Kt, Rt, Dt, ALU.mult, [b_S[0], b_S[1]], [b_S[0]])
                P.dma("sp", rqv[5].rearrange("h t k -> t h k")[rows], h3(Kt), reads=[b_S[0]], writes=[b_rq])
                ps, pb = nps()
                tr(ps[0:32, 0:128], sm[:, 32:64], 128, [b_sm], pb)
                cp(rowst[0:32, 0:128], ps[0:32, 0:128], [pb], [b_rowst])
                P.dma("sp", cc_h.ap()[:, rows], rowst[0:32, 0:128], reads=[b_rowst], writes=[b_cc])
                P.dma("sp", tk[6][rows, :], BVt, reads=[b_S[4]], writes=[b_tk[6]])

            def rec_steps(nsq, nsteps, xr_step, v_step, y_step):
                S3 = Sst[:, 0:nsq * 512].rearrange("p (s v k) -> p s v k", s=nsq, v=8)
                T3 = Stmp[:, 0:nsq * 512].rearrange("p (s v k) -> p s v k", s=nsq, v=8)
                K4 = [128, nsq, 8, 64]
                rdx = b_S[0:5]
                for t in range(nsteps):
                    r_, w_, k_, kk_, nb_ = [xr_step(j, t).unsqueeze(2).to_broadcast(K4) for j in range(5)]
                    sk = skk[:, 0:nsq * 8].rearrange("p (s v) -> p s v", s=nsq)
                    tt(T3, S3, kk_, ALU.mult, [b_Sst] + rdx, [b_Stmp])
                    P.op("dve", lambda e, sk=sk, T3=T3: e.tensor_reduce(out=sk, in_=T3, axis=AX.X, op=ALU.add), reads=[b_Stmp], writes=[b_skk])
                    tt(S3, S3, w_, ALU.mult, [b_Sst] + rdx, [b_Sst])
                    tt(T3, nb_, sk.unsqueeze(3).to_broadcast(K4), ALU.mult, [b_skk] + rdx, [b_Stmp])
                    tt(S3, S3, T3, ALU.add, [b_Sst, b_Stmp], [b_Sst])
                    tt(T3, k_, v_step(t).unsqueeze(3).to_broadcast(K4), ALU.mult, [b_vbuf] + rdx, [b_Stmp])
                    tt(S3, S3, T3, ALU.add, [b_Sst, b_Stmp], [b_Sst])
                    tt(T3, S3, r_, ALU.mult, [b_Sst] + rdx, [b_Stmp])
                    yo = y_step(t)
                    P.op("dve", lambda e, yo=yo, T3=T3: e.tensor_reduce(out=yo, in_=T3, axis=AX.X, op=ALU.add), reads=[b_Stmp], writes=[b_ybuf])

            oS = O["o_S"].ap()[l]
            TC = 16
            SLQ = [3, 5, 1, 4, 2]
            allb = b_S[0:5] + [b for bb_ in bx for b in bb_] + bv + by + bSS + [bT2, bTt] + bKV + [b_blk, b_vbuf, b_ybuf]
            fence(allb)
            SS = [blkF[:, 0:512], blkF[:, 512:1024]]
            T2v = blkF[:, 1024:2048].rearrange("p (q v k) -> p q v k", q=2, v=8)
            Ttv = blkF[:, 2048:2560].rearrange("p (v k) -> p v k", v=8)
            KVv = [blkF[:, 2560 + i * 512:2560 + (i + 1) * 512].rearrange("p (v k) -> p v k", v=8) for i in range(4)]
            memset(SS[0], 0.0, [bSS[0]], eng="dve")
            g = 0
            for ci in range(TP // TC):
                t0 = ci * TC
                par = ci % 2
                xo = par * 1024
                for j in range(5):
                    src = bass.AP(rq_h, SLQ[j] * 16 * T * 64 + t0 * 64, [[T * 64, 16], [0, 8], [1, TC * 64]])
                    P.dma("sp", slot(j, TC * 64, xo), src, reads=[b_rq], writes=[bx[j][par]])
                vb_ = vbuf[:, par * 128:(par + 1) * 128].rearrange("p (t v) -> p t v", t=TC)
                yb_ = ybuf[:, par * 128:(par + 1) * 128].rearrange("p (t v) -> p t v", t=TC)
                cb_ = ccb[:, par * 32:(par + 1) * 32].rearrange("p (j t) -> p j t", j=2)
                sk_ = skys[:, par * 256:(par + 1) * 256].rearrange("p (q t v) -> p q t v", q=2, t=TC)
                P.dma("act", vb_, bass.AP(tk_h, 2 * T * DL + t0 * DL, [[8, 128], [DL, TC], [1, 8]]), reads=[b_tk[2]], writes=[bv[par]])
                for jj in range(2):
                    P.dma("act", cb_[:, jj, :], bass.AP(cc_h, jj * T + t0, [[2 * T, 16], [0, 8], [1, TC]]), reads=[b_cc], writes=[b_ccb[par]])
                for t in range(TC):
                    cur, nxt = g % 2, (g + 1) % 2
                    Sc3 = SS[cur].rearrange("p (v k) -> p v k", v=8)
                    Sn3 = SS[nxt].rearrange("p (v k) -> p v k", v=8)
                    kv = KVv[g % 4]; bkv = bKV[g % 4]
                    o = xo + t * 64
                    tt(kv, slot(4, 64, o).unsqueeze(1).to_broadcast([128, 8, 64]),
                       vb_[:, t, :].unsqueeze(2).to_broadcast([128, 8, 64]), ALU.mult, [bx[4][par], bv[par]], [bkv], eng="pool")
                    tt(T2v, Sc3.unsqueeze(1).to_broadcast([128, 2, 8, 64]),
                       arena[:, 0:2, o:o + 64].unsqueeze(2).to_broadcast([128, 2, 8, 64]), ALU.mult,
                       [bSS[cur], bx[0][par], bx[1][par]], [bT2])
                    P.op("dve", lambda e, out=sk_[:, :, t, :], T2v=T2v: e.tensor_reduce(out=out, in_=T2v, axis=AX.X, op=ALU.add),
                         reads=[bT2], writes=[b_skys[par]])
                    tt(Sn3, Sc3, slot(2, 64, o).unsqueeze(1).to_broadcast([128, 8, 64]), ALU.mult, [bSS[cur], bx[2][par]], [bSS[nxt]], eng="pool")
                    tt(Ttv, sk_[:, 0, t, :].unsqueeze(2).to_broadcast([128, 8, 64]),
                       slot(3, 64, o).unsqueeze(1).to_broadcast([128, 8, 64]), ALU.mult, [b_skys[par], bx[3][par]], [bTt])
                    tt(Sn3, Sn3, Ttv, ALU.add, [bSS[nxt], bTt], [bSS[nxt]])
                    tt(Sn3, Sn3, kv, ALU.add, [bSS[nxt], bkv], [bSS[nxt]])
                    g += 1
                c1b = cb_[:, 0, :].unsqueeze(2).to_broadcast([128, TC, 8])
                c2b = cb_[:, 1, :].unsqueeze(2).to_broadcast([128, TC, 8])
                tt(yb_, sk_[:, 0, :, :], c1b, ALU.mult, [b_skys[par], b_ccb[par]], [by[par]], eng="pool")
                tt(yb_, yb_, sk_[:, 1, :, :], ALU.add, [by[par], b_skys[par]], [by[par]], eng="pool")
                tt(sk_[:, 0, :, :], vb_, c2b, ALU.mult, [bv[par], b_ccb[par]], [b_skys[par]], eng="pool")
                tt(yb_, yb_, sk_[:, 0, :, :], ALU.add, [by[par], b_skys[par]], [by[par]], eng="pool")
                P.dma("act", bass.AP(tk_h, 7 * T * DL + t0 * DL, [[8, 128], [DL, TC], [1, 8]]), yb_, reads=[by[par]], writes=[b_tk[7]])
            P.dma("sp", oS[0].rearrange("h (vb v8) k -> (h vb) (v8 k)", vb=8), SS[g % 2], reads=[bSS[g % 2]], writes=[b_out["o_S"]])
            fence(allb)
            for sb in range(4):
                s0 = sb * 4
                P.dma("sp", Sst[:, 0:2048].rearrange("p (s f) -> p s f", s=4),
                      bass.AP(I["st_S"], (l * NS + s0) * 65536, [[512, 128], [65536, 4], [1, 512]]), writes=[b_Sst])
                for j in range(5):
                    for s in range(4):
                        tok0 = TP + (s0 + s) * 8
                        src = bass.AP(rq_h, j * 16 * T * 64 + tok0 * 64, [[T * 64, 16], [0, 8], [1, 8 * 64]])
                        P.dma("sp" if s % 2 == 0 else "act", slot(j, 512, s * 512), src, reads=[b_rq], writes=[b_S[j]])
                P.dma("act", vbuf[:, 0:256].rearrange("p (t v) -> p t v", t=32),
                      bass.AP(tk_h, 2 * T * DL + (TP + s0 * 8) * DL, [[8, 128], [DL, 32], [1, 8]]), reads=[b_tk[2]], writes=[b_vbuf])
                rec_steps(4, 8,
                          lambda j, t: slot(j, 2048).rearrange("p (s t k) -> p s t k", s=4, t=8)[:, :, t, :],
                          lambda t: vbuf[:, 0:256].rearrange("p (s t v) -> p s t v", s=4, t=8)[:, :, t, :],
                          lambda t: ybuf[:, 0:256].rearrange("p (s t v) -> p s t v", s=4, t=8)[:, :, t, :])
                P.dma("act", bass.AP(tk_h, 7 * T * DL + (TP + s0 * 8) * DL, [[8, 128], [DL, 32], [1, 8]]),
                      ybuf[:, 0:256].rearrange("p (t v) -> p t v", t=32), reads=[b_ybuf], writes=[b_tk[7]])
                P.dma("sp", bass.AP(O["o_S"], (l * NSEQ + 1 + s0) * 65536, [[512, 128], [65536, 4], [1, 512]]),
                      Sst[:, 0:2048].rearrange("p (s f) -> p s f", s=4), reads=[b_Sst], writes=[b_out["o_S"]])

            for i, nm in enumerate(("rw_gn_g", "rw_gn_b")):
                P.dma("act", ct[i][:, :], Wt[nm].ap()[l].partition_broadcast(128), writes=[b_ct[i]])
            for i in range(17):
                rows = slice(i * 128, (i + 1) * 128)
                Yt, BVt, GGt, TMt = half(0, 0), half(0, 1), half(1, 0), half(1, 1)
                P.dma("sp", Yt, tk[7][rows, :], reads=[b_tk[7]], writes=[b_S[0]])
                P.dma("sp", BVt, tk[6][rows, :], reads=[b_tk[6]], writes=[b_S[0]])
                P.dma("sp", GGt, tk[5][rows, :], reads=[b_tk[5]], writes=[b_S[1]])
                P.op("dve", lambda e, Yt=Yt: e.tensor_reduce(out=sm[:, 0:16], in_=h3(Yt), axis=AX.X, op=ALU.add), reads=[b_S[0]], writes=[b_sm])
                ts(sm[:, 0:16], sm[:, 0:16], 1.0 / 64, None, ALU.mult, None, [b_sm], [b_sm])
                tt(h3(Yt), h3(Yt), sm[:, 0:16].unsqueeze(2).to_broadcast([128, 16, 64]), ALU.subtract, [b_S[0], b_sm], [b_S[0]])
                tt(TMt, Yt, Yt, ALU.mult, [b_S[0]], [b_S[1]])
                P.op("dve", lambda e, TMt=TMt: e.tensor_reduce(out=sm[:, 16:32], in_=h3(TMt), axis=AX.X, op=ALU.add), reads=[b_S[1]], writes=[b_sm])
                act(sm[:, 16:32], sm[:, 16:32], AF.Sqrt, [b_sm], [b_sm], scale=1.0 / 64, bias=64e-5)
                P.op("dve", lambda e: e.reciprocal(out=sm[:, 16:32], in_=sm[:, 16:32]), reads=[b_sm], writes=[b_sm])
                tt(h3(Yt), h3(Yt), sm[:, 16:32].unsqueeze(2).to_broadcast([128, 16, 64]), ALU.mult, [b_S[0], b_sm], [b_S[0]])
                tt(Yt, Yt, ct[0][:, :], ALU.mult, [b_S[0], b_ct[0]], [b_S[0]])
                tt(Yt, Yt, ct[1][:, :], ALU.add, [b_S[0], b_ct[1]], [b_S[0]])
                tt(Yt, Yt, BVt, ALU.add, [b_S[0]], [b_S[0]])
                tt(Yt, Yt, GGt, ALU.mult, [b_S[0], b_S[1]], [b_S[0]])
                OTb = slot(2).bitcast(BF16)[:, 0:1024].rearrange("p (c t) -> p c t", c=8)
                for g in range(2):
                    ps, pb = nps()
                    for j in range(4):
                        tr(ps[:, j * 128:(j + 1) * 128], Yt[:, (g * 4 + j) * 128:(g * 4 + j + 1) * 128], 128, [b_S[0]], pb)
                    act(OTb[:, g * 4:(g + 1) * 4, :].rearrange("p a b -> p (a b)"), ps[:, :], AF.Identity, [pb], [b_S[2]])
                P.dma("sp", oT_d.rearrange("(c p) t -> p c t", p=128)[:, :, rows], OTb, reads=[b_S[2]], writes=[b_oT])

            wpa = Wt["w_pa"].ap()[l].rearrange("(kc p) c -> p kc c", p=128)
            wpb = Wt["w_pb"].ap()[l].rearrange("(kc p) c -> p kc c", p=128)
            gav = ga_d.rearrange("(kc p) t -> p kc t", p=128)
            otv = oT_d.rearrange("(kc p) t -> p kc t", p=128)
            mcnt = 0
            for (t0, n) in TB:
                gb = blk[:, 0:16 * 512].rearrange("p (kc t) -> p kc t", kc=16)
                P.dma("sp", gb[:, 0:8, 0:n], gav[:, :, t0:t0 + n], reads=[b_ga], writes=[b_blk])
                P.dma("act", gb[:, 8:16, 0:n], otv[:, :, t0:t0 + n], reads=[b_oT], writes=[b_blk])
                for c in range(16):
                    ws, bw = wslot()
                    wload(ws[:, 0:8, 0:128], wpa[:, :, c * 128:(c + 1) * 128], bw)
                    wload(ws[:, 8:16, 0:128], wpb[:, :, c * 128:(c + 1) * 128], bw)
                    sg_ = sgt[:, mcnt % 2, :]; bsg = b_sgt[mcnt % 2]
                    mcnt += 1
                    P.dma("sp", sg_[:, 0:n], sg[c * 128:(c + 1) * 128, t0:t0 + n], reads=[b_sg], writes=[bsg])
                    P.dma("sp", sg_[:, 512:512 + n], sg[D + c * 128:D + (c + 1) * 128, t0:t0 + n], reads=[b_sg], writes=[bsg])
                    psa, pba = nps()
                    for kc in range(8):
                        mm(psa[:, 0:n], ws[:, kc, 0:128], gb[:, kc, 0:n], kc == 0, kc == 7, [bw, b_blk], pba)
                    psb, pbb = nps()
                    for kc in range(8):
                        mm(psb[:, 0:n], ws[:, 8 + kc, 0:128], gb[:, 8 + kc, 0:n], kc == 0, kc == 7, [bw, b_blk], pbb)
                    tt(tmpb[:, 0:n], psa[:, 0:n], sg_[:, 0:n], ALU.mult, [pba, bsg], [b_tmpb])
                    tt(tmpc[:, 0:n], psb[:, 0:n], sg_[:, 512:512 + n], ALU.mult, [pbb, bsg], [b_tmpc])
                    tt(actT[:, c, t0:t0 + n], tmpb[:, 0:n], tmpc[:, 0:n], ALU.add, [b_tmpb, b_tmpc], [b_act])
            wo = Wt["w_o"].ap()[l].rearrange("(kc p) c -> p kc c", p=128)
            for c in range(16):
                if c % 2 == 0:
                    ws, bw = wslot()
                    wload(ws[:, :, 0:256], wo[:, :, c * 128:(c + 2) * 128], bw)
                for (t0, n) in TB:
                    ps, pb = proj_block(ws, (c % 2) * 128, 128, t0, n, bw)
                    resid_update(c, t0, n, ps, pb, 32)

            make_A(64, "norm_ffn")
            norm_mod(A1, modT[:, 48:64, :], b_A1, b_mod)

            wup = Wt["w_up"].ap()[l].rearrange("(kc p) c -> p kc c", p=128)
            for j in range(44):
                j4 = j % 4
                stv = I["st_fc"].ap()[l].rearrange("s j c -> (s j) c")
                ofv = O["o_fc"].ap()[l].rearrange("s j c -> (s j) c")
                if j4 == 0:
                    for hh in range(2):
                        loadT(stfc[:, hh * 4:hh * 4 + 4, :], b_stfc,
                              lambda c0, w, hh=hh, j=j: stv[:, (hh * 44 + j) * 128 + c0:(hh * 44 + j) * 128 + c0 + w], 32, 4)
                ws, bw = wslot()
                wload(ws[:, :, 0:128], wup[:, :, j * 128:(j + 1) * 128], bw)
                wload(ws[:, :, 128:256], wup[:, :, DFF + j * 128:DFF + (j + 1) * 128], bw)
                outs = []
                for hh in range(2):
                    ch = hh * 44 + j
                    UU = slot(0 + hh * 2); UC = slot(1 + hh * 2)
                    bU = b_S[0 + hh * 2]; bC = b_S[1 + hh * 2]
                    UUs = UU[:, 2050:2050 + 160].rearrange("p (s j) -> p s j", s=NS)
                    memset(UU[:, 0:2], 0.0, [bU])
                    cp(UUs[:, :, 0:2], stfc[:, hh * 4 + j4, :].rearrange("p (s j) -> p s j", s=NS), [b_stfc], [bU], eng="pool")
                    for (t0, n) in TB:
                        ps, pb = proj_block(ws, hh * 128, 128, t0, n, bw)
                        if t0 < TP:
                            act(UU[:, 2 + t0:2 + t0 + n], ps[:, 0:n], AF.Identity, [pb], [bU])
                        else:
                            act(UUs[:, :, 2:10], ps[:, 0:n].rearrange("p (s t) -> p s t", s=NS), AF.Identity, [pb], [bU])
                    k8 = j % 8
                    if hh == 0 and k8 == 0:
                        pass
                    cp(ofc[:, hh * 4 + j4, 0:2], UU[:, 2048:2050], [bU], [b_ofc], eng="pool")
                    cp(ofc[:, hh * 4 + j4, 2:34].rearrange("p (s j) -> p s j", s=NS), UUs[:, :, 8:10], [bU], [b_ofc], eng="pool")
                    UCs = UC[:, TP:T].rearrange("p (s t) -> p s t", s=NS)
                    ts(UC[:, 0:TP], UU[:, 0:TP], pcol("ffn_conv_w", 0 * 88 + ch), pcol("ffn_conv_b", ch), ALU.mult, ALU.add, [bU, b_PT], [bC])
                    ts(UCs, UUs[:, :, 0:8], pcol("ffn_conv_w", 0 * 88 + ch), pcol("ffn_conv_b", ch), ALU.mult, ALU.add, [bU, b_PT], [bC])
                    for jj in range(1, 3):
                        stt(UC[:, 0:TP], UU[:, jj:jj + TP], pcol("ffn_conv_w", jj * 88 + ch), UC[:, 0:TP], ALU.mult, ALU.add, [bU, bC, b_PT], [bC])
                        stt(UCs, UUs[:, :, jj:jj + 8], pcol("ffn_conv_w", jj * 88 + ch), UCs, ALU.mult, ALU.add, [bU, bC, b_PT], [bC])
                    outs.append((UC, bC))
                (UG, bG), (UV, bV) = outs
                act(UG[:, 0:T], UG[:, 0:T], AF.Silu, [bG], [bG])
                ATb = slot(4).bitcast(BF16)[:, 0:T]
                tt(ATb, UG[:, 0:T], UV[:, 0:T], ALU.mult, [bG, bV], [b_S[4]])
                P.dma("sp", aT_d[j * 128:(j + 1) * 128, :], ATb, reads=[b_S[4]], writes=[b_aT])
                if j4 == 3:
                    for hh in range(2):
                        flushT(ofc[:, hh * 4:hh * 4 + 4, :], b_ofc, 4, 34,
                               lambda c0, w, hh=hh, j=j: ofv[:, (hh * 44 + j - 3) * 128 + c0:(hh * 44 + j - 3) * 128 + c0 + w], b_out["o_fc"])

            wdn = Wt["w_down"].ap()[l].rearrange("(kc p) c -> p kc c", p=128)
            atv = aT_d.rearrange("(kc p) t -> p kc t", p=128)
            fence([b_blk, b_ab1])
            abv = [blk[:, 0:11 * 512].rearrange("p (kc t) -> p kc t", kc=11),
                   blk[:, 11 * 512:22 * 512].rearrange("p (kc t) -> p kc t", kc=11)]
            b_ab = [b_blk, b_ab1]
            qcnt = 0
            for (t0, n) in TB:
                for cg in range(4):
                    banks = [nps() for _ in range(4)]
                    for qt in range(4):
                        ab = abv[qcnt % 2]; bab = b_ab[qcnt % 2]
                        P.dma("sp" if qcnt % 2 == 0 else "act", ab[:, :, 0:n], atv[:, qt * 11:(qt + 1) * 11, t0:t0 + n],
                              reads=[b_aT], writes=[bab])
                        qcnt += 1
                        for ci in range(4):
                            c = cg * 4 + ci
                            ws, bw = wslot()
                            wload(ws[:, 0:11, 0:128], wdn[:, qt * 11:(qt + 1) * 11, c * 128:(c + 1) * 128], bw)
                            ps, pb = banks[ci]
                            for k2 in range(11):
                                mm(ps[:, 0:n], ws[:, k2, 0:128], ab[:, k2, 0:n], qt == 0 and k2 == 0, qt == 3 and k2 == 10, [bw, bab], pb)
                    for ci in range(4):
                        ps, pb = banks[ci]
                        resid_update(cg * 4 + ci, t0, n, ps, pb, 80)
            fence([b_blk, b_ab1])

        nf = Wt["norm_final"].ap().rearrange("(r c) -> r c", c=128)
        memset(pst[:], 0.0, [b_pst])
        P.dma("act", pst[0:16, :], nf, writes=[b_pst])
        ps, pb = nps()
        tr(ps[:, 0:128], pst[:, :], 128, [b_pst], pb)
        cp(PT[:, 0:128], ps[:, 0:128], [pb], [b_PT])
        for (t0, n) in TB:
            xb, rstd = norm_stats(t0, n)
            for kc in range(16):
                stt(xb[:, kc, :], xb[:, kc, :], PT[:, kc:kc + 1], rstd, ALU.mult, ALU.mult, b_S[0:5] + [b_PT], b_S[0:4])
            for tt_i in range(n // 128):
                for g in range(4):
                    ps, pb = nps()
                    for j in range(4):
                        kc = g * 4 + j
                        tr(ps[:, j * 128:(j + 1) * 128], xb[:, kc, tt_i * 128:(tt_i + 1) * 128], 128, b_S[0:4], pb)
                    act(rowst[:, 0:512], ps[:, :], AF.Identity, [pb], [b_rowst])
                    r0 = t0 + tt_i * 128
                    if t0 < TP:
                        P.dma("sp", O["y_p"].ap()[r0:r0 + 128, g * 512:(g + 1) * 512], rowst[:, 0:512], reads=[b_rowst], writes=[b_out["y_p"]])
                    else:
                        P.dma("sp", O["y_s"].ap()[:, g * 512:(g + 1) * 512], rowst[:, 0:512], reads=[b_rowst], writes=[b_out["y_s"]])

        P.final_wait("sp", list(b_out.values()))
        P.emit()
    return nc


_NC_CACHE = {}


def kernel(x_prompt, x_sample, c_prompt, c_sample, state_lru_conv, state_lru_h, state_rwkv_shift, state_rwkv_S,
           state_ffn_conv, **weights):
    f = lambda a: np.ascontiguousarray(np.asarray(a, dtype=np.float32))
    if "nc" not in _NC_CACHE:
        _NC_CACHE["nc"] = build()
    nc = _NC_CACHE["nc"]
    wmap = {k: f(weights[k]) for k in W_SHAPES}
    in_maps = []
    for core in range(8):
        b = core % 4
        ss = slice(core * NS, (core + 1) * NS)
        m = dict(wmap)
        m["xp"] = f(x_prompt[b])
        m["xs"] = f(np.asarray(x_sample)[ss].reshape(TS, D))
        m["cc"] = f(np.concatenate([np.asarray(c_prompt)[b:b + 1], np.asarray(c_sample)[ss]], axis=0))
        m["st_lc"] = f(np.asarray(state_lru_conv)[:, ss])
        m["st_lh"] = f(np.asarray(state_lru_h)[:, ss])
        m["st_sh"] = f(np.asarray(state_rwkv_shift)[:, ss])
        m["st_S"] = f(np.asarray(state_rwkv_S)[:, ss])
        m["st_fc"] = f(np.asarray(state_ffn_conv)[:, ss])
        in_maps.append(m)
    res = run_bass_kernel_spmd(nc, in_maps, core_ids=list(range(8)))
    R = res.results
    y_prompt = np.stack([R[b]["y_p"] for b in range(4)], axis=0)
    y_sample = np.concatenate([R[c]["y_s"].reshape(NS, 8, D) for c in range(8)], axis=0)
    outs = [y_prompt, y_sample]
    names = ["o_lc", "o_lh", "o_sh", "o_S", "o_fc"]
    for nm in names:
        outs.append(np.stack([R[b][nm][:, 0] for b in range(4)], axis=1))
    for nm in names:
        outs.append(np.concatenate([R[c][nm][:, 1:] for c in range(8)], axis=1))
    return tuple(np.ascontiguousarray(o.astype(np.float32, copy=False)) for o in outs)
```

```python
import math
import numpy as np
from contextlib import ExitStack
import concourse.bass as bass
import concourse.mybir as mybir
from concourse.bass_utils import run_bass_kernel_spmd

F32 = mybir.dt.float32
BF16 = mybir.dt.bfloat16
ALU = mybir.AluOpType
AF = mybir.ActivationFunctionType
AX = mybir.AxisListType

ENGS = ("pe", "act", "dve", "pool", "sp")

D = 2048
TP = 2048
NS = 16
TS = 128
T = TP + TS
NSEQ = 17
DL = 1024
NRW = 3360
NIN = 9504
DFF = 5632
DEPTH = 4
TB = [(0, 512), (512, 512), (1024, 512), (1536, 512), (2048, 128)]
SLOT = 2240
NLAYERS = DEPTH


class Buf:
    def __init__(self, prog, name):
        self.prog = prog
        self.name = name
        self.w = {}
        self.r = {}
        self.dsem = None
        self.dcnt = 0

    def dma_sem(self):
        if self.dsem is None:
            self.dsem = self.prog.new_sem("d_" + self.name)
        return self.dsem


class Prog:
    def __init__(self, nc, stack):
        self.nc = nc
        self.stack = stack
        self.streams = {e: [] for e in ENGS}
        self.esem = {e: self.new_sem("e_" + e) for e in ENGS}
        self.ecnt = {e: 0 for e in ENGS}
        self.seen = {e: {} for e in ENGS}
        self.nbuf = 0
        self.pend = None

    def pe_group(self, fn, reads, pb):
        if self.pend is not None and self.pend[2] is pb:
            self.pend[0].append(fn)
            self.pend[1].extend(reads)
        else:
            self.flush_pe()
            self.pend = ([fn], list(reads), pb)

    def flush_pe(self):
        if self.pend is None:
            return
        fns, reads, pb = self.pend
        self.pend = None
        uniq = []
        for b in reads:
            if all(b is not u for u in uniq):
                uniq.append(b)

        def fn_all(e, fns=fns):
            last = None
            for f in fns:
                last = f(e)
            return last

        self._op("pe", fn_all, uniq, [pb])

    def new_sem(self, name):
        return self.stack.enter_context(self.nc.semaphore(name))

    def buf(self, name=None):
        self.nbuf += 1
        return Buf(self, name or f"b{self.nbuf}")

    def sbuf(self, name, shape, dtype):
        return self.stack.enter_context(self.nc.sbuf_tensor(name, list(shape), dtype))

    def psum(self, name, shape, dtype=F32):
        return self.stack.enter_context(self.nc.psum_tensor(name, list(shape), dtype))

    def _deps(self, eng, reads, writes):
        need = {}

        def add(tok):
            k = id(tok[0])
            if k not in need or need[k][1] < tok[1]:
                need[k] = tok

        for b in reads:
            for tok in b.w.values():
                add(tok)
        for b in writes:
            for tok in b.w.values():
                add(tok)
            for tok in b.r.values():
                add(tok)
        waits = []
        seen = self.seen[eng]
        for k, (sem, val) in need.items():
            if seen.get(k, 0) < val:
                seen[k] = val
                waits.append((sem, val))
        return waits

    def _mark(self, tok, reads, writes):
        k = id(tok[0])
        for b in reads:
            if b in writes:
                continue
            b.r[k] = tok
        for b in writes:
            b.w = {k: tok}
            b.r = {}

    def op(self, eng, fn, reads=(), writes=()):
        self.flush_pe()
        self._op(eng, fn, reads, writes)

    def _op(self, eng, fn, reads=(), writes=()):
        waits = self._deps(eng, reads, writes)
        self.ecnt[eng] += 1
        sem = self.esem[eng]
        self._mark((sem, self.ecnt[eng]), reads, writes)

        def run(e, waits=waits, fn=fn, sem=sem):
            for s, v in waits:
                e.wait_ge(s, v)
            fn(e).then_inc(sem, 1)

        self.streams[eng].append(run)

    def dma(self, q, out, in_, reads=(), writes=(), **kw):
        self.flush_pe()
        waits = self._deps(q, reads, writes)
        wb = writes[0]
        sem = wb.dma_sem()
        wb.dcnt += 16
        self._mark((sem, wb.dcnt), reads, writes)

        def run(e, waits=waits, sem=sem, out=out, in_=in_, kw=kw):
            for s, v in waits:
                e.wait_ge(s, v)
            e.dma_start(out=out, in_=in_, **kw).then_inc(sem, 16)

        self.streams[q].append(run)

    def final_wait(self, eng, bufs):
        self.flush_pe()
        waits = self._deps(eng, bufs, bufs)

        def run(e, waits=waits):
            for s, v in waits:
                e.wait_ge(s, v)

        self.streams[eng].append(run)

    def emit(self):
        self.flush_pe()
        nc = self.nc
        with nc.Block() as block:
            @block.tensor
            def _(e):
                for f in self.streams["pe"]:
                    f(e)

            @block.scalar
            def _(e):
                for f in self.streams["act"]:
                    f(e)

            @block.vector
            def _(e):
                for f in self.streams["dve"]:
                    f(e)

            @block.gpsimd
            def _(e):
                for f in self.streams["pool"]:
                    f(e)

            @block.sync
            def _(e):
                for f in self.streams["sp"]:
                    f(e)


W_SHAPES = {
    "w_ada": [DEPTH, D, 6 * D], "b_ada": [DEPTH, 6 * D], "norm_mix": [DEPTH, D], "norm_ffn": [DEPTH, D],
    "w_in": [DEPTH, D, NIN], "lru_conv_w": [DEPTH, 4, DL], "lru_conv_b": [DEPTH, DL],
    "lru_wa": [DEPTH, 16, 64, 64], "lru_ba": [DEPTH, DL], "lru_wi": [DEPTH, 16, 64, 64], "lru_bi": [DEPTH, DL],
    "lru_lambda": [DEPTH, DL], "w_pa": [DEPTH, DL, D], "rw_mu": [DEPTH, NRW], "rw_w0": [DEPTH, DL],
    "rw_w2": [DEPTH, 64, DL], "rw_a0": [DEPTH, DL], "rw_a2": [DEPTH, 64, DL], "rw_g2": [DEPTH, 160, DL],
    "rw_kk": [DEPTH, DL], "rw_ka": [DEPTH, DL], "rw_rk": [DEPTH, 16, 64], "rw_gn_g": [DEPTH, DL],
    "rw_gn_b": [DEPTH, DL], "w_pb": [DEPTH, DL, D], "w_o": [DEPTH, D, D], "w_up": [DEPTH, D, 2 * DFF],
    "ffn_conv_w": [DEPTH, 3, 2 * DFF], "ffn_conv_b": [DEPTH, 2 * DFF], "w_down": [DEPTH, DFF, D],
    "norm_final": [D],
}
IN_SHAPES = {
    "xp": [TP, D], "xs": [TS, D], "cc": [NSEQ, D], "st_lc": [DEPTH, NS, 3, DL], "st_lh": [DEPTH, NS, DL],
    "st_sh": [DEPTH, NS, NRW], "st_S": [DEPTH, NS, 16, 64, 64], "st_fc": [DEPTH, NS, 2, 2 * DFF],
}
OUT_SHAPES = {
    "y_p": [TP, D], "y_s": [TS, D], "o_lc": [DEPTH, NSEQ, 3, DL], "o_lh": [DEPTH, NSEQ, DL],
    "o_sh": [DEPTH, NSEQ, NRW], "o_S": [DEPTH, NSEQ, 16, 64, 64], "o_fc": [DEPTH, NSEQ, 2, 2 * DFF],
}

PROWS = [("b_ada", 96), ("norm_mix", 16), ("norm_ffn", 16), ("lru_conv_w", 32), ("lru_conv_b", 8),
         ("lru_ba", 8), ("lru_bi", 8), ("lru_lambda", 8), ("rw_mu", 27), ("rw_w0", 8), ("rw_a0", 8),
         ("ffn_conv_w", 264), ("ffn_conv_b", 88)]
PCOL = {}
_c = 0
for _n, _r in PROWS:
    PCOL[_n] = _c
    _c += _r
NPROW = _c
NPG = (NPROW + 127) // 128


def build():
    nc = bass.Bass("TRN2", target_bir_lowering=False)
    I = {k: nc.dram_tensor(k, s, F32, kind="ExternalInput") for k, s in IN_SHAPES.items()}
    Wt = {k: nc.dram_tensor(k, s, F32, kind="ExternalInput") for k, s in W_SHAPES.items()}
    O = {k: nc.dram_tensor(k, s, F32, kind="ExternalOutput") for k, s in OUT_SHAPES.items()}
    xT_h = nc.dram_tensor("xT_scr", [D, T], F32, kind="Internal")
    rq_h = nc.dram_tensor("rq_scr", [6, 16, T, 64], F32, kind="Internal")
    tk_h = nc.dram_tensor("tk_scr", [8, T, DL], F32, kind="Internal")
    cc_h = nc.dram_tensor("cc_scr", [32, T], F32, kind="Internal")
    sg_h = nc.dram_tensor("sg_scr", [2 * D, T], BF16, kind="Internal")
    ga_h = nc.dram_tensor("ga_scr", [DL, T], BF16, kind="Internal")
    oT_h = nc.dram_tensor("oT_scr", [DL, T], BF16, kind="Internal")
    aT_h = nc.dram_tensor("aT_scr", [DFF, T], BF16, kind="Internal")
    xT = xT_h.ap()
    tk = tk_h.ap()
    sg = sg_h.ap()
    ga_d = ga_h.ap()
    oT_d = oT_h.ap()
    aT_d = aT_h.ap()

    with ExitStack() as st:
        P = Prog(nc, st)
        actT = P.sbuf("actT", [128, 16, T], BF16); b_act = P.buf("actT")
        arena = P.sbuf("arena", [128, 5, SLOT], F32)
        b_S = [P.buf(f"S{i}") for i in range(5)]
        wsl = [P.sbuf(f"wsl{i}", [128, 16, 256], BF16) for i in range(2)]
        b_w = [P.buf(f"w{i}") for i in range(2)]
        blk = P.sbuf("blk", [128, 16384], BF16); b_blk = P.buf("blk")
        blkF = blk[:, :].bitcast(F32)
        ident = P.sbuf("ident", [128, 128], F32); b_id = P.buf("ident")
        ones = P.sbuf("ones", [128, 128], F32); b_ones = P.buf("ones")
        scT = P.sbuf("scT", [128, 16, NSEQ], BF16); b_scT = P.buf("scT")
        modT = P.sbuf("modT", [128, 96, NSEQ], F32); b_mod = P.buf("modT")
        A1 = P.sbuf("A1", [128, 16, NSEQ], F32); b_A1 = P.buf("A1")
        PT = P.sbuf("PT", [128, NPG * 128], F32); b_PT = P.buf("PT")
        pst = P.sbuf("pst", [128, 128], F32); b_pst = P.buf("pst")
        cA = P.sbuf("cA", [128, 16], F32); b_cA = P.buf("cA")
        bda = P.sbuf("bda", [128, 8, 128], BF16); b_bda = P.buf("bda")
        bdi = P.sbuf("bdi", [128, 8, 128], BF16); b_bdi = P.buf("bdi")
        lw2 = P.sbuf("lw2", [128, DL], BF16); b_lw2 = P.buf("lw2")
        g2a = P.sbuf("g2a", [128, DL], BF16); b_g2a = P.buf("g2a")
        g2b = P.sbuf("g2b", [32, DL], BF16); b_g2b = P.buf("g2b")
        stlc = P.sbuf("stlc", [128, 8, 48], F32); b_stlc = P.buf("stlc")
        stlh = P.sbuf("stlh", [128, 8, 16], F32); b_stlh = P.buf("stlh")
        stsh = P.sbuf("stsh", [128, 27, 16], F32); b_stsh = P.buf("stsh")
        olc = P.sbuf("olc", [128, 8, 51], F32); b_olc = P.buf("olc")
        olh = P.sbuf("olh", [128, 8, NSEQ], F32); b_olh = P.buf("olh")
        osh = P.sbuf("osh", [128, 27, NSEQ], F32); b_osh = P.buf("osh")
        ofc = P.sbuf("ofc", [128, 8, 34], F32); b_ofc = P.buf("ofc")
        stfc = P.sbuf("stfc", [128, 8, 32], F32); b_stfc = P.buf("stfc")
        tmpb = P.sbuf("tmpb", [128, 512], F32); b_tmpb = P.buf("tmpb")
        tmpc = P.sbuf("tmpc", [128, 512], F32); b_tmpc = P.buf("tmpc")
        rowst = blkF[:, 7168:8192]; b_rowst = P.buf("rowst")
        Sst = blkF[:, 0:2048]; b_Sst = b_blk
        Stmp = blkF[:, 2048:4096]; b_Stmp = b_blk
        skk = P.sbuf("skk", [128, 32], F32); b_skk = P.buf("skk")
        sgt = P.sbuf("sgt", [128, 2, 1024], BF16); b_sgt = [P.buf("sgt0"), P.buf("sgt1")]
        skys = P.sbuf("skys", [128, 2 * 2 * 16 * 8], F32); b_skys = [P.buf("skys0"), P.buf("skys1")]
        ccb = P.sbuf("ccb", [128, 2 * 32], F32); b_ccb = [P.buf("ccb0"), P.buf("ccb1")]
        dummy = P.sbuf("fence_dummy", [128, 8], F32)
        b_cc = P.buf("cc")
        vbuf = P.sbuf("vbuf", [128, 32 * 8], F32); b_vbuf = P.buf("vbuf")
        ybuf = P.sbuf("ybuf", [128, 32 * 8], F32); b_ybuf = P.buf("ybuf")
        sm = P.sbuf("sm", [128, 64], F32); b_sm = P.buf("sm")
        ct = [blkF[:, 4096 + i * 1024:4096 + (i + 1) * 1024] for i in range(3)]
        b_ct = [b_blk, b_blk, b_blk]
        PS = [P.psum(f"ps{i}", [128, 512]) for i in range(8)]
        b_ps = [P.buf(f"ps{i}") for i in range(8)]
        psc = [0]

        def nps():
            i = psc[0] % 8
            psc[0] += 1
            return PS[i], b_ps[i]

        b_xT = [P.buf(f"xT{c}") for c in range(16)]
        b_rq = P.buf("rq"); b_tk = [P.buf(f"tk{i}") for i in range(8)]
        b_sg = P.buf("sg"); b_ga = P.buf("ga"); b_oT = P.buf("oT"); b_aT = P.buf("aT")
        b_out = {k: P.buf("o_" + k) for k in OUT_SHAPES}

        def slot(i, n=SLOT, off=0):
            return arena[:, i, off:off + n]

        wcnt = [0]

        def wslot():
            i = wcnt[0] % 2
            wcnt[0] += 1
            return wsl[i], b_w[i]

        def act(out, in_, func, reads, writes, bias=None, scale=None):
            kw = {}
            if bias is not None:
                kw["bias"] = bias
            if scale is not None:
                kw["scale"] = scale
            P.op("act", lambda e: e.activation(out=out, in_=in_, func=func, **kw), reads=reads, writes=writes)

        def tt(out, in0, in1, op, reads, writes, eng="dve"):
            P.op(eng, lambda e: e.tensor_tensor(out=out, in0=in0, in1=in1, op=op), reads=reads, writes=writes)

        def ts(out, in0, s1, s2, op0, op1, reads, writes, eng="dve"):
            if s2 is None:
                P.op(eng, lambda e: e.tensor_scalar(out=out, in0=in0, scalar1=s1, scalar2=None, op0=op0), reads=reads, writes=writes)
            else:
                P.op(eng, lambda e: e.tensor_scalar(out=out, in0=in0, scalar1=s1, scalar2=s2, op0=op0, op1=op1), reads=reads, writes=writes)

        def stt(out, in0, scalar, in1, op0, op1, reads, writes):
            P.op("dve", lambda e: e.scalar_tensor_tensor(out=out, in0=in0, scalar=scalar, in1=in1, op0=op0, op1=op1),
                 reads=reads, writes=writes)

        def cp(out, in_, reads, writes, eng="dve"):
            P.op(eng, lambda e: e.tensor_copy(out=out, in_=in_), reads=reads, writes=writes)

        def mm(ps_ap, lhsT, rhs, start, stop, reads, pb):
            P.pe_group(lambda e: e.matmul(ps_ap, lhsT=lhsT, rhs=rhs, start=start, stop=stop), list(reads), pb)

        def tr(ps_ap, in_, n_in_part, reads, pb):
            P.pe_group(lambda e: e.transpose(ps_ap, in_, ident[0:n_in_part, 0:n_in_part]), list(reads) + [b_id], pb)

        def memset(ap, val, writes, eng="pool"):
            P.op(eng, lambda e: e.memset(ap, val), writes=writes)

        def fence(bufs, eng="pool"):
            P.op(eng, lambda e: e.memset(dummy[:, 0:1], 0.0), writes=list(bufs))

        memset(ident[:], 0.0, [b_id])
        P.op("pool", lambda e: e.affine_select(out=ident[:], in_=ident[:], compare_op=ALU.not_equal, fill=1.0,
                                               base=0, pattern=[[-1, 128]], channel_multiplier=1),
             reads=[b_id], writes=[b_id])
        memset(ones[:], 1.0, [b_ones])

        P.dma("sp", rowst[0:NSEQ, :], I["cc"].ap()[:, 0:1024], writes=[b_rowst])
        for half in range(2):
            if half == 1:
                P.dma("sp", rowst[0:NSEQ, :], I["cc"].ap()[:, 1024:2048], writes=[b_rowst])
            act(rowst[0:NSEQ, :], rowst[0:NSEQ, :], AF.Silu, [b_rowst], [b_rowst])
            ps, pb = nps()
            for j in range(8):
                tr(ps[:, j * NSEQ:(j + 1) * NSEQ], rowst[0:NSEQ, j * 128:(j + 1) * 128], NSEQ, [b_rowst], pb)
            cp(scT[:, half * 8:(half + 1) * 8, :].rearrange("p a b -> p (a b)"), ps[:, 0:8 * NSEQ], [pb], [b_scT])

        xT_v = xT.rearrange("(kc p) t -> p kc t", p=128)
        for i in range(17):
            src = I["xp"].ap()[i * 128:(i + 1) * 128, :] if i < 16 else I["xs"].ap()[:, :]
            P.dma("sp", slot(0, 2048), src, writes=[b_S[0]])
            for g in range(4):
                ps, pb = nps()
                for j in range(4):
                    kc = g * 4 + j
                    tr(ps[:, j * 128:(j + 1) * 128], slot(0, 128, kc * 128), 128, [b_S[0]], pb)
                P.op("act", lambda e, ps=ps, g=g: e.activation(out=slot(1, 512, g * 512), in_=ps[:, :], func=AF.Identity),
                     reads=[pb], writes=[b_S[1]])
            P.dma("sp", xT_v[:, :, i * 128:(i + 1) * 128], slot(1, 2048).rearrange("p (kc t) -> p kc t", kc=16),
                  reads=[b_S[1]], writes=b_xT)

        def load_params(l):
            for g in range(NPG):
                memset(pst[:], 0.0, [b_pst])
                r = 0
                for name, nrows in PROWS:
                    for rr in range(nrows):
                        grow = PCOL[name] + rr
                        if grow // 128 != g:
                            continue
                    lo = max(PCOL[name], g * 128)
                    hi = min(PCOL[name] + nrows, (g + 1) * 128)
                    if lo >= hi:
                        continue
                    r0 = lo - PCOL[name]
                    n = hi - lo
                    flat = Wt[name].ap()[l]
                    if name in ("lru_conv_w", "ffn_conv_w"):
                        flat = flat.rearrange("j c -> (j c)")
                    if name == "rw_mu":
                        nfull = min(n, max(0, 26 - r0))
                        if nfull > 0:
                            P.dma("act", pst[lo - g * 128:lo - g * 128 + nfull, :],
                                  flat[r0 * 128:(r0 + nfull) * 128].rearrange("(r c) -> r c", c=128), writes=[b_pst])
                        if r0 + n == 27:
                            P.dma("act", pst[hi - 1 - g * 128:hi - g * 128, 0:32],
                                  flat[26 * 128:26 * 128 + 32].rearrange("(r c) -> r c", c=32), writes=[b_pst])
                    else:
                        P.dma("act", pst[lo - g * 128:hi - g * 128, :],
                              flat[r0 * 128:(r0 + n) * 128].rearrange("(r c) -> r c", c=128), writes=[b_pst])
                ps, pb = nps()
                tr(ps[:, 0:128], pst[:, :], 128, [b_pst], pb)
                cp(PT[:, g * 128:(g + 1) * 128], ps[:, 0:128], [pb], [b_PT])

        def pcol(name, row):
            c = PCOL[name] + row
            return PT[:, c:c + 1]

        def loadT(dst3, bdst, src_rows_fn, nrows, nchunks, last_cols=128):
            c = 0
            while c < nchunks:
                g = min(8, nchunks - c)
                width = (g - 1) * 128 + (last_cols if c + g == nchunks else 128)
                P.dma("sp", rowst[0:nrows, 0:width], src_rows_fn(c * 128, width), writes=[b_rowst])
                per = 512 // nrows
                j = 0
                while j < g:
                    gg = min(per, g - j)
                    ps, pb = nps()
                    for k in range(gg):
                        cc_ = c + j + k
                        ncol = last_cols if cc_ == nchunks - 1 else 128
                        tr(ps[0:ncol, k * nrows:(k + 1) * nrows], rowst[0:nrows, (j + k) * 128:(j + k) * 128 + ncol], nrows, [b_rowst], pb)
                    cp(dst3[:, c + j:c + j + gg, :].rearrange("p a b -> p (a b)"), ps[:, 0:gg * nrows], [pb], [bdst])
                    j += gg
                c += g

        def flushT(src3, bsrc, nchunks, n, dst_fn, bout, last_cols=128):
            c = 0
            while c < nchunks:
                g = min(4, nchunks - c)
                ps, pb = nps()
                width = 0
                for k in range(g):
                    ncol = last_cols if c + k == nchunks - 1 else 128
                    tr(ps[0:n, k * 128:k * 128 + ncol], src3[0:ncol, c + k, :], ncol, [bsrc], pb)
                    width += ncol
                cp(rowst[0:n, 0:width], ps[0:n, 0:width], [pb], [b_rowst])
                P.dma("sp", dst_fn(c * 128, width), rowst[0:n, 0:width], reads=[b_rowst], writes=[bout])
                c += g

        def wload(dst, src, bw):
            P.dma("pool", dst, src, writes=[bw])

        def norm_stats(t0, n):
            xb = arena[:, 0:4, :].rearrange("p a b -> p (a b)")[:, 0:16 * n].rearrange("p (kc t) -> p kc t", kc=16)
            P.dma("sp", xb, xT_v[:, :, t0:t0 + n], reads=b_xT, writes=b_S[0:4])
            ps, pb = nps()
            for kc in range(16):
                sqb, bsq = (tmpb, b_tmpb) if kc % 2 == 0 else (tmpc, b_tmpc)
                act(sqb[:, 0:n], xb[:, kc, :], AF.Square, b_S[0:4], [bsq])
                mm(ps[:, 0:n], ones[:], sqb[:, 0:n], kc == 0, kc == 15, [b_ones, bsq], pb)
            rstd = slot(4, n)
            act(rstd, ps[:, 0:n], AF.Sqrt, [pb], [b_S[4]], bias=1e-6, scale=1.0 / D)
            P.op("dve", lambda e: e.reciprocal(out=rstd, in_=rstd), reads=[b_S[4]], writes=[b_S[4]])
            return xb, rstd

        def norm_mod(Aap, Bap, bA, bB):
            for (t0, n) in TB:
                xb, rstd = norm_stats(t0, n)
                for kc in range(16):
                    tt(xb[:, kc, :], xb[:, kc, :], rstd, ALU.mult, b_S[0:5], b_S[0:4])
                    if t0 < TP:
                        ts(actT[:, kc, t0:t0 + n], xb[:, kc, :], Aap[:, kc, 0:1], Bap[:, kc, 0:1], ALU.mult, ALU.add,
                           b_S[0:4] + [bA, bB], [b_act])
                    else:
                        x3 = xb[:, kc, :].rearrange("p (s t) -> p s t", s=NS)
                        tt(x3, x3, Aap[:, kc, 1:NSEQ].unsqueeze(2).to_broadcast([128, NS, 8]), ALU.mult, b_S[0:4] + [bA], b_S[0:4])
                        tt(actT[:, kc, t0:t0 + n].rearrange("p (s t) -> p s t", s=NS), x3,
                           Bap[:, kc, 1:NSEQ].unsqueeze(2).to_broadcast([128, NS, 8]), ALU.add, b_S[0:4] + [bB], [b_act])

        def resid_update(c, t0, n, ps, pb, gcol0):
            xs_ = tmpb[:, 0:n]
            P.dma("sp", xs_, xT_v[:, c, t0:t0 + n], reads=[b_xT[c]], writes=[b_tmpb])
            if t0 < TP:
                stt(xs_, ps[:, 0:n], modT[:, gcol0 + c, 0:1], xs_, ALU.mult, ALU.add, [pb, b_mod, b_tmpb], [b_tmpb])
            else:
                g3 = modT[:, gcol0 + c, 1:NSEQ].unsqueeze(2).to_broadcast([128, NS, 8])
                t3 = tmpc[:, 0:n].rearrange("p (s t) -> p s t", s=NS)
                tt(t3, ps[:, 0:n].rearrange("p (s t) -> p s t", s=NS), g3, ALU.mult, [pb, b_mod], [b_tmpc])
                tt(xs_, xs_, tmpc[:, 0:n], ALU.add, [b_tmpb, b_tmpc], [b_tmpb])
            P.dma("sp", xT_v[:, c, t0:t0 + n], xs_, reads=[b_tmpb], writes=[b_xT[c]])

        def to_tokmajor(src_slot_ap, bsrc, qi, c0):
            tkv = tk[qi].rearrange("(tt p) c -> p tt c", p=128)
            for g0 in range(0, 17, 4):
                g = min(4, 17 - g0)
                ps, pb = nps()
                for k in range(g):
                    tr(ps[:, k * 128:(k + 1) * 128], src_slot_ap[:, (g0 + k) * 128:(g0 + k + 1) * 128], 128, [bsrc], pb)
                act(rowst[:, 0:g * 128], ps[:, 0:g * 128], AF.Identity, [pb], [b_rowst])
                P.dma("sp", tkv[:, g0:g0 + g, c0:c0 + 128], rowst[:, 0:g * 128].rearrange("p (a b) -> p a b", a=g),
                      reads=[b_rowst], writes=[b_tk[qi]])

        bx = [[P.buf(f"x_{j}_{p}") for p in range(2)] for j in range(5)]
        bv = [P.buf(f"v_{p}") for p in range(2)]
        by = [P.buf(f"y_{p}") for p in range(2)]
        bSS = [P.buf("SA"), P.buf("SB")]
        bT2 = P.buf("T2"); bTt = P.buf("Tt")
        bKV = [P.buf(f"KV_{i}") for i in range(4)]
        b_ab1 = P.buf("ab1")
        for l in range(NLAYERS):
            load_params(l)
            memset(bda[:], 0.0, [b_bda]); memset(bdi[:], 0.0, [b_bdi])
            for (dst, bd_, nm) in ((bda, b_bda, "lru_wa"), (bdi, b_bdi, "lru_wi")):
                wv = Wt[nm].ap()[l].rearrange("(c two) d e -> two d c e", two=2)
                P.dma("pool", dst[0:64, :, 0:64], wv[0], writes=[bd_])
                P.dma("pool", dst[64:128, :, 64:128], wv[1], writes=[bd_])
            P.dma("pool", lw2[0:64, :], Wt["rw_w2"].ap()[l], writes=[b_lw2])
            P.dma("pool", lw2[64:128, :], Wt["rw_a2"].ap()[l], writes=[b_lw2])
            P.dma("pool", g2a[:, :], Wt["rw_g2"].ap()[l, 0:128, :], writes=[b_g2a])
            P.dma("pool", g2b[:, :], Wt["rw_g2"].ap()[l, 128:160, :], writes=[b_g2b])
            lam = PT[:, PCOL["lru_lambda"]:PCOL["lru_lambda"] + 8]
            act(cA[:, 0:8], lam, AF.Exp, [b_PT], [b_cA], scale=-1.0)
            act(cA[:, 0:8], cA[:, 0:8], AF.Ln, [b_cA], [b_cA], bias=1.0)
            ts(cA[:, 8:16], cA[:, 0:8], -16.0, None, ALU.mult, None, [b_cA], [b_cA])
            ts(cA[:, 0:8], cA[:, 0:8], -8.0, None, ALU.mult, None, [b_cA], [b_cA])
            loadT(stlc, b_stlc, lambda c0, w: I["st_lc"].ap()[l].rearrange("s j c -> (s j) c")[:, c0:c0 + w], 48, 8)
            loadT(stlh, b_stlh, lambda c0, w: I["st_lh"].ap()[l][:, c0:c0 + w], 16, 8)
            loadT(stsh, b_stsh, lambda c0, w: I["st_sh"].ap()[l][:, c0:c0 + w], 16, 27, last_cols=32)

            for blk_i in range(48):
                ws, bw = wslot()
                wload(ws[:, :, 0:256], Wt["w_ada"].ap()[l].rearrange("(kc p) c -> p kc c", p=128)[:, :, blk_i * 256:(blk_i + 1) * 256], bw)
                ps, pb = nps()
                for j2 in range(2):
                    for kc in range(16):
                        mm(ps[:, j2 * 32:j2 * 32 + NSEQ], ws[:, kc, j2 * 128:(j2 + 1) * 128], scT[:, kc, :], kc == 0, kc == 15,
                           [bw, b_scT], pb)
                for j2 in range(2):
                    j = blk_i * 2 + j2
                    act(modT[:, j, :], ps[:, j2 * 32:j2 * 32 + NSEQ], AF.Identity, [pb, b_PT], [b_mod], bias=pcol("b_ada", j))

            def make_A(sc0, nm_name):
                nmb = PT[:, PCOL[nm_name]:PCOL[nm_name] + 16].unsqueeze(2).to_broadcast([128, 16, NSEQ])
                stt(A1[:, :, :], modT[:, sc0:sc0 + 16, :], 1.0, nmb, ALU.add, ALU.mult, [b_mod, b_PT], [b_A1])

            make_A(16, "norm_mix")
            norm_mod(A1, modT[:, 0:16, :], b_A1, b_mod)

            win = Wt["w_in"].ap()[l].rearrange("(kc p) c -> p kc c", p=128)

            def proj_block(ws, wc0, m, t0, n, bw):
                ps, pb = nps()
                for kc in range(16):
                    mm(ps[0:m, 0:n], ws[:, kc, wc0:wc0 + m], actT[:, kc, t0:t0 + n], kc == 0, kc == 15, [bw, b_act], pb)
                return ps, pb

            for c in range(8):
                ws, bw = wslot()
                wload(ws[:, :, 0:128], win[:, :, c * 128:(c + 1) * 128], bw)
                wload(ws[:, :, 128:256], win[:, :, DL + c * 128:DL + (c + 1) * 128], bw)
                LX = slot(0); XC = slot(1); AA = slot(3); INP = slot(4)
                LXs = LX[:, 2051:2051 + 176].rearrange("p (s j) -> p s j", s=NS)
                memset(LX[:, 0:3], 0.0, [b_S[0]])
                cp(LXs[:, :, 0:3], stlc[:, c, :].rearrange("p (s j) -> p s j", s=NS), [b_stlc], [b_S[0]], eng="pool")
                for (t0, n) in TB:
                    ps, pb = proj_block(ws, 0, 128, t0, n, bw)
                    if t0 < TP:
                        act(LX[:, 3 + t0:3 + t0 + n], ps[:, 0:n], AF.Identity, [pb], [b_S[0]])
                    else:
                        act(LXs[:, :, 3:11], ps[:, 0:n].rearrange("p (s t) -> p s t", s=NS), AF.Identity, [pb], [b_S[0]])
                cp(olc[:, c, 0:3], LX[:, 2048:2051], [b_S[0]], [b_olc], eng="pool")
                cp(olc[:, c, 3:51].rearrange("p (s j) -> p s j", s=NS), LXs[:, :, 8:11], [b_S[0]], [b_olc], eng="pool")
                XCs = XC[:, TP:T].rearrange("p (s t) -> p s t", s=NS)
                ts(XC[:, 0:TP], LX[:, 0:TP], pcol("lru_conv_w", 0 * 8 + c), pcol("lru_conv_b", c), ALU.mult, ALU.add,
                   [b_S[0], b_PT], [b_S[1]])
                ts(XCs, LXs[:, :, 0:8], pcol("lru_conv_w", 0 * 8 + c), pcol("lru_conv_b", c), ALU.mult, ALU.add,
                   [b_S[0], b_PT], [b_S[1]])
                for j in range(1, 4):
                    stt(XC[:, 0:TP], LX[:, j:j + TP], pcol("lru_conv_w", j * 8 + c), XC[:, 0:TP], ALU.mult, ALU.add,
                        [b_S[0], b_S[1], b_PT], [b_S[1]])
                    stt(XCs, LXs[:, :, j:j + 8], pcol("lru_conv_w", j * 8 + c), XCs, ALU.mult, ALU.add,
                        [b_S[0], b_S[1], b_PT], [b_S[1]])
                XCb = slot(2).bitcast(BF16)[:, 0:T]
                act(XCb, XC[:, 0:T], AF.Identity, [b_S[1]], [b_S[2]])
                for (t0, n) in TB:
                    psr, pbr = nps()
                    mm(psr[:, 0:n], bda[:, c, :], XCb[:, t0:t0 + n], True, True, [b_bda, b_S[2]], pbr)
                    psi, pbi = nps()
                    mm(psi[:, 0:n], bdi[:, c, :], XCb[:, t0:t0 + n], True, True, [b_bdi, b_S[2]], pbi)
                    rr = INP[:, t0:t0 + n]
                    act(rr, psr[:, 0:n], AF.Sigmoid, [pbr, b_PT], [b_S[4]], bias=pcol("lru_ba", c))
                    act(AA[:, t0:t0 + n], rr, AF.Exp, [b_S[4], b_cA], [b_S[3]], scale=cA[:, c:c + 1])
                    act(rr, rr, AF.Exp, [b_S[4], b_cA], [b_S[4]], scale=cA[:, 8 + c:9 + c])
                    act(rr, rr, AF.Sqrt, [b_S[4]], [b_S[4]], scale=-1.0, bias=1.0)
                    act(tmpb[:, 0:n], psi[:, 0:n], AF.Sigmoid, [pbi, b_PT], [b_tmpb], bias=pcol("lru_bi", c))
                    tt(rr, rr, tmpb[:, 0:n], ALU.mult, [b_S[4], b_tmpb], [b_S[4]])
                    tt(rr, rr, XC[:, t0:t0 + n], ALU.mult, [b_S[4], b_S[1]], [b_S[4]])
                AAs = AA[:, TP:T].rearrange("p (s t) -> p s t", s=NS)
                INs = INP[:, TP:T].rearrange("p (s t) -> p s t", s=NS)
                tt(sm[:, 0:NS], AAs[:, :, 0], stlh[:, c, :], ALU.mult, [b_S[3], b_stlh], [b_sm])
                tt(INs[:, :, 0], INs[:, :, 0], sm[:, 0:NS], ALU.add, [b_S[4], b_sm], [b_S[4]])
                memset(AAs[:, :, 0], 0.0, [b_S[3]], eng="dve")
                HL = slot(1)
                P.op("dve", lambda e, HL=HL, AA=AA, INP=INP: e.tensor_tensor_scan(out=HL[:, 0:TP], data0=AA[:, 0:TP], data1=INP[:, 0:TP],
                                                                           initial=0.0, op0=ALU.mult, op1=ALU.add),
                     reads=[b_S[3], b_S[4]], writes=[b_S[1]])
                P.op("dve", lambda e, HL=HL, AA=AA, INP=INP: e.tensor_tensor_scan(out=HL[:, TP:T], data0=AA[:, TP:T], data1=INP[:, TP:T],
                                                                           initial=0.0, op0=ALU.mult, op1=ALU.add),
                     reads=[b_S[3], b_S[4]], writes=[b_S[1]])
                cp(olh[:, c, 0:1], HL[:, TP - 1:TP], [b_S[1]], [b_olh], eng="pool")
                cp(olh[:, c, 1:NSEQ], HL[:, TP:T].rearrange("p (s t) -> p s t", s=NS)[:, :, 7], [b_S[1]], [b_olh], eng="pool")
                GG = slot(0)
                GAb = slot(2).bitcast(BF16)[:, 0:T]
                for (t0, n) in TB:
                    ps, pb = proj_block(ws, 128, 128, t0, n, bw)
                    u = GG[:, t0:t0 + n]
                    act(u, ps[:, 0:n], AF.Identity, [pb], [b_S[0]])
                    tt(tmpb[:, 0:n], u, u, ALU.mult, [b_S[0]], [b_tmpb])
                    ts(tmpb[:, 0:n], tmpb[:, 0:n], 0.044715, 1.0, ALU.mult, ALU.add, [b_tmpb], [b_tmpb])
                    tt(tmpb[:, 0:n], tmpb[:, 0:n], u, ALU.mult, [b_tmpb, b_S[0]], [b_tmpb])
                    act(tmpb[:, 0:n], tmpb[:, 0:n], AF.Sigmoid, [b_tmpb], [b_tmpb], scale=1.5957691216057308)
                    tt(u, u, tmpb[:, 0:n], ALU.mult, [b_S[0], b_tmpb], [b_S[0]])
                    tt(GAb[:, t0:t0 + n], u, HL[:, t0:t0 + n], ALU.mult, [b_S[0], b_S[1]], [b_S[2]])
                P.dma("sp", ga_d[c * 128:(c + 1) * 128, :], GAb, reads=[b_S[2]], writes=[b_ga])
            flushT(olc, b_olc, 8, 51, lambda c0, w: O["o_lc"].ap()[l].rearrange("s j c -> (s j) c")[:, c0:c0 + w], b_out["o_lc"])
            flushT(olh, b_olh, 8, NSEQ, lambda c0, w: O["o_lh"].ap()[l][:, c0:c0 + w], b_out["o_lh"])

            TL = blk[:, :]
            TLa = TL[:, 0:T]
            SG1 = TL[:, T:2 * T]
            SG2 = TL[:, 2 * T:3 * T]
            b_TL = [b_blk]
            def rw_A(q, par):
                m = 32 if q == 26 else 128
                ws, bw = wslot()
                wload(ws[:, :, 0:m], win[:, :, 2 * DL + q * 128:2 * DL + q * 128 + m], bw)
                RW = slot(2 * par); bRW = b_S[2 * par]
                RWs = RW[:, 2049:2049 + 144].rearrange("p (s j) -> p s j", s=NS)
                memset(RW[0:m, 0:1], 0.0, [bRW])
                cp(RWs[0:m, :, 0], stsh[0:m, q, :], [b_stsh], [bRW], eng="pool")
                for (t0, n) in TB:
                    ps, pb = proj_block(ws, 0, m, t0, n, bw)
                    if t0 < TP:
                        act(RW[0:m, 1 + t0:1 + t0 + n], ps[0:m, 0:n], AF.Identity, [pb], [bRW])
                    else:
                        act(RWs[0:m, :, 1:9], ps[0:m, 0:n].rearrange("p (s t) -> p s t", s=NS), AF.Identity, [pb], [bRW])

            def rw_B(q, par):
                m = 32 if q == 26 else 128
                RW = slot(2 * par); bRW = b_S[2 * par]
                XS = slot(2 * par + 1); bXS = b_S[2 * par + 1]
                RWs = RW[:, 2049:2049 + 144].rearrange("p (s j) -> p s j", s=NS)
                cp(osh[0:m, q, 0:1], RW[0:m, TP:TP + 1], [bRW], [b_osh], eng="pool")
                cp(osh[0:m, q, 1:NSEQ], RWs[0:m, :, 8], [bRW], [b_osh], eng="pool")
                XSs = XS[:, TP:T].rearrange("p (s t) -> p s t", s=NS)
                mu = pcol("rw_mu", q)
                tt(XS[0:m, 0:TP], RW[0:m, 0:TP], RW[0:m, 1:TP + 1], ALU.subtract, [bRW], [bXS])
                tt(XSs[0:m], RWs[0:m, :, 0:8], RWs[0:m, :, 1:9], ALU.subtract, [bRW], [bXS])
                stt(XS[0:m, 0:TP], XS[0:m, 0:TP], mu[0:m], RW[0:m, 1:TP + 1], ALU.mult, ALU.add, [bRW, bXS, b_PT], [bXS])
                stt(XSs[0:m], XSs[0:m], mu[0:m], RWs[0:m, :, 1:9], ALU.mult, ALU.add, [bRW, bXS, b_PT], [bXS])
                if q == 24:
                    act(TLa[0:64, :], XS[0:64, 0:T], AF.Tanh, [bXS], b_TL)
                    act(TLa[64:128, :], XS[64:128, 0:T], AF.Identity, [bXS], b_TL)
                elif q == 25:
                    act(SG1, XS[:, 0:T], AF.Sigmoid, [bXS], b_TL)
                elif q == 26:
                    act(SG2[0:32, :], XS[0:32, 0:T], AF.Sigmoid, [bXS], b_TL)
                else:
                    to_tokmajor(XS, bXS, q // 8, (q % 8) * 128)

            qorder = [24, 25, 26] + list(range(24))
            rw_A(qorder[0], 0)
            for qi_, q in enumerate(qorder):
                if qi_ + 1 < len(qorder):
                    rw_A(qorder[qi_ + 1], (qi_ + 1) % 2)
                rw_B(q, qi_ % 2)
            flushT(osh, b_osh, 27, NSEQ, lambda c0, w: O["o_sh"].ap()[l][:, c0:c0 + w], b_out["o_sh"], last_cols=32)
            for c in range(8):
                DC = slot(0); AC = slot(1); GC = slot(2)
                for (t0, n) in TB:
                    ps, pb = nps()
                    mm(ps[:, 0:n], lw2[0:64, c * 128:(c + 1) * 128], TLa[0:64, t0:t0 + n], True, True, [b_lw2] + b_TL, pb)
                    act(DC[:, t0:t0 + n], ps[:, 0:n], AF.Sigmoid, [pb, b_PT], [b_S[0]], bias=pcol("rw_w0", c))
                    act(DC[:, t0:t0 + n], DC[:, t0:t0 + n], AF.Exp, [b_S[0]], [b_S[0]], scale=-math.exp(-0.5))
                    ps, pb = nps()
                    mm(ps[:, 0:n], lw2[64:128, c * 128:(c + 1) * 128], TLa[64:128, t0:t0 + n], True, True, [b_lw2] + b_TL, pb)
                    act(AC[:, t0:t0 + n], ps[:, 0:n], AF.Sigmoid, [pb, b_PT], [b_S[1]], bias=pcol("rw_a0", c))
                    ps, pb = nps()
                    mm(ps[:, 0:n], g2a[:, c * 128:(c + 1) * 128], SG1[:, t0:t0 + n], True, False, [b_g2a] + b_TL, pb)
                    mm(ps[:, 0:n], g2b[0:32, c * 128:(c + 1) * 128], SG2[0:32, t0:t0 + n], False, True, [b_g2b] + b_TL, pb)
                    act(GC[:, t0:t0 + n], ps[:, 0:n], AF.Identity, [pb], [b_S[2]])
                to_tokmajor(DC, b_S[0], 3, c * 128)
                to_tokmajor(AC, b_S[1], 4, c * 128)
                to_tokmajor(GC, b_S[2], 5, c * 128)
            for gc in range(32):
                if gc % 2 == 0:
                    ws, bw = wslot()
                    wload(ws[:, :, 0:256], win[:, :, 2 * DL + NRW + gc * 128:2 * DL + NRW + (gc + 2) * 128], bw)
                SGb = slot(gc % 2).bitcast(BF16)[:, 0:T]
                bsl = b_S[gc % 2]
                for (t0, n) in TB:
                    ps, pb = proj_block(ws, (gc % 2) * 128, 128, t0, n, bw)
                    act(SGb[:, t0:t0 + n], ps[:, 0:n], AF.Sigmoid, [pb], [bsl])
                P.dma("sp", sg[gc * 128:(gc + 1) * 128, :], SGb, reads=[bsl], writes=[b_sg])

            for i, nm in enumerate(("rw_kk", "rw_ka", "rw_rk")):
                src = Wt[nm].ap()[l]
                if nm == "rw_rk":
                    src = src.rearrange("h k -> (h k)")
                P.dma("act", ct[i][:, :], src.partition_broadcast(128), writes=[b_ct[i]])
            rqv = rq_h.ap()

            def half(i, h):
                return arena[:, i, h * 1024:(h + 1) * 1024]

            def h3(ap):
                return ap.rearrange("p (h k) -> p h k", h=16)

            for i in range(17):
                rows = slice(i * 128, (i + 1) * 128)
                Rt, Kt, Vt, Dt, At = half(0, 0), half(0, 1), half(1, 0), half(1, 1), half(2, 0)
                KKt, TMt, KMt, NBt, BVt = half(2, 1), half(3, 0), half(3, 1), half(4, 0), half(4, 1)
                for (dst, qi, bb) in ((Rt, 0, b_S[0]), (Kt, 1, b_S[0]), (Vt, 2, b_S[1]), (Dt, 3, b_S[1]), (At, 4, b_S[2])):
                    P.dma("sp", dst, tk[qi][rows, :], reads=[b_tk[qi]], writes=[bb])
                tt(KKt, Kt, ct[0][:, :], ALU.mult, [b_S[0], b_ct[0]], [b_S[2]])
                tt(TMt, KKt, KKt, ALU.mult, [b_S[2]], [b_S[3]])
                P.op("dve", lambda e, TMt=TMt: e.tensor_reduce(out=sm[:, 0:16], in_=h3(TMt), axis=AX.X, op=ALU.add),
                     reads=[b_S[3]], writes=[b_sm])
                act(sm[:, 0:16], sm[:, 0:16], AF.Sqrt, [b_sm], [b_sm])
                ts(sm[:, 0:16], sm[:, 0:16], 1e-12, None, ALU.max, None, [b_sm], [b_sm])
                P.op("dve", lambda e: e.reciprocal(out=sm[:, 0:16], in_=sm[:, 0:16]), reads=[b_sm], writes=[b_sm])
                tt(h3(KKt), h3(KKt), sm[:, 0:16].unsqueeze(2).to_broadcast([128, 16, 64]), ALU.mult, [b_S[2], b_sm], [b_S[2]])
                stt(TMt, At, -1.0, ct[1][:, :], ALU.add, ALU.mult, [b_S[2], b_ct[1]], [b_S[3]])
                stt(KMt, TMt, 1.0, Kt, ALU.add, ALU.mult, [b_S[3], b_S[0]], [b_S[3]])
                stt(NBt, KKt, -1.0, At, ALU.mult, ALU.mult, [b_S[2]], [b_S[4]])
                ccv = sm[:, 32:64].rearrange("p (h j) -> p h j", j=2)
                tt(TMt, Rt, KMt, ALU.mult, [b_S[0], b_S[3]], [b_S[3]])
                P.op("dve", lambda e, TMt=TMt, ccv=ccv: e.tensor_reduce(out=ccv[:, :, 1], in_=h3(TMt), axis=AX.X, op=ALU.add),
                     reads=[b_S[3]], writes=[b_sm])
                tt(TMt, TMt, ct[2][:, :], ALU.mult, [b_S[3], b_ct[2]], [b_S[3]])
                P.op("dve", lambda e, TMt=TMt: e.tensor_reduce(out=sm[:, 16:32], in_=h3(TMt), axis=AX.X, op=ALU.add),
                     reads=[b_S[3]], writes=[b_sm])
                tt(h3(BVt), h3(Vt), sm[:, 16:32].unsqueeze(2).to_broadcast([128, 16, 64]), ALU.mult, [b_S[1], b_sm], [b_S[4]])
                for (srct, qi, bb) in ((Rt, 0, b_S[0]), (Dt, 1, b_S[1]), (KMt, 2, b_S[3]), (KKt, 3, b_S[2]), (NBt, 4, b_S[4])):
                    P.dma("sp", rqv[qi].rearrange("h t k -> t h k")[rows], h3(srct), reads=[bb], writes=[b_rq])
                tt(Kt, NBt, Rt, ALU.mult, [b_S[4], b_S[0]], [b_S[0]])
                P.op("dve", lambda e, Kt=Kt, ccv=ccv: e.tensor_reduce(out=ccv[:, :, 0], in_=h3(Kt), axis=AX.X, op=ALU.add),
                     reads=[b_S[0]], writes=[b_sm])
                tt(Kt, Rt, Dt, ALU.mult, [b_S[0], b_S[1]], [b_S[0]])
                P.dma("sp", rqv[5].rearrange("h t k -> t h k")[rows], h3(Kt), reads=[b_S[0]], writes=[b_rq])
                ps, pb = nps()
                tr(ps[0:32, 0:128], sm[:, 32:64], 128, [b_sm], pb)
                cp(rowst[0:32, 0:128], ps[0:32, 0:128], [pb], [b_rowst])
                P.dma("sp", cc_h.ap()[:, rows], rowst[0:32, 0:128], reads=[b_rowst], writes=[b_cc])
                P.dma("sp", tk[6][rows, :], BVt, reads=[b_S[4]], writes=[b_tk[6]])

            def rec_steps(nsq, nsteps, xr_step, v_step, y_step):
                S3 = Sst[:, 0:nsq * 512].rearrange("p (s v k) -> p s v k", s=nsq, v=8)
                T3 = Stmp[:, 0:nsq * 512].rearrange("p (s v k) -> p s v k", s=nsq, v=8)
                K4 = [128, nsq, 8, 64]
                rdx = b_S[0:5]
                for t in range(nsteps):
                    r_, w_, k_, kk_, nb_ = [xr_step(j, t).unsqueeze(2).to_broadcast(K4) for j in range(5)]
                    sk = skk[:, 0:nsq * 8].rearrange("p (s v) -> p s v", s=nsq)
                    tt(T3, S3, kk_, ALU.mult, [b_Sst] + rdx, [b_Stmp])
                    P.op("dve", lambda e, sk=sk, T3=T3: e.tensor_reduce(out=sk, in_=T3, axis=AX.X, op=ALU.add), reads=[b_Stmp], writes=[b_skk])
                    tt(S3, S3, w_, ALU.mult, [b_Sst] + rdx, [b_Sst])
                    tt(T3, nb_, sk.unsqueeze(3).to_broadcast(K4), ALU.mult, [b_skk] + rdx, [b_Stmp])
                    tt(S3, S3, T3, ALU.add, [b_Sst, b_Stmp], [b_Sst])
                    tt(T3, k_, v_step(t).unsqueeze(3).to_broadcast(K4), ALU.mult, [b_vbuf] + rdx, [b_Stmp])
                    tt(S3, S3, T3, ALU.add, [b_Sst, b_Stmp], [b_Sst])
                    tt(T3, S3, r_, ALU.mult, [b_Sst] + rdx, [b_Stmp])
                    yo = y_step(t)
                    P.op("dve", lambda e, yo=yo, T3=T3: e.tensor_reduce(out=yo, in_=T3, axis=AX.X, op=ALU.add), reads=[b_Stmp], writes=[b_ybuf])

            oS = O["o_S"].ap()[l]
            TC = 16
            SLQ = [3, 5, 1, 4, 2]
            allb = b_S[0:5] + [b for bb_ in bx for b in bb_] + bv + by + bSS + [bT2, bTt] + bKV + [b_blk, b_vbuf, b_ybuf]
            fence(allb)
            SS = [blkF[:, 0:512], blkF[:, 512:1024]]
            T2v = blkF[:, 1024:2048].rearrange("p (q v k) -> p q v k", q=2, v=8)
            Ttv = blkF[:, 2048:2560].rearrange("p (v k) -> p v k", v=8)
            KVv = [blkF[:, 2560 + i * 512:2560 + (i + 1) * 512].rearrange("p (v k) -> p v k", v=8) for i in range(4)]
            memset(SS[0], 0.0, [bSS[0]], eng="dve")
            g = 0
            for ci in range(TP // TC):
                t0 = ci * TC
                par = ci % 2
                xo = par * 1024
                for j in range(5):
                    src = bass.AP(rq_h, SLQ[j] * 16 * T * 64 + t0 * 64, [[T * 64, 16], [0, 8], [1, TC * 64]])
                    P.dma("sp", slot(j, TC * 64, xo), src, reads=[b_rq], writes=[bx[j][par]])
                vb_ = vbuf[:, par * 128:(par + 1) * 128].rearrange("p (t v) -> p t v", t=TC)
                yb_ = ybuf[:, par * 128:(par + 1) * 128].rearrange("p (t v) -> p t v", t=TC)
                cb_ = ccb[:, par * 32:(par + 1) * 32].rearrange("p (j t) -> p j t", j=2)
                sk_ = skys[:, par * 256:(par + 1) * 256].rearrange("p (q t v) -> p q t v", q=2, t=TC)
                P.dma("act", vb_, bass.AP(tk_h, 2 * T * DL + t0 * DL, [[8, 128], [DL, TC], [1, 8]]), reads=[b_tk[2]], writes=[bv[par]])
                for jj in range(2):
                    P.dma("act", cb_[:, jj, :], bass.AP(cc_h, jj * T + t0, [[2 * T, 16], [0, 8], [1, TC]]), reads=[b_cc], writes=[b_ccb[par]])
                for t in range(TC):
                    cur, nxt = g % 2, (g + 1) % 2
                    Sc3 = SS[cur].rearrange("p (v k) -> p v k", v=8)
                    Sn3 = SS[nxt].rearrange("p (v k) -> p v k", v=8)
                    kv = KVv[g % 4]; bkv = bKV[g % 4]
                    o = xo + t * 64
                    tt(kv, slot(4, 64, o).unsqueeze(1).to_broadcast([128, 8, 64]),
                       vb_[:, t, :].unsqueeze(2).to_broadcast([128, 8, 64]), ALU.mult, [bx[4][par], bv[par]], [bkv], eng="pool")
                    tt(T2v, Sc3.unsqueeze(1).to_broadcast([128, 2, 8, 64]),
                       arena[:, 0:2, o:o + 64].unsqueeze(2).to_broadcast([128, 2, 8, 64]), ALU.mult,
                       [bSS[cur], bx[0][par], bx[1][par]], [bT2])
                    P.op("dve", lambda e, out=sk_[:, :, t, :], T2v=T2v: e.tensor_reduce(out=out, in_=T2v, axis=AX.X, op=ALU.add),
                         reads=[bT2], writes=[b_skys[par]])
                    tt(Sn3, Sc3, slot(2, 64, o).unsqueeze(1).to_broadcast([128, 8, 64]), ALU.mult, [bSS[cur], bx[2][par]], [bSS[nxt]], eng="pool")
                    tt(Ttv, sk_[:, 0, t, :].unsqueeze(2).to_broadcast([128, 8, 64]),
                       slot(3, 64, o).unsqueeze(1).to_broadcast([128, 8, 64]), ALU.mult, [b_skys[par], bx[3][par]], [bTt])
                    tt(Sn3, Sn3, Ttv, ALU.add, [bSS[nxt], bTt], [bSS[nxt]])
                    tt(Sn3, Sn3, kv, ALU.add, [bSS[nxt], bkv], [bSS[nxt]])
                    g += 1
                c1b = cb_[:, 0, :].unsqueeze(2).to_broadcast([128, TC, 8])
                c2b = cb_[:, 1, :].unsqueeze(2).to_broadcast([128, TC, 8])
                tt(yb_, sk_[:, 0, :, :], c1b, ALU.mult, [b_skys[par], b_ccb[par]], [by[par]], eng="pool")
                tt(yb_, yb_, sk_[:, 1, :, :], ALU.add, [by[par], b_skys[par]], [by[par]], eng="pool")
                tt(sk_[:, 0, :, :], vb_, c2b, ALU.mult, [bv[par], b_ccb[par]], [b_skys[par]], eng="pool")
                tt(yb_, yb_, sk_[:, 0, :, :], ALU.add, [by[par], b_skys[par]], [by[par]], eng="pool")
                P.dma("act", bass.AP(tk_h, 7 * T * DL + t0 * DL, [[8, 128], [DL, TC], [1, 8]]), yb_, reads=[by[par]], writes=[b_tk[7]])
            P.dma("sp", oS[0].rearrange("h (vb v8) k -> (h vb) (v8 k)", vb=8), SS[g % 2], reads=[bSS[g % 2]], writes=[b_out["o_S"]])
            fence(allb)
            for sb in range(4):
                s0 = sb * 4
                P.dma("sp", Sst[:, 0:2048].rearrange("p (s f) -> p s f", s=4),
                      bass.AP(I["st_S"], (l * NS + s0) * 65536, [[512, 128], [65536, 4], [1, 512]]), writes=[b_Sst])
                for j in range(5):
                    for s in range(4):
                        tok0 = TP + (s0 + s) * 8
                        src = bass.AP(rq_h, j * 16 * T * 64 + tok0 * 64, [[T * 64, 16], [0, 8], [1, 8 * 64]])
                        P.dma("sp" if s % 2 == 0 else "act", slot(j, 512, s * 512), src, reads=[b_rq], writes=[b_S[j]])
                P.dma("act", vbuf[:, 0:256].rearrange("p (t v) -> p t v", t=32),
                      bass.AP(tk_h, 2 * T * DL + (TP + s0 * 8) * DL, [[8, 128], [DL, 32], [1, 8]]), reads=[b_tk[2]], writes=[b_vbuf])
                rec_steps(4, 8,
                          lambda j, t: slot(j, 2048).rearrange("p (s t k) -> p s t k", s=4, t=8)[:, :, t, :],
                          lambda t: vbuf[:, 0:256].rearrange("p (s t v) -> p s t v", s=4, t=8)[:, :, t, :],
                          lambda t: ybuf[:, 0:256].rearrange("p (s t v) -> p s t v", s=4, t=8)[:, :, t, :])
                P.dma("act", bass.AP(tk_h, 7 * T * DL + (TP + s0 * 8) * DL, [[8, 128], [DL, 32], [1, 8]]),
                      ybuf[:, 0:256].rearrange("p (t v) -> p t v", t=32), reads=[b_ybuf], writes=[b_tk[7]])
                P.dma("sp", bass.AP(O["o_S"], (l * NSEQ + 1 + s0) * 65536, [[512, 128], [65536, 4], [1, 512]]),
                      Sst[:, 0:2048].rearrange("p (s f) -> p s f", s=4), reads=[b_Sst], writes=[b_out["o_S"]])

            for i, nm in enumerate(("rw_gn_g", "rw_gn_b")):
                P.dma("act", ct[i][:, :], Wt[nm].ap()[l].partition_broadcast(128), writes=[b_ct[i]])
            for i in range(17):
                rows = slice(i * 128, (i + 1) * 128)
                Yt, BVt, GGt, TMt = half(0, 0), half(0, 1), half(1, 0), half(1, 1)
                P.dma("sp", Yt, tk[7][rows, :], reads=[b_tk[7]], writes=[b_S[0]])
                P.dma("sp", BVt, tk[6][rows, :], reads=[b_tk[6]], writes=[b_S[0]])
                P.dma("sp", GGt, tk[5][rows, :], reads=[b_tk[5]], writes=[b_S[1]])
                P.op("dve", lambda e, Yt=Yt: e.tensor_reduce(out=sm[:, 0:16], in_=h3(Yt), axis=AX.X, op=ALU.add), reads=[b_S[0]], writes=[b_sm])
                ts(sm[:, 0:16], sm[:, 0:16], 1.0 / 64, None, ALU.mult, None, [b_sm], [b_sm])
                tt(h3(Yt), h3(Yt), sm[:, 0:16].unsqueeze(2).to_broadcast([128, 16, 64]), ALU.subtract, [b_S[0], b_sm], [b_S[0]])
                tt(TMt, Yt, Yt, ALU.mult, [b_S[0]], [b_S[1]])
                P.op("dve", lambda e, TMt=TMt: e.tensor_reduce(out=sm[:, 16:32], in_=h3(TMt), axis=AX.X, op=ALU.add), reads=[b_S[1]], writes=[b_sm])
                act(sm[:, 16:32], sm[:, 16:32], AF.Sqrt, [b_sm], [b_sm], scale=1.0 / 64, bias=64e-5)
                P.op("dve", lambda e: e.reciprocal(out=sm[:, 16:32], in_=sm[:, 16:32]), reads=[b_sm], writes=[b_sm])
                tt(h3(Yt), h3(Yt), sm[:, 16:32].unsqueeze(2).to_broadcast([128, 16, 64]), ALU.mult, [b_S[0], b_sm], [b_S[0]])
                tt(Yt, Yt, ct[0][:, :], ALU.mult, [b_S[0], b_ct[0]], [b_S[0]])
                tt(Yt, Yt, ct[1][:, :], ALU.add, [b_S[0], b_ct[1]], [b_S[0]])
                tt(Yt, Yt, BVt, ALU.add, [b_S[0]], [b_S[0]])
                tt(Yt, Yt, GGt, ALU.mult, [b_S[0], b_S[1]], [b_S[0]])
                OTb = slot(2).bitcast(BF16)[:, 0:1024].rearrange("p (c t) -> p c t", c=8)
                for g in range(2):
                    ps, pb = nps()
                    for j in range(4):
                        tr(ps[:, j * 128:(j + 1) * 128], Yt[:, (g * 4 + j) * 128:(g * 4 + j + 1) * 128], 128, [b_S[0]], pb)
                    act(OTb[:, g * 4:(g + 1) * 4, :].rearrange("p a b -> p (a b)"), ps[:, :], AF.Identity, [pb], [b_S[2]])
                P.dma("sp", oT_d.rearrange("(c p) t -> p c t", p=128)[:, :, rows], OTb, reads=[b_S[2]], writes=[b_oT])

            wpa = Wt["w_pa"].ap()[l].rearrange("(kc p) c -> p kc c", p=128)
            wpb = Wt["w_pb"].ap()[l].rearrange("(kc p) c -> p kc c", p=128)
            gav = ga_d.rearrange("(kc p) t -> p kc t", p=128)
            otv = oT_d.rearrange("(kc p) t -> p kc t", p=128)
            mcnt = 0
            for (t0, n) in TB:
                gb = blk[:, 0:16 * 512].rearrange("p (kc t) -> p kc t", kc=16)
                P.dma("sp", gb[:, 0:8, 0:n], gav[:, :, t0:t0 + n], reads=[b_ga], writes=[b_blk])
                P.dma("act", gb[:, 8:16, 0:n], otv[:, :, t0:t0 + n], reads=[b_oT], writes=[b_blk])
                for c in range(16):
                    ws, bw = wslot()
                    wload(ws[:, 0:8, 0:128], wpa[:, :, c * 128:(c + 1) * 128], bw)
                    wload(ws[:, 8:16, 0:128], wpb[:, :, c * 128:(c + 1) * 128], bw)
                    sg_ = sgt[:, mcnt % 2, :]; bsg = b_sgt[mcnt % 2]
                    mcnt += 1
                    P.dma("sp", sg_[:, 0:n], sg[c * 128:(c + 1) * 128, t0:t0 + n], reads=[b_sg], writes=[bsg])
                    P.dma("sp", sg_[:, 512:512 + n], sg[D + c * 128:D + (c + 1) * 128, t0:t0 + n], reads=[b_sg], writes=[bsg])
                    psa, pba = nps()
                    for kc in range(8):
                        mm(psa[:, 0:n], ws[:, kc, 0:128], gb[:, kc, 0:n], kc == 0, kc == 7, [bw, b_blk], pba)
                    psb, pbb = nps()
                    for kc in range(8):
                        mm(psb[:, 0:n], ws[:, 8 + kc, 0:128], gb[:, 8 + kc, 0:n], kc == 0, kc == 7, [bw, b_blk], pbb)
                    tt(tmpb[:, 0:n], psa[:, 0:n], sg_[:, 0:n], ALU.mult, [pba, bsg], [b_tmpb])
                    tt(tmpc[:, 0:n], psb[:, 0:n], sg_[:, 512:512 + n], ALU.mult, [pbb, bsg], [b_tmpc])
                    tt(actT[:, c, t0:t0 + n], tmpb[:, 0:n], tmpc[:, 0:n], ALU.add, [b_tmpb, b_tmpc], [b_act])
            wo = Wt["w_o"].ap()[l].rearrange("(kc p) c -> p kc c", p=128)
            for c in range(16):
                if c % 2 == 0:
                    ws, bw = wslot()
                    wload(ws[:, :, 0:256], wo[:, :, c * 128:(c + 2) * 128], bw)
                for (t0, n) in TB:
                    ps, pb = proj_block(ws, (c % 2) * 128, 128, t0, n, bw)
                    resid_update(c, t0, n, ps, pb, 32)

            make_A(64, "norm_ffn")
            norm_mod(A1, modT[:, 48:64, :], b_A1, b_mod)

            wup = Wt["w_up"].ap()[l].rearrange("(kc p) c -> p kc c", p=128)
            for j in range(44):
                j4 = j % 4
                stv = I["st_fc"].ap()[l].rearrange("s j c -> (s j) c")
                ofv = O["o_fc"].ap()[l].rearrange("s j c -> (s j) c")
                if j4 == 0:
                    for hh in range(2):
                        loadT(stfc[:, hh * 4:hh * 4 + 4, :], b_stfc,
                              lambda c0, w, hh=hh, j=j: stv[:, (hh * 44 + j) * 128 + c0:(hh * 44 + j) * 128 + c0 + w], 32, 4)
                ws, bw = wslot()
                wload(ws[:, :, 0:128], wup[:, :, j * 128:(j + 1) * 128], bw)
                wload(ws[:, :, 128:256], wup[:, :, DFF + j * 128:DFF + (j + 1) * 128], bw)
                outs = []
                for hh in range(2):
                    ch = hh * 44 + j
                    UU = slot(0 + hh * 2); UC = slot(1 + hh * 2)
                    bU = b_S[0 + hh * 2]; bC = b_S[1 + hh * 2]
                    UUs = UU[:, 2050:2050 + 160].rearrange("p (s j) -> p s j", s=NS)
                    memset(UU[:, 0:2], 0.0, [bU])
                    cp(UUs[:, :, 0:2], stfc[:, hh * 4 + j4, :].rearrange("p (s j) -> p s j", s=NS), [b_stfc], [bU], eng="pool")
                    for (t0, n) in TB:
                        ps, pb = proj_block(ws, hh * 128, 128, t0, n, bw)
                        if t0 < TP:
                            act(UU[:, 2 + t0:2 + t0 + n], ps[:, 0:n], AF.Identity, [pb], [bU])
                        else:
                            act(UUs[:, :, 2:10], ps[:, 0:n].rearrange("p (s t) -> p s t", s=NS), AF.Identity, [pb], [bU])
                    k8 = j % 8
                    if hh == 0 and k8 == 0:
                        pass
                    cp(ofc[:, hh * 4 + j4, 0:2], UU[:, 2048:2050], [bU], [b_ofc], eng="pool")
                    cp(ofc[:, hh * 4 + j4, 2:34].rearrange("p (s j) -> p s j", s=NS), UUs[:, :, 8:10], [bU], [b_ofc], eng="pool")
                    UCs = UC[:, TP:T].rearrange("p (s t) -> p s t", s=NS)
                    ts(UC[:, 0:TP], UU[:, 0:TP], pcol("ffn_conv_w", 0 * 88 + ch), pcol("ffn_conv_b", ch), ALU.mult, ALU.add, [bU, b_PT], [bC])
                    ts(UCs, UUs[:, :, 0:8], pcol("ffn_conv_w", 0 * 88 + ch), pcol("ffn_conv_b", ch), ALU.mult, ALU.add, [bU, b_PT], [bC])
                    for jj in range(1, 3):
                        stt(UC[:, 0:TP], UU[:, jj:jj + TP], pcol("ffn_conv_w", jj * 88 + ch), UC[:, 0:TP], ALU.mult, ALU.add, [bU, bC, b_PT], [bC])
                        stt(UCs, UUs[:, :, jj:jj + 8], pcol("ffn_conv_w", jj * 88 + ch), UCs, ALU.mult, ALU.add, [bU, bC, b_PT], [bC])
                    outs.append((UC, bC))
                (UG, bG), (UV, bV) = outs
                act(UG[:, 0:T], UG[:, 0:T], AF.Silu, [bG], [bG])
                ATb = slot(4).bitcast(BF16)[:, 0:T]
                tt(ATb, UG[:, 0:T], UV[:, 0:T], ALU.mult, [bG, bV], [b_S[4]])
                P.dma("sp", aT_d[j * 128:(j + 1) * 128, :], ATb, reads=[b_S[4]], writes=[b_aT])
                if j4 == 3:
                    for hh in range(2):
                        flushT(ofc[:, hh * 4:hh * 4 + 4, :], b_ofc, 4, 34,
                               lambda c0, w, hh=hh, j=j: ofv[:, (hh * 44 + j - 3) * 128 + c0:(hh * 44 + j - 3) * 128 + c0 + w], b_out["o_fc"])

            wdn = Wt["w_down"].ap()[l].rearrange("(kc p) c -> p kc c", p=128)
            atv = aT_d.rearrange("(kc p) t -> p kc t", p=128)
            fence([b_blk, b_ab1])
            abv = [blk[:, 0:11 * 512].rearrange("p (kc t) -> p kc t", kc=11),
                   blk[:, 11 * 512:22 * 512].rearrange("p (kc t) -> p kc t", kc=11)]
            b_ab = [b_blk, b_ab1]
            qcnt = 0
            for (t0, n) in TB:
                for cg in range(4):
                    banks = [nps() for _ in range(4)]
                    for qt in range(4):
                        ab = abv[qcnt % 2]; bab = b_ab[qcnt % 2]
                        P.dma("sp" if qcnt % 2 == 0 else "act", ab[:, :, 0:n], atv[:, qt * 11:(qt + 1) * 11, t0:t0 + n],
                              reads=[b_aT], writes=[bab])
                        qcnt += 1
                        for ci in range(4):
                            c = cg * 4 + ci
                            ws, bw = wslot()
                            wload(ws[:, 0:11, 0:128], wdn[:, qt * 11:(qt + 1) * 11, c * 128:(c + 1) * 128], bw)
                            ps, pb = banks[ci]
                            for k2 in range(11):
                                mm(ps[:, 0:n], ws[:, k2, 0:128], ab[:, k2, 0:n], qt == 0 and k2 == 0, qt == 3 and k2 == 10, [bw, bab], pb)
                    for ci in range(4):
                        ps, pb = banks[ci]
                        resid_update(cg * 4 + ci, t0, n, ps, pb, 80)
            fence([b_blk, b_ab1])

        nf = Wt["norm_final"].ap().rearrange("(r c) -> r c", c=128)
        memset(pst[:], 0.0, [b_pst])
        P.dma("act", pst[0:16, :], nf, writes=[b_pst])
        ps, pb = nps()
        tr(ps[:, 0:128], pst[:, :], 128, [b_pst], pb)
        cp(PT[:, 0:128], ps[:, 0:128], [pb], [b_PT])
        for (t0, n) in TB:
            xb, rstd = norm_stats(t0, n)
            for kc in range(16):
                stt(xb[:, kc, :], xb[:, kc, :], PT[:, kc:kc + 1], rstd, ALU.mult, ALU.mult, b_S[0:5] + [b_PT], b_S[0:4])
            for tt_i in range(n // 128):
                for g in range(4):
                    ps, pb = nps()
                    for j in range(4):
                        kc = g * 4 + j
                        tr(ps[:, j * 128:(j + 1) * 128], xb[:, kc, tt_i * 128:(tt_i + 1) * 128], 128, b_S[0:4], pb)
                    act(rowst[:, 0:512], ps[:, :], AF.Identity, [pb], [b_rowst])
                    r0 = t0 + tt_i * 128
                    if t0 < TP:
                        P.dma("sp", O["y_p"].ap()[r0:r0 + 128, g * 512:(g + 1) * 512], rowst[:, 0:512], reads=[b_rowst], writes=[b_out["y_p"]])
                    else:
                        P.dma("sp", O["y_s"].ap()[:, g * 512:(g + 1) * 512], rowst[:, 0:512], reads=[b_rowst], writes=[b_out["y_s"]])

        P.final_wait("sp", list(b_out.values()))
        P.emit()
    return nc


_NC_CACHE = {}


def kernel(x_prompt, x_sample, c_prompt, c_sample, state_lru_conv, state_lru_h, state_rwkv_shift, state_rwkv_S,
           state_ffn_conv, **weights):
    f = lambda a: np.ascontiguousarray(np.asarray(a, dtype=np.float32))
    if "nc" not in _NC_CACHE:
        _NC_CACHE["nc"] = build()
    nc = _NC_CACHE["nc"]
    wmap = {k: f(weights[k]) for k in W_SHAPES}
    in_maps = []
    for core in range(8):
        b = core % 4
        ss = slice(core * NS, (core + 1) * NS)
        m = dict(wmap)
        m["xp"] = f(x_prompt[b])
        m["xs"] = f(np.asarray(x_sample)[ss].reshape(TS, D))
        m["cc"] = f(np.concatenate([np.asarray(c_prompt)[b:b + 1], np.asarray(c_sample)[ss]], axis=0))
        m["st_lc"] = f(np.asarray(state_lru_conv)[:, ss])
        m["st_lh"] = f(np.asarray(state_lru_h)[:, ss])
        m["st_sh"] = f(np.asarray(state_rwkv_shift)[:, ss])
        m["st_S"] = f(np.asarray(state_rwkv_S)[:, ss])
        m["st_fc"] = f(np.asarray(state_ffn_conv)[:, ss])
        in_maps.append(m)
    res = run_bass_kernel_spmd(nc, in_maps, core_ids=list(range(8)))
    R = res.results
    y_prompt = np.stack([R[b]["y_p"] for b in range(4)], axis=0)
    y_sample = np.concatenate([R[c]["y_s"].reshape(NS, 8, D) for c in range(8)], axis=0)
    outs = [y_prompt, y_sample]
    names = ["o_lc", "o_lh", "o_sh", "o_S", "o_fc"]
    for nm in names:
        outs.append(np.stack([R[b][nm][:, 0] for b in range(4)], axis=1))
    for nm in names:
        outs.append(np.concatenate([R[c][nm][:, 1:] for c in range(8)], axis=1))
    return tuple(np.ascontiguousarray(o.astype(np.float32, copy=False)) for o in outs)
```

```python
import math
import numpy as np
from contextlib import ExitStack
import concourse.bass as bass
import concourse.mybir as mybir
from concourse.bass_utils import run_bass_kernel_spmd

F32 = mybir.dt.float32
BF16 = mybir.dt.bfloat16
ALU = mybir.AluOpType
AF = mybir.ActivationFunctionType
AX = mybir.AxisListType

ENGS = ("pe", "act", "dve", "pool", "sp")

D = 2048
TP = 2048
NS = 16
TS = 128
T = TP + TS
NSEQ = 17
DL = 1024
NRW = 3360
NIN = 9504
DFF = 5632
DEPTH = 4
TB = [(0, 512), (512, 512), (1024, 512), (1536, 512), (2048, 128)]
SLOT = 2240
NLAYERS = DEPTH


class Buf:
    def __init__(self, prog, name):
        self.prog = prog
        self.name = name
        self.w = {}
        self.r = {}
        self.dsem = None
        self.dcnt = 0

    def dma_sem(self):
        if self.dsem is None:
            self.dsem = self.prog.new_sem("d_" + self.name)
        return self.dsem


class Prog:
    def __init__(self, nc, stack):
        self.nc = nc
        self.stack = stack
        self.streams = {e: [] for e in ENGS}
        self.esem = {e: self.new_sem("e_" + e) for e in ENGS}
        self.ecnt = {e: 0 for e in ENGS}
        self.seen = {e: {} for e in ENGS}
        self.nbuf = 0
        self.pend = None

    def pe_group(self, fn, reads, pb):
        if self.pend is not None and self.pend[2] is pb:
            self.pend[0].append(fn)
            self.pend[1].extend(reads)
        else:
            self.flush_pe()
            self.pend = ([fn], list(reads), pb)

    def flush_pe(self):
        if self.pend is None:
            return
        fns, reads, pb = self.pend
        self.pend = None
        uniq = []
        for b in reads:
            if all(b is not u for u in uniq):
                uniq.append(b)

        def fn_all(e, fns=fns):
            last = None
            for f in fns:
                last = f(e)
            return last

        self._op("pe", fn_all, uniq, [pb])

    def new_sem(self, name):
        return self.stack.enter_context(self.nc.semaphore(name))

    def buf(self, name=None):
        self.nbuf += 1
        return Buf(self, name or f"b{self.nbuf}")

    def sbuf(self, name, shape, dtype):
        return self.stack.enter_context(self.nc.sbuf_tensor(name, list(shape), dtype))

    def psum(self, name, shape, dtype=F32):
        return self.stack.enter_context(self.nc.psum_tensor(name, list(shape), dtype))

    def _deps(self, eng, reads, writes):
        need = {}

        def add(tok):
            k = id(tok[0])
            if k not in need or need[k][1] < tok[1]:
                need[k] = tok

        for b in reads:
            for tok in b.w.values():
                add(tok)
        for b in writes:
            for tok in b.w.values():
                add(tok)
            for tok in b.r.values():
                add(tok)
        waits = []
        seen = self.seen[eng]
        for k, (sem, val) in need.items():
            if seen.get(k, 0) < val:
                seen[k] = val
                waits.append((sem, val))
        return waits

    def _mark(self, tok, reads, writes):
        k = id(tok[0])
        for b in reads:
            if b in writes:
                continue
            b.r[k] = tok
        for b in writes:
            b.w = {k: tok}
            b.r = {}

    def op(self, eng, fn, reads=(), writes=()):
        self.flush_pe()
        self._op(eng, fn, reads, writes)

    def _op(self, eng, fn, reads=(), writes=()):
        waits = self._deps(eng, reads, writes)
        self.ecnt[eng] += 1
        sem = self.esem[eng]
        self._mark((sem, self.ecnt[eng]), reads, writes)

        def run(e, waits=waits, fn=fn, sem=sem):
            for s, v in waits:
                e.wait_ge(s, v)
            fn(e).then_inc(sem, 1)

        self.streams[eng].append(run)

    def dma(self, q, out, in_, reads=(), writes=(), **kw):
        self.flush_pe()
        waits = self._deps(q, reads, writes)
        wb = writes[0]
        sem = wb.dma_sem()
        wb.dcnt += 16
        self._mark((sem, wb.dcnt), reads, writes)

        def run(e, waits=waits, sem=sem, out=out, in_=in_, kw=kw):
            for s, v in waits:
                e.wait_ge(s, v)
            e.dma_start(out=out, in_=in_, **kw).then_inc(sem, 16)

        self.streams[q].append(run)

    def final_wait(self, eng, bufs):
        self.flush_pe()
        waits = self._deps(eng, bufs, bufs)

        def run(e, waits=waits):
            for s, v in waits:
                e.wait_ge(s, v)

        self.streams[eng].append(run)

    def emit(self):
        self.flush_pe()
        nc = self.nc
        with nc.Block() as block:
            @block.tensor
            def _(e):
                for f in self.streams["pe"]:
                    f(e)

            @block.scalar
            def _(e):
                for f in self.streams["act"]:
                    f(e)

            @block.vector
            def _(e):
                for f in self.streams["dve"]:
                    f(e)

            @block.gpsimd
            def _(e):
                for f in self.streams["pool"]:
                    f(e)

            @block.sync
            def _(e):
                for f in self.streams["sp"]:
                    f(e)


W_SHAPES = {
    "w_ada": [DEPTH, D, 6 * D], "b_ada": [DEPTH, 6 * D], "norm_mix": [DEPTH, D], "norm_ffn": [DEPTH, D],
    "w_in": [DEPTH, D, NIN], "lru_conv_w": [DEPTH, 4, DL], "lru_conv_b": [DEPTH, DL],
    "lru_wa": [DEPTH, 16, 64, 64], "lru_ba": [DEPTH, DL], "lru_wi": [DEPTH, 16, 64, 64], "lru_bi": [DEPTH, DL],
    "lru_lambda": [DEPTH, DL], "w_pa": [DEPTH, DL, D], "rw_mu": [DEPTH, NRW], "rw_w0": [DEPTH, DL],
    "rw_w2": [DEPTH, 64, DL], "rw_a0": [DEPTH, DL], "rw_a2": [DEPTH, 64, DL], "rw_g2": [DEPTH, 160, DL],
    "rw_kk": [DEPTH, DL], "rw_ka": [DEPTH, DL], "rw_rk": [DEPTH, 16, 64], "rw_gn_g": [DEPTH, DL],
    "rw_gn_b": [DEPTH, DL], "w_pb": [DEPTH, DL, D], "w_o": [DEPTH, D, D], "w_up": [DEPTH, D, 2 * DFF],
    "ffn_conv_w": [DEPTH, 3, 2 * DFF], "ffn_conv_b": [DEPTH, 2 * DFF], "w_down": [DEPTH, DFF, D],
    "norm_final": [D],
}
IN_SHAPES = {
    "xp": [TP, D], "xs": [TS, D], "cc": [NSEQ, D], "st_lc": [DEPTH, NS, 3, DL], "st_lh": [DEPTH, NS, DL],
    "st_sh": [DEPTH, NS, NRW], "st_S": [DEPTH, NS, 16, 64, 64], "st_fc": [DEPTH, NS, 2, 2 * DFF],
}
OUT_SHAPES = {
    "y_p": [TP, D], "y_s": [TS, D], "o_lc": [DEPTH, NSEQ, 3, DL], "o_lh": [DEPTH, NSEQ, DL],
    "o_sh": [DEPTH, NSEQ, NRW], "o_S": [DEPTH, NSEQ, 16, 64, 64], "o_fc": [DEPTH, NSEQ, 2, 2 * DFF],
}

PROWS = [("b_ada", 96), ("norm_mix", 16), ("norm_ffn", 16), ("lru_conv_w", 32), ("lru_conv_b", 8),
         ("lru_ba", 8), ("lru_bi", 8), ("lru_lambda", 8), ("rw_mu", 27), ("rw_w0", 8), ("rw_a0", 8),
         ("ffn_conv_w", 264), ("ffn_conv_b", 88)]
PCOL = {}
_c = 0
for _n, _r in PROWS:
    PCOL[_n] = _c
    _c += _r
NPROW = _c
NPG = (NPROW + 127) // 128


def build():
    nc = bass.Bass("TRN2", target_bir_lowering=False)
    I = {k: nc.dram_tensor(k, s, F32, kind="ExternalInput") for k, s in IN_SHAPES.items()}
    Wt = {k: nc.dram_tensor(k, s, F32, kind="ExternalInput") for k, s in W_SHAPES.items()}
    O = {k: nc.dram_tensor(k, s, F32, kind="ExternalOutput") for k, s in OUT_SHAPES.items()}
    xT_h = nc.dram_tensor("xT_scr", [D, T], F32, kind="Internal")
    rq_h = nc.dram_tensor("rq_scr", [6, 16, T, 64], F32, kind="Internal")
    tk_h = nc.dram_tensor("tk_scr", [8, T, DL], F32, kind="Internal")
    cc_h = nc.dram_tensor("cc_scr", [32, T], F32, kind="Internal")
    sg_h = nc.dram_tensor("sg_scr", [2 * D, T], BF16, kind="Internal")
    ga_h = nc.dram_tensor("ga_scr", [DL, T], BF16, kind="Internal")
    oT_h = nc.dram_tensor("oT_scr", [DL, T], BF16, kind="Internal")
    aT_h = nc.dram_tensor("aT_scr", [DFF, T], BF16, kind="Internal")
    xT = xT_h.ap()
    tk = tk_h.ap()
    sg = sg_h.ap()
    ga_d = ga_h.ap()
    oT_d = oT_h.ap()
    aT_d = aT_h.ap()

    with ExitStack() as st:
        P = Prog(nc, st)
        actT = P.sbuf("actT", [128, 16, T], BF16); b_act = P.buf("actT")
        arena = P.sbuf("arena", [128, 5, SLOT], F32)
        b_S = [P.buf(f"S{i}") for i in range(5)]
        wsl = [P.sbuf(f"wsl{i}", [128, 16, 256], BF16) for i in range(2)]
        b_w = [P.buf(f"w{i}") for i in range(2)]
        blk = P.sbuf("blk", [128, 16384], BF16); b_blk = P.buf("blk")
        blkF = blk[:, :].bitcast(F32)
        ident = P.sbuf("ident", [128, 128], F32); b_id = P.buf("ident")
        ones = P.sbuf("ones", [128, 128], F32); b_ones = P.buf("ones")
        scT = P.sbuf("scT", [128, 16, NSEQ], BF16); b_scT = P.buf("scT")
        modT = P.sbuf("modT", [128, 96, NSEQ], F32); b_mod = P.buf("modT")
        A1 = P.sbuf("A1", [128, 16, NSEQ], F32); b_A1 = P.buf("A1")
        PT = P.sbuf("PT", [128, NPG * 128], F32); b_PT = P.buf("PT")
        pst = P.sbuf("pst", [128, 128], F32); b_pst = P.buf("pst")
        cA = P.sbuf("cA", [128, 16], F32); b_cA = P.buf("cA")
        bda = P.sbuf("bda", [128, 8, 128], BF16); b_bda = P.buf("bda")
        bdi = P.sbuf("bdi", [128, 8, 128], BF16); b_bdi = P.buf("bdi")
        lw2 = P.sbuf("lw2", [128, DL], BF16); b_lw2 = P.buf("lw2")
        g2a = P.sbuf("g2a", [128, DL], BF16); b_g2a = P.buf("g2a")
        g2b = P.sbuf("g2b", [32, DL], BF16); b_g2b = P.buf("g2b")
        stlc = P.sbuf("stlc", [128, 8, 48], F32); b_stlc = P.buf("stlc")
        stlh = P.sbuf("stlh", [128, 8, 16], F32); b_stlh = P.buf("stlh")
        stsh = P.sbuf("stsh", [128, 27, 16], F32); b_stsh = P.buf("stsh")
        olc = P.sbuf("olc", [128, 8, 51], F32); b_olc = P.buf("olc")
        olh = P.sbuf("olh", [128, 8, NSEQ], F32); b_olh = P.buf("olh")
        osh = P.sbuf("osh", [128, 27, NSEQ], F32); b_osh = P.buf("osh")
        ofc = P.sbuf("ofc", [128, 8, 34], F32); b_ofc = P.buf("ofc")
        stfc = P.sbuf("stfc", [128, 8, 32], F32); b_stfc = P.buf("stfc")
        tmpb = P.sbuf("tmpb", [128, 512], F32); b_tmpb = P.buf("tmpb")
        tmpc = P.sbuf("tmpc", [128, 512], F32); b_tmpc = P.buf("tmpc")
        rowst = blkF[:, 7168:8192]; b_rowst = P.buf("rowst")
        Sst = blkF[:, 0:2048]; b_Sst = b_blk
        Stmp = blkF[:, 2048:4096]; b_Stmp = b_blk
        skk = P.sbuf("skk", [128, 32], F32); b_skk = P.buf("skk")
        sgt = P.sbuf("sgt", [128, 2, 1024], BF16); b_sgt = [P.buf("sgt0"), P.buf("sgt1")]
        skys = P.sbuf("skys", [128, 2 * 2 * 16 * 8], F32); b_skys = [P.buf("skys0"), P.buf("skys1")]
        ccb = P.sbuf("ccb", [128, 2 * 32], F32); b_ccb = [P.buf("ccb0"), P.buf("ccb1")]
        dummy = P.sbuf("fence_dummy", [128, 8], F32)
        b_cc = P.buf("cc")
        vbuf = P.sbuf("vbuf", [128, 32 * 8], F32); b_vbuf = P.buf("vbuf")
        ybuf = P.sbuf("ybuf", [128, 32 * 8], F32); b_ybuf = P.buf("ybuf")
        sm = P.sbuf("sm", [128, 64], F32); b_sm = P.buf("sm")
        ct = [blkF[:, 4096 + i * 1024:4096 + (i + 1) * 1024] for i in range(3)]
        b_ct = [b_blk, b_blk, b_blk]
        PS = [P.psum(f"ps{i}", [128, 512]) for i in range(8)]
        b_ps = [P.buf(f"ps{i}") for i in range(8)]
        psc = [0]

        def nps():
            i = psc[0] % 8
            psc[0] += 1
            return PS[i], b_ps[i]

        b_xT = [P.buf(f"xT{c}") for c in range(16)]
        b_rq = P.buf("rq"); b_tk = [P.buf(f"tk{i}") for i in range(8)]
        b_sg = P.buf("sg"); b_ga = P.buf("ga"); b_oT = P.buf("oT"); b_aT = P.buf("aT")
        b_out = {k: P.buf("o_" + k) for k in OUT_SHAPES}

        def slot(i, n=SLOT, off=0):
            return arena[:, i, off:off + n]

        wcnt = [0]

        def wslot():
            i = wcnt[0] % 2
            wcnt[0] += 1
            return wsl[i], b_w[i]

        def act(out, in_, func, reads, writes, bias=None, scale=None):
            kw = {}
            if bias is not None:
                kw["bias"] = bias
            if scale is not None:
                kw["scale"] = scale
            P.op("act", lambda e: e.activation(out=out, in_=in_, func=func, **kw), reads=reads, writes=writes)

        def tt(out, in0, in1, op, reads, writes, eng="dve"):
            P.op(eng, lambda e: e.tensor_tensor(out=out, in0=in0, in1=in1, op=op), reads=reads, writes=writes)

        def ts(out, in0, s1, s2, op0, op1, reads, writes, eng="dve"):
            if s2 is None:
                P.op(eng, lambda e: e.tensor_scalar(out=out, in0=in0, scalar1=s1, scalar2=None, op0=op0), reads=reads, writes=writes)
            else:
                P.op(eng, lambda e: e.tensor_scalar(out=out, in0=in0, scalar1=s1, scalar2=s2, op0=op0, op1=op1), reads=reads, writes=writes)

        def stt(out, in0, scalar, in1, op0, op1, reads, writes):
            P.op("dve", lambda e: e.scalar_tensor_tensor(out=out, in0=in0, scalar=scalar, in1=in1, op0=op0, op1=op1),
                 reads=reads, writes=writes)

        def cp(out, in_, reads, writes, eng="dve"):
            P.op(eng, lambda e: e.tensor_copy(out=out, in_=in_), reads=reads, writes=writes)

        def mm(ps_ap, lhsT, rhs, start, stop, reads, pb):
            P.pe_group(lambda e: e.matmul(ps_ap, lhsT=lhsT, rhs=rhs, start=start, stop=stop), list(reads), pb)

        def tr(ps_ap, in_, n_in_part, reads, pb):
            P.pe_group(lambda e: e.transpose(ps_ap, in_, ident[0:n_in_part, 0:n_in_part]), list(reads) + [b_id], pb)

        def memset(ap, val, writes, eng="pool"):
            P.op(eng, lambda e: e.memset(ap, val), writes=writes)

        def fence(bufs, eng="pool"):
            P.op(eng, lambda e: e.memset(dummy[:, 0:1], 0.0), writes=list(bufs))

        memset(ident[:], 0.0, [b_id])
        P.op("pool", lambda e: e.affine_select(out=ident[:], in_=ident[:], compare_op=ALU.not_equal, fill=1.0,
                                               base=0, pattern=[[-1, 128]], channel_multiplier=1),
             reads=[b_id], writes=[b_id])
        memset(ones[:], 1.0, [b_ones])

        P.dma("sp", rowst[0:NSEQ, :], I["cc"].ap()[:, 0:1024], writes=[b_rowst])
        for half in range(2):
            if half == 1:
                P.dma("sp", rowst[0:NSEQ, :], I["cc"].ap()[:, 1024:2048], writes=[b_rowst])
            act(rowst[0:NSEQ, :], rowst[0:NSEQ, :], AF.Silu, [b_rowst], [b_rowst])
            ps, pb = nps()
            for j in range(8):
                tr(ps[:, j * NSEQ:(j + 1) * NSEQ], rowst[0:NSEQ, j * 128:(j + 1) * 128], NSEQ, [b_rowst], pb)
            cp(scT[:, half * 8:(half + 1) * 8, :].rearrange("p a b -> p (a b)"), ps[:, 0:8 * NSEQ], [pb], [b_scT])

        xT_v = xT.rearrange("(kc p) t -> p kc t", p=128)
        for i in range(17):
            src = I["xp"].ap()[i * 128:(i + 1) * 128, :] if i < 16 else I["xs"].ap()[:, :]
            P.dma("sp", slot(0, 2048), src, writes=[b_S[0]])
            for g in range(4):
                ps, pb = nps()
                for j in range(4):
                    kc = g * 4 + j
                    tr(ps[:, j * 128:(j + 1) * 128], slot(0, 128, kc * 128), 128, [b_S[0]], pb)
                P.op("act", lambda e, ps=ps, g=g: e.activation(out=slot(1, 512, g * 512), in_=ps[:, :], func=AF.Identity),
                     reads=[pb], writes=[b_S[1]])
            P.dma("sp", xT_v[:, :, i * 128:(i + 1) * 128], slot(1, 2048).rearrange("p (kc t) -> p kc t", kc=16),
                  reads=[b_S[1]], writes=b_xT)

        def load_params(l):
            for g in range(NPG):
                memset(pst[:], 0.0, [b_pst])
                r = 0
                for name, nrows in PROWS:
                    for rr in range(nrows):
                        grow = PCOL[name] + rr
                        if grow // 128 != g:
                            continue
                    lo = max(PCOL[name], g * 128)
                    hi = min(PCOL[name] + nrows, (g + 1) * 128)
                    if lo >= hi:
                        continue
                    r0 = lo - PCOL[name]
                    n = hi - lo
                    flat = Wt[name].ap()[l]
                    if name in ("lru_conv_w", "ffn_conv_w"):
                        flat = flat.rearrange("j c -> (j c)")
                    if name == "rw_mu":
                        nfull = min(n, max(0, 26 - r0))
                        if nfull > 0:
                            P.dma("act", pst[lo - g * 128:lo - g * 128 + nfull, :],
                                  flat[r0 * 128:(r0 + nfull) * 128].rearrange("(r c) -> r c", c=128), writes=[b_pst])
                        if r0 + n == 27:
                            P.dma("act", pst[hi - 1 - g * 128:hi - g * 128, 0:32],
                                  flat[26 * 128:26 * 128 + 32].rearrange("(r c) -> r c", c=32), writes=[b_pst])
                    else:
                        P.dma("act", pst[lo - g * 128:hi - g * 128, :],
                              flat[r0 * 128:(r0 + n) * 128].rearrange("(r c) -> r c", c=128), writes=[b_pst])
                ps, pb = nps()
                tr(ps[:, 0:128], pst[:, :], 128, [b_pst], pb)
                cp(PT[:, g * 128:(g + 1) * 128], ps[:, 0:128], [pb], [b_PT])

        def pcol(name, row):
            c = PCOL[name] + row
            return PT[:, c:c + 1]

        def loadT(dst3, bdst, src_rows_fn, nrows, nchunks, last_cols=128):
            c = 0
            while c < nchunks:
                g = min(8, nchunks - c)
                width = (g - 1) * 128 + (last_cols if c + g == nchunks else 128)
                P.dma("sp", rowst[0:nrows, 0:width], src_rows_fn(c * 128, width), writes=[b_rowst])
                per = 512 // nrows
                j = 0
                while j < g:
                    gg = min(per, g - j)
                    ps, pb = nps()
                    for k in range(gg):
                        cc_ = c + j + k
                        ncol = last_cols if cc_ == nchunks - 1 else 128
                        tr(ps[0:ncol, k * nrows:(k + 1) * nrows], rowst[0:nrows, (j + k) * 128:(j + k) * 128 + ncol], nrows, [b_rowst], pb)
                    cp(dst3[:, c + j:c + j + gg, :].rearrange("p a b -> p (a b)"), ps[:, 0:gg * nrows], [pb], [bdst])
                    j += gg
                c += g

        def flushT(src3, bsrc, nchunks, n, dst_fn, bout, last_cols=128):
            c = 0
            while c < nchunks:
                g = min(4, nchunks - c)
                ps, pb = nps()
                width = 0
                for k in range(g):
                    ncol = last_cols if c + k == nchunks - 1 else 128
                    tr(ps[0:n, k * 128:k * 128 + ncol], src3[0:ncol, c + k, :], ncol, [bsrc], pb)
                    width += ncol
                cp(rowst[0:n, 0:width], ps[0:n, 0:width], [pb], [b_rowst])
                P.dma("sp", dst_fn(c * 128, width), rowst[0:n, 0:width], reads=[b_rowst], writes=[bout])
                c += g

        def wload(dst, src, bw):
            P.dma("pool", dst, src, writes=[bw])

        def norm_stats(t0, n):
            xb = arena[:, 0:4, :].rearrange("p a b -> p (a b)")[:, 0:16 * n].rearrange("p (kc t) -> p kc t", kc=16)
            P.dma("sp", xb, xT_v[:, :, t0:t0 + n], reads=b_xT, writes=b_S[0:4])
            ps, pb = nps()
            for kc in range(16):
                sqb, bsq = (tmpb, b_tmpb) if kc % 2 == 0 else (tmpc, b_tmpc)
                act(sqb[:, 0:n], xb[:, kc, :], AF.Square, b_S[0:4], [bsq])
                mm(ps[:, 0:n], ones[:], sqb[:, 0:n], kc == 0, kc == 15, [b_ones, bsq], pb)
            rstd = slot(4, n)
            act(rstd, ps[:, 0:n], AF.Sqrt, [pb], [b_S[4]], bias=1e-6, scale=1.0 / D)
            P.op("dve", lambda e: e.reciprocal(out=rstd, in_=rstd), reads=[b_S[4]], writes=[b_S[4]])
            return xb, rstd

        def norm_mod(Aap, Bap, bA, bB):
            for (t0, n) in TB:
                xb, rstd = norm_stats(t0, n)
                for kc in range(16):
                    tt(xb[:, kc, :], xb[:, kc, :], rstd, ALU.mult, b_S[0:5], b_S[0:4])
                    if t0 < TP:
                        ts(actT[:, kc, t0:t0 + n], xb[:, kc, :], Aap[:, kc, 0:1], Bap[:, kc, 0:1], ALU.mult, ALU.add,
                           b_S[0:4] + [bA, bB], [b_act])
                    else:
                        x3 = xb[:, kc, :].rearrange("p (s t) -> p s t", s=NS)
                        tt(x3, x3, Aap[:, kc, 1:NSEQ].unsqueeze(2).to_broadcast([128, NS, 8]), ALU.mult, b_S[0:4] + [bA], b_S[0:4])
                        tt(actT[:, kc, t0:t0 + n].rearrange("p (s t) -> p s t", s=NS), x3,
                           Bap[:, kc, 1:NSEQ].unsqueeze(2).to_broadcast([128, NS, 8]), ALU.add, b_S[0:4] + [bB], [b_act])

        def resid_update(c, t0, n, ps, pb, gcol0):
            xs_ = tmpb[:, 0:n]
            P.dma("sp", xs_, xT_v[:, c, t0:t0 + n], reads=[b_xT[c]], writes=[b_tmpb])
            if t0 < TP:
                stt(xs_, ps[:, 0:n], modT[:, gcol0 + c, 0:1], xs_, ALU.mult, ALU.add, [pb, b_mod, b_tmpb], [b_tmpb])
            else:
                g3 = modT[:, gcol0 + c, 1:NSEQ].unsqueeze(2).to_broadcast([128, NS, 8])
                t3 = tmpc[:, 0:n].rearrange("p (s t) -> p s t", s=NS)
                tt(t3, ps[:, 0:n].rearrange("p (s t) -> p s t", s=NS), g3, ALU.mult, [pb, b_mod], [b_tmpc])
                tt(xs_, xs_, tmpc[:, 0:n], ALU.add, [b_tmpb, b_tmpc], [b_tmpb])
            P.dma("sp", xT_v[:, c, t0:t0 + n], xs_, reads=[b_tmpb], writes=[b_xT[c]])

        def to_tokmajor(src_slot_ap, bsrc, qi, c0):
            tkv = tk[qi].rearrange("(tt p) c -> p tt c", p=128)
            for g0 in range(0, 17, 4):
                g = min(4, 17 - g0)
                ps, pb = nps()
                for k in range(g):
                    tr(ps[:, k * 128:(k + 1) * 128], src_slot_ap[:, (g0 + k) * 128:(g0 + k + 1) * 128], 128, [bsrc], pb)
                act(rowst[:, 0:g * 128], ps[:, 0:g * 128], AF.Identity, [pb], [b_rowst])
                P.dma("sp", tkv[:, g0:g0 + g, c0:c0 + 128], rowst[:, 0:g * 128].rearrange("p (a b) -> p a b", a=g),
                      reads=[b_rowst], writes=[b_tk[qi]])

        bx = [[P.buf(f"x_{j}_{p}") for p in range(2)] for j in range(5)]
        bv = [P.buf(f"v_{p}") for p in range(2)]
        by = [P.buf(f"y_{p}") for p in range(2)]
        bSS = [P.buf("SA"), P.buf("SB")]
        bT2 = P.buf("T2"); bTt = P.buf("Tt")
        bKV = [P.buf(f"KV_{i}") for i in range(4)]
        b_ab1 = P.buf("ab1")
        for l in range(NLAYERS):
            load_params(l)
            memset(bda[:], 0.0, [b_bda]); memset(bdi[:], 0.0, [b_bdi])
            for (dst, bd_, nm) in ((bda, b_bda, "lru_wa"), (bdi, b_bdi, "lru_wi")):
                wv = Wt[nm].ap()[l].rearrange("(c two) d e -> two d c e", two=2)
                P.dma("pool", dst[0:64, :, 0:64], wv[0], writes=[bd_])
                P.dma("pool", dst[64:128, :, 64:128], wv[1], writes=[bd_])
            P.dma("pool", lw2[0:64, :], Wt["rw_w2"].ap()[l], writes=[b_lw2])
            P.dma("pool", lw2[64:128, :], Wt["rw_a2"].ap()[l], writes=[b_lw2])
            P.dma("pool", g2a[:, :], Wt["rw_g2"].ap()[l, 0:128, :], writes=[b_g2a])
            P.dma("pool", g2b[:, :], Wt["rw_g2"].ap()[l, 128:160, :], writes=[b_g2b])
            lam = PT[:, PCOL["lru_lambda"]:PCOL["lru_lambda"] + 8]
            act(cA[:, 0:8], lam, AF.Exp, [b_PT], [b_cA], scale=-1.0)
            act(cA[:, 0:8], cA[:, 0:8], AF.Ln, [b_cA], [b_cA], bias=1.0)
            ts(cA[:, 8:16], cA[:, 0:8], -16.0, None, ALU.mult, None, [b_cA], [b_cA])
            ts(cA[:, 0:8], cA[:, 0:8], -8.0, None, ALU.mult, None, [b_cA], [b_cA])
            loadT(stlc, b_stlc, lambda c0, w: I["st_lc"].ap()[l].rearrange("s j c -> (s j) c")[:, c0:c0 + w], 48, 8)
            loadT(stlh, b_stlh, lambda c0, w: I["st_lh"].ap()[l][:, c0:c0 + w], 16, 8)
            loadT(stsh, b_stsh, lambda c0, w: I["st_sh"].ap()[l][:, c0:c0 + w], 16, 27, last_cols=32)

            for blk_i in range(48):
                ws, bw = wslot()
                wload(ws[:, :, 0:256], Wt["w_ada"].ap()[l].rearrange("(kc p) c -> p kc c", p=128)[:, :, blk_i * 256:(blk_i + 1) * 256], bw)
                ps, pb = nps()
                for j2 in range(2):
                    for kc in range(16):
                        mm(ps[:, j2 * 32:j2 * 32 + NSEQ], ws[:, kc, j2 * 128:(j2 + 1) * 128], scT[:, kc, :], kc == 0, kc == 15,
                           [bw, b_scT], pb)
                for j2 in range(2):
                    j = blk_i * 2 + j2
                    act(modT[:, j, :], ps[:, j2 * 32:j2 * 32 + NSEQ], AF.Identity, [pb, b_PT], [b_mod], bias=pcol("b_ada", j))

            def make_A(sc0, nm_name):
                nmb = PT[:, PCOL[nm_name]:PCOL[nm_name] + 16].unsqueeze(2).to_broadcast([128, 16, NSEQ])
                stt(A1[:, :, :], modT[:, sc0:sc0 + 16, :], 1.0, nmb, ALU.add, ALU.mult, [b_mod, b_PT], [b_A1])

            make_A(16, "norm_mix")
            norm_mod(A1, modT[:, 0:16, :], b_A1, b_mod)

            win = Wt["w_in"].ap()[l].rearrange("(kc p) c -> p kc c", p=128)

            def proj_block(ws, wc0, m, t0, n, bw):
                ps, pb = nps()
                for kc in range(16):
                    mm(ps[0:m, 0:n], ws[:, kc, wc0:wc0 + m], actT[:, kc, t0:t0 + n], kc == 0, kc == 15, [bw, b_act], pb)
                return ps, pb

            for c in range(8):
                ws, bw = wslot()
                wload(ws[:, :, 0:128], win[:, :, c * 128:(c + 1) * 128], bw)
                wload(ws[:, :, 128:256], win[:, :, DL + c * 128:DL + (c + 1) * 128], bw)
                LX = slot(0); XC = slot(1); AA = slot(3); INP = slot(4)
                LXs = LX[:, 2051:2051 + 176].rearrange("p (s j) -> p s j", s=NS)
                memset(LX[:, 0:3], 0.0, [b_S[0]])
                cp(LXs[:, :, 0:3], stlc[:, c, :].rearrange("p (s j) -> p s j", s=NS), [b_stlc], [b_S[0]], eng="pool")
                for (t0, n) in TB:
                    ps, pb = proj_block(ws, 0, 128, t0, n, bw)
                    if t0 < TP:
                        act(LX[:, 3 + t0:3 + t0 + n], ps[:, 0:n], AF.Identity, [pb], [b_S[0]])
                    else:
                        act(LXs[:, :, 3:11], ps[:, 0:n].rearrange("p (s t) -> p s t", s=NS), AF.Identity, [pb], [b_S[0]])
                cp(olc[:, c, 0:3], LX[:, 2048:2051], [b_S[0]], [b_olc], eng="pool")
                cp(olc[:, c, 3:51].rearrange("p (s j) -> p s j", s=NS), LXs[:, :, 8:11], [b_S[0]], [b_olc], eng="pool")
                XCs = XC[:, TP:T].rearrange("p (s t) -> p s t", s=NS)
                ts(XC[:, 0:TP], LX[:, 0:TP], pcol("lru_conv_w", 0 * 8 + c), pcol("lru_conv_b", c), ALU.mult, ALU.add,
                   [b_S[0], b_PT], [b_S[1]])
                ts(XCs, LXs[:, :, 0:8], pcol("lru_conv_w", 0 * 8 + c), pcol("lru_conv_b", c), ALU.mult, ALU.add,
                   [b_S[0], b_PT], [b_S[1]])
                for j in range(1, 4):
                    stt(XC[:, 0:TP], LX[:, j:j + TP], pcol("lru_conv_w", j * 8 + c), XC[:, 0:TP], ALU.mult, ALU.add,
                        [b_S[0], b_S[1], b_PT], [b_S[1]])
                    stt(XCs, LXs[:, :, j:j + 8], pcol("lru_conv_w", j * 8 + c), XCs, ALU.mult, ALU.add,
                        [b_S[0], b_S[1], b_PT], [b_S[1]])
                XCb = slot(2).bitcast(BF16)[:, 0:T]
                act(XCb, XC[:, 0:T], AF.Identity, [b_S[1]], [b_S[2]])
                for (t0, n) in TB:
                    psr, pbr = nps()
                    mm(psr[:, 0:n], bda[:, c, :], XCb[:, t0:t0 + n], True, True, [b_bda, b_S[2]], pbr)
                    psi, pbi = nps()
                    mm(psi[:, 0:n], bdi[:, c, :], XCb[:, t0:t0 + n], True, True, [b_bdi, b_S[2]], pbi)
                    rr = INP[:, t0:t0 + n]
                    act(rr, psr[:, 0:n], AF.Sigmoid, [pbr, b_PT], [b_S[4]], bias=pcol("lru_ba", c))
                    act(AA[:, t0:t0 + n], rr, AF.Exp, [b_S[4], b_cA], [b_S[3]], scale=cA[:, c:c + 1])
                    act(rr, rr, AF.Exp, [b_S[4], b_cA], [b_S[4]], scale=cA[:, 8 + c:9 + c])
                    act(rr, rr, AF.Sqrt, [b_S[4]], [b_S[4]], scale=-1.0, bias=1.0)
                    act(tmpb[:, 0:n], psi[:, 0:n], AF.Sigmoid, [pbi, b_PT], [b_tmpb], bias=pcol("lru_bi", c))
                    tt(rr, rr, tmpb[:, 0:n], ALU.mult, [b_S[4], b_tmpb], [b_S[4]])
                    tt(rr, rr, XC[:, t0:t0 + n], ALU.mult, [b_S[4], b_S[1]], [b_S[4]])
                AAs = AA[:, TP:T].rearrange("p (s t) -> p s t", s=NS)
                INs = INP[:, TP:T].rearrange("p (s t) -> p s t", s=NS)
                tt(sm[:, 0:NS], AAs[:, :, 0], stlh[:, c, :], ALU.mult, [b_S[3], b_stlh], [b_sm])
                tt(INs[:, :, 0], INs[:, :, 0], sm[:, 0:NS], ALU.add, [b_S[4], b_sm], [b_S[4]])
                memset(AAs[:, :, 0], 0.0, [b_S[3]], eng="dve")
                HL = slot(1)
                P.op("dve", lambda e, HL=HL, AA=AA, INP=INP: e.tensor_tensor_scan(out=HL[:, 0:TP], data0=AA[:, 0:TP], data1=INP[:, 0:TP],
                                                                           initial=0.0, op0=ALU.mult, op1=ALU.add),
                     reads=[b_S[3], b_S[4]], writes=[b_S[1]])
                P.op("dve", lambda e, HL=HL, AA=AA, INP=INP: e.tensor_tensor_scan(out=HL[:, TP:T], data0=AA[:, TP:T], data1=INP[:, TP:T],
                                                                           initial=0.0, op0=ALU.mult, op1=ALU.add),
                     reads=[b_S[3], b_S[4]], writes=[b_S[1]])
                cp(olh[:, c, 0:1], HL[:, TP - 1:TP], [b_S[1]], [b_olh], eng="pool")
                cp(olh[:, c, 1:NSEQ], HL[:, TP:T].rearrange("p (s t) -> p s t", s=NS)[:, :, 7], [b_S[1]], [b_olh], eng="pool")
                GG = slot(0)
                GAb = slot(2).bitcast(BF16)[:, 0:T]
                for (t0, n) in TB:
                    ps, pb = proj_block(ws, 128, 128, t0, n, bw)
                    u = GG[:, t0:t0 + n]
                    act(u, ps[:, 0:n], AF.Identity, [pb], [b_S[0]])
                    tt(tmpb[:, 0:n], u, u, ALU.mult, [b_S[0]], [b_tmpb])
                    ts(tmpb[:, 0:n], tmpb[:, 0:n], 0.044715, 1.0, ALU.mult, ALU.add, [b_tmpb], [b_tmpb])
                    tt(tmpb[:, 0:n], tmpb[:, 0:n], u, ALU.mult, [b_tmpb, b_S[0]], [b_tmpb])
                    act(tmpb[:, 0:n], tmpb[:, 0:n], AF.Sigmoid, [b_tmpb], [b_tmpb], scale=1.5957691216057308)
                    tt(u, u, tmpb[:, 0:n], ALU.mult, [b_S[0], b_tmpb], [b_S[0]])
                    tt(GAb[:, t0:t0 + n], u, HL[:, t0:t0 + n], ALU.mult, [b_S[0], b_S[1]], [b_S[2]])
                P.dma("sp", ga_d[c * 128:(c + 1) * 128, :], GAb, reads=[b_S[2]], writes=[b_ga])
            flushT(olc, b_olc, 8, 51, lambda c0, w: O["o_lc"].ap()[l].rearrange("s j c -> (s j) c")[:, c0:c0 + w], b_out["o_lc"])
            flushT(olh, b_olh, 8, NSEQ, lambda c0, w: O["o_lh"].ap()[l][:, c0:c0 + w], b_out["o_lh"])

            TL = blk[:, :]
            TLa = TL[:, 0:T]
            SG1 = TL[:, T:2 * T]
            SG2 = TL[:, 2 * T:3 * T]
            b_TL = [b_blk]
            def rw_A(q, par):
                m = 32 if q == 26 else 128
                ws, bw = wslot()
                wload(ws[:, :, 0:m], win[:, :, 2 * DL + q * 128:2 * DL + q * 128 + m], bw)
                RW = slot(2 * par); bRW = b_S[2 * par]
                RWs = RW[:, 2049:2049 + 144].rearrange("p (s j) -> p s j", s=NS)
                memset(RW[0:m, 0:1], 0.0, [bRW])
                cp(RWs[0:m, :, 0], stsh[0:m, q, :], [b_stsh], [bRW], eng="pool")
                for (t0, n) in TB:
                    ps, pb = proj_block(ws, 0, m, t0, n, bw)
                    if t0 < TP:
                        act(RW[0:m, 1 + t0:1 + t0 + n], ps[0:m, 0:n], AF.Identity, [pb], [bRW])
                    else:
                        act(RWs[0:m, :, 1:9], ps[0:m, 0:n].rearrange("p (s t) -> p s t", s=NS), AF.Identity, [pb], [bRW])

            def rw_B(q, par):
                m = 32 if q == 26 else 128
                RW = slot(2 * par); bRW = b_S[2 * par]
                XS = slot(2 * par + 1); bXS = b_S[2 * par + 1]
                RWs = RW[:, 2049:2049 + 144].rearrange("p (s j) -> p s j", s=NS)
                cp(osh[0:m, q, 0:1], RW[0:m, TP:TP + 1], [bRW], [b_osh], eng="pool")
                cp(osh[0:m, q, 1:NSEQ], RWs[0:m, :, 8], [bRW], [b_osh], eng="pool")
                XSs = XS[:, TP:T].rearrange("p (s t) -> p s t", s=NS)
                mu = pcol("rw_mu", q)
                tt(XS[0:m, 0:TP], RW[0:m, 0:TP], RW[0:m, 1:TP + 1], ALU.subtract, [bRW], [bXS])
                tt(XSs[0:m], RWs[0:m, :, 0:8], RWs[0:m, :, 1:9], ALU.subtract, [bRW], [bXS])
                stt(XS[0:m, 0:TP], XS[0:m, 0:TP], mu[0:m], RW[0:m, 1:TP + 1], ALU.mult, ALU.add, [bRW, bXS, b_PT], [bXS])
                stt(XSs[0:m], XSs[0:m], mu[0:m], RWs[0:m, :, 1:9], ALU.mult, ALU.add, [bRW, bXS, b_PT], [bXS])
                if q == 24:
                    act(TLa[0:64, :], XS[0:64, 0:T], AF.Tanh, [bXS], b_TL)
                    act(TLa[64:128, :], XS[64:128, 0:T], AF.Identity, [bXS], b_TL)
                elif q == 25:
                    act(SG1, XS[:, 0:T], AF.Sigmoid, [bXS], b_TL)
                elif q == 26:
                    act(SG2[0:32, :], XS[0:32, 0:T], AF.Sigmoid, [bXS], b_TL)
                else:
                    to_tokmajor(XS, bXS, q // 8, (q % 8) * 128)

            qorder = [24, 25, 26] + list(range(24))
            rw_A(qorder[0], 0)
            for qi_, q in enumerate(qorder):
                if qi_ + 1 < len(qorder):
                    rw_A(qorder[qi_ + 1], (qi_ + 1) % 2)
                rw_B(q, qi_ % 2)
            flushT(osh, b_osh, 27, NSEQ, lambda c0, w: O["o_sh"].ap()[l][:, c0:c0 + w], b_out["o_sh"], last_cols=32)
            for c in range(8):
                DC = slot(0); AC = slot(1); GC = slot(2)
                for (t0, n) in TB:
                    ps, pb = nps()
                    mm(ps[:, 0:n], lw2[0:64, c * 128:(c + 1) * 128], TLa[0:64, t0:t0 + n], True, True, [b_lw2] + b_TL, pb)
                    act(DC[:, t0:t0 + n], ps[:, 0:n], AF.Sigmoid, [pb, b_PT], [b_S[0]], bias=pcol("rw_w0", c))
                    act(DC[:, t0:t0 + n], DC[:, t0:t0 + n], AF.Exp, [b_S[0]], [b_S[0]], scale=-math.exp(-0.5))
                    ps, pb = nps()
                    mm(ps[:, 0:n], lw2[64:128, c * 128:(c + 1) * 128], TLa[64:128, t0:t0 + n], True, True, [b_lw2] + b_TL, pb)
                    act(AC[:, t0:t0 + n], ps[:, 0:n], AF.Sigmoid, [pb, b_PT], [b_S[1]], bias=pcol("rw_a0", c))
                    ps, pb = nps()
                    mm(ps[:, 0:n], g2a[:, c * 128:(c + 1) * 128], SG1[:, t0:t0 + n], True, False, [b_g2a] + b_TL, pb)
                    mm(ps[:, 0:n], g2b[0:32, c * 128:(c + 1) * 128], SG2[0:32, t0:t0 + n], False, True, [b_g2b] + b_TL, pb)
                    act(GC[:, t0:t0 + n], ps[:, 0:n], AF.Identity, [pb], [b_S[2]])
                to_tokmajor(DC, b_S[0], 3, c * 128)
                to_tokmajor(AC, b_S[1], 4, c * 128)
                to_tokmajor(GC, b_S[2], 5, c * 128)
            for gc in range(32):
                if gc % 2 == 0:
                    ws, bw = wslot()
                    wload(ws[:, :, 0:256], win[:, :, 2 * DL + NRW + gc * 128:2 * DL + NRW + (gc + 2) * 128], bw)
                SGb = slot(gc % 2).bitcast(BF16)[:, 0:T]
                bsl = b_S[gc % 2]
                for (t0, n) in TB:
                    ps, pb = proj_block(ws, (gc % 2) * 128, 128, t0, n, bw)
                    act(SGb[:, t0:t0 + n], ps[:, 0:n], AF.Sigmoid, [pb], [bsl])
                P.dma("sp", sg[gc * 128:(gc + 1) * 128, :], SGb, reads=[bsl], writes=[b_sg])

            for i, nm in enumerate(("rw_kk", "rw_ka", "rw_rk")):
                src = Wt[nm].ap()[l]
                if nm == "rw_rk":
                    src = src.rearrange("h k -> (h k)")
                P.dma("act", ct[i][:, :], src.partition_broadcast(128), writes=[b_ct[i]])
            rqv = rq_h.ap()

            def half(i, h):
                return arena[:, i, h * 1024:(h + 1) * 1024]

            def h3(ap):
                return ap.rearrange("p (h k) -> p h k", h=16)

            for i in range(17):
                rows = slice(i * 128, (i + 1) * 128)
                Rt, Kt, Vt, Dt, At = half(0, 0), half(0, 1), half(1, 0), half(1, 1), half(2, 0)
                KKt, TMt, KMt, NBt, BVt = half(2, 1), half(3, 0), half(3, 1), half(4, 0), half(4, 1)
                for (dst, qi, bb) in ((Rt, 0, b_S[0]), (Kt, 1, b_S[0]), (Vt, 2, b_S[1]), (Dt, 3, b_S[1]), (At, 4, b_S[2])):
                    P.dma("sp", dst, tk[qi][rows, :], reads=[b_tk[qi]], writes=[bb])
                tt(KKt, Kt, ct[0][:, :], ALU.mult, [b_S[0], b_ct[0]], [b_S[2]])
                tt(TMt, KKt, KKt, ALU.mult, [b_S[2]], [b_S[3]])
                P.op("dve", lambda e, TMt=TMt: e.tensor_reduce(out=sm[:, 0:16], in_=h3(TMt), axis=AX.X, op=ALU.add),
                     reads=[b_S[3]], writes=[b_sm])
                act(sm[:, 0:16], sm[:, 0:16], AF.Sqrt, [b_sm], [b_sm])
                ts(sm[:, 0:16], sm[:, 0:16], 1e-12, None, ALU.max, None, [b_sm], [b_sm])
                P.op("dve", lambda e: e.reciprocal(out=sm[:, 0:16], in_=sm[:, 0:16]), reads=[b_sm], writes=[b_sm])
                tt(h3(KKt), h3(KKt), sm[:, 0:16].unsqueeze(2).to_broadcast([128, 16, 64]), ALU.mult, [b_S[2], b_sm], [b_S[2]])
                stt(TMt, At, -1.0, ct[1][:, :], ALU.add, ALU.mult, [b_S[2], b_ct[1]], [b_S[3]])
                stt(KMt, TMt, 1.0, Kt, ALU.add, ALU.mult, [b_S[3], b_S[0]], [b_S[3]])
                stt(NBt, KKt, -1.0, At, ALU.mult, ALU.mult, [b_S[2]], [b_S[4]])
                ccv = sm[:, 32:64].rearrange("p (h j) -> p h j", j=2)
                tt(TMt, Rt, KMt, ALU.mult, [b_S[0], b_S[3]], [b_S[3]])
                P.op("dve", lambda e, TMt=TMt, ccv=ccv: e.tensor_reduce(out=ccv[:, :, 1], in_=h3(TMt), axis=AX.X, op=ALU.add),
                     reads=[b_S[3]], writes=[b_sm])
                tt(TMt, TMt, ct[2][:, :], ALU.mult, [b_S[3], b_ct[2]], [b_S[3]])
                P.op("dve", lambda e, TMt=TMt: e.tensor_reduce(out=sm[:, 16:32], in_=h3(TMt), axis=AX.X, op=ALU.add),
                     reads=[b_S[3]], writes=[b_sm])
                tt(h3(BVt), h3(Vt), sm[:, 16:32].unsqueeze(2).to_broadcast([128, 16, 64]), ALU.mult, [b_S[1], b_sm], [b_S[4]])
                for (srct, qi, bb) in ((Rt, 0, b_S[0]), (Dt, 1, b_S[1]), (KMt, 2, b_S[3]), (KKt, 3, b_S[2]), (NBt, 4, b_S[4])):
                    P.dma("sp", rqv[qi].rearrange("h t k -> t h k")[rows], h3(srct), reads=[bb], writes=[b_rq])
                tt(Kt, NBt, Rt, ALU.mult, [b_S[4], b_S[0]], [b_S[0]])
                P.op("dve", lambda e, Kt=Kt, ccv=ccv: e.tensor_reduce(out=ccv[:, :, 0], in_=h3(Kt), axis=AX.X, op=ALU.add),
                     reads=[b_S[0]], writes=[b_sm])
                tt(Kt, Rt, Dt, ALU.mult, [b_S[0], b_S[1]], [b_S[0]])
                P.dma("sp", rqv[5].rearrange("h t k -> t h k")[rows], h3(Kt), reads=[b_S[0]], writes=[b_rq])
                ps, pb = nps()
                tr(ps[0:32, 0:128], sm[:, 32:64], 128, [b_sm], pb)
                cp(rowst[0:32, 0:128], ps[0:32, 0:128], [pb], [b_rowst])
                P.dma("sp", cc_h.ap()[:, rows], rowst[0:32, 0:128], reads=[b_rowst], writes=[b_cc])
                P.dma("sp", tk[6][rows, :], BVt, reads=[b_S[4]], writes=[b_tk[6]])

            def rec_steps(nsq, nsteps, xr_step, v_step, y_step):
                S3 = Sst[:, 0:nsq * 512].rearrange("p (s v k) -> p s v k", s=nsq, v=8)
                T3 = Stmp[:, 0:nsq * 512].rearrange("p (s v k) -> p s v k", s=nsq, v=8)
                K4 = [128, nsq, 8, 64]
                rdx = b_S[0:5]
                for t in range(nsteps):
                    r_, w_, k_, kk_, nb_ = [xr_step(j, t).unsqueeze(2).to_broadcast(K4) for j in range(5)]
                    sk = skk[:, 0:nsq * 8].rearrange("p (s v) -> p s v", s=nsq)
                    tt(T3, S3, kk_, ALU.mult, [b_Sst] + rdx, [b_Stmp])
                    P.op("dve", lambda e, sk=sk, T3=T3: e.tensor_reduce(out=sk, in_=T3, axis=AX.X, op=ALU.add), reads=[b_Stmp], writes=[b_skk])
                    tt(S3, S3, w_, ALU.mult, [b_Sst] + rdx, [b_Sst])
                    tt(T3, nb_, sk.unsqueeze(3).to_broadcast(K4), ALU.mult, [b_skk] + rdx, [b_Stmp])
                    tt(S3, S3, T3, ALU.add, [b_Sst, b_Stmp], [b_Sst])
                    tt(T3, k_, v_step(t).unsqueeze(3).to_broadcast(K4), ALU.mult, [b_vbuf] + rdx, [b_Stmp])
                    tt(S3, S3, T3, ALU.add, [b_Sst, b_Stmp], [b_Sst])
                    tt(T3, S3, r_, ALU.mult, [b_Sst] + rdx, [b_Stmp])
                    yo = y_step(t)
                    P.op("dve", lambda e, yo=yo, T3=T3: e.tensor_reduce(out=yo, in_=T3, axis=AX.X, op=ALU.add), reads=[b_Stmp], writes=[b_ybuf])

            oS = O["o_S"].ap()[l]
            TC = 16
            SLQ = [3, 5, 1, 4, 2]
            allb = b_S[0:5] + [b for bb_ in bx for b in bb_] + bv + by + bSS + [bT2, bTt] + bKV + [b_blk, b_vbuf, b_ybuf]
            fence(allb)
            SS = [blkF[:, 0:512], blkF[:, 512:1024]]
            T2v = blkF[:, 1024:2048].rearrange("p (q v k) -> p q v k", q=2, v=8)
            Ttv = blkF[:, 2048:2560].rearrange("p (v k) -> p v k", v=8)
            KVv = [blkF[:, 2560 + i * 512:2560 + (i + 1) * 512].rearrange("p (v k) -> p v k", v=8) for i in range(4)]
            memset(SS[0], 0.0, [bSS[0]], eng="dve")
            g = 0
            NCH = TP // TC

            def chunk_views(ci):
                par = ci % 2
                vb_ = vbuf[:, par * 128:(par + 1) * 128].rearrange("p (t v) -> p t v", t=TC)
                yb_ = ybuf[:, par * 128:(par + 1) * 128].rearrange("p (t v) -> p t v", t=TC)
                cb_ = ccb[:, par * 32:(par + 1) * 32].rearrange("p (j t) -> p j t", j=2)
                sk_ = skys[:, par * 256:(par + 1) * 256].rearrange("p (q t v) -> p q t v", q=2, t=TC)
                return par, vb_, yb_, cb_, sk_

            def chunk_loads(ci):
                t0 = ci * TC
                par, vb_, yb_, cb_, sk_ = chunk_views(ci)
                xo = par * 1024
                for j in range(5):
                    src = bass.AP(rq_h, SLQ[j] * 16 * T * 64 + t0 * 64, [[T * 64, 16], [0, 8], [1, TC * 64]])
                    P.dma("sp", slot(j, TC * 64, xo), src, reads=[b_rq], writes=[bx[j][par]])
                P.dma("act", vb_, bass.AP(tk_h, 2 * T * DL + t0 * DL, [[8, 128], [DL, TC], [1, 8]]), reads=[b_tk[2]], writes=[bv[par]])
                for jj in range(2):
                    P.dma("act", cb_[:, jj, :], bass.AP(cc_h, jj * T + t0, [[2 * T, 16], [0, 8], [1, TC]]), reads=[b_cc], writes=[b_ccb[par]])

            chunk_loads(0)
            for ci in range(NCH):
                t0 = ci * TC
                par, vb_, yb_, cb_, sk_ = chunk_views(ci)
                xo = par * 1024
                if ci + 1 < NCH:
                    chunk_loads(ci + 1)
                for t in range(TC):
                    cur, nxt = g % 2, (g + 1) % 2
                    Sc3 = SS[cur].rearrange("p (v k) -> p v k", v=8)
                    Sn3 = SS[nxt].rearrange("p (v k) -> p v k", v=8)
                    kv = KVv[g % 4]; bkv = bKV[g % 4]
                    o = xo + t * 64

                    def kv_fn(e, kv=kv, o=o, vb_=vb_, t=t):
                        last = None
                        for v8 in range(8):
                            last = e.activation(out=kv[:, v8, :], in_=slot(4, 64, o), func=AF.Identity, scale=vb_[:, t, v8:v8 + 1])
                        return last
                    P.op("act", kv_fn, reads=[bx[4][par], bv[par]], writes=[bkv])
                    tt(T2v, Sc3.unsqueeze(1).to_broadcast([128, 2, 8, 64]),
                       arena[:, 0:2, o:o + 64].unsqueeze(2).to_broadcast([128, 2, 8, 64]), ALU.mult,
                       [bSS[cur], bx[0][par], bx[1][par]], [bT2])
                    P.op("dve", lambda e, out=sk_[:, :, t, :], T2v=T2v: e.tensor_reduce(out=out, in_=T2v, axis=AX.X, op=ALU.add),
                         reads=[bT2], writes=[b_skys[par]])
                    tt(Sn3, Sc3, slot(2, 64, o).unsqueeze(1).to_broadcast([128, 8, 64]), ALU.mult, [bSS[cur], bx[2][par]], [bSS[nxt]], eng="pool")
                    tt(Sn3, Sn3, kv, ALU.add, [bSS[nxt], bkv], [bSS[nxt]], eng="pool")
                    tt(Ttv, sk_[:, 0, t, :].unsqueeze(2).to_broadcast([128, 8, 64]),
                       slot(3, 64, o).unsqueeze(1).to_broadcast([128, 8, 64]), ALU.mult, [b_skys[par], bx[3][par]], [bTt])
                    tt(Sn3, Sn3, Ttv, ALU.add, [bSS[nxt], bTt], [bSS[nxt]])
                    g += 1
                c1b = cb_[:, 0, :].unsqueeze(2).to_broadcast([128, TC, 8])
                c2b = cb_[:, 1, :].unsqueeze(2).to_broadcast([128, TC, 8])
                tt(yb_, sk_[:, 0, :, :], c1b, ALU.mult, [b_skys[par], b_ccb[par]], [by[par]], eng="pool")
                tt(yb_, yb_, sk_[:, 1, :, :], ALU.add, [by[par], b_skys[par]], [by[par]], eng="pool")
                tt(sk_[:, 0, :, :], vb_, c2b, ALU.mult, [bv[par], b_ccb[par]], [b_skys[par]], eng="pool")
                tt(yb_, yb_, sk_[:, 0, :, :], ALU.add, [by[par], b_skys[par]], [by[par]], eng="pool")
                P.dma("act", bass.AP(tk_h, 7 * T * DL + t0 * DL, [[8, 128], [DL, TC], [1, 8]]), yb_, reads=[by[par]], writes=[b_tk[7]])
            P.dma("sp", oS[0].rearrange("h (vb v8) k -> (h vb) (v8 k)", vb=8), SS[g % 2], reads=[bSS[g % 2]], writes=[b_out["o_S"]])
            fence(allb)
            for sb in range(4):
                s0 = sb * 4
                P.dma("sp", Sst[:, 0:2048].rearrange("p (s f) -> p s f", s=4),
                      bass.AP(I["st_S"], (l * NS + s0) * 65536, [[512, 128], [65536, 4], [1, 512]]), writes=[b_Sst])
                for j in range(5):
                    for s in range(4):
                        tok0 = TP + (s0 + s) * 8
                        src = bass.AP(rq_h, j * 16 * T * 64 + tok0 * 64, [[T * 64, 16], [0, 8], [1, 8 * 64]])
                        P.dma("sp" if s % 2 == 0 else "act", slot(j, 512, s * 512), src, reads=[b_rq], writes=[b_S[j]])
                P.dma("act", vbuf[:, 0:256].rearrange("p (t v) -> p t v", t=32),
                      bass.AP(tk_h, 2 * T * DL + (TP + s0 * 8) * DL, [[8, 128], [DL, 32], [1, 8]]), reads=[b_tk[2]], writes=[b_vbuf])
                rec_steps(4, 8,
                          lambda j, t: slot(j, 2048).rearrange("p (s t k) -> p s t k", s=4, t=8)[:, :, t, :],
                          lambda t: vbuf[:, 0:256].rearrange("p (s t v) -> p s t v", s=4, t=8)[:, :, t, :],
                          lambda t: ybuf[:, 0:256].rearrange("p (s t v) -> p s t v", s=4, t=8)[:, :, t, :])
                P.dma("act", bass.AP(tk_h, 7 * T * DL + (TP + s0 * 8) * DL, [[8, 128], [DL, 32], [1, 8]]),
                      ybuf[:, 0:256].rearrange("p (t v) -> p t v", t=32), reads=[b_ybuf], writes=[b_tk[7]])
                P.dma("sp", bass.AP(O["o_S"], (l * NSEQ + 1 + s0) * 65536, [[512, 128], [65536, 4], [1, 512]]),
                      Sst[:, 0:2048].rearrange("p (s f) -> p s f", s=4), reads=[b_Sst], writes=[b_out["o_S"]])

            for i, nm in enumerate(("rw_gn_g", "rw_gn_b")):
                P.dma("act", ct[i][:, :], Wt[nm].ap()[l].partition_broadcast(128), writes=[b_ct[i]])
            for i in range(17):
                rows = slice(i * 128, (i + 1) * 128)
                Yt, BVt, GGt, TMt = half(0, 0), half(0, 1), half(1, 0), half(1, 1)
                P.dma("sp", Yt, tk[7][rows, :], reads=[b_tk[7]], writes=[b_S[0]])
                P.dma("sp", BVt, tk[6][rows, :], reads=[b_tk[6]], writes=[b_S[0]])
                P.dma("sp", GGt, tk[5][rows, :], reads=[b_tk[5]], writes=[b_S[1]])
                P.op("dve", lambda e, Yt=Yt: e.tensor_reduce(out=sm[:, 0:16], in_=h3(Yt), axis=AX.X, op=ALU.add), reads=[b_S[0]], writes=[b_sm])
                ts(sm[:, 0:16], sm[:, 0:16], 1.0 / 64, None, ALU.mult, None, [b_sm], [b_sm])
                tt(h3(Yt), h3(Yt), sm[:, 0:16].unsqueeze(2).to_broadcast([128, 16, 64]), ALU.subtract, [b_S[0], b_sm], [b_S[0]])
                tt(TMt, Yt, Yt, ALU.mult, [b_S[0]], [b_S[1]])
                P.op("dve", lambda e, TMt=TMt: e.tensor_reduce(out=sm[:, 16:32], in_=h3(TMt), axis=AX.X, op=ALU.add), reads=[b_S[1]], writes=[b_sm])
                act(sm[:, 16:32], sm[:, 16:32], AF.Sqrt, [b_sm], [b_sm], scale=1.0 / 64, bias=64e-5)
                P.op("dve", lambda e: e.reciprocal(out=sm[:, 16:32], in_=sm[:, 16:32]), reads=[b_sm], writes=[b_sm])
                tt(h3(Yt), h3(Yt), sm[:, 16:32].unsqueeze(2).to_broadcast([128, 16, 64]), ALU.mult, [b_S[0], b_sm], [b_S[0]])
                tt(Yt, Yt, ct[0][:, :], ALU.mult, [b_S[0], b_ct[0]], [b_S[0]])
                tt(Yt, Yt, ct[1][:, :], ALU.add, [b_S[0], b_ct[1]], [b_S[0]])
                tt(Yt, Yt, BVt, ALU.add, [b_S[0]], [b_S[0]])
                tt(Yt, Yt, GGt, ALU.mult, [b_S[0], b_S[1]], [b_S[0]])
                OTb = slot(2).bitcast(BF16)[:, 0:1024].rearrange("p (c t) -> p c t", c=8)
                for g in range(2):
                    ps, pb = nps()
                    for j in range(4):
                        tr(ps[:, j * 128:(j + 1) * 128], Yt[:, (g * 4 + j) * 128:(g * 4 + j + 1) * 128], 128, [b_S[0]], pb)
                    act(OTb[:, g * 4:(g + 1) * 4, :].rearrange("p a b -> p (a b)"), ps[:, :], AF.Identity, [pb], [b_S[2]])
                P.dma("sp", oT_d.rearrange("(c p) t -> p c t", p=128)[:, :, rows], OTb, reads=[b_S[2]], writes=[b_oT])

            wpa = Wt["w_pa"].ap()[l].rearrange("(kc p) c -> p kc c", p=128)
            wpb = Wt["w_pb"].ap()[l].rearrange("(kc p) c -> p kc c", p=128)
            gav = ga_d.rearrange("(kc p) t -> p kc t", p=128)
            otv = oT_d.rearrange("(kc p) t -> p kc t", p=128)
            mcnt = 0
            for (t0, n) in TB:
                gb = blk[:, 0:16 * 512].rearrange("p (kc t) -> p kc t", kc=16)
                P.dma("sp", gb[:, 0:8, 0:n], gav[:, :, t0:t0 + n], reads=[b_ga], writes=[b_blk])
                P.dma("act", gb[:, 8:16, 0:n], otv[:, :, t0:t0 + n], reads=[b_oT], writes=[b_blk])
                for c in range(16):
                    ws, bw = wslot()
                    wload(ws[:, 0:8, 0:128], wpa[:, :, c * 128:(c + 1) * 128], bw)
                    wload(ws[:, 8:16, 0:128], wpb[:, :, c * 128:(c + 1) * 128], bw)
                    sg_ = sgt[:, mcnt % 2, :]; bsg = b_sgt[mcnt % 2]
                    mcnt += 1
                    P.dma("sp", sg_[:, 0:n], sg[c * 128:(c + 1) * 128, t0:t0 + n], reads=[b_sg], writes=[bsg])
                    P.dma("sp", sg_[:, 512:512 + n], sg[D + c * 128:D + (c + 1) * 128, t0:t0 + n], reads=[b_sg], writes=[bsg])
                    psa, pba = nps()
                    for kc in range(8):
                        mm(psa[:, 0:n], ws[:, kc, 0:128], gb[:, kc, 0:n], kc == 0, kc == 7, [bw, b_blk], pba)
                    psb, pbb = nps()
                    for kc in range(8):
                        mm(psb[:, 0:n], ws[:, 8 + kc, 0:128], gb[:, 8 + kc, 0:n], kc == 0, kc == 7, [bw, b_blk], pbb)
                    tt(tmpb[:, 0:n], psa[:, 0:n], sg_[:, 0:n], ALU.mult, [pba, bsg], [b_tmpb])
                    tt(tmpc[:, 0:n], psb[:, 0:n], sg_[:, 512:512 + n], ALU.mult, [pbb, bsg], [b_tmpc])
                    tt(actT[:, c, t0:t0 + n], tmpb[:, 0:n], tmpc[:, 0:n], ALU.add, [b_tmpb, b_tmpc], [b_act])
            wo = Wt["w_o"].ap()[l].rearrange("(kc p) c -> p kc c", p=128)
            for c in range(16):
                if c % 2 == 0:
                    ws, bw = wslot()
                    wload(ws[:, :, 0:256], wo[:, :, c * 128:(c + 2) * 128], bw)
                for (t0, n) in TB:
                    ps, pb = proj_block(ws, (c % 2) * 128, 128, t0, n, bw)
                    resid_update(c, t0, n, ps, pb, 32)

            make_A(64, "norm_ffn")
            norm_mod(A1, modT[:, 48:64, :], b_A1, b_mod)

            wup = Wt["w_up"].ap()[l].rearrange("(kc p) c -> p kc c", p=128)
            for j in range(44):
                j4 = j % 4
                stv = I["st_fc"].ap()[l].rearrange("s j c -> (s j) c")
                ofv = O["o_fc"].ap()[l].rearrange("s j c -> (s j) c")
                if j4 == 0:
                    for hh in range(2):
                        loadT(stfc[:, hh * 4:hh * 4 + 4, :], b_stfc,
                              lambda c0, w, hh=hh, j=j: stv[:, (hh * 44 + j) * 128 + c0:(hh * 44 + j) * 128 + c0 + w], 32, 4)
                ws, bw = wslot()
                wload(ws[:, :, 0:128], wup[:, :, j * 128:(j + 1) * 128], bw)
                wload(ws[:, :, 128:256], wup[:, :, DFF + j * 128:DFF + (j + 1) * 128], bw)
                outs = []
                for hh in range(2):
                    ch = hh * 44 + j
                    UU = slot(0 + hh * 2); UC = slot(1 + hh * 2)
                    bU = b_S[0 + hh * 2]; bC = b_S[1 + hh * 2]
                    UUs = UU[:, 2050:2050 + 160].rearrange("p (s j) -> p s j", s=NS)
                    memset(UU[:, 0:2], 0.0, [bU])
                    cp(UUs[:, :, 0:2], stfc[:, hh * 4 + j4, :].rearrange("p (s j) -> p s j", s=NS), [b_stfc], [bU], eng="pool")
                    for (t0, n) in TB:
                        ps, pb = proj_block(ws, hh * 128, 128, t0, n, bw)
                        if t0 < TP:
                            act(UU[:, 2 + t0:2 + t0 + n], ps[:, 0:n], AF.Identity, [pb], [bU])
                        else:
                            act(UUs[:, :, 2:10], ps[:, 0:n].rearrange("p (s t) -> p s t", s=NS), AF.Identity, [pb], [bU])
                    k8 = j % 8
                    if hh == 0 and k8 == 0:
                        pass
                    cp(ofc[:, hh * 4 + j4, 0:2], UU[:, 2048:2050], [bU], [b_ofc], eng="pool")
                    cp(ofc[:, hh * 4 + j4, 2:34].rearrange("p (s j) -> p s j", s=NS), UUs[:, :, 8:10], [bU], [b_ofc], eng="pool")
                    UCs = UC[:, TP:T].rearrange("p (s t) -> p s t", s=NS)
                    ts(UC[:, 0:TP], UU[:, 0:TP], pcol("ffn_conv_w", 0 * 88 + ch), pcol("ffn_conv_b", ch), ALU.mult, ALU.add, [bU, b_PT], [bC])
                    ts(UCs, UUs[:, :, 0:8], pcol("ffn_conv_w", 0 * 88 + ch), pcol("ffn_conv_b", ch), ALU.mult, ALU.add, [bU, b_PT], [bC])
                    for jj in range(1, 3):
                        stt(UC[:, 0:TP], UU[:, jj:jj + TP], pcol("ffn_conv_w", jj * 88 + ch), UC[:, 0:TP], ALU.mult, ALU.add, [bU, bC, b_PT], [bC])
                        stt(UCs, UUs[:, :, jj:jj + 8], pcol("ffn_conv_w", jj * 88 + ch), UCs, ALU.mult, ALU.add, [bU, bC, b_PT], [bC])
                    outs.append((UC, bC))
                (UG, bG), (UV, bV) = outs
                act(UG[:, 0:T], UG[:, 0:T], AF.Silu, [bG], [bG])
                ATb = slot(4).bitcast(BF16)[:, 0:T]
                tt(ATb, UG[:, 0:T], UV[:, 0:T], ALU.mult, [bG, bV], [b_S[4]])
                P.dma("sp", aT_d[j * 128:(j + 1) * 128, :], ATb, reads=[b_S[4]], writes=[b_aT])
                if j4 == 3:
                    for hh in range(2):
                        flushT(ofc[:, hh * 4:hh * 4 + 4, :], b_ofc, 4, 34,
                               lambda c0, w, hh=hh, j=j: ofv[:, (hh * 44 + j - 3) * 128 + c0:(hh * 44 + j - 3) * 128 + c0 + w], b_out["o_fc"])

            wdn = Wt["w_down"].ap()[l].rearrange("(kc p) c -> p kc c", p=128)
            atv = aT_d.rearrange("(kc p) t -> p kc t", p=128)
            fence([b_blk, b_ab1])
            abv = [blk[:, 0:11 * 512].rearrange("p (kc t) -> p kc t", kc=11),
                   blk[:, 11 * 512:22 * 512].rearrange("p (kc t) -> p kc t", kc=11)]
            b_ab = [b_blk, b_ab1]
            qcnt = 0
            for (t0, n) in TB:
                for cg in range(4):
                    banks = [nps() for _ in range(4)]
                    for qt in range(4):
                        ab = abv[qcnt % 2]; bab = b_ab[qcnt % 2]
                        P.dma("sp" if qcnt % 2 == 0 else "act", ab[:, :, 0:n], atv[:, qt * 11:(qt + 1) * 11, t0:t0 + n],
                              reads=[b_aT], writes=[bab])
                        qcnt += 1
                        for ci in range(4):
                            c = cg * 4 + ci
                            if ci % 2 == 0:
                                ws, bw = wslot()
                                wload(ws[:, 0:11, 0:256], wdn[:, qt * 11:(qt + 1) * 11, c * 128:(c + 2) * 128], bw)
                            ps, pb = banks[ci]
                            wc = (ci % 2) * 128
                            for k2 in range(11):
                                mm(ps[:, 0:n], ws[:, k2, wc:wc + 128], ab[:, k2, 0:n], qt == 0 and k2 == 0, qt == 3 and k2 == 10, [bw, bab], pb)
                    for ci in range(4):
                        ps, pb = banks[ci]
                        resid_update(cg * 4 + ci, t0, n, ps, pb, 80)
            fence([b_blk, b_ab1])

        nf = Wt["norm_final"].ap().rearrange("(r c) -> r c", c=128)
        memset(pst[:], 0.0, [b_pst])
        P.dma("act", pst[0:16, :], nf, writes=[b_pst])
        ps, pb = nps()
        tr(ps[:, 0:128], pst[:, :], 128, [b_pst], pb)
        cp(PT[:, 0:128], ps[:, 0:128], [pb], [b_PT])
        for (t0, n) in TB:
            xb, rstd = norm_stats(t0, n)
            for kc in range(16):
                stt(xb[:, kc, :], xb[:, kc, :], PT[:, kc:kc + 1], rstd, ALU.mult, ALU.mult, b_S[0:5] + [b_PT], b_S[0:4])
            for tt_i in range(n // 128):
                for g in range(4):
                    ps, pb = nps()
                    for j in range(4):
                        kc = g * 4 + j
                        tr(ps[:, j * 128:(j + 1) * 128], xb[:, kc, tt_i * 128:(tt_i + 1) * 128], 128, b_S[0:4], pb)
                    act(rowst[:, 0:512], ps[:, :], AF.Identity, [pb], [b_rowst])
                    r0 = t0 + tt_i * 128
                    if t0 < TP:
                        P.dma("sp", O["y_p"].ap()[r0:r0 + 128, g * 512:(g + 1) * 512], rowst[:, 0:512], reads=[b_rowst], writes=[b_out["y_p"]])
                    else:
                        P.dma("sp", O["y_s"].ap()[:, g * 512:(g + 1) * 512], rowst[:, 0:512], reads=[b_rowst], writes=[b_out["y_s"]])

        P.final_wait("sp", list(b_out.values()))
        P.emit()
    return nc


_NC_CACHE = {}


def kernel(x_prompt, x_sample, c_prompt, c_sample, state_lru_conv, state_lru_h, state_rwkv_shift, state_rwkv_S,
           state_ffn_conv, **weights):
    f = lambda a: np.ascontiguousarray(np.asarray(a, dtype=np.float32))
    if "nc" not in _NC_CACHE:
        _NC_CACHE["nc"] = build()
    nc = _NC_CACHE["nc"]
    wmap = {k: f(weights[k]) for k in W_SHAPES}
    in_maps = []
    for core in range(8):
        b = core % 4
        ss = slice(core * NS, (core + 1) * NS)
        m = dict(wmap)
        m["xp"] = f(x_prompt[b])
        m["xs"] = f(np.asarray(x_sample)[ss].reshape(TS, D))
        m["cc"] = f(np.concatenate([np.asarray(c_prompt)[b:b + 1], np.asarray(c_sample)[ss]], axis=0))
        m["st_lc"] = f(np.asarray(state_lru_conv)[:, ss])
        m["st_lh"] = f(np.asarray(state_lru_h)[:, ss])
        m["st_sh"] = f(np.asarray(state_rwkv_shift)[:, ss])
        m["st_S"] = f(np.asarray(state_rwkv_S)[:, ss])
        m["st_fc"] = f(np.asarray(state_ffn_conv)[:, ss])
        in_maps.append(m)
    res = run_bass_kernel_spmd(nc, in_maps, core_ids=list(range(8)))
    R = res.results
    y_prompt = np.stack([R[b]["y_p"] for b in range(4)], axis=0)
    y_sample = np.concatenate([R[c]["y_s"].reshape(NS, 8, D) for c in range(8)], axis=0)
    outs = [y_prompt, y_sample]
    names = ["o_lc", "o_lh", "o_sh", "o_S", "o_fc"]
    for nm in names:
        outs.append(np.stack([R[b][nm][:, 0] for b in range(4)], axis=1))
    for nm in names:
        outs.append(np.concatenate([R[c][nm][:, 1:] for c in range(8)], axis=1))
    return tuple(np.ascontiguousarray(o.astype(np.float32, copy=False)) for o in outs)
```

```python
import math
import numpy as np
from contextlib import ExitStack
import concourse.bass as bass
import concourse.mybir as mybir
from concourse.bass_utils import run_bass_kernel_spmd

F32 = mybir.dt.float32
BF16 = mybir.dt.bfloat16
ALU = mybir.AluOpType
AF = mybir.ActivationFunctionType
AX = mybir.AxisListType

ENGS = ("pe", "act", "dve", "pool", "sp")

D = 2048
TP = 2048
NS = 16
TS = 128
T = TP + TS
NSEQ = 17
DL = 1024
NRW = 3360
NIN = 9504
DFF = 5632
DEPTH = 4
TB = [(0, 512), (512, 512), (1024, 512), (1536, 512), (2048, 128)]
SLOT = 2240
NLAYERS = DEPTH


class Buf:
    def __init__(self, prog, name):
        self.prog = prog
        self.name = name
        self.w = {}
        self.r = {}
        self.dsem = None
        self.dcnt = 0

    def dma_sem(self):
        if self.dsem is None:
            self.dsem = self.prog.new_sem("d_" + self.name)
        return self.dsem


class Prog:
    def __init__(self, nc, stack):
        self.nc = nc
        self.stack = stack
        self.streams = {e: [] for e in ENGS}
        self.esem = {e: self.new_sem("e_" + e) for e in ENGS}
        self.ecnt = {e: 0 for e in ENGS}
        self.seen = {e: {} for e in ENGS}
        self.nbuf = 0
        self.pend = None

    def pe_group(self, fn, reads, pb):
        if self.pend is not None and self.pend[2] is pb:
            self.pend[0].append(fn)
            self.pend[1].extend(reads)
        else:
            self.flush_pe()
            self.pend = ([fn], list(reads), pb)

    def flush_pe(self):
        if self.pend is None:
            return
        fns, reads, pb = self.pend
        self.pend = None
        uniq = []
        for b in reads:
            if all(b is not u for u in uniq):
                uniq.append(b)

        def fn_all(e, fns=fns):
            last = None
            for f in fns:
                last = f(e)
            return last

        self._op("pe", fn_all, uniq, [pb])

    def new_sem(self, name):
        return self.stack.enter_context(self.nc.semaphore(name))

    def buf(self, name=None):
        self.nbuf += 1
        return Buf(self, name or f"b{self.nbuf}")

    def sbuf(self, name, shape, dtype):
        return self.stack.enter_context(self.nc.sbuf_tensor(name, list(shape), dtype))

    def psum(self, name, shape, dtype=F32):
        return self.stack.enter_context(self.nc.psum_tensor(name, list(shape), dtype))

    def _deps(self, eng, reads, writes):
        need = {}

        def add(tok):
            k = id(tok[0])
            if k not in need or need[k][1] < tok[1]:
                need[k] = tok

        for b in reads:
            for tok in b.w.values():
                add(tok)
        for b in writes:
            for tok in b.w.values():
                add(tok)
            for tok in b.r.values():
                add(tok)
        waits = []
        seen = self.seen[eng]
        for k, (sem, val) in need.items():
            if seen.get(k, 0) < val:
                seen[k] = val
                waits.append((sem, val))
        return waits

    def _mark(self, tok, reads, writes):
        k = id(tok[0])
        for b in reads:
            if b in writes:
                continue
            b.r[k] = tok
        for b in writes:
            b.w = {k: tok}
            b.r = {}

    def op(self, eng, fn, reads=(), writes=()):
        self.flush_pe()
        self._op(eng, fn, reads, writes)

    def _op(self, eng, fn, reads=(), writes=()):
        waits = self._deps(eng, reads, writes)
        self.ecnt[eng] += 1
        sem = self.esem[eng]
        self._mark((sem, self.ecnt[eng]), reads, writes)

        def run(e, waits=waits, fn=fn, sem=sem):
            for s, v in waits:
                e.wait_ge(s, v)
            fn(e).then_inc(sem, 1)

        self.streams[eng].append(run)

    def dma(self, q, out, in_, reads=(), writes=(), **kw):
        self.flush_pe()
        waits = self._deps(q, reads, writes)
        wb = writes[0]
        sem = wb.dma_sem()
        wb.dcnt += 16
        self._mark((sem, wb.dcnt), reads, writes)

        def run(e, waits=waits, sem=sem, out=out, in_=in_, kw=kw):
            for s, v in waits:
                e.wait_ge(s, v)
            e.dma_start(out=out, in_=in_, **kw).then_inc(sem, 16)

        self.streams[q].append(run)

    def final_wait(self, eng, bufs):
        self.flush_pe()
        waits = self._deps(eng, bufs, bufs)

        def run(e, waits=waits):
            for s, v in waits:
                e.wait_ge(s, v)

        self.streams[eng].append(run)

    def emit(self):
        self.flush_pe()
        nc = self.nc
        with nc.Block() as block:
            @block.tensor
            def _(e):
                for f in self.streams["pe"]:
                    f(e)

            @block.scalar
            def _(e):
                for f in self.streams["act"]:
                    f(e)

            @block.vector
            def _(e):
                for f in self.streams["dve"]:
                    f(e)

            @block.gpsimd
            def _(e):
                for f in self.streams["pool"]:
                    f(e)

            @block.sync
            def _(e):
                for f in self.streams["sp"]:
                    f(e)


W_SHAPES = {
    "w_ada": [DEPTH, D, 6 * D], "b_ada": [DEPTH, 6 * D], "norm_mix": [DEPTH, D], "norm_ffn": [DEPTH, D],
    "w_in": [DEPTH, D, NIN], "lru_conv_w": [DEPTH, 4, DL], "lru_conv_b": [DEPTH, DL],
    "lru_wa": [DEPTH, 16, 64, 64], "lru_ba": [DEPTH, DL], "lru_wi": [DEPTH, 16, 64, 64], "lru_bi": [DEPTH, DL],
    "lru_lambda": [DEPTH, DL], "w_pa": [DEPTH, DL, D], "rw_mu": [DEPTH, NRW], "rw_w0": [DEPTH, DL],
    "rw_w2": [DEPTH, 64, DL], "rw_a0": [DEPTH, DL], "rw_a2": [DEPTH, 64, DL], "rw_g2": [DEPTH, 160, DL],
    "rw_kk": [DEPTH, DL], "rw_ka": [DEPTH, DL], "rw_rk": [DEPTH, 16, 64], "rw_gn_g": [DEPTH, DL],
    "rw_gn_b": [DEPTH, DL], "w_pb": [DEPTH, DL, D], "w_o": [DEPTH, D, D], "w_up": [DEPTH, D, 2 * DFF],
    "ffn_conv_w": [DEPTH, 3, 2 * DFF], "ffn_conv_b": [DEPTH, 2 * DFF], "w_down": [DEPTH, DFF, D],
    "norm_final": [D],
}
IN_SHAPES = {
    "xp": [TP, D], "xs": [TS, D], "cc": [NSEQ, D], "st_lc": [DEPTH, NS, 3, DL], "st_lh": [DEPTH, NS, DL],
    "st_sh": [DEPTH, NS, NRW], "st_S": [DEPTH, NS, 16, 64, 64], "st_fc": [DEPTH, NS, 2, 2 * DFF],
}
OUT_SHAPES = {
    "y_p": [TP, D], "y_s": [TS, D], "o_lc": [DEPTH, NSEQ, 3, DL], "o_lh": [DEPTH, NSEQ, DL],
    "o_sh": [DEPTH, NSEQ, NRW], "o_S": [DEPTH, NSEQ, 16, 64, 64], "o_fc": [DEPTH, NSEQ, 2, 2 * DFF],
}

PROWS = [("b_ada", 96), ("norm_mix", 16), ("norm_ffn", 16), ("lru_conv_w", 32), ("lru_conv_b", 8),
         ("lru_ba", 8), ("lru_bi", 8), ("lru_lambda", 8), ("rw_mu", 27), ("rw_w0", 8), ("rw_a0", 8),
         ("ffn_conv_w", 264), ("ffn_conv_b", 88)]
PCOL = {}
_c = 0
for _n, _r in PROWS:
    PCOL[_n] = _c
    _c += _r
NPROW = _c
NPG = (NPROW + 127) // 128


def build():
    nc = bass.Bass("TRN2", target_bir_lowering=False)
    I = {k: nc.dram_tensor(k, s, F32, kind="ExternalInput") for k, s in IN_SHAPES.items()}
    Wt = {k: nc.dram_tensor(k, s, F32, kind="ExternalInput") for k, s in W_SHAPES.items()}
    O = {k: nc.dram_tensor(k, s, F32, kind="ExternalOutput") for k, s in OUT_SHAPES.items()}
    xT_h = nc.dram_tensor("xT_scr", [D, T], F32, kind="Internal")
    rq_h = nc.dram_tensor("rq_scr", [6, 16, T, 64], F32, kind="Internal")
    tk_h = nc.dram_tensor("tk_scr", [8, T, DL], F32, kind="Internal")
    cc_h = nc.dram_tensor("cc_scr", [32, T], F32, kind="Internal")
    sg_h = nc.dram_tensor("sg_scr", [2 * D, T], BF16, kind="Internal")
    ga_h = nc.dram_tensor("ga_scr", [DL, T], BF16, kind="Internal")
    oT_h = nc.dram_tensor("oT_scr", [DL, T], BF16, kind="Internal")
    aT_h = nc.dram_tensor("aT_scr", [DFF, T], BF16, kind="Internal")
    xT = xT_h.ap()
    tk = tk_h.ap()
    sg = sg_h.ap()
    ga_d = ga_h.ap()
    oT_d = oT_h.ap()
    aT_d = aT_h.ap()

    with ExitStack() as st:
        P = Prog(nc, st)
        actT = P.sbuf("actT", [128, 16, T], BF16); b_act = P.buf("actT")
        arena = P.sbuf("arena", [128, 5, SLOT], F32)
        b_S = [P.buf(f"S{i}") for i in range(5)]
        wsl = [P.sbuf(f"wsl{i}", [128, 16, 256], BF16) for i in range(2)]
        b_w = [P.buf(f"w{i}") for i in range(2)]
        blk = P.sbuf("blk", [128, 16384], BF16); b_blk = P.buf("blk")
        blkF = blk[:, :].bitcast(F32)
        ident = P.sbuf("ident", [128, 128], F32); b_id = P.buf("ident")
        ones = P.sbuf("ones", [128, 128], F32); b_ones = P.buf("ones")
        scT = P.sbuf("scT", [128, 16, NSEQ], BF16); b_scT = P.buf("scT")
        modT = P.sbuf("modT", [128, 96, NSEQ], F32); b_mod = P.buf("modT")
        A1 = P.sbuf("A1", [128, 16, NSEQ], F32); b_A1 = P.buf("A1")
        PT = P.sbuf("PT", [128, NPG * 128], F32); b_PT = P.buf("PT")
        pst = P.sbuf("pst", [128, 128], F32); b_pst = P.buf("pst")
        cA = P.sbuf("cA", [128, 16], F32); b_cA = P.buf("cA")
        bda = P.sbuf("bda", [128, 8, 128], BF16); b_bda = P.buf("bda")
        bdi = P.sbuf("bdi", [128, 8, 128], BF16); b_bdi = P.buf("bdi")
        lw2 = P.sbuf("lw2", [128, DL], BF16); b_lw2 = P.buf("lw2")
        g2a = P.sbuf("g2a", [128, DL], BF16); b_g2a = P.buf("g2a")
        g2b = P.sbuf("g2b", [32, DL], BF16); b_g2b = P.buf("g2b")
        stlc = P.sbuf("stlc", [128, 8, 48], F32); b_stlc = P.buf("stlc")
        stlh = P.sbuf("stlh", [128, 8, 16], F32); b_stlh = P.buf("stlh")
        stsh = P.sbuf("stsh", [128, 27, 16], F32); b_stsh = P.buf("stsh")
        olc = P.sbuf("olc", [128, 8, 51], F32); b_olc = P.buf("olc")
        olh = P.sbuf("olh", [128, 8, NSEQ], F32); b_olh = P.buf("olh")
        osh = P.sbuf("osh", [128, 27, NSEQ], F32); b_osh = P.buf("osh")
        ofc = P.sbuf("ofc", [128, 8, 34], F32); b_ofc = P.buf("ofc")
        stfc = P.sbuf("stfc", [128, 8, 32], F32); b_stfc = P.buf("stfc")
        tmpb = P.sbuf("tmpb", [128, 512], F32); b_tmpb = P.buf("tmpb")
        tmpc = P.sbuf("tmpc", [128, 512], F32); b_tmpc = P.buf("tmpc")
        rowst = blkF[:, 7168:8192]; b_rowst = P.buf("rowst")
        Sst = blkF[:, 0:2048]; b_Sst = b_blk
        Stmp = blkF[:, 2048:4096]; b_Stmp = b_blk
        skk = P.sbuf("skk", [128, 32], F32); b_skk = P.buf("skk")
        sgt = P.sbuf("sgt", [128, 2, 1024], BF16); b_sgt = [P.buf("sgt0"), P.buf("sgt1")]
        skys = P.sbuf("skys", [128, 2 * 2 * 16 * 8], F32); b_skys = [P.buf("skys0"), P.buf("skys1")]
        ccb = P.sbuf("ccb", [128, 2 * 32], F32); b_ccb = [P.buf("ccb0"), P.buf("ccb1")]
        dummy = P.sbuf("fence_dummy", [128, 8], F32)
        b_cc = P.buf("cc")
        vbuf = P.sbuf("vbuf", [128, 32 * 8], F32); b_vbuf = P.buf("vbuf")
        ybuf = P.sbuf("ybuf", [128, 32 * 8], F32); b_ybuf = P.buf("ybuf")
        sm = P.sbuf("sm", [128, 64], F32); b_sm = P.buf("sm")
        ct = [blkF[:, 4096 + i * 1024:4096 + (i + 1) * 1024] for i in range(3)]
        b_ct = [b_blk, b_blk, b_blk]
        PS = [P.psum(f"ps{i}", [128, 512]) for i in range(8)]
        b_ps = [P.buf(f"ps{i}") for i in range(8)]
        psc = [0]

        def nps():
            i = psc[0] % 8
            psc[0] += 1
            return PS[i], b_ps[i]

        b_xT = [P.buf(f"xT{c}") for c in range(16)]
        b_rq = P.buf("rq"); b_tk = [P.buf(f"tk{i}") for i in range(8)]
        b_sg = P.buf("sg"); b_ga = P.buf("ga"); b_oT = P.buf("oT"); b_aT = P.buf("aT")
        b_out = {k: P.buf("o_" + k) for k in OUT_SHAPES}

        def slot(i, n=SLOT, off=0):
            return arena[:, i, off:off + n]

        wcnt = [0]

        def wslot():
            i = wcnt[0] % 2
            wcnt[0] += 1
            return wsl[i], b_w[i]

        def act(out, in_, func, reads, writes, bias=None, scale=None):
            kw = {}
            if bias is not None:
                kw["bias"] = bias
            if scale is not None:
                kw["scale"] = scale
            P.op("act", lambda e: e.activation(out=out, in_=in_, func=func, **kw), reads=reads, writes=writes)

        def tt(out, in0, in1, op, reads, writes, eng="dve"):
            P.op(eng, lambda e: e.tensor_tensor(out=out, in0=in0, in1=in1, op=op), reads=reads, writes=writes)

        def ts(out, in0, s1, s2, op0, op1, reads, writes, eng="dve"):
            if s2 is None:
                P.op(eng, lambda e: e.tensor_scalar(out=out, in0=in0, scalar1=s1, scalar2=None, op0=op0), reads=reads, writes=writes)
            else:
                P.op(eng, lambda e: e.tensor_scalar(out=out, in0=in0, scalar1=s1, scalar2=s2, op0=op0, op1=op1), reads=reads, writes=writes)

        def stt(out, in0, scalar, in1, op0, op1, reads, writes):
            P.op("dve", lambda e: e.scalar_tensor_tensor(out=out, in0=in0, scalar=scalar, in1=in1, op0=op0, op1=op1),
                 reads=reads, writes=writes)

        def cp(out, in_, reads, writes, eng="dve"):
            P.op(eng, lambda e: e.tensor_copy(out=out, in_=in_), reads=reads, writes=writes)

        def mm(ps_ap, lhsT, rhs, start, stop, reads, pb):
            P.pe_group(lambda e: e.matmul(ps_ap, lhsT=lhsT, rhs=rhs, start=start, stop=stop), list(reads), pb)

        def tr(ps_ap, in_, n_in_part, reads, pb):
            P.pe_group(lambda e: e.transpose(ps_ap, in_, ident[0:n_in_part, 0:n_in_part]), list(reads) + [b_id], pb)

        def memset(ap, val, writes, eng="pool"):
            P.op(eng, lambda e: e.memset(ap, val), writes=writes)

        def fence(bufs, eng="pool"):
            P.op(eng, lambda e: e.memset(dummy[:, 0:1], 0.0), writes=list(bufs))

        memset(ident[:], 0.0, [b_id])
        P.op("pool", lambda e: e.affine_select(out=ident[:], in_=ident[:], compare_op=ALU.not_equal, fill=1.0,
                                               base=0, pattern=[[-1, 128]], channel_multiplier=1),
             reads=[b_id], writes=[b_id])
        memset(ones[:], 1.0, [b_ones])

        P.dma("sp", rowst[0:NSEQ, :], I["cc"].ap()[:, 0:1024], writes=[b_rowst])
        for half in range(2):
            if half == 1:
                P.dma("sp", rowst[0:NSEQ, :], I["cc"].ap()[:, 1024:2048], writes=[b_rowst])
            act(rowst[0:NSEQ, :], rowst[0:NSEQ, :], AF.Silu, [b_rowst], [b_rowst])
            ps, pb = nps()
            for j in range(8):
                tr(ps[:, j * NSEQ:(j + 1) * NSEQ], rowst[0:NSEQ, j * 128:(j + 1) * 128], NSEQ, [b_rowst], pb)
            cp(scT[:, half * 8:(half + 1) * 8, :].rearrange("p a b -> p (a b)"), ps[:, 0:8 * NSEQ], [pb], [b_scT])

        xT_v = xT.rearrange("(kc p) t -> p kc t", p=128)
        for i in range(17):
            src = I["xp"].ap()[i * 128:(i + 1) * 128, :] if i < 16 else I["xs"].ap()[:, :]
            P.dma("sp", slot(0, 2048), src, writes=[b_S[0]])
            for g in range(4):
                ps, pb = nps()
                for j in range(4):
                    kc = g * 4 + j
                    tr(ps[:, j * 128:(j + 1) * 128], slot(0, 128, kc * 128), 128, [b_S[0]], pb)
                P.op("act", lambda e, ps=ps, g=g: e.activation(out=slot(1, 512, g * 512), in_=ps[:, :], func=AF.Identity),
                     reads=[pb], writes=[b_S[1]])
            P.dma("sp", xT_v[:, :, i * 128:(i + 1) * 128], slot(1, 2048).rearrange("p (kc t) -> p kc t", kc=16),
                  reads=[b_S[1]], writes=b_xT)

        def load_params(l):
            for g in range(NPG):
                memset(pst[:], 0.0, [b_pst])
                r = 0
                for name, nrows in PROWS:
                    for rr in range(nrows):
                        grow = PCOL[name] + rr
                        if grow // 128 != g:
                            continue
                    lo = max(PCOL[name], g * 128)
                    hi = min(PCOL[name] + nrows, (g + 1) * 128)
                    if lo >= hi:
                        continue
                    r0 = lo - PCOL[name]
                    n = hi - lo
                    flat = Wt[name].ap()[l]
                    if name in ("lru_conv_w", "ffn_conv_w"):
                        flat = flat.rearrange("j c -> (j c)")
                    if name == "rw_mu":
                        nfull = min(n, max(0, 26 - r0))
                        if nfull > 0:
                            P.dma("act", pst[lo - g * 128:lo - g * 128 + nfull, :],
                                  flat[r0 * 128:(r0 + nfull) * 128].rearrange("(r c) -> r c", c=128), writes=[b_pst])
                        if r0 + n == 27:
                            P.dma("act", pst[hi - 1 - g * 128:hi - g * 128, 0:32],
                                  flat[26 * 128:26 * 128 + 32].rearrange("(r c) -> r c", c=32), writes=[b_pst])
                    else:
                        P.dma("act", pst[lo - g * 128:hi - g * 128, :],
                              flat[r0 * 128:(r0 + n) * 128].rearrange("(r c) -> r c", c=128), writes=[b_pst])
                ps, pb = nps()
                tr(ps[:, 0:128], pst[:, :], 128, [b_pst], pb)
                cp(PT[:, g * 128:(g + 1) * 128], ps[:, 0:128], [pb], [b_PT])

        def pcol(name, row):
            c = PCOL[name] + row
            return PT[:, c:c + 1]

        def loadT(dst3, bdst, src_rows_fn, nrows, nchunks, last_cols=128):
            c = 0
            while c < nchunks:
                g = min(8, nchunks - c)
                width = (g - 1) * 128 + (last_cols if c + g == nchunks else 128)
                P.dma("sp", rowst[0:nrows, 0:width], src_rows_fn(c * 128, width), writes=[b_rowst])
                per = 512 // nrows
                j = 0
                while j < g:
                    gg = min(per, g - j)
                    ps, pb = nps()
                    for k in range(gg):
                        cc_ = c + j + k
                        ncol = last_cols if cc_ == nchunks - 1 else 128
                        tr(ps[0:ncol, k * nrows:(k + 1) * nrows], rowst[0:nrows, (j + k) * 128:(j + k) * 128 + ncol], nrows, [b_rowst], pb)
                    cp(dst3[:, c + j:c + j + gg, :].rearrange("p a b -> p (a b)"), ps[:, 0:gg * nrows], [pb], [bdst])
                    j += gg
                c += g

        def flushT(src3, bsrc, nchunks, n, dst_fn, bout, last_cols=128):
            c = 0
            while c < nchunks:
                g = min(4, nchunks - c)
                ps, pb = nps()
                width = 0
                for k in range(g):
                    ncol = last_cols if c + k == nchunks - 1 else 128
                    tr(ps[0:n, k * 128:k * 128 + ncol], src3[0:ncol, c + k, :], ncol, [bsrc], pb)
                    width += ncol
                cp(rowst[0:n, 0:width], ps[0:n, 0:width], [pb], [b_rowst])
                P.dma("sp", dst_fn(c * 128, width), rowst[0:n, 0:width], reads=[b_rowst], writes=[bout])
                c += g

        def wload(dst, src, bw):
            P.dma("pool", dst, src, writes=[bw])

        class WStream:
            def __init__(self, loaders):
                self.loaders = loaders
                self.slots = {}
                self.nxt = 0

            def get(self, i):
                while self.nxt < len(self.loaders) and self.nxt <= i + 1:
                    ws, bw = wslot()
                    self.loaders[self.nxt](ws, bw)
                    self.slots[self.nxt] = (ws, bw)
                    self.nxt += 1
                return self.slots[i]

        def norm_stats(t0, n):
            xb = arena[:, 0:4, :].rearrange("p a b -> p (a b)")[:, 0:16 * n].rearrange("p (kc t) -> p kc t", kc=16)
            P.dma("sp", xb, xT_v[:, :, t0:t0 + n], reads=b_xT, writes=b_S[0:4])
            ps, pb = nps()
            for kc in range(16):
                sqb, bsq = (tmpb, b_tmpb) if kc % 2 == 0 else (tmpc, b_tmpc)
                act(sqb[:, 0:n], xb[:, kc, :], AF.Square, b_S[0:4], [bsq])
                mm(ps[:, 0:n], ones[:], sqb[:, 0:n], kc == 0, kc == 15, [b_ones, bsq], pb)
            rstd = slot(4, n)
            act(rstd, ps[:, 0:n], AF.Sqrt, [pb], [b_S[4]], bias=1e-6, scale=1.0 / D)
            P.op("dve", lambda e: e.reciprocal(out=rstd, in_=rstd), reads=[b_S[4]], writes=[b_S[4]])
            return xb, rstd

        def norm_mod(Aap, Bap, bA, bB):
            for (t0, n) in TB:
                xb, rstd = norm_stats(t0, n)
                for kc in range(16):
                    tt(xb[:, kc, :], xb[:, kc, :], rstd, ALU.mult, b_S[0:5], b_S[0:4])
                    if t0 < TP:
                        ts(actT[:, kc, t0:t0 + n], xb[:, kc, :], Aap[:, kc, 0:1], Bap[:, kc, 0:1], ALU.mult, ALU.add,
                           b_S[0:4] + [bA, bB], [b_act])
                    else:
                        x3 = xb[:, kc, :].rearrange("p (s t) -> p s t", s=NS)
                        tt(x3, x3, Aap[:, kc, 1:NSEQ].unsqueeze(2).to_broadcast([128, NS, 8]), ALU.mult, b_S[0:4] + [bA], b_S[0:4])
                        tt(actT[:, kc, t0:t0 + n].rearrange("p (s t) -> p s t", s=NS), x3,
                           Bap[:, kc, 1:NSEQ].unsqueeze(2).to_broadcast([128, NS, 8]), ALU.add, b_S[0:4] + [bB], [b_act])

        def resid_update(c, t0, n, ps, pb, gcol0):
            xs_ = tmpb[:, 0:n]
            P.dma("sp", xs_, xT_v[:, c, t0:t0 + n], reads=[b_xT[c]], writes=[b_tmpb])
            if t0 < TP:
                stt(xs_, ps[:, 0:n], modT[:, gcol0 + c, 0:1], xs_, ALU.mult, ALU.add, [pb, b_mod, b_tmpb], [b_tmpb])
            else:
                g3 = modT[:, gcol0 + c, 1:NSEQ].unsqueeze(2).to_broadcast([128, NS, 8])
                t3 = tmpc[:, 0:n].rearrange("p (s t) -> p s t", s=NS)
                tt(t3, ps[:, 0:n].rearrange("p (s t) -> p s t", s=NS), g3, ALU.mult, [pb, b_mod], [b_tmpc])
                tt(xs_, xs_, tmpc[:, 0:n], ALU.add, [b_tmpb, b_tmpc], [b_tmpb])
            P.dma("sp", xT_v[:, c, t0:t0 + n], xs_, reads=[b_tmpb], writes=[b_xT[c]])

        def to_tokmajor(src_slot_ap, bsrc, qi, c0):
            tkv = tk[qi].rearrange("(tt p) c -> p tt c", p=128)
            for g0 in range(0, 17, 4):
                g = min(4, 17 - g0)
                ps, pb = nps()
                for k in range(g):
                    tr(ps[:, k * 128:(k + 1) * 128], src_slot_ap[:, (g0 + k) * 128:(g0 + k + 1) * 128], 128, [bsrc], pb)
                act(rowst[:, 0:g * 128], ps[:, 0:g * 128], AF.Identity, [pb], [b_rowst])
                P.dma("sp", tkv[:, g0:g0 + g, c0:c0 + 128], rowst[:, 0:g * 128].rearrange("p (a b) -> p a b", a=g),
                      reads=[b_rowst], writes=[b_tk[qi]])

        bx = [[P.buf(f"x_{j}_{p}") for p in range(2)] for j in range(5)]
        bv = [P.buf(f"v_{p}") for p in range(2)]
        by = [P.buf(f"y_{p}") for p in range(2)]
        bSS = [P.buf("SA"), P.buf("SB")]
        bT2 = P.buf("T2"); bTt = P.buf("Tt")
        bKV = [P.buf(f"KV_{i}") for i in range(4)]
        b_ab1 = P.buf("ab1")
        for l in range(NLAYERS):
            load_params(l)
            memset(bda[:], 0.0, [b_bda]); memset(bdi[:], 0.0, [b_bdi])
            for (dst, bd_, nm) in ((bda, b_bda, "lru_wa"), (bdi, b_bdi, "lru_wi")):
                wv = Wt[nm].ap()[l].rearrange("(c two) d e -> two d c e", two=2)
                P.dma("pool", dst[0:64, :, 0:64], wv[0], writes=[bd_])
                P.dma("pool", dst[64:128, :, 64:128], wv[1], writes=[bd_])
            P.dma("pool", lw2[0:64, :], Wt["rw_w2"].ap()[l], writes=[b_lw2])
            P.dma("pool", lw2[64:128, :], Wt["rw_a2"].ap()[l], writes=[b_lw2])
            P.dma("pool", g2a[:, :], Wt["rw_g2"].ap()[l, 0:128, :], writes=[b_g2a])
            P.dma("pool", g2b[:, :], Wt["rw_g2"].ap()[l, 128:160, :], writes=[b_g2b])
            lam = PT[:, PCOL["lru_lambda"]:PCOL["lru_lambda"] + 8]
            act(cA[:, 0:8], lam, AF.Exp, [b_PT], [b_cA], scale=-1.0)
            act(cA[:, 0:8], cA[:, 0:8], AF.Ln, [b_cA], [b_cA], bias=1.0)
            ts(cA[:, 8:16], cA[:, 0:8], -16.0, None, ALU.mult, None, [b_cA], [b_cA])
            ts(cA[:, 0:8], cA[:, 0:8], -8.0, None, ALU.mult, None, [b_cA], [b_cA])
            loadT(stlc, b_stlc, lambda c0, w: I["st_lc"].ap()[l].rearrange("s j c -> (s j) c")[:, c0:c0 + w], 48, 8)
            loadT(stlh, b_stlh, lambda c0, w: I["st_lh"].ap()[l][:, c0:c0 + w], 16, 8)
            loadT(stsh, b_stsh, lambda c0, w: I["st_sh"].ap()[l][:, c0:c0 + w], 16, 27, last_cols=32)

            wada = Wt["w_ada"].ap()[l].rearrange("(kc p) c -> p kc c", p=128)
            st1 = WStream([(lambda ws, bw, b=b: wload(ws[:, :, 0:256], wada[:, :, b * 256:(b + 1) * 256], bw)) for b in range(48)])
            for blk_i in range(48):
                ws, bw = st1.get(blk_i)
                ps, pb = nps()
                for j2 in range(2):
                    for kc in range(16):
                        mm(ps[:, j2 * 32:j2 * 32 + NSEQ], ws[:, kc, j2 * 128:(j2 + 1) * 128], scT[:, kc, :], kc == 0, kc == 15,
                           [bw, b_scT], pb)
                for j2 in range(2):
                    j = blk_i * 2 + j2
                    act(modT[:, j, :], ps[:, j2 * 32:j2 * 32 + NSEQ], AF.Identity, [pb, b_PT], [b_mod], bias=pcol("b_ada", j))

            def make_A(sc0, nm_name):
                nmb = PT[:, PCOL[nm_name]:PCOL[nm_name] + 16].unsqueeze(2).to_broadcast([128, 16, NSEQ])
                stt(A1[:, :, :], modT[:, sc0:sc0 + 16, :], 1.0, nmb, ALU.add, ALU.mult, [b_mod, b_PT], [b_A1])

            make_A(16, "norm_mix")
            norm_mod(A1, modT[:, 0:16, :], b_A1, b_mod)

            win = Wt["w_in"].ap()[l].rearrange("(kc p) c -> p kc c", p=128)

            def proj_block(ws, wc0, m, t0, n, bw):
                ps, pb = nps()
                for kc in range(16):
                    mm(ps[0:m, 0:n], ws[:, kc, wc0:wc0 + m], actT[:, kc, t0:t0 + n], kc == 0, kc == 15, [bw, b_act], pb)
                return ps, pb

            def lru_loader(c):
                def f(ws, bw):
                    wload(ws[:, :, 0:128], win[:, :, c * 128:(c + 1) * 128], bw)
                    wload(ws[:, :, 128:256], win[:, :, DL + c * 128:DL + (c + 1) * 128], bw)
                return f
            st3a = WStream([lru_loader(c) for c in range(8)])
            for c in range(8):
                ws, bw = st3a.get(c)
                LX = slot(0); XC = slot(1); AA = slot(3); INP = slot(4)
                LXs = LX[:, 2051:2051 + 176].rearrange("p (s j) -> p s j", s=NS)
                memset(LX[:, 0:3], 0.0, [b_S[0]])
                cp(LXs[:, :, 0:3], stlc[:, c, :].rearrange("p (s j) -> p s j", s=NS), [b_stlc], [b_S[0]], eng="pool")
                for (t0, n) in TB:
                    ps, pb = proj_block(ws, 0, 128, t0, n, bw)
                    if t0 < TP:
                        act(LX[:, 3 + t0:3 + t0 + n], ps[:, 0:n], AF.Identity, [pb], [b_S[0]])
                    else:
                        act(LXs[:, :, 3:11], ps[:, 0:n].rearrange("p (s t) -> p s t", s=NS), AF.Identity, [pb], [b_S[0]])
                cp(olc[:, c, 0:3], LX[:, 2048:2051], [b_S[0]], [b_olc], eng="pool")
                cp(olc[:, c, 3:51].rearrange("p (s j) -> p s j", s=NS), LXs[:, :, 8:11], [b_S[0]], [b_olc], eng="pool")
                XCs = XC[:, TP:T].rearrange("p (s t) -> p s t", s=NS)
                ts(XC[:, 0:TP], LX[:, 0:TP], pcol("lru_conv_w", 0 * 8 + c), pcol("lru_conv_b", c), ALU.mult, ALU.add,
                   [b_S[0], b_PT], [b_S[1]])
                ts(XCs, LXs[:, :, 0:8], pcol("lru_conv_w", 0 * 8 + c), pcol("lru_conv_b", c), ALU.mult, ALU.add,
                   [b_S[0], b_PT], [b_S[1]])
                for j in range(1, 4):
                    stt(XC[:, 0:TP], LX[:, j:j + TP], pcol("lru_conv_w", j * 8 + c), XC[:, 0:TP], ALU.mult, ALU.add,
                        [b_S[0], b_S[1], b_PT], [b_S[1]])
                    stt(XCs, LXs[:, :, j:j + 8], pcol("lru_conv_w", j * 8 + c), XCs, ALU.mult, ALU.add,
                        [b_S[0], b_S[1], b_PT], [b_S[1]])
                XCb = slot(2).bitcast(BF16)[:, 0:T]
                act(XCb, XC[:, 0:T], AF.Identity, [b_S[1]], [b_S[2]])
                for (t0, n) in TB:
                    psr, pbr = nps()
                    mm(psr[:, 0:n], bda[:, c, :], XCb[:, t0:t0 + n], True, True, [b_bda, b_S[2]], pbr)
                    psi, pbi = nps()
                    mm(psi[:, 0:n], bdi[:, c, :], XCb[:, t0:t0 + n], True, True, [b_bdi, b_S[2]], pbi)
                    rr = INP[:, t0:t0 + n]
                    act(rr, psr[:, 0:n], AF.Sigmoid, [pbr, b_PT], [b_S[4]], bias=pcol("lru_ba", c))
                    act(AA[:, t0:t0 + n], rr, AF.Exp, [b_S[4], b_cA], [b_S[3]], scale=cA[:, c:c + 1])
                    act(rr, rr, AF.Exp, [b_S[4], b_cA], [b_S[4]], scale=cA[:, 8 + c:9 + c])
                    act(rr, rr, AF.Sqrt, [b_S[4]], [b_S[4]], scale=-1.0, bias=1.0)
                    act(tmpb[:, 0:n], psi[:, 0:n], AF.Sigmoid, [pbi, b_PT], [b_tmpb], bias=pcol("lru_bi", c))
                    tt(rr, rr, tmpb[:, 0:n], ALU.mult, [b_S[4], b_tmpb], [b_S[4]])
                    tt(rr, rr, XC[:, t0:t0 + n], ALU.mult, [b_S[4], b_S[1]], [b_S[4]])
                AAs = AA[:, TP:T].rearrange("p (s t) -> p s t", s=NS)
                INs = INP[:, TP:T].rearrange("p (s t) -> p s t", s=NS)
                tt(sm[:, 0:NS], AAs[:, :, 0], stlh[:, c, :], ALU.mult, [b_S[3], b_stlh], [b_sm])
                tt(INs[:, :, 0], INs[:, :, 0], sm[:, 0:NS], ALU.add, [b_S[4], b_sm], [b_S[4]])
                memset(AAs[:, :, 0], 0.0, [b_S[3]], eng="dve")
                HL = slot(1)
                P.op("dve", lambda e, HL=HL, AA=AA, INP=INP: e.tensor_tensor_scan(out=HL[:, 0:TP], data0=AA[:, 0:TP], data1=INP[:, 0:TP],
                                                                           initial=0.0, op0=ALU.mult, op1=ALU.add),
                     reads=[b_S[3], b_S[4]], writes=[b_S[1]])
                P.op("dve", lambda e, HL=HL, AA=AA, INP=INP: e.tensor_tensor_scan(out=HL[:, TP:T], data0=AA[:, TP:T], data1=INP[:, TP:T],
                                                                           initial=0.0, op0=ALU.mult, op1=ALU.add),
                     reads=[b_S[3], b_S[4]], writes=[b_S[1]])
                cp(olh[:, c, 0:1], HL[:, TP - 1:TP], [b_S[1]], [b_olh], eng="pool")
                cp(olh[:, c, 1:NSEQ], HL[:, TP:T].rearrange("p (s t) -> p s t", s=NS)[:, :, 7], [b_S[1]], [b_olh], eng="pool")
                GG = slot(0)
                GAb = slot(2).bitcast(BF16)[:, 0:T]
                for (t0, n) in TB:
                    ps, pb = proj_block(ws, 128, 128, t0, n, bw)
                    u = GG[:, t0:t0 + n]
                    act(u, ps[:, 0:n], AF.Identity, [pb], [b_S[0]])
                    tt(tmpb[:, 0:n], u, u, ALU.mult, [b_S[0]], [b_tmpb])
                    ts(tmpb[:, 0:n], tmpb[:, 0:n], 0.044715, 1.0, ALU.mult, ALU.add, [b_tmpb], [b_tmpb])
                    tt(tmpb[:, 0:n], tmpb[:, 0:n], u, ALU.mult, [b_tmpb, b_S[0]], [b_tmpb])
                    act(tmpb[:, 0:n], tmpb[:, 0:n], AF.Sigmoid, [b_tmpb], [b_tmpb], scale=1.5957691216057308)
                    tt(u, u, tmpb[:, 0:n], ALU.mult, [b_S[0], b_tmpb], [b_S[0]])
                    tt(GAb[:, t0:t0 + n], u, HL[:, t0:t0 + n], ALU.mult, [b_S[0], b_S[1]], [b_S[2]])
                P.dma("sp", ga_d[c * 128:(c + 1) * 128, :], GAb, reads=[b_S[2]], writes=[b_ga])
            flushT(olc, b_olc, 8, 51, lambda c0, w: O["o_lc"].ap()[l].rearrange("s j c -> (s j) c")[:, c0:c0 + w], b_out["o_lc"])
            flushT(olh, b_olh, 8, NSEQ, lambda c0, w: O["o_lh"].ap()[l][:, c0:c0 + w], b_out["o_lh"])

            TL = blk[:, :]
            TLa = TL[:, 0:T]
            SG1 = TL[:, T:2 * T]
            SG2 = TL[:, 2 * T:3 * T]
            b_TL = [b_blk]
            qorder = [24, 25, 26] + list(range(24))

            def rw_loader(q):
                m = 32 if q == 26 else 128
                return lambda ws, bw: wload(ws[:, :, 0:m], win[:, :, 2 * DL + q * 128:2 * DL + q * 128 + m], bw)
            st3b = WStream([rw_loader(q) for q in qorder])

            def rw_A(q, par):
                m = 32 if q == 26 else 128
                ws, bw = st3b.get(qorder.index(q))
                RW = slot(2 * par); bRW = b_S[2 * par]
                RWs = RW[:, 2049:2049 + 144].rearrange("p (s j) -> p s j", s=NS)
                memset(RW[0:m, 0:1], 0.0, [bRW])
                cp(RWs[0:m, :, 0], stsh[0:m, q, :], [b_stsh], [bRW], eng="pool")
                for (t0, n) in TB:
                    ps, pb = proj_block(ws, 0, m, t0, n, bw)
                    if t0 < TP:
                        act(RW[0:m, 1 + t0:1 + t0 + n], ps[0:m, 0:n], AF.Identity, [pb], [bRW])
                    else:
                        act(RWs[0:m, :, 1:9], ps[0:m, 0:n].rearrange("p (s t) -> p s t", s=NS), AF.Identity, [pb], [bRW])

            def rw_B(q, par):
                m = 32 if q == 26 else 128
                RW = slot(2 * par); bRW = b_S[2 * par]
                XS = slot(2 * par + 1); bXS = b_S[2 * par + 1]
                RWs = RW[:, 2049:2049 + 144].rearrange("p (s j) -> p s j", s=NS)
                cp(osh[0:m, q, 0:1], RW[0:m, TP:TP + 1], [bRW], [b_osh], eng="pool")
                cp(osh[0:m, q, 1:NSEQ], RWs[0:m, :, 8], [bRW], [b_osh], eng="pool")
                XSs = XS[:, TP:T].rearrange("p (s t) -> p s t", s=NS)
                mu = pcol("rw_mu", q)
                tt(XS[0:m, 0:TP], RW[0:m, 0:TP], RW[0:m, 1:TP + 1], ALU.subtract, [bRW], [bXS])
                tt(XSs[0:m], RWs[0:m, :, 0:8], RWs[0:m, :, 1:9], ALU.subtract, [bRW], [bXS])
                stt(XS[0:m, 0:TP], XS[0:m, 0:TP], mu[0:m], RW[0:m, 1:TP + 1], ALU.mult, ALU.add, [bRW, bXS, b_PT], [bXS])
                stt(XSs[0:m], XSs[0:m], mu[0:m], RWs[0:m, :, 1:9], ALU.mult, ALU.add, [bRW, bXS, b_PT], [bXS])
                if q == 24:
                    act(TLa[0:64, :], XS[0:64, 0:T], AF.Tanh, [bXS], b_TL)
                    act(TLa[64:128, :], XS[64:128, 0:T], AF.Identity, [bXS], b_TL)
                elif q == 25:
                    act(SG1, XS[:, 0:T], AF.Sigmoid, [bXS], b_TL)
                elif q == 26:
                    act(SG2[0:32, :], XS[0:32, 0:T], AF.Sigmoid, [bXS], b_TL)
                else:
                    to_tokmajor(XS, bXS, q // 8, (q % 8) * 128)

            rw_A(qorder[0], 0)
            for qi_, q in enumerate(qorder):
                if qi_ + 1 < len(qorder):
                    rw_A(qorder[qi_ + 1], (qi_ + 1) % 2)
                rw_B(q, qi_ % 2)
            flushT(osh, b_osh, 27, NSEQ, lambda c0, w: O["o_sh"].ap()[l][:, c0:c0 + w], b_out["o_sh"], last_cols=32)
            for c in range(8):
                DC = slot(0); AC = slot(1); GC = slot(2)
                for (t0, n) in TB:
                    ps, pb = nps()
                    mm(ps[:, 0:n], lw2[0:64, c * 128:(c + 1) * 128], TLa[0:64, t0:t0 + n], True, True, [b_lw2] + b_TL, pb)
                    act(DC[:, t0:t0 + n], ps[:, 0:n], AF.Sigmoid, [pb, b_PT], [b_S[0]], bias=pcol("rw_w0", c))
                    act(DC[:, t0:t0 + n], DC[:, t0:t0 + n], AF.Exp, [b_S[0]], [b_S[0]], scale=-math.exp(-0.5))
                    ps, pb = nps()
                    mm(ps[:, 0:n], lw2[64:128, c * 128:(c + 1) * 128], TLa[64:128, t0:t0 + n], True, True, [b_lw2] + b_TL, pb)
                    act(AC[:, t0:t0 + n], ps[:, 0:n], AF.Sigmoid, [pb, b_PT], [b_S[1]], bias=pcol("rw_a0", c))
                    ps, pb = nps()
                    mm(ps[:, 0:n], g2a[:, c * 128:(c + 1) * 128], SG1[:, t0:t0 + n], True, False, [b_g2a] + b_TL, pb)
                    mm(ps[:, 0:n], g2b[0:32, c * 128:(c + 1) * 128], SG2[0:32, t0:t0 + n], False, True, [b_g2b] + b_TL, pb)
                    act(GC[:, t0:t0 + n], ps[:, 0:n], AF.Identity, [pb], [b_S[2]])
                to_tokmajor(DC, b_S[0], 3, c * 128)
                to_tokmajor(AC, b_S[1], 4, c * 128)
                to_tokmajor(GC, b_S[2], 5, c * 128)
            st3c = WStream([(lambda ws, bw, g2=g2: wload(ws[:, :, 0:256], win[:, :, 2 * DL + NRW + g2 * 256:2 * DL + NRW + (g2 + 1) * 256], bw))
                            for g2 in range(16)])
            for gc in range(32):
                if gc % 2 == 0:
                    ws, bw = st3c.get(gc // 2)
                SGb = slot(gc % 2).bitcast(BF16)[:, 0:T]
                bsl = b_S[gc % 2]
                for (t0, n) in TB:
                    ps, pb = proj_block(ws, (gc % 2) * 128, 128, t0, n, bw)
                    act(SGb[:, t0:t0 + n], ps[:, 0:n], AF.Sigmoid, [pb], [bsl])
                P.dma("sp", sg[gc * 128:(gc + 1) * 128, :], SGb, reads=[bsl], writes=[b_sg])

            for i, nm in enumerate(("rw_kk", "rw_ka", "rw_rk")):
                src = Wt[nm].ap()[l]
                if nm == "rw_rk":
                    src = src.rearrange("h k -> (h k)")
                P.dma("act", ct[i][:, :], src.partition_broadcast(128), writes=[b_ct[i]])
            rqv = rq_h.ap()

            def half(i, h):
                return arena[:, i, h * 1024:(h + 1) * 1024]

            def h3(ap):
                return ap.rearrange("p (h k) -> p h k", h=16)

            for i in range(17):
                rows = slice(i * 128, (i + 1) * 128)
                Rt, Kt, Vt, Dt, At = half(0, 0), half(0, 1), half(1, 0), half(1, 1), half(2, 0)
                KKt, TMt, KMt, NBt, BVt = half(2, 1), half(3, 0), half(3, 1), half(4, 0), half(4, 1)
                for (dst, qi, bb) in ((Rt, 0, b_S[0]), (Kt, 1, b_S[0]), (Vt, 2, b_S[1]), (Dt, 3, b_S[1]), (At, 4, b_S[2])):
                    P.dma("sp", dst, tk[qi][rows, :], reads=[b_tk[qi]], writes=[bb])
                tt(KKt, Kt, ct[0][:, :], ALU.mult, [b_S[0], b_ct[0]], [b_S[2]])
                tt(TMt, KKt, KKt, ALU.mult, [b_S[2]], [b_S[3]])
                P.op("dve", lambda e, TMt=TMt: e.tensor_reduce(out=sm[:, 0:16], in_=h3(TMt), axis=AX.X, op=ALU.add),
                     reads=[b_S[3]], writes=[b_sm])
                act(sm[:, 0:16], sm[:, 0:16], AF.Sqrt, [b_sm], [b_sm])
                ts(sm[:, 0:16], sm[:, 0:16], 1e-12, None, ALU.max, None, [b_sm], [b_sm])
                P.op("dve", lambda e: e.reciprocal(out=sm[:, 0:16], in_=sm[:, 0:16]), reads=[b_sm], writes=[b_sm])
                tt(h3(KKt), h3(KKt), sm[:, 0:16].unsqueeze(2).to_broadcast([128, 16, 64]), ALU.mult, [b_S[2], b_sm], [b_S[2]])
                stt(TMt, At, -1.0, ct[1][:, :], ALU.add, ALU.mult, [b_S[2], b_ct[1]], [b_S[3]])
                stt(KMt, TMt, 1.0, Kt, ALU.add, ALU.mult, [b_S[3], b_S[0]], [b_S[3]])
                stt(NBt, KKt, -1.0, At, ALU.mult, ALU.mult, [b_S[2]], [b_S[4]])
                ccv = sm[:, 32:64].rearrange("p (h j) -> p h j", j=2)
                tt(TMt, Rt, KMt, ALU.mult, [b_S[0], b_S[3]], [b_S[3]])
                P.op("dve", lambda e, TMt=TMt, ccv=ccv: e.tensor_reduce(out=ccv[:, :, 1], in_=h3(TMt), axis=AX.X, op=ALU.add),
                     reads=[b_S[3]], writes=[b_sm])
                tt(TMt, TMt, ct[2][:, :], ALU.mult, [b_S[3], b_ct[2]], [b_S[3]])
                P.op("dve", lambda e, TMt=TMt: e.tensor_reduce(out=sm[:, 16:32], in_=h3(TMt), axis=AX.X, op=ALU.add),
                     reads=[b_S[3]], writes=[b_sm])
                tt(h3(BVt), h3(Vt), sm[:, 16:32].unsqueeze(2).to_broadcast([128, 16, 64]), ALU.mult, [b_S[1], b_sm], [b_S[4]])
                for (srct, qi, bb) in ((Rt, 0, b_S[0]), (Dt, 1, b_S[1]), (KMt, 2, b_S[3]), (KKt, 3, b_S[2]), (NBt, 4, b_S[4])):
                    P.dma("sp", rqv[qi].rearrange("h t k -> t h k")[rows], h3(srct), reads=[bb], writes=[b_rq])
                tt(Kt, NBt, Rt, ALU.mult, [b_S[4], b_S[0]], [b_S[0]])
                P.op("dve", lambda e, Kt=Kt, ccv=ccv: e.tensor_reduce(out=ccv[:, :, 0], in_=h3(Kt), axis=AX.X, op=ALU.add),
                     reads=[b_S[0]], writes=[b_sm])
                tt(Kt, Rt, Dt, ALU.mult, [b_S[0], b_S[1]], [b_S[0]])
                P.dma("sp", rqv[5].rearrange("h t k -> t h k")[rows], h3(Kt), reads=[b_S[0]], writes=[b_rq])
                ps, pb = nps()
                tr(ps[0:32, 0:128], sm[:, 32:64], 128, [b_sm], pb)
                cp(rowst[0:32, 0:128], ps[0:32, 0:128], [pb], [b_rowst])
                P.dma("sp", cc_h.ap()[:, rows], rowst[0:32, 0:128], reads=[b_rowst], writes=[b_cc])
                P.dma("sp", tk[6][rows, :], BVt, reads=[b_S[4]], writes=[b_tk[6]])

            def rec_steps(nsq, nsteps, xr_step, v_step, y_step):
                S3 = Sst[:, 0:nsq * 512].rearrange("p (s v k) -> p s v k", s=nsq, v=8)
                T3 = Stmp[:, 0:nsq * 512].rearrange("p (s v k) -> p s v k", s=nsq, v=8)
                K4 = [128, nsq, 8, 64]
                rdx = b_S[0:5]
                for t in range(nsteps):
                    r_, w_, k_, kk_, nb_ = [xr_step(j, t).unsqueeze(2).to_broadcast(K4) for j in range(5)]
                    sk = skk[:, 0:nsq * 8].rearrange("p (s v) -> p s v", s=nsq)
                    tt(T3, S3, kk_, ALU.mult, [b_Sst] + rdx, [b_Stmp])
                    P.op("dve", lambda e, sk=sk, T3=T3: e.tensor_reduce(out=sk, in_=T3, axis=AX.X, op=ALU.add), reads=[b_Stmp], writes=[b_skk])
                    tt(S3, S3, w_, ALU.mult, [b_Sst] + rdx, [b_Sst])
                    tt(T3, nb_, sk.unsqueeze(3).to_broadcast(K4), ALU.mult, [b_skk] + rdx, [b_Stmp])
                    tt(S3, S3, T3, ALU.add, [b_Sst, b_Stmp], [b_Sst])
                    tt(T3, k_, v_step(t).unsqueeze(3).to_broadcast(K4), ALU.mult, [b_vbuf] + rdx, [b_Stmp])
                    tt(S3, S3, T3, ALU.add, [b_Sst, b_Stmp], [b_Sst])
                    tt(T3, S3, r_, ALU.mult, [b_Sst] + rdx, [b_Stmp])
                    yo = y_step(t)
                    P.op("dve", lambda e, yo=yo, T3=T3: e.tensor_reduce(out=yo, in_=T3, axis=AX.X, op=ALU.add), reads=[b_Stmp], writes=[b_ybuf])

            oS = O["o_S"].ap()[l]
            TC = 16
            SLQ = [3, 5, 1, 4, 2]
            allb = b_S[0:5] + [b for bb_ in bx for b in bb_] + bv + by + bSS + [bT2, bTt] + bKV + [b_blk, b_vbuf, b_ybuf]
            fence(allb)
            SS = [blkF[:, 0:512], blkF[:, 512:1024]]
            T2v = blkF[:, 1024:2048].rearrange("p (q v k) -> p q v k", q=2, v=8)
            Ttv = blkF[:, 2048:2560].rearrange("p (v k) -> p v k", v=8)
            KVv = [blkF[:, 2560 + i * 512:2560 + (i + 1) * 512].rearrange("p (v k) -> p v k", v=8) for i in range(4)]
            memset(SS[0], 0.0, [bSS[0]], eng="dve")
            g = 0
            NCH = TP // TC

            def chunk_views(ci):
                par = ci % 2
                vb_ = vbuf[:, par * 128:(par + 1) * 128].rearrange("p (t v) -> p t v", t=TC)
                yb_ = ybuf[:, par * 128:(par + 1) * 128].rearrange("p (t v) -> p t v", t=TC)
                cb_ = ccb[:, par * 32:(par + 1) * 32].rearrange("p (j t) -> p j t", j=2)
                sk_ = skys[:, par * 256:(par + 1) * 256].rearrange("p (q t v) -> p q t v", q=2, t=TC)
                return par, vb_, yb_, cb_, sk_

            def chunk_loads(ci):
                t0 = ci * TC
                par, vb_, yb_, cb_, sk_ = chunk_views(ci)
                xo = par * 1024
                for j in range(5):
                    src = bass.AP(rq_h, SLQ[j] * 16 * T * 64 + t0 * 64, [[T * 64, 16], [0, 8], [1, TC * 64]])
                    P.dma("sp", slot(j, TC * 64, xo), src, reads=[b_rq], writes=[bx[j][par]])
                P.dma("act", vb_, bass.AP(tk_h, 2 * T * DL + t0 * DL, [[8, 128], [DL, TC], [1, 8]]), reads=[b_tk[2]], writes=[bv[par]])
                for jj in range(2):
                    P.dma("act", cb_[:, jj, :], bass.AP(cc_h, jj * T + t0, [[2 * T, 16], [0, 8], [1, TC]]), reads=[b_cc], writes=[b_ccb[par]])

            chunk_loads(0)
            for ci in range(NCH):
                t0 = ci * TC
                par, vb_, yb_, cb_, sk_ = chunk_views(ci)
                xo = par * 1024
                if ci + 1 < NCH:
                    chunk_loads(ci + 1)
                for t in range(TC):
                    cur, nxt = g % 2, (g + 1) % 2
                    Sc3 = SS[cur].rearrange("p (v k) -> p v k", v=8)
                    Sn3 = SS[nxt].rearrange("p (v k) -> p v k", v=8)
                    kv = KVv[g % 4]; bkv = bKV[g % 4]
                    o = xo + t * 64

                    def kv_fn(e, kv=kv, o=o, vb_=vb_, t=t):
                        last = None
                        for v8 in range(8):
                            last = e.activation(out=kv[:, v8, :], in_=slot(4, 64, o), func=AF.Identity, scale=vb_[:, t, v8:v8 + 1])
                        return last
                    P.op("act", kv_fn, reads=[bx[4][par], bv[par]], writes=[bkv])
                    tt(T2v, Sc3.unsqueeze(1).to_broadcast([128, 2, 8, 64]),
                       arena[:, 0:2, o:o + 64].unsqueeze(2).to_broadcast([128, 2, 8, 64]), ALU.mult,
                       [bSS[cur], bx[0][par], bx[1][par]], [bT2])
                    P.op("dve", lambda e, out=sk_[:, :, t, :], T2v=T2v: e.tensor_reduce(out=out, in_=T2v, axis=AX.X, op=ALU.add),
                         reads=[bT2], writes=[b_skys[par]])
                    tt(Sn3, Sc3, slot(2, 64, o).unsqueeze(1).to_broadcast([128, 8, 64]), ALU.mult, [bSS[cur], bx[2][par]], [bSS[nxt]], eng="pool")
                    tt(Sn3, Sn3, kv, ALU.add, [bSS[nxt], bkv], [bSS[nxt]], eng="pool")
                    tt(Ttv, sk_[:, 0, t, :].unsqueeze(2).to_broadcast([128, 8, 64]),
                       slot(3, 64, o).unsqueeze(1).to_broadcast([128, 8, 64]), ALU.mult, [b_skys[par], bx[3][par]], [bTt])
                    tt(Sn3, Sn3, Ttv, ALU.add, [bSS[nxt], bTt], [bSS[nxt]])
                    g += 1
                c1b = cb_[:, 0, :].unsqueeze(2).to_broadcast([128, TC, 8])
                c2b = cb_[:, 1, :].unsqueeze(2).to_broadcast([128, TC, 8])
                tt(yb_, sk_[:, 0, :, :], c1b, ALU.mult, [b_skys[par], b_ccb[par]], [by[par]], eng="pool")
                tt(yb_, yb_, sk_[:, 1, :, :], ALU.add, [by[par], b_skys[par]], [by[par]], eng="pool")
                tt(sk_[:, 0, :, :], vb_, c2b, ALU.mult, [bv[par], b_ccb[par]], [b_skys[par]], eng="pool")
                tt(yb_, yb_, sk_[:, 0, :, :], ALU.add, [by[par], b_skys[par]], [by[par]], eng="pool")
                P.dma("act", bass.AP(tk_h, 7 * T * DL + t0 * DL, [[8, 128], [DL, TC], [1, 8]]), yb_, reads=[by[par]], writes=[b_tk[7]])
            P.dma("sp", oS[0].rearrange("h (vb v8) k -> (h vb) (v8 k)", vb=8), SS[g % 2], reads=[bSS[g % 2]], writes=[b_out["o_S"]])
            fence(allb)
            for sb in range(4):
                s0 = sb * 4
                P.dma("sp", Sst[:, 0:2048].rearrange("p (s f) -> p s f", s=4),
                      bass.AP(I["st_S"], (l * NS + s0) * 65536, [[512, 128], [65536, 4], [1, 512]]), writes=[b_Sst])
                for j in range(5):
                    for s in range(4):
                        tok0 = TP + (s0 + s) * 8
                        src = bass.AP(rq_h, j * 16 * T * 64 + tok0 * 64, [[T * 64, 16], [0, 8], [1, 8 * 64]])
                        P.dma("sp" if s % 2 == 0 else "act", slot(j, 512, s * 512), src, reads=[b_rq], writes=[b_S[j]])
                P.dma("act", vbuf[:, 0:256].rearrange("p (t v) -> p t v", t=32),
                      bass.AP(tk_h, 2 * T * DL + (TP + s0 * 8) * DL, [[8, 128], [DL, 32], [1, 8]]), reads=[b_tk[2]], writes=[b_vbuf])
                rec_steps(4, 8,
                          lambda j, t: slot(j, 2048).rearrange("p (s t k) -> p s t k", s=4, t=8)[:, :, t, :],
                          lambda t: vbuf[:, 0:256].rearrange("p (s t v) -> p s t v", s=4, t=8)[:, :, t, :],
                          lambda t: ybuf[:, 0:256].rearrange("p (s t v) -> p s t v", s=4, t=8)[:, :, t, :])
                P.dma("act", bass.AP(tk_h, 7 * T * DL + (TP + s0 * 8) * DL, [[8, 128], [DL, 32], [1, 8]]),
                      ybuf[:, 0:256].rearrange("p (t v) -> p t v", t=32), reads=[b_ybuf], writes=[b_tk[7]])
                P.dma("sp", bass.AP(O["o_S"], (l * NSEQ + 1 + s0) * 65536, [[512, 128], [65536, 4], [1, 512]]),
                      Sst[:, 0:2048].rearrange("p (s f) -> p s f", s=4), reads=[b_Sst], writes=[b_out["o_S"]])

            for i, nm in enumerate(("rw_gn_g", "rw_gn_b")):
                P.dma("act", ct[i][:, :], Wt[nm].ap()[l].partition_broadcast(128), writes=[b_ct[i]])
            for i in range(17):
                rows = slice(i * 128, (i + 1) * 128)
                Yt, BVt, GGt, TMt = half(0, 0), half(0, 1), half(1, 0), half(1, 1)
                P.dma("sp", Yt, tk[7][rows, :], reads=[b_tk[7]], writes=[b_S[0]])
                P.dma("sp", BVt, tk[6][rows, :], reads=[b_tk[6]], writes=[b_S[0]])
                P.dma("sp", GGt, tk[5][rows, :], reads=[b_tk[5]], writes=[b_S[1]])
                P.op("dve", lambda e, Yt=Yt: e.tensor_reduce(out=sm[:, 0:16], in_=h3(Yt), axis=AX.X, op=ALU.add), reads=[b_S[0]], writes=[b_sm])
                ts(sm[:, 0:16], sm[:, 0:16], 1.0 / 64, None, ALU.mult, None, [b_sm], [b_sm])
                tt(h3(Yt), h3(Yt), sm[:, 0:16].unsqueeze(2).to_broadcast([128, 16, 64]), ALU.subtract, [b_S[0], b_sm], [b_S[0]])
                tt(TMt, Yt, Yt, ALU.mult, [b_S[0]], [b_S[1]])
                P.op("dve", lambda e, TMt=TMt: e.tensor_reduce(out=sm[:, 16:32], in_=h3(TMt), axis=AX.X, op=ALU.add), reads=[b_S[1]], writes=[b_sm])
                act(sm[:, 16:32], sm[:, 16:32], AF.Sqrt, [b_sm], [b_sm], scale=1.0 / 64, bias=64e-5)
                P.op("dve", lambda e: e.reciprocal(out=sm[:, 16:32], in_=sm[:, 16:32]), reads=[b_sm], writes=[b_sm])
                tt(h3(Yt), h3(Yt), sm[:, 16:32].unsqueeze(2).to_broadcast([128, 16, 64]), ALU.mult, [b_S[0], b_sm], [b_S[0]])
                tt(Yt, Yt, ct[0][:, :], ALU.mult, [b_S[0], b_ct[0]], [b_S[0]])
                tt(Yt, Yt, ct[1][:, :], ALU.add, [b_S[0], b_ct[1]], [b_S[0]])
                tt(Yt, Yt, BVt, ALU.add, [b_S[0]], [b_S[0]])
                tt(Yt, Yt, GGt, ALU.mult, [b_S[0], b_S[1]], [b_S[0]])
                OTb = slot(2).bitcast(BF16)[:, 0:1024].rearrange("p (c t) -> p c t", c=8)
                for g in range(2):
                    ps, pb = nps()
                    for j in range(4):
                        tr(ps[:, j * 128:(j + 1) * 128], Yt[:, (g * 4 + j) * 128:(g * 4 + j + 1) * 128], 128, [b_S[0]], pb)
                    act(OTb[:, g * 4:(g + 1) * 4, :].rearrange("p a b -> p (a b)"), ps[:, :], AF.Identity, [pb], [b_S[2]])
                P.dma("sp", oT_d.rearrange("(c p) t -> p c t", p=128)[:, :, rows], OTb, reads=[b_S[2]], writes=[b_oT])

            wpa = Wt["w_pa"].ap()[l].rearrange("(kc p) c -> p kc c", p=128)
            wpb = Wt["w_pb"].ap()[l].rearrange("(kc p) c -> p kc c", p=128)
            gav = ga_d.rearrange("(kc p) t -> p kc t", p=128)
            otv = oT_d.rearrange("(kc p) t -> p kc t", p=128)
            mcnt = 0

            def mg_loader(c):
                def f(ws, bw):
                    wload(ws[:, 0:8, 0:128], wpa[:, :, c * 128:(c + 1) * 128], bw)
                    wload(ws[:, 8:16, 0:128], wpb[:, :, c * 128:(c + 1) * 128], bw)
                return f
            st7 = WStream([mg_loader(c) for _tb in TB for c in range(16)])
            for (t0, n) in TB:
                gb = blk[:, 0:16 * 512].rearrange("p (kc t) -> p kc t", kc=16)
                P.dma("sp", gb[:, 0:8, 0:n], gav[:, :, t0:t0 + n], reads=[b_ga], writes=[b_blk])
                P.dma("act", gb[:, 8:16, 0:n], otv[:, :, t0:t0 + n], reads=[b_oT], writes=[b_blk])
                for c in range(16):
                    ws, bw = st7.get(mcnt)
                    sg_ = sgt[:, mcnt % 2, :]; bsg = b_sgt[mcnt % 2]
                    mcnt += 1
                    P.dma("sp", sg_[:, 0:n], sg[c * 128:(c + 1) * 128, t0:t0 + n], reads=[b_sg], writes=[bsg])
                    P.dma("sp", sg_[:, 512:512 + n], sg[D + c * 128:D + (c + 1) * 128, t0:t0 + n], reads=[b_sg], writes=[bsg])
                    psa, pba = nps()
                    for kc in range(8):
                        mm(psa[:, 0:n], ws[:, kc, 0:128], gb[:, kc, 0:n], kc == 0, kc == 7, [bw, b_blk], pba)
                    psb, pbb = nps()
                    for kc in range(8):
                        mm(psb[:, 0:n], ws[:, 8 + kc, 0:128], gb[:, 8 + kc, 0:n], kc == 0, kc == 7, [bw, b_blk], pbb)
                    tt(tmpb[:, 0:n], psa[:, 0:n], sg_[:, 0:n], ALU.mult, [pba, bsg], [b_tmpb])
                    tt(tmpc[:, 0:n], psb[:, 0:n], sg_[:, 512:512 + n], ALU.mult, [pbb, bsg], [b_tmpc])
                    tt(actT[:, c, t0:t0 + n], tmpb[:, 0:n], tmpc[:, 0:n], ALU.add, [b_tmpb, b_tmpc], [b_act])
            wo = Wt["w_o"].ap()[l].rearrange("(kc p) c -> p kc c", p=128)
            st8 = WStream([(lambda ws, bw, c2=c2: wload(ws[:, :, 0:256], wo[:, :, c2 * 256:(c2 + 1) * 256], bw)) for c2 in range(8)])
            for c in range(16):
                if c % 2 == 0:
                    ws, bw = st8.get(c // 2)
                for (t0, n) in TB:
                    ps, pb = proj_block(ws, (c % 2) * 128, 128, t0, n, bw)
                    resid_update(c, t0, n, ps, pb, 32)

            make_A(64, "norm_ffn")
            norm_mod(A1, modT[:, 48:64, :], b_A1, b_mod)

            wup = Wt["w_up"].ap()[l].rearrange("(kc p) c -> p kc c", p=128)
            def up_loader(j):
                def f(ws, bw):
                    wload(ws[:, :, 0:128], wup[:, :, j * 128:(j + 1) * 128], bw)
                    wload(ws[:, :, 128:256], wup[:, :, DFF + j * 128:DFF + (j + 1) * 128], bw)
                return f
            st10 = WStream([up_loader(j) for j in range(44)])
            for j in range(44):
                j4 = j % 4
                stv = I["st_fc"].ap()[l].rearrange("s j c -> (s j) c")
                ofv = O["o_fc"].ap()[l].rearrange("s j c -> (s j) c")
                if j4 == 0:
                    for hh in range(2):
                        loadT(stfc[:, hh * 4:hh * 4 + 4, :], b_stfc,
                              lambda c0, w, hh=hh, j=j: stv[:, (hh * 44 + j) * 128 + c0:(hh * 44 + j) * 128 + c0 + w], 32, 4)
                ws, bw = st10.get(j)
                outs = []
                for hh in range(2):
                    ch = hh * 44 + j
                    UU = slot(0 + hh * 2); UC = slot(1 + hh * 2)
                    bU = b_S[0 + hh * 2]; bC = b_S[1 + hh * 2]
                    UUs = UU[:, 2050:2050 + 160].rearrange("p (s j) -> p s j", s=NS)
                    memset(UU[:, 0:2], 0.0, [bU])
                    cp(UUs[:, :, 0:2], stfc[:, hh * 4 + j4, :].rearrange("p (s j) -> p s j", s=NS), [b_stfc], [bU], eng="pool")
                    for (t0, n) in TB:
                        ps, pb = proj_block(ws, hh * 128, 128, t0, n, bw)
                        if t0 < TP:
                            act(UU[:, 2 + t0:2 + t0 + n], ps[:, 0:n], AF.Identity, [pb], [bU])
                        else:
                            act(UUs[:, :, 2:10], ps[:, 0:n].rearrange("p (s t) -> p s t", s=NS), AF.Identity, [pb], [bU])
                    k8 = j % 8
                    if hh == 0 and k8 == 0:
                        pass
                    cp(ofc[:, hh * 4 + j4, 0:2], UU[:, 2048:2050], [bU], [b_ofc], eng="pool")
                    cp(ofc[:, hh * 4 + j4, 2:34].rearrange("p (s j) -> p s j", s=NS), UUs[:, :, 8:10], [bU], [b_ofc], eng="pool")
                    UCs = UC[:, TP:T].rearrange("p (s t) -> p s t", s=NS)
                    ts(UC[:, 0:TP], UU[:, 0:TP], pcol("ffn_conv_w", 0 * 88 + ch), pcol("ffn_conv_b", ch), ALU.mult, ALU.add, [bU, b_PT], [bC])
                    ts(UCs, UUs[:, :, 0:8], pcol("ffn_conv_w", 0 * 88 + ch), pcol("ffn_conv_b", ch), ALU.mult, ALU.add, [bU, b_PT], [bC])
                    for jj in range(1, 3):
                        stt(UC[:, 0:TP], UU[:, jj:jj + TP], pcol("ffn_conv_w", jj * 88 + ch), UC[:, 0:TP], ALU.mult, ALU.add, [bU, bC, b_PT], [bC])
                        stt(UCs, UUs[:, :, jj:jj + 8], pcol("ffn_conv_w", jj * 88 + ch), UCs, ALU.mult, ALU.add, [bU, bC, b_PT], [bC])
                    outs.append((UC, bC))
                (UG, bG), (UV, bV) = outs
                act(UG[:, 0:T], UG[:, 0:T], AF.Silu, [bG], [bG])
                ATb = slot(4).bitcast(BF16)[:, 0:T]
                tt(ATb, UG[:, 0:T], UV[:, 0:T], ALU.mult, [bG, bV], [b_S[4]])
                P.dma("sp", aT_d[j * 128:(j + 1) * 128, :], ATb, reads=[b_S[4]], writes=[b_aT])
                if j4 == 3:
                    for hh in range(2):
                        flushT(ofc[:, hh * 4:hh * 4 + 4, :], b_ofc, 4, 34,
                               lambda c0, w, hh=hh, j=j: ofv[:, (hh * 44 + j - 3) * 128 + c0:(hh * 44 + j - 3) * 128 + c0 + w], b_out["o_fc"])

            wdn = Wt["w_down"].ap()[l].rearrange("(kc p) c -> p kc c", p=128)
            atv = aT_d.rearrange("(kc p) t -> p kc t", p=128)
            fence([b_blk, b_ab1])
            abv = [blk[:, 0:11 * 512].rearrange("p (kc t) -> p kc t", kc=11),
                   blk[:, 11 * 512:22 * 512].rearrange("p (kc t) -> p kc t", kc=11)]
            b_ab = [b_blk, b_ab1]
            qcnt = 0
            dn_items = [(cg, qt, h2) for _tb in TB for cg in range(4) for qt in range(4) for h2 in range(2)]
            st11 = WStream([(lambda ws, bw, cg=cg, qt=qt, h2=h2:
                             wload(ws[:, 0:11, 0:256], wdn[:, qt * 11:(qt + 1) * 11, (cg * 4 + h2 * 2) * 128:(cg * 4 + h2 * 2 + 2) * 128], bw))
                            for (cg, qt, h2) in dn_items])
            dcnt = 0
            for (t0, n) in TB:
                for cg in range(4):
                    banks = [nps() for _ in range(4)]
                    for qt in range(4):
                        ab = abv[qcnt % 2]; bab = b_ab[qcnt % 2]
                        P.dma("sp" if qcnt % 2 == 0 else "act", ab[:, :, 0:n], atv[:, qt * 11:(qt + 1) * 11, t0:t0 + n],
                              reads=[b_aT], writes=[bab])
                        qcnt += 1
                        for ci in range(4):
                            c = cg * 4 + ci
                            if ci % 2 == 0:
                                ws, bw = st11.get(dcnt)
                                dcnt += 1
                            ps, pb = banks[ci]
                            wc = (ci % 2) * 128
                            for k2 in range(11):
                                mm(ps[:, 0:n], ws[:, k2, wc:wc + 128], ab[:, k2, 0:n], qt == 0 and k2 == 0, qt == 3 and k2 == 10, [bw, bab], pb)
                    for ci in range(4):
                        ps, pb = banks[ci]
                        resid_update(cg * 4 + ci, t0, n, ps, pb, 80)
            fence([b_blk, b_ab1])

        nf = Wt["norm_final"].ap().rearrange("(r c) -> r c", c=128)
        memset(pst[:], 0.0, [b_pst])
        P.dma("act", pst[0:16, :], nf, writes=[b_pst])
        ps, pb = nps()
        tr(ps[:, 0:128], pst[:, :], 128, [b_pst], pb)
        cp(PT[:, 0:128], ps[:, 0:128], [pb], [b_PT])
        for (t0, n) in TB:
            xb, rstd = norm_stats(t0, n)
            for kc in range(16):
                stt(xb[:, kc, :], xb[:, kc, :], PT[:, kc:kc + 1], rstd, ALU.mult, ALU.mult, b_S[0:5] + [b_PT], b_S[0:4])
            for tt_i in range(n // 128):
                for g in range(4):
                    ps, pb = nps()
                    for j in range(4):
                        kc = g * 4 + j
                        tr(ps[:, j * 128:(j + 1) * 128], xb[:, kc, tt_i * 128:(tt_i + 1) * 128], 128, b_S[0:4], pb)
                    act(rowst[:, 0:512], ps[:, :], AF.Identity, [pb], [b_rowst])
                    r0 = t0 + tt_i * 128
                    if t0 < TP:
                        P.dma("sp", O["y_p"].ap()[r0:r0 + 128, g * 512:(g + 1) * 512], rowst[:, 0:512], reads=[b_rowst], writes=[b_out["y_p"]])
                    else:
                        P.dma("sp", O["y_s"].ap()[:, g * 512:(g + 1) * 512], rowst[:, 0:512], reads=[b_rowst], writes=[b_out["y_s"]])

        P.final_wait("sp", list(b_out.values()))
        P.emit()
    return nc


_NC_CACHE = {}


def kernel(x_prompt, x_sample, c_prompt, c_sample, state_lru_conv, state_lru_h, state_rwkv_shift, state_rwkv_S,
           state_ffn_conv, **weights):
    f = lambda a: np.ascontiguousarray(np.asarray(a, dtype=np.float32))
    if "nc" not in _NC_CACHE:
        _NC_CACHE["nc"] = build()
    nc = _NC_CACHE["nc"]
    wmap = {k: f(weights[k]) for k in W_SHAPES}
    in_maps = []
    for core in range(8):
        b = core % 4
        ss = slice(core * NS, (core + 1) * NS)
        m = dict(wmap)
        m["xp"] = f(x_prompt[b])
        m["xs"] = f(np.asarray(x_sample)[ss].reshape(TS, D))
        m["cc"] = f(np.concatenate([np.asarray(c_prompt)[b:b + 1], np.asarray(c_sample)[ss]], axis=0))
        m["st_lc"] = f(np.asarray(state_lru_conv)[:, ss])
        m["st_lh"] = f(np.asarray(state_lru_h)[:, ss])
        m["st_sh"] = f(np.asarray(state_rwkv_shift)[:, ss])
        m["st_S"] = f(np.asarray(state_rwkv_S)[:, ss])
        m["st_fc"] = f(np.asarray(state_ffn_conv)[:, ss])
        in_maps.append(m)
    res = run_bass_kernel_spmd(nc, in_maps, core_ids=list(range(8)))
    R = res.results
    y_prompt = np.stack([R[b]["y_p"] for b in range(4)], axis=0)
    y_sample = np.concatenate([R[c]["y_s"].reshape(NS, 8, D) for c in range(8)], axis=0)
    outs = [y_prompt, y_sample]
    names = ["o_lc", "o_lh", "o_sh", "o_S", "o_fc"]
    for nm in names:
        outs.append(np.stack([R[b][nm][:, 0] for b in range(4)], axis=1))
    for nm in names:
        outs.append(np.concatenate([R[c][nm][:, 1:] for c in range(8)], axis=1))
    return tuple(np.ascontiguousarray(o.astype(np.float32, copy=False)) for o in outs)
```

```python
import math
import numpy as np
from contextlib import ExitStack
import concourse.bass as bass
import concourse.mybir as mybir
from concourse.bass_utils import run_bass_kernel_spmd

F32 = mybir.dt.float32
BF16 = mybir.dt.bfloat16
ALU = mybir.AluOpType
AF = mybir.ActivationFunctionType
AX = mybir.AxisListType

ENGS = ("pe", "act", "dve", "pool", "sp")

D = 2048
TP = 2048
NS = 16
TS = 128
T = TP + TS
NSEQ = 17
DL = 1024
NRW = 3360
NIN = 9504
DFF = 5632
DEPTH = 4
TB = [(0, 512), (512, 512), (1024, 512), (1536, 512), (2048, 128)]
SLOT = 2240
NLAYERS = DEPTH


class Buf:
    def __init__(self, prog, name):
        self.prog = prog
        self.name = name
        self.w = {}
        self.r = {}
        self.dsem = None
        self.dcnt = 0

    def dma_sem(self):
        if self.dsem is None:
            self.dsem = self.prog.new_sem("d_" + self.name)
        return self.dsem


class Prog:
    def __init__(self, nc, stack):
        self.nc = nc
        self.stack = stack
        self.streams = {e: [] for e in ENGS}
        self.esem = {e: self.new_sem("e_" + e) for e in ENGS}
        self.ecnt = {e: 0 for e in ENGS}
        self.seen = {e: {} for e in ENGS}
        self.nbuf = 0
        self.pend = None

    def pe_group(self, fn, reads, pb):
        if self.pend is not None and self.pend[2] is pb:
            self.pend[0].append(fn)
            self.pend[1].extend(reads)
        else:
            self.flush_pe()
            self.pend = ([fn], list(reads), pb)

    def flush_pe(self):
        if self.pend is None:
            return
        fns, reads, pb = self.pend
        self.pend = None
        uniq = []
        for b in reads:
            if all(b is not u for u in uniq):
                uniq.append(b)

        def fn_all(e, fns=fns):
            last = None
            for f in fns:
                last = f(e)
            return last

        self._op("pe", fn_all, uniq, [pb])

    def new_sem(self, name):
        return self.stack.enter_context(self.nc.semaphore(name))

    def buf(self, name=None):
        self.nbuf += 1
        return Buf(self, name or f"b{self.nbuf}")

    def sbuf(self, name, shape, dtype):
        return self.stack.enter_context(self.nc.sbuf_tensor(name, list(shape), dtype))

    def psum(self, name, shape, dtype=F32):
        return self.stack.enter_context(self.nc.psum_tensor(name, list(shape), dtype))

    def _deps(self, eng, reads, writes):
        need = {}

        def add(tok):
            k = id(tok[0])
            if k not in need or need[k][1] < tok[1]:
                need[k] = tok

        for b in reads:
            for tok in b.w.values():
                add(tok)
        for b in writes:
            for tok in b.w.values():
                add(tok)
            for tok in b.r.values():
                add(tok)
        waits = []
        seen = self.seen[eng]
        for k, (sem, val) in need.items():
            if seen.get(k, 0) < val:
                seen[k] = val
                waits.append((sem, val))
        return waits

    def _mark(self, tok, reads, writes):
        k = id(tok[0])
        for b in reads:
            if b in writes:
                continue
            b.r[k] = tok
        for b in writes:
            b.w = {k: tok}
            b.r = {}

    def op(self, eng, fn, reads=(), writes=()):
        self.flush_pe()
        self._op(eng, fn, reads, writes)

    def _op(self, eng, fn, reads=(), writes=()):
        waits = self._deps(eng, reads, writes)
        self.ecnt[eng] += 1
        sem = self.esem[eng]
        self._mark((sem, self.ecnt[eng]), reads, writes)

        def run(e, waits=waits, fn=fn, sem=sem):
            for s, v in waits:
                e.wait_ge(s, v)
            fn(e).then_inc(sem, 1)

        self.streams[eng].append(run)

    def dma(self, q, out, in_, reads=(), writes=(), **kw):
        self.flush_pe()
        waits = self._deps(q, reads, writes)
        wb = writes[0]
        sem = wb.dma_sem()
        wb.dcnt += 16
        self._mark((sem, wb.dcnt), reads, writes)

        def run(e, waits=waits, sem=sem, out=out, in_=in_, kw=kw):
            for s, v in waits:
                e.wait_ge(s, v)
            e.dma_start(out=out, in_=in_, **kw).then_inc(sem, 16)

        self.streams[q].append(run)

    def final_wait(self, eng, bufs):
        self.flush_pe()
        waits = self._deps(eng, bufs, bufs)

        def run(e, waits=waits):
            for s, v in waits:
                e.wait_ge(s, v)

        self.streams[eng].append(run)

    def emit(self):
        self.flush_pe()
        nc = self.nc
        with nc.Block() as block:
            @block.tensor
            def _(e):
                for f in self.streams["pe"]:
                    f(e)

            @block.scalar
            def _(e):
                for f in self.streams["act"]:
                    f(e)

            @block.vector
            def _(e):
                for f in self.streams["dve"]:
                    f(e)

            @block.gpsimd
            def _(e):
                for f in self.streams["pool"]:
                    f(e)

            @block.sync
            def _(e):
                for f in self.streams["sp"]:
                    f(e)


W_SHAPES = {
    "w_ada": [DEPTH, D, 6 * D], "b_ada": [DEPTH, 6 * D], "norm_mix": [DEPTH, D], "norm_ffn": [DEPTH, D],
    "w_in": [DEPTH, D, NIN], "lru_conv_w": [DEPTH, 4, DL], "lru_conv_b": [DEPTH, DL],
    "lru_wa": [DEPTH, 16, 64, 64], "lru_ba": [DEPTH, DL], "lru_wi": [DEPTH, 16, 64, 64], "lru_bi": [DEPTH, DL],
    "lru_lambda": [DEPTH, DL], "w_pa": [DEPTH, DL, D], "rw_mu": [DEPTH, NRW], "rw_w0": [DEPTH, DL],
    "rw_w2": [DEPTH, 64, DL], "rw_a0": [DEPTH, DL], "rw_a2": [DEPTH, 64, DL], "rw_g2": [DEPTH, 160, DL],
    "rw_kk": [DEPTH, DL], "rw_ka": [DEPTH, DL], "rw_rk": [DEPTH, 16, 64], "rw_gn_g": [DEPTH, DL],
    "rw_gn_b": [DEPTH, DL], "w_pb": [DEPTH, DL, D], "w_o": [DEPTH, D, D], "w_up": [DEPTH, D, 2 * DFF],
    "ffn_conv_w": [DEPTH, 3, 2 * DFF], "ffn_conv_b": [DEPTH, 2 * DFF], "w_down": [DEPTH, DFF, D],
    "norm_final": [D],
}
IN_SHAPES = {
    "xp": [TP, D], "xs": [TS, D], "cc": [NSEQ, D], "st_lc": [DEPTH, NS, 3, DL], "st_lh": [DEPTH, NS, DL],
    "st_sh": [DEPTH, NS, NRW], "st_S": [DEPTH, NS, 16, 64, 64], "st_fc": [DEPTH, NS, 2, 2 * DFF],
}
OUT_SHAPES = {
    "y_p": [TP, D], "y_s": [TS, D], "o_lc": [DEPTH, NSEQ, 3, DL], "o_lh": [DEPTH, NSEQ, DL],
    "o_sh": [DEPTH, NSEQ, NRW], "o_S": [DEPTH, NSEQ, 16, 64, 64], "o_fc": [DEPTH, NSEQ, 2, 2 * DFF],
}

PROWS = [("b_ada", 96), ("norm_mix", 16), ("norm_ffn", 16), ("lru_conv_w", 32), ("lru_conv_b", 8),
         ("lru_ba", 8), ("lru_bi", 8), ("lru_lambda", 8), ("rw_mu", 27), ("rw_w0", 8), ("rw_a0", 8),
         ("ffn_conv_w", 264), ("ffn_conv_b", 88)]
PCOL = {}
_c = 0
for _n, _r in PROWS:
    PCOL[_n] = _c
    _c += _r
NPROW = _c
NPG = (NPROW + 127) // 128


def build():
    nc = bass.Bass("TRN2", target_bir_lowering=False)
    I = {k: nc.dram_tensor(k, s, F32, kind="ExternalInput") for k, s in IN_SHAPES.items()}
    Wt = {k: nc.dram_tensor(k, s, F32, kind="ExternalInput") for k, s in W_SHAPES.items()}
    O = {k: nc.dram_tensor(k, s, F32, kind="ExternalOutput") for k, s in OUT_SHAPES.items()}
    xT_h = nc.dram_tensor("xT_scr", [D, T], F32, kind="Internal")
    rq_h = nc.dram_tensor("rq_scr", [6, 16, T, 64], F32, kind="Internal")
    tk_h = nc.dram_tensor("tk_scr", [8, T, DL], F32, kind="Internal")
    cc_h = nc.dram_tensor("cc_scr", [32, T], F32, kind="Internal")
    sg_h = nc.dram_tensor("sg_scr", [2 * D, T], BF16, kind="Internal")
    ga_h = nc.dram_tensor("ga_scr", [DL, T], BF16, kind="Internal")
    oT_h = nc.dram_tensor("oT_scr", [DL, T], BF16, kind="Internal")
    aT_h = nc.dram_tensor("aT_scr", [DFF, T], BF16, kind="Internal")
    xT = xT_h.ap()
    tk = tk_h.ap()
    sg = sg_h.ap()
    ga_d = ga_h.ap()
    oT_d = oT_h.ap()
    aT_d = aT_h.ap()

    with ExitStack() as st:
        P = Prog(nc, st)
        actT = P.sbuf("actT", [128, 16, T], BF16); b_act = P.buf("actT")
        arena = P.sbuf("arena", [128, 5, SLOT], F32)
        b_S = [P.buf(f"S{i}") for i in range(5)]
        wsl = [P.sbuf(f"wsl{i}", [128, 16, 256], BF16) for i in range(2)]
        b_w = [P.buf(f"w{i}") for i in range(2)]
        blk = P.sbuf("blk", [128, 16384], BF16); b_blk = P.buf("blk")
        blkF = blk[:, :].bitcast(F32)
        ident = P.sbuf("ident", [128, 128], F32); b_id = P.buf("ident")
        ones = P.sbuf("ones", [128, 128], F32); b_ones = P.buf("ones")
        scT = P.sbuf("scT", [128, 16, NSEQ], BF16); b_scT = P.buf("scT")
        modT = P.sbuf("modT", [128, 96, NSEQ], F32); b_mod = P.buf("modT")
        A1 = P.sbuf("A1", [128, 16, NSEQ], F32); b_A1 = P.buf("A1")
        PT = P.sbuf("PT", [128, NPG * 128], F32); b_PT = P.buf("PT")
        pst = P.sbuf("pst", [128, 128], F32); b_pst = P.buf("pst")
        cA = P.sbuf("cA", [128, 16], F32); b_cA = P.buf("cA")
        bda = P.sbuf("bda", [128, 8, 128], BF16); b_bda = P.buf("bda")
        bdi = P.sbuf("bdi", [128, 8, 128], BF16); b_bdi = P.buf("bdi")
        lw2 = P.sbuf("lw2", [128, DL], BF16); b_lw2 = P.buf("lw2")
        g2a = P.sbuf("g2a", [128, DL], BF16); b_g2a = P.buf("g2a")
        g2b = P.sbuf("g2b", [32, DL], BF16); b_g2b = P.buf("g2b")
        stlc = P.sbuf("stlc", [128, 8, 48], F32); b_stlc = P.buf("stlc")
        stlh = P.sbuf("stlh", [128, 8, 16], F32); b_stlh = P.buf("stlh")
        stsh = P.sbuf("stsh", [128, 27, 16], F32); b_stsh = P.buf("stsh")
        olc = P.sbuf("olc", [128, 8, 51], F32); b_olc = P.buf("olc")
        olh = P.sbuf("olh", [128, 8, NSEQ], F32); b_olh = P.buf("olh")
        osh = P.sbuf("osh", [128, 27, NSEQ], F32); b_osh = P.buf("osh")
        ofc = P.sbuf("ofc", [128, 8, 34], F32); b_ofc = P.buf("ofc")
        stfc = P.sbuf("stfc", [128, 8, 32], F32); b_stfc = P.buf("stfc")
        tmpb = P.sbuf("tmpb", [128, 512], F32); b_tmpb = P.buf("tmpb")
        tmpc = P.sbuf("tmpc", [128, 512], F32); b_tmpc = P.buf("tmpc")
        rowst = blkF[:, 7168:8192]; b_rowst = P.buf("rowst")
        b_rowst2 = P.buf("rowst2")
        Sst = blkF[:, 0:2048]; b_Sst = b_blk
        Stmp = blkF[:, 2048:4096]; b_Stmp = b_blk
        skk = P.sbuf("skk", [128, 32], F32); b_skk = P.buf("skk")
        sgt = P.sbuf("sgt", [128, 2, 1024], BF16); b_sgt = [P.buf("sgt0"), P.buf("sgt1")]
        skys = P.sbuf("skys", [128, 2 * 2 * 16 * 8], F32); b_skys = [P.buf("skys0"), P.buf("skys1")]
        ccb = P.sbuf("ccb", [128, 2 * 32], F32); b_ccb = [P.buf("ccb0"), P.buf("ccb1")]
        dummy = P.sbuf("fence_dummy", [128, 8], F32)
        b_cc = P.buf("cc")
        vbuf = P.sbuf("vbuf", [128, 32 * 8], F32); b_vbuf = P.buf("vbuf")
        ybuf = P.sbuf("ybuf", [128, 32 * 8], F32); b_ybuf = P.buf("ybuf")
        sm = P.sbuf("sm", [128, 64], F32); b_sm = P.buf("sm")
        ct = [blkF[:, 4096 + i * 1024:4096 + (i + 1) * 1024] for i in range(3)]
        b_ct = [b_blk, b_blk, b_blk]
        PS = [P.psum(f"ps{i}", [128, 512]) for i in range(8)]
        b_ps = [P.buf(f"ps{i}") for i in range(8)]
        psc = [0]

        def nps():
            i = psc[0] % 8
            psc[0] += 1
            return PS[i], b_ps[i]

        b_xT = [P.buf(f"xT{c}") for c in range(16)]
        b_rq = P.buf("rq"); b_tk = [P.buf(f"tk{i}") for i in range(8)]
        b_sg = P.buf("sg"); b_ga = P.buf("ga"); b_oT = P.buf("oT"); b_aT = P.buf("aT")
        b_out = {k: P.buf("o_" + k) for k in OUT_SHAPES}

        def slot(i, n=SLOT, off=0):
            return arena[:, i, off:off + n]

        wcnt = [0]

        def wslot():
            i = wcnt[0] % 2
            wcnt[0] += 1
            return wsl[i], b_w[i]

        def act(out, in_, func, reads, writes, bias=None, scale=None):
            kw = {}
            if bias is not None:
                kw["bias"] = bias
            if scale is not None:
                kw["scale"] = scale
            P.op("act", lambda e: e.activation(out=out, in_=in_, func=func, **kw), reads=reads, writes=writes)

        def tt(out, in0, in1, op, reads, writes, eng="dve"):
            P.op(eng, lambda e: e.tensor_tensor(out=out, in0=in0, in1=in1, op=op), reads=reads, writes=writes)

        def ts(out, in0, s1, s2, op0, op1, reads, writes, eng="dve"):
            if s2 is None:
                P.op(eng, lambda e: e.tensor_scalar(out=out, in0=in0, scalar1=s1, scalar2=None, op0=op0), reads=reads, writes=writes)
            else:
                P.op(eng, lambda e: e.tensor_scalar(out=out, in0=in0, scalar1=s1, scalar2=s2, op0=op0, op1=op1), reads=reads, writes=writes)

        def stt(out, in0, scalar, in1, op0, op1, reads, writes):
            P.op("dve", lambda e: e.scalar_tensor_tensor(out=out, in0=in0, scalar=scalar, in1=in1, op0=op0, op1=op1),
                 reads=reads, writes=writes)

        def cp(out, in_, reads, writes, eng="dve"):
            P.op(eng, lambda e: e.tensor_copy(out=out, in_=in_), reads=reads, writes=writes)

        def mm(ps_ap, lhsT, rhs, start, stop, reads, pb):
            P.pe_group(lambda e: e.matmul(ps_ap, lhsT=lhsT, rhs=rhs, start=start, stop=stop), list(reads), pb)

        def tr(ps_ap, in_, n_in_part, reads, pb):
            P.pe_group(lambda e: e.transpose(ps_ap, in_, ident[0:n_in_part, 0:n_in_part]), list(reads) + [b_id], pb)

        def memset(ap, val, writes, eng="pool"):
            P.op(eng, lambda e: e.memset(ap, val), writes=writes)

        def fence(bufs, eng="pool"):
            P.op(eng, lambda e: e.memset(dummy[:, 0:1], 0.0), writes=list(bufs))

        memset(ident[:], 0.0, [b_id])
        P.op("pool", lambda e: e.affine_select(out=ident[:], in_=ident[:], compare_op=ALU.not_equal, fill=1.0,
                                               base=0, pattern=[[-1, 128]], channel_multiplier=1),
             reads=[b_id], writes=[b_id])
        memset(ones[:], 1.0, [b_ones])

        P.dma("sp", rowst[0:NSEQ, :], I["cc"].ap()[:, 0:1024], writes=[b_rowst, b_rowst2])
        for half in range(2):
            if half == 1:
                P.dma("sp", rowst[0:NSEQ, :], I["cc"].ap()[:, 1024:2048], writes=[b_rowst, b_rowst2])
            act(rowst[0:NSEQ, :], rowst[0:NSEQ, :], AF.Silu, [b_rowst, b_rowst2], [b_rowst, b_rowst2])
            ps, pb = nps()
            for j in range(8):
                tr(ps[:, j * NSEQ:(j + 1) * NSEQ], rowst[0:NSEQ, j * 128:(j + 1) * 128], NSEQ, [b_rowst, b_rowst2], pb)
            cp(scT[:, half * 8:(half + 1) * 8, :].rearrange("p a b -> p (a b)"), ps[:, 0:8 * NSEQ], [pb], [b_scT])

        xT_v = xT.rearrange("(kc p) t -> p kc t", p=128)
        for i in range(17):
            src = I["xp"].ap()[i * 128:(i + 1) * 128, :] if i < 16 else I["xs"].ap()[:, :]
            P.dma("sp", slot(0, 2048), src, writes=[b_S[0]])
            for g in range(4):
                ps, pb = nps()
                for j in range(4):
                    kc = g * 4 + j
                    tr(ps[:, j * 128:(j + 1) * 128], slot(0, 128, kc * 128), 128, [b_S[0]], pb)
                P.op("act", lambda e, ps=ps, g=g: e.activation(out=slot(1, 512, g * 512), in_=ps[:, :], func=AF.Identity),
                     reads=[pb], writes=[b_S[1]])
            P.dma("sp", xT_v[:, :, i * 128:(i + 1) * 128], slot(1, 2048).rearrange("p (kc t) -> p kc t", kc=16),
                  reads=[b_S[1]], writes=b_xT)

        def load_params(l):
            for g in range(NPG):
                memset(pst[:], 0.0, [b_pst])
                r = 0
                for name, nrows in PROWS:
                    for rr in range(nrows):
                        grow = PCOL[name] + rr
                        if grow // 128 != g:
                            continue
                    lo = max(PCOL[name], g * 128)
                    hi = min(PCOL[name] + nrows, (g + 1) * 128)
                    if lo >= hi:
                        continue
                    r0 = lo - PCOL[name]
                    n = hi - lo
                    flat = Wt[name].ap()[l]
                    if name in ("lru_conv_w", "ffn_conv_w"):
                        flat = flat.rearrange("j c -> (j c)")
                    if name == "rw_mu":
                        nfull = min(n, max(0, 26 - r0))
                        if nfull > 0:
                            P.dma("act", pst[lo - g * 128:lo - g * 128 + nfull, :],
                                  flat[r0 * 128:(r0 + nfull) * 128].rearrange("(r c) -> r c", c=128), writes=[b_pst])
                        if r0 + n == 27:
                            P.dma("act", pst[hi - 1 - g * 128:hi - g * 128, 0:32],
                                  flat[26 * 128:26 * 128 + 32].rearrange("(r c) -> r c", c=32), writes=[b_pst])
                    else:
                        P.dma("act", pst[lo - g * 128:hi - g * 128, :],
                              flat[r0 * 128:(r0 + n) * 128].rearrange("(r c) -> r c", c=128), writes=[b_pst])
                ps, pb = nps()
                tr(ps[:, 0:128], pst[:, :], 128, [b_pst], pb)
                cp(PT[:, g * 128:(g + 1) * 128], ps[:, 0:128], [pb], [b_PT])

        def pcol(name, row):
            c = PCOL[name] + row
            return PT[:, c:c + 1]

        def loadT(dst3, bdst, src_rows_fn, nrows, nchunks, last_cols=128):
            c = 0
            while c < nchunks:
                g = min(8, nchunks - c)
                width = (g - 1) * 128 + (last_cols if c + g == nchunks else 128)
                P.dma("sp", rowst[0:nrows, 0:width], src_rows_fn(c * 128, width), writes=[b_rowst, b_rowst2])
                per = 512 // nrows
                j = 0
                while j < g:
                    gg = min(per, g - j)
                    ps, pb = nps()
                    for k in range(gg):
                        cc_ = c + j + k
                        ncol = last_cols if cc_ == nchunks - 1 else 128
                        tr(ps[0:ncol, k * nrows:(k + 1) * nrows], rowst[0:nrows, (j + k) * 128:(j + k) * 128 + ncol], nrows, [b_rowst, b_rowst2], pb)
                    cp(dst3[:, c + j:c + j + gg, :].rearrange("p a b -> p (a b)"), ps[:, 0:gg * nrows], [pb], [bdst])
                    j += gg
                c += g

        def flushT(src3, bsrc, nchunks, n, dst_fn, bout, last_cols=128):
            c = 0
            while c < nchunks:
                g = min(4, nchunks - c)
                ps, pb = nps()
                width = 0
                for k in range(g):
                    ncol = last_cols if c + k == nchunks - 1 else 128
                    tr(ps[0:n, k * 128:k * 128 + ncol], src3[0:ncol, c + k, :], ncol, [bsrc], pb)
                    width += ncol
                cp(rowst[0:n, 0:width], ps[0:n, 0:width], [pb], [b_rowst])
                P.dma("sp", dst_fn(c * 128, width), rowst[0:n, 0:width], reads=[b_rowst], writes=[bout])
                c += g

        def wload(dst, src, bw):
            P.dma("pool", dst, src, writes=[bw])

        class WStream:
            def __init__(self, loaders):
                self.loaders = loaders
                self.slots = {}
                self.nxt = 0

            def get(self, i):
                while self.nxt < len(self.loaders) and self.nxt <= i + 1:
                    ws, bw = wslot()
                    self.loaders[self.nxt](ws, bw)
                    self.slots[self.nxt] = (ws, bw)
                    self.nxt += 1
                return self.slots[i]

        def norm_stats(t0, n):
            xb = arena[:, 0:4, :].rearrange("p a b -> p (a b)")[:, 0:16 * n].rearrange("p (kc t) -> p kc t", kc=16)
            P.dma("sp", xb, xT_v[:, :, t0:t0 + n], reads=b_xT, writes=b_S[0:4])
            ps, pb = nps()
            for kc in range(16):
                sqb, bsq = (tmpb, b_tmpb) if kc % 2 == 0 else (tmpc, b_tmpc)
                act(sqb[:, 0:n], xb[:, kc, :], AF.Square, b_S[0:4], [bsq])
                mm(ps[:, 0:n], ones[:], sqb[:, 0:n], kc == 0, kc == 15, [b_ones, bsq], pb)
            rstd = slot(4, n)
            act(rstd, ps[:, 0:n], AF.Sqrt, [pb], [b_S[4]], bias=1e-6, scale=1.0 / D)
            P.op("dve", lambda e: e.reciprocal(out=rstd, in_=rstd), reads=[b_S[4]], writes=[b_S[4]])
            return xb, rstd

        def norm_mod(Aap, Bap, bA, bB):
            for (t0, n) in TB:
                xb, rstd = norm_stats(t0, n)
                for kc in range(16):
                    tt(xb[:, kc, :], xb[:, kc, :], rstd, ALU.mult, b_S[0:5], b_S[0:4])
                    if t0 < TP:
                        ts(actT[:, kc, t0:t0 + n], xb[:, kc, :], Aap[:, kc, 0:1], Bap[:, kc, 0:1], ALU.mult, ALU.add,
                           b_S[0:4] + [bA, bB], [b_act])
                    else:
                        x3 = xb[:, kc, :].rearrange("p (s t) -> p s t", s=NS)
                        tt(x3, x3, Aap[:, kc, 1:NSEQ].unsqueeze(2).to_broadcast([128, NS, 8]), ALU.mult, b_S[0:4] + [bA], b_S[0:4])
                        tt(actT[:, kc, t0:t0 + n].rearrange("p (s t) -> p s t", s=NS), x3,
                           Bap[:, kc, 1:NSEQ].unsqueeze(2).to_broadcast([128, NS, 8]), ALU.add, b_S[0:4] + [bB], [b_act])

        def resid_update(c, t0, n, ps, pb, gcol0):
            xs_ = tmpb[:, 0:n]
            P.dma("sp", xs_, xT_v[:, c, t0:t0 + n], reads=[b_xT[c]], writes=[b_tmpb])
            if t0 < TP:
                stt(xs_, ps[:, 0:n], modT[:, gcol0 + c, 0:1], xs_, ALU.mult, ALU.add, [pb, b_mod, b_tmpb], [b_tmpb])
            else:
                g3 = modT[:, gcol0 + c, 1:NSEQ].unsqueeze(2).to_broadcast([128, NS, 8])
                t3 = tmpc[:, 0:n].rearrange("p (s t) -> p s t", s=NS)
                tt(t3, ps[:, 0:n].rearrange("p (s t) -> p s t", s=NS), g3, ALU.mult, [pb, b_mod], [b_tmpc])
                tt(xs_, xs_, tmpc[:, 0:n], ALU.add, [b_tmpb, b_tmpc], [b_tmpb])
            P.dma("sp", xT_v[:, c, t0:t0 + n], xs_, reads=[b_tmpb], writes=[b_xT[c]])

        tkc = [0]

        def to_tokmajor(src_slot_ap, bsrc, qi, c0):
            tkv = tk[qi].rearrange("(tt p) c -> p tt c", p=128)
            for g0 in range(0, 17, 4):
                g = min(4, 17 - g0)
                ps, pb = nps()
                for k in range(g):
                    tr(ps[:, k * 128:(k + 1) * 128], src_slot_ap[:, (g0 + k) * 128:(g0 + k + 1) * 128], 128, [bsrc], pb)
                hh_ = tkc[0] % 2
                tkc[0] += 1
                rs = rowst[:, hh_ * 512:hh_ * 512 + g * 128]
                brs = b_rowst if hh_ == 0 else b_rowst2
                act(rs, ps[:, 0:g * 128], AF.Identity, [pb], [brs])
                P.dma("sp", tkv[:, g0:g0 + g, c0:c0 + 128], rs.rearrange("p (a b) -> p a b", a=g),
                      reads=[brs], writes=[b_tk[qi]])

        bx = [[P.buf(f"x_{j}_{p}") for p in range(2)] for j in range(5)]
        bv = [P.buf(f"v_{p}") for p in range(2)]
        by = [P.buf(f"y_{p}") for p in range(2)]
        bSS = [P.buf("SA"), P.buf("SB")]
        bT2 = P.buf("T2"); bTt = P.buf("Tt")
        bKV = [P.buf(f"KV_{i}") for i in range(4)]
        b_ab1 = P.buf("ab1")
        for l in range(NLAYERS):
            load_params(l)
            memset(bda[:], 0.0, [b_bda]); memset(bdi[:], 0.0, [b_bdi])
            for (dst, bd_, nm) in ((bda, b_bda, "lru_wa"), (bdi, b_bdi, "lru_wi")):
                wv = Wt[nm].ap()[l].rearrange("(c two) d e -> two d c e", two=2)
                P.dma("pool", dst[0:64, :, 0:64], wv[0], writes=[bd_])
                P.dma("pool", dst[64:128, :, 64:128], wv[1], writes=[bd_])
            P.dma("pool", lw2[0:64, :], Wt["rw_w2"].ap()[l], writes=[b_lw2])
            P.dma("pool", lw2[64:128, :], Wt["rw_a2"].ap()[l], writes=[b_lw2])
            P.dma("pool", g2a[:, :], Wt["rw_g2"].ap()[l, 0:128, :], writes=[b_g2a])
            P.dma("pool", g2b[:, :], Wt["rw_g2"].ap()[l, 128:160, :], writes=[b_g2b])
            lam = PT[:, PCOL["lru_lambda"]:PCOL["lru_lambda"] + 8]
            act(cA[:, 0:8], lam, AF.Exp, [b_PT], [b_cA], scale=-1.0)
            act(cA[:, 0:8], cA[:, 0:8], AF.Ln, [b_cA], [b_cA], bias=1.0)
            ts(cA[:, 8:16], cA[:, 0:8], -16.0, None, ALU.mult, None, [b_cA], [b_cA])
            ts(cA[:, 0:8], cA[:, 0:8], -8.0, None, ALU.mult, None, [b_cA], [b_cA])
            loadT(stlc, b_stlc, lambda c0, w: I["st_lc"].ap()[l].rearrange("s j c -> (s j) c")[:, c0:c0 + w], 48, 8)
            loadT(stlh, b_stlh, lambda c0, w: I["st_lh"].ap()[l][:, c0:c0 + w], 16, 8)
            loadT(stsh, b_stsh, lambda c0, w: I["st_sh"].ap()[l][:, c0:c0 + w], 16, 27, last_cols=32)

            wada = Wt["w_ada"].ap()[l].rearrange("(kc p) c -> p kc c", p=128)
            st1 = WStream([(lambda ws, bw, b=b: wload(ws[:, :, 0:256], wada[:, :, b * 256:(b + 1) * 256], bw)) for b in range(48)])
            for blk_i in range(48):
                ws, bw = st1.get(blk_i)
                ps, pb = nps()
                for j2 in range(2):
                    for kc in range(16):
                        mm(ps[:, j2 * 32:j2 * 32 + NSEQ], ws[:, kc, j2 * 128:(j2 + 1) * 128], scT[:, kc, :], kc == 0, kc == 15,
                           [bw, b_scT], pb)
                for j2 in range(2):
                    j = blk_i * 2 + j2
                    act(modT[:, j, :], ps[:, j2 * 32:j2 * 32 + NSEQ], AF.Identity, [pb, b_PT], [b_mod], bias=pcol("b_ada", j))

            def make_A(sc0, nm_name):
                nmb = PT[:, PCOL[nm_name]:PCOL[nm_name] + 16].unsqueeze(2).to_broadcast([128, 16, NSEQ])
                stt(A1[:, :, :], modT[:, sc0:sc0 + 16, :], 1.0, nmb, ALU.add, ALU.mult, [b_mod, b_PT], [b_A1])

            make_A(16, "norm_mix")
            norm_mod(A1, modT[:, 0:16, :], b_A1, b_mod)

            win = Wt["w_in"].ap()[l].rearrange("(kc p) c -> p kc c", p=128)

            def proj_block(ws, wc0, m, t0, n, bw):
                ps, pb = nps()
                for kc in range(16):
                    mm(ps[0:m, 0:n], ws[:, kc, wc0:wc0 + m], actT[:, kc, t0:t0 + n], kc == 0, kc == 15, [bw, b_act], pb)
                return ps, pb

            def lru_loader(c):
                def f(ws, bw):
                    wload(ws[:, :, 0:128], win[:, :, c * 128:(c + 1) * 128], bw)
                    wload(ws[:, :, 128:256], win[:, :, DL + c * 128:DL + (c + 1) * 128], bw)
                return f
            st3a = WStream([lru_loader(c) for c in range(8)])
            for c in range(8):
                ws, bw = st3a.get(c)
                LX = slot(0); XC = slot(1); AA = slot(3); INP = slot(4)
                LXs = LX[:, 2051:2051 + 176].rearrange("p (s j) -> p s j", s=NS)
                memset(LX[:, 0:3], 0.0, [b_S[0]])
                cp(LXs[:, :, 0:3], stlc[:, c, :].rearrange("p (s j) -> p s j", s=NS), [b_stlc], [b_S[0]], eng="pool")
                for (t0, n) in TB:
                    ps, pb = proj_block(ws, 0, 128, t0, n, bw)
                    if t0 < TP:
                        act(LX[:, 3 + t0:3 + t0 + n], ps[:, 0:n], AF.Identity, [pb], [b_S[0]])
                    else:
                        act(LXs[:, :, 3:11], ps[:, 0:n].rearrange("p (s t) -> p s t", s=NS), AF.Identity, [pb], [b_S[0]])
                cp(olc[:, c, 0:3], LX[:, 2048:2051], [b_S[0]], [b_olc], eng="pool")
                cp(olc[:, c, 3:51].rearrange("p (s j) -> p s j", s=NS), LXs[:, :, 8:11], [b_S[0]], [b_olc], eng="pool")
                XCs = XC[:, TP:T].rearrange("p (s t) -> p s t", s=NS)
                ts(XC[:, 0:TP], LX[:, 0:TP], pcol("lru_conv_w", 0 * 8 + c), pcol("lru_conv_b", c), ALU.mult, ALU.add,
                   [b_S[0], b_PT], [b_S[1]])
                ts(XCs, LXs[:, :, 0:8], pcol("lru_conv_w", 0 * 8 + c), pcol("lru_conv_b", c), ALU.mult, ALU.add,
                   [b_S[0], b_PT], [b_S[1]])
                for j in range(1, 4):
                    stt(XC[:, 0:TP], LX[:, j:j + TP], pcol("lru_conv_w", j * 8 + c), XC[:, 0:TP], ALU.mult, ALU.add,
                        [b_S[0], b_S[1], b_PT], [b_S[1]])
                    stt(XCs, LXs[:, :, j:j + 8], pcol("lru_conv_w", j * 8 + c), XCs, ALU.mult, ALU.add,
                        [b_S[0], b_S[1], b_PT], [b_S[1]])
                XCb = slot(2).bitcast(BF16)[:, 0:T]
                act(XCb, XC[:, 0:T], AF.Identity, [b_S[1]], [b_S[2]])
                for (t0, n) in TB:
                    psr, pbr = nps()
                    mm(psr[:, 0:n], bda[:, c, :], XCb[:, t0:t0 + n], True, True, [b_bda, b_S[2]], pbr)
                    psi, pbi = nps()
                    mm(psi[:, 0:n], bdi[:, c, :], XCb[:, t0:t0 + n], True, True, [b_bdi, b_S[2]], pbi)
                    rr = INP[:, t0:t0 + n]
                    act(rr, psr[:, 0:n], AF.Sigmoid, [pbr, b_PT], [b_S[4]], bias=pcol("lru_ba", c))
                    act(AA[:, t0:t0 + n], rr, AF.Exp, [b_S[4], b_cA], [b_S[3]], scale=cA[:, c:c + 1])
                    act(rr, rr, AF.Exp, [b_S[4], b_cA], [b_S[4]], scale=cA[:, 8 + c:9 + c])
                    act(rr, rr, AF.Sqrt, [b_S[4]], [b_S[4]], scale=-1.0, bias=1.0)
                    act(tmpb[:, 0:n], psi[:, 0:n], AF.Sigmoid, [pbi, b_PT], [b_tmpb], bias=pcol("lru_bi", c))
                    tt(rr, rr, tmpb[:, 0:n], ALU.mult, [b_S[4], b_tmpb], [b_S[4]])
                    tt(rr, rr, XC[:, t0:t0 + n], ALU.mult, [b_S[4], b_S[1]], [b_S[4]])
                AAs = AA[:, TP:T].rearrange("p (s t) -> p s t", s=NS)
                INs = INP[:, TP:T].rearrange("p (s t) -> p s t", s=NS)
                tt(sm[:, 0:NS], AAs[:, :, 0], stlh[:, c, :], ALU.mult, [b_S[3], b_stlh], [b_sm])
                tt(INs[:, :, 0], INs[:, :, 0], sm[:, 0:NS], ALU.add, [b_S[4], b_sm], [b_S[4]])
                memset(AAs[:, :, 0], 0.0, [b_S[3]], eng="dve")
                HL = slot(1)
                P.op("dve", lambda e, HL=HL, AA=AA, INP=INP: e.tensor_tensor_scan(out=HL[:, 0:TP], data0=AA[:, 0:TP], data1=INP[:, 0:TP],
                                                                           initial=0.0, op0=ALU.mult, op1=ALU.add),
                     reads=[b_S[3], b_S[4]], writes=[b_S[1]])
                P.op("dve", lambda e, HL=HL, AA=AA, INP=INP: e.tensor_tensor_scan(out=HL[:, TP:T], data0=AA[:, TP:T], data1=INP[:, TP:T],
                                                                           initial=0.0, op0=ALU.mult, op1=ALU.add),
                     reads=[b_S[3], b_S[4]], writes=[b_S[1]])
                cp(olh[:, c, 0:1], HL[:, TP - 1:TP], [b_S[1]], [b_olh], eng="pool")
                cp(olh[:, c, 1:NSEQ], HL[:, TP:T].rearrange("p (s t) -> p s t", s=NS)[:, :, 7], [b_S[1]], [b_olh], eng="pool")
                GG = slot(0)
                GAb = slot(2).bitcast(BF16)[:, 0:T]
                for (t0, n) in TB:
                    ps, pb = proj_block(ws, 128, 128, t0, n, bw)
                    u = GG[:, t0:t0 + n]
                    act(u, ps[:, 0:n], AF.Identity, [pb], [b_S[0]])
                    tt(tmpb[:, 0:n], u, u, ALU.mult, [b_S[0]], [b_tmpb])
                    ts(tmpb[:, 0:n], tmpb[:, 0:n], 0.044715, 1.0, ALU.mult, ALU.add, [b_tmpb], [b_tmpb])
                    tt(tmpb[:, 0:n], tmpb[:, 0:n], u, ALU.mult, [b_tmpb, b_S[0]], [b_tmpb])
                    act(tmpb[:, 0:n], tmpb[:, 0:n], AF.Sigmoid, [b_tmpb], [b_tmpb], scale=1.5957691216057308)
                    tt(u, u, tmpb[:, 0:n], ALU.mult, [b_S[0], b_tmpb], [b_S[0]])
                    tt(GAb[:, t0:t0 + n], u, HL[:, t0:t0 + n], ALU.mult, [b_S[0], b_S[1]], [b_S[2]])
                P.dma("sp", ga_d[c * 128:(c + 1) * 128, :], GAb, reads=[b_S[2]], writes=[b_ga])
            flushT(olc, b_olc, 8, 51, lambda c0, w: O["o_lc"].ap()[l].rearrange("s j c -> (s j) c")[:, c0:c0 + w], b_out["o_lc"])
            flushT(olh, b_olh, 8, NSEQ, lambda c0, w: O["o_lh"].ap()[l][:, c0:c0 + w], b_out["o_lh"])

            TL = blk[:, :]
            TLa = TL[:, 0:T]
            SG1 = TL[:, T:2 * T]
            SG2 = TL[:, 2 * T:3 * T]
            b_TL = [b_blk]
            qorder = [24, 25, 26] + list(range(24))

            def rw_loader(q):
                m = 32 if q == 26 else 128
                return lambda ws, bw: wload(ws[:, :, 0:m], win[:, :, 2 * DL + q * 128:2 * DL + q * 128 + m], bw)
            st3b = WStream([rw_loader(q) for q in qorder])

            def rw_A(q, par):
                m = 32 if q == 26 else 128
                ws, bw = st3b.get(qorder.index(q))
                RW = slot(2 * par); bRW = b_S[2 * par]
                RWs = RW[:, 2049:2049 + 144].rearrange("p (s j) -> p s j", s=NS)
                memset(RW[0:m, 0:1], 0.0, [bRW])
                cp(RWs[0:m, :, 0], stsh[0:m, q, :], [b_stsh], [bRW], eng="pool")
                for (t0, n) in TB:
                    ps, pb = proj_block(ws, 0, m, t0, n, bw)
                    if t0 < TP:
                        act(RW[0:m, 1 + t0:1 + t0 + n], ps[0:m, 0:n], AF.Identity, [pb], [bRW])
                    else:
                        act(RWs[0:m, :, 1:9], ps[0:m, 0:n].rearrange("p (s t) -> p s t", s=NS), AF.Identity, [pb], [bRW])

            def rw_B(q, par):
                m = 32 if q == 26 else 128
                RW = slot(2 * par); bRW = b_S[2 * par]
                XS = slot(2 * par + 1); bXS = b_S[2 * par + 1]
                RWs = RW[:, 2049:2049 + 144].rearrange("p (s j) -> p s j", s=NS)
                cp(osh[0:m, q, 0:1], RW[0:m, TP:TP + 1], [bRW], [b_osh], eng="pool")
                cp(osh[0:m, q, 1:NSEQ], RWs[0:m, :, 8], [bRW], [b_osh], eng="pool")
                XSs = XS[:, TP:T].rearrange("p (s t) -> p s t", s=NS)
                mu = pcol("rw_mu", q)
                tt(XS[0:m, 0:TP], RW[0:m, 0:TP], RW[0:m, 1:TP + 1], ALU.subtract, [bRW], [bXS])
                tt(XSs[0:m], RWs[0:m, :, 0:8], RWs[0:m, :, 1:9], ALU.subtract, [bRW], [bXS])
                stt(XS[0:m, 0:TP], XS[0:m, 0:TP], mu[0:m], RW[0:m, 1:TP + 1], ALU.mult, ALU.add, [bRW, bXS, b_PT], [bXS])
                stt(XSs[0:m], XSs[0:m], mu[0:m], RWs[0:m, :, 1:9], ALU.mult, ALU.add, [bRW, bXS, b_PT], [bXS])
                if q == 24:
                    act(TLa[0:64, :], XS[0:64, 0:T], AF.Tanh, [bXS], b_TL)
                    act(TLa[64:128, :], XS[64:128, 0:T], AF.Identity, [bXS], b_TL)
                elif q == 25:
                    act(SG1, XS[:, 0:T], AF.Sigmoid, [bXS], b_TL)
                elif q == 26:
                    act(SG2[0:32, :], XS[0:32, 0:T], AF.Sigmoid, [bXS], b_TL)
                else:
                    to_tokmajor(XS, bXS, q // 8, (q % 8) * 128)

            rw_A(qorder[0], 0)
            for qi_, q in enumerate(qorder):
                if qi_ + 1 < len(qorder):
                    rw_A(qorder[qi_ + 1], (qi_ + 1) % 2)
                rw_B(q, qi_ % 2)
            flushT(osh, b_osh, 27, NSEQ, lambda c0, w: O["o_sh"].ap()[l][:, c0:c0 + w], b_out["o_sh"], last_cols=32)
            for c in range(8):
                DC = slot(0); AC = slot(1); GC = slot(2)
                for (t0, n) in TB:
                    ps, pb = nps()
                    mm(ps[:, 0:n], lw2[0:64, c * 128:(c + 1) * 128], TLa[0:64, t0:t0 + n], True, True, [b_lw2] + b_TL, pb)
                    act(DC[:, t0:t0 + n], ps[:, 0:n], AF.Sigmoid, [pb, b_PT], [b_S[0]], bias=pcol("rw_w0", c))
                    act(DC[:, t0:t0 + n], DC[:, t0:t0 + n], AF.Exp, [b_S[0]], [b_S[0]], scale=-math.exp(-0.5))
                    ps, pb = nps()
                    mm(ps[:, 0:n], lw2[64:128, c * 128:(c + 1) * 128], TLa[64:128, t0:t0 + n], True, True, [b_lw2] + b_TL, pb)
                    act(AC[:, t0:t0 + n], ps[:, 0:n], AF.Sigmoid, [pb, b_PT], [b_S[1]], bias=pcol("rw_a0", c))
                    ps, pb = nps()
                    mm(ps[:, 0:n], g2a[:, c * 128:(c + 1) * 128], SG1[:, t0:t0 + n], True, False, [b_g2a] + b_TL, pb)
                    mm(ps[:, 0:n], g2b[0:32, c * 128:(c + 1) * 128], SG2[0:32, t0:t0 + n], False, True, [b_g2b] + b_TL, pb)
                    act(GC[:, t0:t0 + n], ps[:, 0:n], AF.Identity, [pb], [b_S[2]])
                to_tokmajor(DC, b_S[0], 3, c * 128)
                to_tokmajor(AC, b_S[1], 4, c * 128)
                to_tokmajor(GC, b_S[2], 5, c * 128)
            st3c = WStream([(lambda ws, bw, g2=g2: wload(ws[:, :, 0:256], win[:, :, 2 * DL + NRW + g2 * 256:2 * DL + NRW + (g2 + 1) * 256], bw))
                            for g2 in range(16)])
            for gc in range(32):
                if gc % 2 == 0:
                    ws, bw = st3c.get(gc // 2)
                SGb = slot(gc % 2).bitcast(BF16)[:, 0:T]
                bsl = b_S[gc % 2]
                for (t0, n) in TB:
                    ps, pb = proj_block(ws, (gc % 2) * 128, 128, t0, n, bw)
                    act(SGb[:, t0:t0 + n], ps[:, 0:n], AF.Sigmoid, [pb], [bsl])
                P.dma("sp", sg[gc * 128:(gc + 1) * 128, :], SGb, reads=[bsl], writes=[b_sg])

            for i, nm in enumerate(("rw_kk", "rw_ka", "rw_rk")):
                src = Wt[nm].ap()[l]
                if nm == "rw_rk":
                    src = src.rearrange("h k -> (h k)")
                P.dma("act", ct[i][:, :], src.partition_broadcast(128), writes=[b_ct[i]])
            rqv = rq_h.ap()

            def half(i, h):
                return arena[:, i, h * 1024:(h + 1) * 1024]

            def h3(ap):
                return ap.rearrange("p (h k) -> p h k", h=16)

            for i in range(17):
                rows = slice(i * 128, (i + 1) * 128)
                Rt, Kt, Vt, Dt, At = half(0, 0), half(0, 1), half(1, 0), half(1, 1), half(2, 0)
                KKt, TMt, KMt, NBt, BVt = half(2, 1), half(3, 0), half(3, 1), half(4, 0), half(4, 1)
                for (dst, qi, bb) in ((Rt, 0, b_S[0]), (Kt, 1, b_S[0]), (Vt, 2, b_S[1]), (Dt, 3, b_S[1]), (At, 4, b_S[2])):
                    P.dma("sp", dst, tk[qi][rows, :], reads=[b_tk[qi]], writes=[bb])
                tt(KKt, Kt, ct[0][:, :], ALU.mult, [b_S[0], b_ct[0]], [b_S[2]])
                tt(TMt, KKt, KKt, ALU.mult, [b_S[2]], [b_S[3]])
                P.op("dve", lambda e, TMt=TMt: e.tensor_reduce(out=sm[:, 0:16], in_=h3(TMt), axis=AX.X, op=ALU.add),
                     reads=[b_S[3]], writes=[b_sm])
                act(sm[:, 0:16], sm[:, 0:16], AF.Sqrt, [b_sm], [b_sm])
                ts(sm[:, 0:16], sm[:, 0:16], 1e-12, None, ALU.max, None, [b_sm], [b_sm])
                P.op("dve", lambda e: e.reciprocal(out=sm[:, 0:16], in_=sm[:, 0:16]), reads=[b_sm], writes=[b_sm])
                tt(h3(KKt), h3(KKt), sm[:, 0:16].unsqueeze(2).to_broadcast([128, 16, 64]), ALU.mult, [b_S[2], b_sm], [b_S[2]])
                stt(TMt, At, -1.0, ct[1][:, :], ALU.add, ALU.mult, [b_S[2], b_ct[1]], [b_S[3]])
                stt(KMt, TMt, 1.0, Kt, ALU.add, ALU.mult, [b_S[3], b_S[0]], [b_S[3]])
                stt(NBt, KKt, -1.0, At, ALU.mult, ALU.mult, [b_S[2]], [b_S[4]])
                ccv = sm[:, 32:64].rearrange("p (h j) -> p h j", j=2)
                tt(TMt, Rt, KMt, ALU.mult, [b_S[0], b_S[3]], [b_S[3]])
                P.op("dve", lambda e, TMt=TMt, ccv=ccv: e.tensor_reduce(out=ccv[:, :, 1], in_=h3(TMt), axis=AX.X, op=ALU.add),
                     reads=[b_S[3]], writes=[b_sm])
                tt(TMt, TMt, ct[2][:, :], ALU.mult, [b_S[3], b_ct[2]], [b_S[3]])
                P.op("dve", lambda e, TMt=TMt: e.tensor_reduce(out=sm[:, 16:32], in_=h3(TMt), axis=AX.X, op=ALU.add),
                     reads=[b_S[3]], writes=[b_sm])
                tt(h3(BVt), h3(Vt), sm[:, 16:32].unsqueeze(2).to_broadcast([128, 16, 64]), ALU.mult, [b_S[1], b_sm], [b_S[4]])
                for (srct, qi, bb) in ((Rt, 0, b_S[0]), (Dt, 1, b_S[1]), (KMt, 2, b_S[3]), (KKt, 3, b_S[2]), (NBt, 4, b_S[4])):
                    P.dma("sp", rqv[qi].rearrange("h t k -> t h k")[rows], h3(srct), reads=[bb], writes=[b_rq])
                tt(Kt, NBt, Rt, ALU.mult, [b_S[4], b_S[0]], [b_S[0]])
                P.op("dve", lambda e, Kt=Kt, ccv=ccv: e.tensor_reduce(out=ccv[:, :, 0], in_=h3(Kt), axis=AX.X, op=ALU.add),
                     reads=[b_S[0]], writes=[b_sm])
                tt(Kt, Rt, Dt, ALU.mult, [b_S[0], b_S[1]], [b_S[0]])
                P.dma("sp", rqv[5].rearrange("h t k -> t h k")[rows], h3(Kt), reads=[b_S[0]], writes=[b_rq])
                ps, pb = nps()
                tr(ps[0:32, 0:128], sm[:, 32:64], 128, [b_sm], pb)
                cp(rowst[0:32, 0:128], ps[0:32, 0:128], [pb], [b_rowst])
                P.dma("sp", cc_h.ap()[:, rows], rowst[0:32, 0:128], reads=[b_rowst], writes=[b_cc])
                P.dma("sp", tk[6][rows, :], BVt, reads=[b_S[4]], writes=[b_tk[6]])

            def rec_steps(nsq, nsteps, xr_step, v_step, y_step):
                S3 = Sst[:, 0:nsq * 512].rearrange("p (s v k) -> p s v k", s=nsq, v=8)
                T3 = Stmp[:, 0:nsq * 512].rearrange("p (s v k) -> p s v k", s=nsq, v=8)
                K4 = [128, nsq, 8, 64]
                rdx = b_S[0:5]
                for t in range(nsteps):
                    r_, w_, k_, kk_, nb_ = [xr_step(j, t).unsqueeze(2).to_broadcast(K4) for j in range(5)]
                    sk = skk[:, 0:nsq * 8].rearrange("p (s v) -> p s v", s=nsq)
                    tt(T3, S3, kk_, ALU.mult, [b_Sst] + rdx, [b_Stmp])
                    P.op("dve", lambda e, sk=sk, T3=T3: e.tensor_reduce(out=sk, in_=T3, axis=AX.X, op=ALU.add), reads=[b_Stmp], writes=[b_skk])
                    tt(S3, S3, w_, ALU.mult, [b_Sst] + rdx, [b_Sst])
                    tt(T3, nb_, sk.unsqueeze(3).to_broadcast(K4), ALU.mult, [b_skk] + rdx, [b_Stmp])
                    tt(S3, S3, T3, ALU.add, [b_Sst, b_Stmp], [b_Sst])
                    tt(T3, k_, v_step(t).unsqueeze(3).to_broadcast(K4), ALU.mult, [b_vbuf] + rdx, [b_Stmp])
                    tt(S3, S3, T3, ALU.add, [b_Sst, b_Stmp], [b_Sst])
                    tt(T3, S3, r_, ALU.mult, [b_Sst] + rdx, [b_Stmp])
                    yo = y_step(t)
                    P.op("dve", lambda e, yo=yo, T3=T3: e.tensor_reduce(out=yo, in_=T3, axis=AX.X, op=ALU.add), reads=[b_Stmp], writes=[b_ybuf])

            oS = O["o_S"].ap()[l]
            TC = 16
            SLQ = [3, 5, 1, 4, 2]
            allb = b_S[0:5] + [b for bb_ in bx for b in bb_] + bv + by + bSS + [bT2, bTt] + bKV + [b_blk, b_vbuf, b_ybuf]
            fence(allb)
            SS = [blkF[:, 0:512], blkF[:, 512:1024]]
            T2v = blkF[:, 1024:2048].rearrange("p (q v k) -> p q v k", q=2, v=8)
            Ttv = blkF[:, 2048:2560].rearrange("p (v k) -> p v k", v=8)
            KVv = [blkF[:, 2560 + i * 512:2560 + (i + 1) * 512].rearrange("p (v k) -> p v k", v=8) for i in range(4)]
            memset(SS[0], 0.0, [bSS[0]], eng="dve")
            g = 0
            NCH = TP // TC

            def chunk_views(ci):
                par = ci % 2
                vb_ = vbuf[:, par * 128:(par + 1) * 128].rearrange("p (t v) -> p t v", t=TC)
                yb_ = ybuf[:, par * 128:(par + 1) * 128].rearrange("p (t v) -> p t v", t=TC)
                cb_ = ccb[:, par * 32:(par + 1) * 32].rearrange("p (j t) -> p j t", j=2)
                sk_ = skys[:, par * 256:(par + 1) * 256].rearrange("p (q t v) -> p q t v", q=2, t=TC)
                return par, vb_, yb_, cb_, sk_

            def chunk_loads(ci):
                t0 = ci * TC
                par, vb_, yb_, cb_, sk_ = chunk_views(ci)
                xo = par * 1024
                for j in range(5):
                    src = bass.AP(rq_h, SLQ[j] * 16 * T * 64 + t0 * 64, [[T * 64, 16], [0, 8], [1, TC * 64]])
                    P.dma("sp", slot(j, TC * 64, xo), src, reads=[b_rq], writes=[bx[j][par]])
                P.dma("act", vb_, bass.AP(tk_h, 2 * T * DL + t0 * DL, [[8, 128], [DL, TC], [1, 8]]), reads=[b_tk[2]], writes=[bv[par]])
                for jj in range(2):
                    P.dma("act", cb_[:, jj, :], bass.AP(cc_h, jj * T + t0, [[2 * T, 16], [0, 8], [1, TC]]), reads=[b_cc], writes=[b_ccb[par]])

            chunk_loads(0)
            for ci in range(NCH):
                t0 = ci * TC
                par, vb_, yb_, cb_, sk_ = chunk_views(ci)
                xo = par * 1024
                if ci + 1 < NCH:
                    chunk_loads(ci + 1)
                for t in range(TC):
                    cur, nxt = g % 2, (g + 1) % 2
                    Sc3 = SS[cur].rearrange("p (v k) -> p v k", v=8)
                    Sn3 = SS[nxt].rearrange("p (v k) -> p v k", v=8)
                    kv = KVv[g % 4]; bkv = bKV[g % 4]
                    o = xo + t * 64

                    def kv_fn(e, kv=kv, o=o, vb_=vb_, t=t):
                        last = None
                        for v8 in range(8):
                            last = e.activation(out=kv[:, v8, :], in_=slot(4, 64, o), func=AF.Identity, scale=vb_[:, t, v8:v8 + 1])
                        return last
                    P.op("act", kv_fn, reads=[bx[4][par], bv[par]], writes=[bkv])
                    tt(T2v, Sc3.unsqueeze(1).to_broadcast([128, 2, 8, 64]),
                       arena[:, 0:2, o:o + 64].unsqueeze(2).to_broadcast([128, 2, 8, 64]), ALU.mult,
                       [bSS[cur], bx[0][par], bx[1][par]], [bT2])
                    P.op("dve", lambda e, out=sk_[:, :, t, :], T2v=T2v: e.tensor_reduce(out=out, in_=T2v, axis=AX.X, op=ALU.add),
                         reads=[bT2], writes=[b_skys[par]])
                    tt(Sn3, Sc3, slot(2, 64, o).unsqueeze(1).to_broadcast([128, 8, 64]), ALU.mult, [bSS[cur], bx[2][par]], [bSS[nxt]], eng="pool")
                    tt(Sn3, Sn3, kv, ALU.add, [bSS[nxt], bkv], [bSS[nxt]], eng="pool")
                    tt(Ttv, sk_[:, 0, t, :].unsqueeze(2).to_broadcast([128, 8, 64]),
                       slot(3, 64, o).unsqueeze(1).to_broadcast([128, 8, 64]), ALU.mult, [b_skys[par], bx[3][par]], [bTt])
                    tt(Sn3, Sn3, Ttv, ALU.add, [bSS[nxt], bTt], [bSS[nxt]])
                    g += 1
                c1b = cb_[:, 0, :].unsqueeze(2).to_broadcast([128, TC, 8])
                c2b = cb_[:, 1, :].unsqueeze(2).to_broadcast([128, TC, 8])
                tt(yb_, sk_[:, 0, :, :], c1b, ALU.mult, [b_skys[par], b_ccb[par]], [by[par]], eng="pool")
                tt(yb_, yb_, sk_[:, 1, :, :], ALU.add, [by[par], b_skys[par]], [by[par]], eng="pool")
                tt(sk_[:, 0, :, :], vb_, c2b, ALU.mult, [bv[par], b_ccb[par]], [b_skys[par]], eng="pool")
                tt(yb_, yb_, sk_[:, 0, :, :], ALU.add, [by[par], b_skys[par]], [by[par]], eng="pool")
                P.dma("act", bass.AP(tk_h, 7 * T * DL + t0 * DL, [[8, 128], [DL, TC], [1, 8]]), yb_, reads=[by[par]], writes=[b_tk[7]])
            P.dma("sp", oS[0].rearrange("h (vb v8) k -> (h vb) (v8 k)", vb=8), SS[g % 2], reads=[bSS[g % 2]], writes=[b_out["o_S"]])
            fence(allb)
            for sb in range(4):
                s0 = sb * 4
                P.dma("sp", Sst[:, 0:2048].rearrange("p (s f) -> p s f", s=4),
                      bass.AP(I["st_S"], (l * NS + s0) * 65536, [[512, 128], [65536, 4], [1, 512]]), writes=[b_Sst])
                for j in range(5):
                    for s in range(4):
                        tok0 = TP + (s0 + s) * 8
                        src = bass.AP(rq_h, j * 16 * T * 64 + tok0 * 64, [[T * 64, 16], [0, 8], [1, 8 * 64]])
                        P.dma("sp" if s % 2 == 0 else "act", slot(j, 512, s * 512), src, reads=[b_rq], writes=[b_S[j]])
                P.dma("act", vbuf[:, 0:256].rearrange("p (t v) -> p t v", t=32),
                      bass.AP(tk_h, 2 * T * DL + (TP + s0 * 8) * DL, [[8, 128], [DL, 32], [1, 8]]), reads=[b_tk[2]], writes=[b_vbuf])
                rec_steps(4, 8,
                          lambda j, t: slot(j, 2048).rearrange("p (s t k) -> p s t k", s=4, t=8)[:, :, t, :],
                          lambda t: vbuf[:, 0:256].rearrange("p (s t v) -> p s t v", s=4, t=8)[:, :, t, :],
                          lambda t: ybuf[:, 0:256].rearrange("p (s t v) -> p s t v", s=4, t=8)[:, :, t, :])
                P.dma("act", bass.AP(tk_h, 7 * T * DL + (TP + s0 * 8) * DL, [[8, 128], [DL, 32], [1, 8]]),
                      ybuf[:, 0:256].rearrange("p (t v) -> p t v", t=32), reads=[b_ybuf], writes=[b_tk[7]])
                P.dma("sp", bass.AP(O["o_S"], (l * NSEQ + 1 + s0) * 65536, [[512, 128], [65536, 4], [1, 512]]),
                      Sst[:, 0:2048].rearrange("p (s f) -> p s f", s=4), reads=[b_Sst], writes=[b_out["o_S"]])

            for i, nm in enumerate(("rw_gn_g", "rw_gn_b")):
                P.dma("act", ct[i][:, :], Wt[nm].ap()[l].partition_broadcast(128), writes=[b_ct[i]])
            for i in range(17):
                rows = slice(i * 128, (i + 1) * 128)
                pp = i % 2
                bA_, bB_ = b_S[2 * pp], b_S[2 * pp + 1]
                Yt, BVt, GGt, TMt = half(2 * pp, 0), half(2 * pp, 1), half(2 * pp + 1, 0), half(2 * pp + 1, 1)
                P.dma("sp", Yt, tk[7][rows, :], reads=[b_tk[7]], writes=[bA_])
                P.dma("sp", BVt, tk[6][rows, :], reads=[b_tk[6]], writes=[bA_])
                P.dma("sp", GGt, tk[5][rows, :], reads=[b_tk[5]], writes=[bB_])
                P.op("dve", lambda e, Yt=Yt: e.tensor_reduce(out=sm[:, 0:16], in_=h3(Yt), axis=AX.X, op=ALU.add), reads=[bA_], writes=[b_sm])
                ts(sm[:, 0:16], sm[:, 0:16], 1.0 / 64, None, ALU.mult, None, [b_sm], [b_sm])
                tt(h3(Yt), h3(Yt), sm[:, 0:16].unsqueeze(2).to_broadcast([128, 16, 64]), ALU.subtract, [bA_, b_sm], [bA_])
                tt(TMt, Yt, Yt, ALU.mult, [bA_], [bB_])
                P.op("dve", lambda e, TMt=TMt: e.tensor_reduce(out=sm[:, 16:32], in_=h3(TMt), axis=AX.X, op=ALU.add), reads=[bB_], writes=[b_sm])
                act(sm[:, 16:32], sm[:, 16:32], AF.Sqrt, [b_sm], [b_sm], scale=1.0 / 64, bias=64e-5)
                P.op("dve", lambda e: e.reciprocal(out=sm[:, 16:32], in_=sm[:, 16:32]), reads=[b_sm], writes=[b_sm])
                tt(h3(Yt), h3(Yt), sm[:, 16:32].unsqueeze(2).to_broadcast([128, 16, 64]), ALU.mult, [bA_, b_sm], [bA_])
                tt(Yt, Yt, ct[0][:, :], ALU.mult, [bA_, b_ct[0]], [bA_])
                tt(Yt, Yt, ct[1][:, :], ALU.add, [bA_, b_ct[1]], [bA_])
                tt(Yt, Yt, BVt, ALU.add, [bA_], [bA_])
                tt(Yt, Yt, GGt, ALU.mult, [bA_, bB_], [bA_])
                OTb = slot(4).bitcast(BF16)[:, pp * 1024:(pp + 1) * 1024].rearrange("p (c t) -> p c t", c=8)
                for g in range(2):
                    ps, pb = nps()
                    for j in range(4):
                        tr(ps[:, j * 128:(j + 1) * 128], Yt[:, (g * 4 + j) * 128:(g * 4 + j + 1) * 128], 128, [bA_], pb)
                    act(OTb[:, g * 4:(g + 1) * 4, :].rearrange("p a b -> p (a b)"), ps[:, :], AF.Identity, [pb], [b_S[4]])
                P.dma("sp", oT_d.rearrange("(c p) t -> p c t", p=128)[:, :, rows], OTb, reads=[b_S[4]], writes=[b_oT])

            wpa = Wt["w_pa"].ap()[l].rearrange("(kc p) c -> p kc c", p=128)
            wpb = Wt["w_pb"].ap()[l].rearrange("(kc p) c -> p kc c", p=128)
            gav = ga_d.rearrange("(kc p) t -> p kc t", p=128)
            otv = oT_d.rearrange("(kc p) t -> p kc t", p=128)
            mcnt = 0

            def mg_loader(c):
                def f(ws, bw):
                    wload(ws[:, 0:8, 0:128], wpa[:, :, c * 128:(c + 1) * 128], bw)
                    wload(ws[:, 8:16, 0:128], wpb[:, :, c * 128:(c + 1) * 128], bw)
                return f
            st7 = WStream([mg_loader(c) for _tb in TB for c in range(16)])
            for (t0, n) in TB:
                gb = blk[:, 0:16 * 512].rearrange("p (kc t) -> p kc t", kc=16)
                P.dma("sp", gb[:, 0:8, 0:n], gav[:, :, t0:t0 + n], reads=[b_ga], writes=[b_blk])
                P.dma("act", gb[:, 8:16, 0:n], otv[:, :, t0:t0 + n], reads=[b_oT], writes=[b_blk])
                for c in range(16):
                    ws, bw = st7.get(mcnt)
                    sg_ = sgt[:, mcnt % 2, :]; bsg = b_sgt[mcnt % 2]
                    mcnt += 1
                    P.dma("sp", sg_[:, 0:n], sg[c * 128:(c + 1) * 128, t0:t0 + n], reads=[b_sg], writes=[bsg])
                    P.dma("sp", sg_[:, 512:512 + n], sg[D + c * 128:D + (c + 1) * 128, t0:t0 + n], reads=[b_sg], writes=[bsg])
                    psa, pba = nps()
                    for kc in range(8):
                        mm(psa[:, 0:n], ws[:, kc, 0:128], gb[:, kc, 0:n], kc == 0, kc == 7, [bw, b_blk], pba)
                    psb, pbb = nps()
                    for kc in range(8):
                        mm(psb[:, 0:n], ws[:, 8 + kc, 0:128], gb[:, 8 + kc, 0:n], kc == 0, kc == 7, [bw, b_blk], pbb)
                    tt(tmpb[:, 0:n], psa[:, 0:n], sg_[:, 0:n], ALU.mult, [pba, bsg], [b_tmpb])
                    tt(tmpc[:, 0:n], psb[:, 0:n], sg_[:, 512:512 + n], ALU.mult, [pbb, bsg], [b_tmpc])
                    tt(actT[:, c, t0:t0 + n], tmpb[:, 0:n], tmpc[:, 0:n], ALU.add, [b_tmpb, b_tmpc], [b_act])
            wo = Wt["w_o"].ap()[l].rearrange("(kc p) c -> p kc c", p=128)
            st8 = WStream([(lambda ws, bw, c2=c2: wload(ws[:, :, 0:256], wo[:, :, c2 * 256:(c2 + 1) * 256], bw)) for c2 in range(8)])
            for c in range(16):
                if c % 2 == 0:
                    ws, bw = st8.get(c // 2)
                for (t0, n) in TB:
                    ps, pb = proj_block(ws, (c % 2) * 128, 128, t0, n, bw)
                    resid_update(c, t0, n, ps, pb, 32)

            make_A(64, "norm_ffn")
            norm_mod(A1, modT[:, 48:64, :], b_A1, b_mod)

            wup = Wt["w_up"].ap()[l].rearrange("(kc p) c -> p kc c", p=128)
            def up_loader(j):
                def f(ws, bw):
                    wload(ws[:, :, 0:128], wup[:, :, j * 128:(j + 1) * 128], bw)
                    wload(ws[:, :, 128:256], wup[:, :, DFF + j * 128:DFF + (j + 1) * 128], bw)
                return f
            st10 = WStream([up_loader(j) for j in range(44)])
            for j in range(44):
                j4 = j % 4
                stv = I["st_fc"].ap()[l].rearrange("s j c -> (s j) c")
                ofv = O["o_fc"].ap()[l].rearrange("s j c -> (s j) c")
                if j4 == 0:
                    for hh in range(2):
                        loadT(stfc[:, hh * 4:hh * 4 + 4, :], b_stfc,
                              lambda c0, w, hh=hh, j=j: stv[:, (hh * 44 + j) * 128 + c0:(hh * 44 + j) * 128 + c0 + w], 32, 4)
                ws, bw = st10.get(j)
                outs = []
                for hh in range(2):
                    ch = hh * 44 + j
                    UU = slot(0 + hh * 2); UC = slot(1 + hh * 2)
                    bU = b_S[0 + hh * 2]; bC = b_S[1 + hh * 2]
                    UUs = UU[:, 2050:2050 + 160].rearrange("p (s j) -> p s j", s=NS)
                    memset(UU[:, 0:2], 0.0, [bU])
                    cp(UUs[:, :, 0:2], stfc[:, hh * 4 + j4, :].rearrange("p (s j) -> p s j", s=NS), [b_stfc], [bU], eng="pool")
                    for (t0, n) in TB:
                        ps, pb = proj_block(ws, hh * 128, 128, t0, n, bw)
                        if t0 < TP:
                            act(UU[:, 2 + t0:2 + t0 + n], ps[:, 0:n], AF.Identity, [pb], [bU])
                        else:
                            act(UUs[:, :, 2:10], ps[:, 0:n].rearrange("p (s t) -> p s t", s=NS), AF.Identity, [pb], [bU])
                    k8 = j % 8
                    if hh == 0 and k8 == 0:
                        pass
                    cp(ofc[:, hh * 4 + j4, 0:2], UU[:, 2048:2050], [bU], [b_ofc], eng="pool")
                    cp(ofc[:, hh * 4 + j4, 2:34].rearrange("p (s j) -> p s j", s=NS), UUs[:, :, 8:10], [bU], [b_ofc], eng="pool")
                    UCs = UC[:, TP:T].rearrange("p (s t) -> p s t", s=NS)
                    ts(UC[:, 0:TP], UU[:, 0:TP], pcol("ffn_conv_w", 0 * 88 + ch), pcol("ffn_conv_b", ch), ALU.mult, ALU.add, [bU, b_PT], [bC])
                    ts(UCs, UUs[:, :, 0:8], pcol("ffn_conv_w", 0 * 88 + ch), pcol("ffn_conv_b", ch), ALU.mult, ALU.add, [bU, b_PT], [bC])
                    for jj in range(1, 3):
                        stt(UC[:, 0:TP], UU[:, jj:jj + TP], pcol("ffn_conv_w", jj * 88 + ch), UC[:, 0:TP], ALU.mult, ALU.add, [bU, bC, b_PT], [bC])
                        stt(UCs, UUs[:, :, jj:jj + 8], pcol("ffn_conv_w", jj * 88 + ch), UCs, ALU.mult, ALU.add, [bU, bC, b_PT], [bC])
                    outs.append((UC, bC))
                (UG, bG), (UV, bV) = outs
                act(UG[:, 0:T], UG[:, 0:T], AF.Silu, [bG], [bG])
                ATb = slot(4).bitcast(BF16)[:, 0:T]
                tt(ATb, UG[:, 0:T], UV[:, 0:T], ALU.mult, [bG, bV], [b_S[4]])
                P.dma("sp", aT_d[j * 128:(j + 1) * 128, :], ATb, reads=[b_S[4]], writes=[b_aT])
                if j4 == 3:
                    for hh in range(2):
                        flushT(ofc[:, hh * 4:hh * 4 + 4, :], b_ofc, 4, 34,
                               lambda c0, w, hh=hh, j=j: ofv[:, (hh * 44 + j - 3) * 128 + c0:(hh * 44 + j - 3) * 128 + c0 + w], b_out["o_fc"])

            wdn = Wt["w_down"].ap()[l].rearrange("(kc p) c -> p kc c", p=128)
            atv = aT_d.rearrange("(kc p) t -> p kc t", p=128)
            fence([b_blk, b_ab1])
            abv = [blk[:, 0:11 * 512].rearrange("p (kc t) -> p kc t", kc=11),
                   blk[:, 11 * 512:22 * 512].rearrange("p (kc t) -> p kc t", kc=11)]
            b_ab = [b_blk, b_ab1]
            qcnt = 0
            dn_items = [(cg, qt, h2) for _tb in TB for cg in range(4) for qt in range(4) for h2 in range(2)]
            st11 = WStream([(lambda ws, bw, cg=cg, qt=qt, h2=h2:
                             wload(ws[:, 0:11, 0:256], wdn[:, qt * 11:(qt + 1) * 11, (cg * 4 + h2 * 2) * 128:(cg * 4 + h2 * 2 + 2) * 128], bw))
                            for (cg, qt, h2) in dn_items])
            dcnt = 0
            for (t0, n) in TB:
                for cg in range(4):
                    banks = [nps() for _ in range(4)]
                    for qt in range(4):
                        ab = abv[qcnt % 2]; bab = b_ab[qcnt % 2]
                        P.dma("sp" if qcnt % 2 == 0 else "act", ab[:, :, 0:n], atv[:, qt * 11:(qt + 1) * 11, t0:t0 + n],
                              reads=[b_aT], writes=[bab])
                        qcnt += 1
                        for ci in range(4):
                            c = cg * 4 + ci
                            if ci % 2 == 0:
                                ws, bw = st11.get(dcnt)
                                dcnt += 1
                            ps, pb = banks[ci]
                            wc = (ci % 2) * 128
                            for k2 in range(11):
                                mm(ps[:, 0:n], ws[:, k2, wc:wc + 128], ab[:, k2, 0:n], qt == 0 and k2 == 0, qt == 3 and k2 == 10, [bw, bab], pb)
                    for ci in range(4):
                        ps, pb = banks[ci]
                        resid_update(cg * 4 + ci, t0, n, ps, pb, 80)
            fence([b_blk, b_ab1])

        nf = Wt["norm_final"].ap().rearrange("(r c) -> r c", c=128)
        memset(pst[:], 0.0, [b_pst])
        P.dma("act", pst[0:16, :], nf, writes=[b_pst])
        ps, pb = nps()
        tr(ps[:, 0:128], pst[:, :], 128, [b_pst], pb)
        cp(PT[:, 0:128], ps[:, 0:128], [pb], [b_PT])
        for (t0, n) in TB:
            xb, rstd = norm_stats(t0, n)
            for kc in range(16):
                stt(xb[:, kc, :], xb[:, kc, :], PT[:, kc:kc + 1], rstd, ALU.mult, ALU.mult, b_S[0:5] + [b_PT], b_S[0:4])
            for tt_i in range(n // 128):
                for g in range(4):
                    ps, pb = nps()
                    for j in range(4):
                        kc = g * 4 + j
                        tr(ps[:, j * 128:(j + 1) * 128], xb[:, kc, tt_i * 128:(tt_i + 1) * 128], 128, b_S[0:4], pb)
                    act(rowst[:, 0:512], ps[:, :], AF.Identity, [pb], [b_rowst])
                    r0 = t0 + tt_i * 128
                    if t0 < TP:
                        P.dma("sp", O["y_p"].ap()[r0:r0 + 128, g * 512:(g + 1) * 512], rowst[:, 0:512], reads=[b_rowst], writes=[b_out["y_p"]])
                    else:
                        P.dma("sp", O["y_s"].ap()[:, g * 512:(g + 1) * 512], rowst[:, 0:512], reads=[b_rowst], writes=[b_out["y_s"]])

        P.final_wait("sp", list(b_out.values()))
        P.emit()
    return nc


_NC_CACHE = {}


def kernel(x_prompt, x_sample, c_prompt, c_sample, state_lru_conv, state_lru_h, state_rwkv_shift, state_rwkv_S,
           state_ffn_conv, **weights):
    f = lambda a: np.ascontiguousarray(np.asarray(a, dtype=np.float32))
    if "nc" not in _NC_CACHE:
        _NC_CACHE["nc"] = build()
    nc = _NC_CACHE["nc"]
    wmap = {k: f(weights[k]) for k in W_SHAPES}
    in_maps = []
    for core in range(8):
        b = core % 4
        ss = slice(core * NS, (core + 1) * NS)
        m = dict(wmap)
        m["xp"] = f(x_prompt[b])
        m["xs"] = f(np.asarray(x_sample)[ss].reshape(TS, D))
        m["cc"] = f(np.concatenate([np.asarray(c_prompt)[b:b + 1], np.asarray(c_sample)[ss]], axis=0))
        m["st_lc"] = f(np.asarray(state_lru_conv)[:, ss])
        m["st_lh"] = f(np.asarray(state_lru_h)[:, ss])
        m["st_sh"] = f(np.asarray(state_rwkv_shift)[:, ss])
        m["st_S"] = f(np.asarray(state_rwkv_S)[:, ss])
        m["st_fc"] = f(np.asarray(state_ffn_conv)[:, ss])
        in_maps.append(m)
    res = run_bass_kernel_spmd(nc, in_maps, core_ids=list(range(8)))
    R = res.results
    y_prompt = np.stack([R[b]["y_p"] for b in range(4)], axis=0)
    y_sample = np.concatenate([R[c]["y_s"].reshape(NS, 8, D) for c in range(8)], axis=0)
    outs = [y_prompt, y_sample]
    names = ["o_lc", "o_lh", "o_sh", "o_S", "o_fc"]
    for nm in names:
        outs.append(np.stack([R[b][nm][:, 0] for b in range(4)], axis=1))
    for nm in names:
        outs.append(np.concatenate([R[c][nm][:, 1:] for c in range(8)], axis=1))
    return tuple(np.ascontiguousarray(o.astype(np.float32, copy=False)) for o in outs)
```

```python
import math
import numpy as np
from contextlib import ExitStack
import concourse.bass as bass
import concourse.mybir as mybir
from concourse.bass_utils import run_bass_kernel_spmd

F32 = mybir.dt.float32
BF16 = mybir.dt.bfloat16
ALU = mybir.AluOpType
AF = mybir.ActivationFunctionType
AX = mybir.AxisListType

ENGS = ("pe", "act", "dve", "pool", "sp")

D = 2048
TP = 2048
NS = 16
TS = 128
T = TP + TS
NSEQ = 17
DL = 1024
NRW = 3360
NIN = 9504
DFF = 5632
DEPTH = 4
TB = [(0, 512), (512, 512), (1024, 512), (1536, 512), (2048, 128)]
SLOT = 2240
NLAYERS = DEPTH


class Buf:
    def __init__(self, prog, name):
        self.prog = prog
        self.name = name
        self.w = {}
        self.r = {}
        self.dsem = None
        self.dcnt = 0

    def dma_sem(self):
        if self.dsem is None:
            self.dsem = self.prog.new_sem("d_" + self.name)
        return self.dsem


class Prog:
    def __init__(self, nc, stack):
        self.nc = nc
        self.stack = stack
        self.streams = {e: [] for e in ENGS}
        self.esem = {e: self.new_sem("e_" + e) for e in ENGS}
        self.ecnt = {e: 0 for e in ENGS}
        self.seen = {e: {} for e in ENGS}
        self.nbuf = 0
        self.pend = None

    def pe_group(self, fn, reads, pb):
        if self.pend is not None and self.pend[2] is pb:
            self.pend[0].append(fn)
            self.pend[1].extend(reads)
        else:
            self.flush_pe()
            self.pend = ([fn], list(reads), pb)

    def flush_pe(self):
        if self.pend is None:
            return
        fns, reads, pb = self.pend
        self.pend = None
        uniq = []
        for b in reads:
            if all(b is not u for u in uniq):
                uniq.append(b)

        def fn_all(e, fns=fns):
            last = None
            for f in fns:
                last = f(e)
            return last

        self._op("pe", fn_all, uniq, [pb])

    def new_sem(self, name):
        return self.stack.enter_context(self.nc.semaphore(name))

    def buf(self, name=None):
        self.nbuf += 1
        return Buf(self, name or f"b{self.nbuf}")

    def sbuf(self, name, shape, dtype):
        return self.stack.enter_context(self.nc.sbuf_tensor(name, list(shape), dtype))

    def psum(self, name, shape, dtype=F32):
        return self.stack.enter_context(self.nc.psum_tensor(name, list(shape), dtype))

    def _deps(self, eng, reads, writes):
        need = {}

        def add(tok):
            k = id(tok[0])
            if k not in need or need[k][1] < tok[1]:
                need[k] = tok

        for b in reads:
            for tok in b.w.values():
                add(tok)
        for b in writes:
            for tok in b.w.values():
                add(tok)
            for tok in b.r.values():
                add(tok)
        waits = []
        seen = self.seen[eng]
        for k, (sem, val) in need.items():
            if seen.get(k, 0) < val:
                seen[k] = val
                waits.append((sem, val))
        return waits

    def _mark(self, tok, reads, writes):
        k = id(tok[0])
        for b in reads:
            if b in writes:
                continue
            b.r[k] = tok
        for b in writes:
            b.w = {k: tok}
            b.r = {}

    def op(self, eng, fn, reads=(), writes=()):
        self.flush_pe()
        self._op(eng, fn, reads, writes)

    def _op(self, eng, fn, reads=(), writes=()):
        waits = self._deps(eng, reads, writes)
        self.ecnt[eng] += 1
        sem = self.esem[eng]
        self._mark((sem, self.ecnt[eng]), reads, writes)

        def run(e, waits=waits, fn=fn, sem=sem):
            for s, v in waits:
                e.wait_ge(s, v)
            fn(e).then_inc(sem, 1)

        self.streams[eng].append(run)

    def dma(self, q, out, in_, reads=(), writes=(), **kw):
        self.flush_pe()
        waits = self._deps(q, reads, writes)
        wb = writes[0]
        sem = wb.dma_sem()
        wb.dcnt += 16
        self._mark((sem, wb.dcnt), reads, writes)

        def run(e, waits=waits, sem=sem, out=out, in_=in_, kw=kw):
            for s, v in waits:
                e.wait_ge(s, v)
            e.dma_start(out=out, in_=in_, **kw).then_inc(sem, 16)

        self.streams[q].append(run)

    def final_wait(self, eng, bufs):
        self.flush_pe()
        waits = self._deps(eng, bufs, bufs)

        def run(e, waits=waits):
            for s, v in waits:
                e.wait_ge(s, v)

        self.streams[eng].append(run)

    def emit(self):
        self.flush_pe()
        nc = self.nc
        with nc.Block() as block:
            @block.tensor
            def _(e):
                for f in self.streams["pe"]:
                    f(e)

            @block.scalar
            def _(e):
                for f in self.streams["act"]:
                    f(e)

            @block.vector
            def _(e):
                for f in self.streams["dve"]:
                    f(e)

            @block.gpsimd
            def _(e):
                for f in self.streams["pool"]:
                    f(e)

            @block.sync
            def _(e):
                for f in self.streams["sp"]:
                    f(e)


W_SHAPES = {
    "w_ada": [DEPTH, D, 6 * D], "b_ada": [DEPTH, 6 * D], "norm_mix": [DEPTH, D], "norm_ffn": [DEPTH, D],
    "w_in": [DEPTH, D, NIN], "lru_conv_w": [DEPTH, 4, DL], "lru_conv_b": [DEPTH, DL],
    "lru_wa": [DEPTH, 16, 64, 64], "lru_ba": [DEPTH, DL], "lru_wi": [DEPTH, 16, 64, 64], "lru_bi": [DEPTH, DL],
    "lru_lambda": [DEPTH, DL], "w_pa": [DEPTH, DL, D], "rw_mu": [DEPTH, NRW], "rw_w0": [DEPTH, DL],
    "rw_w2": [DEPTH, 64, DL], "rw_a0": [DEPTH, DL], "rw_a2": [DEPTH, 64, DL], "rw_g2": [DEPTH, 160, DL],
    "rw_kk": [DEPTH, DL], "rw_ka": [DEPTH, DL], "rw_rk": [DEPTH, 16, 64], "rw_gn_g": [DEPTH, DL],
    "rw_gn_b": [DEPTH, DL], "w_pb": [DEPTH, DL, D], "w_o": [DEPTH, D, D], "w_up": [DEPTH, D, 2 * DFF],
    "ffn_conv_w": [DEPTH, 3, 2 * DFF], "ffn_conv_b": [DEPTH, 2 * DFF], "w_down": [DEPTH, DFF, D],
    "norm_final": [D],
}
IN_SHAPES = {
    "xp": [TP, D], "xs": [TS, D], "cc": [NSEQ, D], "st_lc": [DEPTH, NS, 3, DL], "st_lh": [DEPTH, NS, DL],
    "st_sh": [DEPTH, NS, NRW], "st_S": [DEPTH, NS, 16, 64, 64], "st_fc": [DEPTH, NS, 2, 2 * DFF],
}
OUT_SHAPES = {
    "y_p": [TP, D], "y_s": [TS, D], "o_lc": [DEPTH, NSEQ, 3, DL], "o_lh": [DEPTH, NSEQ, DL],
    "o_sh": [DEPTH, NSEQ, NRW], "o_S": [DEPTH, NSEQ, 16, 64, 64], "o_fc": [DEPTH, NSEQ, 2, 2 * DFF],
}

PROWS = [("b_ada", 96), ("norm_mix", 16), ("norm_ffn", 16), ("lru_conv_w", 32), ("lru_conv_b", 8),
         ("lru_ba", 8), ("lru_bi", 8), ("lru_lambda", 8), ("rw_mu", 27), ("rw_w0", 8), ("rw_a0", 8),
         ("ffn_conv_w", 264), ("ffn_conv_b", 88)]
PCOL = {}
_c = 0
for _n, _r in PROWS:
    PCOL[_n] = _c
    _c += _r
NPROW = _c
NPG = (NPROW + 127) // 128


def build():
    nc = bass.Bass("TRN2", target_bir_lowering=False)
    I = {k: nc.dram_tensor(k, s, F32, kind="ExternalInput") for k, s in IN_SHAPES.items()}
    Wt = {k: nc.dram_tensor(k, s, F32, kind="ExternalInput") for k, s in W_SHAPES.items()}
    O = {k: nc.dram_tensor(k, s, F32, kind="ExternalOutput") for k, s in OUT_SHAPES.items()}
    xT_h = nc.dram_tensor("xT_scr", [D, T], F32, kind="Internal")
    rq_h = nc.dram_tensor("rq_scr", [6, 16, T, 64], F32, kind="Internal")
    tk_h = nc.dram_tensor("tk_scr", [8, T, DL], F32, kind="Internal")
    cc_h = nc.dram_tensor("cc_scr", [32, T], F32, kind="Internal")
    sg_h = nc.dram_tensor("sg_scr", [2 * D, T], BF16, kind="Internal")
    ga_h = nc.dram_tensor("ga_scr", [DL, T], BF16, kind="Internal")
    oT_h = nc.dram_tensor("oT_scr", [DL, T], BF16, kind="Internal")
    aT_h = nc.dram_tensor("aT_scr", [DFF, T], BF16, kind="Internal")
    xT = xT_h.ap()
    tk = tk_h.ap()
    sg = sg_h.ap()
    ga_d = ga_h.ap()
    oT_d = oT_h.ap()
    aT_d = aT_h.ap()

    with ExitStack() as st:
        P = Prog(nc, st)
        actT = P.sbuf("actT", [128, 16, T], BF16); b_act = P.buf("actT")
        arena = P.sbuf("arena", [128, 5, SLOT], F32)
        b_S = [P.buf(f"S{i}") for i in range(5)]
        wsl = [P.sbuf(f"wsl{i}", [128, 16, 256], BF16) for i in range(2)]
        b_w = [P.buf(f"w{i}") for i in range(2)]
        blk = P.sbuf("blk", [128, 16384], BF16); b_blk = P.buf("blk")
        blkF = blk[:, :].bitcast(F32)
        ident = P.sbuf("ident", [128, 128], F32); b_id = P.buf("ident")
        ones = P.sbuf("ones", [128, 128], F32); b_ones = P.buf("ones")
        scT = P.sbuf("scT", [128, 16, NSEQ], BF16); b_scT = P.buf("scT")
        modT = P.sbuf("modT", [128, 96, NSEQ], F32); b_mod = P.buf("modT")
        A1 = P.sbuf("A1", [128, 16, NSEQ], F32); b_A1 = P.buf("A1")
        PT = P.sbuf("PT", [128, NPG * 128], F32); b_PT = P.buf("PT")
        pst = P.sbuf("pst", [128, 128], F32); b_pst = P.buf("pst")
        cA = P.sbuf("cA", [128, 16], F32); b_cA = P.buf("cA")
        bda = P.sbuf("bda", [128, 8, 128], BF16); b_bda = P.buf("bda")
        bdi = P.sbuf("bdi", [128, 8, 128], BF16); b_bdi = P.buf("bdi")
        lw2 = P.sbuf("lw2", [128, DL], BF16); b_lw2 = P.buf("lw2")
        g2a = P.sbuf("g2a", [128, DL], BF16); b_g2a = P.buf("g2a")
        g2b = P.sbuf("g2b", [32, DL], BF16); b_g2b = P.buf("g2b")
        stlc = P.sbuf("stlc", [128, 8, 48], F32); b_stlc = P.buf("stlc")
        stlh = P.sbuf("stlh", [128, 8, 16], F32); b_stlh = P.buf("stlh")
        stsh = P.sbuf("stsh", [128, 27, 16], F32); b_stsh = P.buf("stsh")
        olc = P.sbuf("olc", [128, 8, 51], F32); b_olc = P.buf("olc")
        olh = P.sbuf("olh", [128, 8, NSEQ], F32); b_olh = P.buf("olh")
        osh = P.sbuf("osh", [128, 27, NSEQ], F32); b_osh = P.buf("osh")
        ofc = P.sbuf("ofc", [128, 8, 34], F32); b_ofc = P.buf("ofc")
        stfc = P.sbuf("stfc", [128, 8, 32], F32); b_stfc = P.buf("stfc")
        tmpb = P.sbuf("tmpb", [128, 512], F32); b_tmpb = P.buf("tmpb")
        tmpc = P.sbuf("tmpc", [128, 512], F32); b_tmpc = P.buf("tmpc")
        rowst = blkF[:, 7168:8192]; b_rowst = P.buf("rowst")
        b_rowst2 = P.buf("rowst2")
        Sst = blkF[:, 0:2048]; b_Sst = b_blk
        Stmp = blkF[:, 2048:4096]; b_Stmp = b_blk
        skk = P.sbuf("skk", [128, 32], F32); b_skk = P.buf("skk")
        sgt = P.sbuf("sgt", [128, 2, 1024], BF16); b_sgt = [P.buf("sgt0"), P.buf("sgt1")]
        skys = P.sbuf("skys", [128, 2 * 2 * 16 * 8], F32); b_skys = [P.buf("skys0"), P.buf("skys1")]
        ccb = P.sbuf("ccb", [128, 2 * 32], F32); b_ccb = [P.buf("ccb0"), P.buf("ccb1")]
        dummy = P.sbuf("fence_dummy", [128, 8], F32)
        b_cc = P.buf("cc")
        vbuf = P.sbuf("vbuf", [128, 32 * 8], F32); b_vbuf = P.buf("vbuf")
        ybuf = P.sbuf("ybuf", [128, 32 * 8], F32); b_ybuf = P.buf("ybuf")
        sm = P.sbuf("sm", [128, 64], F32); b_sm = P.buf("sm")
        ct = [blkF[:, 4096 + i * 1024:4096 + (i + 1) * 1024] for i in range(3)]
        b_ct = [b_blk, b_blk, b_blk]
        PS = [P.psum(f"ps{i}", [128, 512]) for i in range(8)]
        b_ps = [P.buf(f"ps{i}") for i in range(8)]
        psc = [0]

        def nps():
            i = psc[0] % 8
            psc[0] += 1
            return PS[i], b_ps[i]

        b_xT = [P.buf(f"xT{c}") for c in range(16)]
        b_rq = P.buf("rq"); b_tk = [P.buf(f"tk{i}") for i in range(8)]
        b_sg = P.buf("sg"); b_ga = P.buf("ga"); b_oT = P.buf("oT"); b_aT = P.buf("aT")
        b_out = {k: P.buf("o_" + k) for k in OUT_SHAPES}

        def slot(i, n=SLOT, off=0):
            return arena[:, i, off:off + n]

        wcnt = [0]

        def wslot():
            i = wcnt[0] % 2
            wcnt[0] += 1
            return wsl[i], b_w[i]

        def act(out, in_, func, reads, writes, bias=None, scale=None):
            kw = {}
            if bias is not None:
                kw["bias"] = bias
            if scale is not None:
                kw["scale"] = scale
            P.op("act", lambda e: e.activation(out=out, in_=in_, func=func, **kw), reads=reads, writes=writes)

        def tt(out, in0, in1, op, reads, writes, eng="dve"):
            P.op(eng, lambda e: e.tensor_tensor(out=out, in0=in0, in1=in1, op=op), reads=reads, writes=writes)

        def ts(out, in0, s1, s2, op0, op1, reads, writes, eng="dve"):
            if s2 is None:
                P.op(eng, lambda e: e.tensor_scalar(out=out, in0=in0, scalar1=s1, scalar2=None, op0=op0), reads=reads, writes=writes)
            else:
                P.op(eng, lambda e: e.tensor_scalar(out=out, in0=in0, scalar1=s1, scalar2=s2, op0=op0, op1=op1), reads=reads, writes=writes)

        def stt(out, in0, scalar, in1, op0, op1, reads, writes):
            P.op("dve", lambda e: e.scalar_tensor_tensor(out=out, in0=in0, scalar=scalar, in1=in1, op0=op0, op1=op1),
                 reads=reads, writes=writes)

        def cp(out, in_, reads, writes, eng="dve"):
            P.op(eng, lambda e: e.tensor_copy(out=out, in_=in_), reads=reads, writes=writes)

        def mm(ps_ap, lhsT, rhs, start, stop, reads, pb):
            P.pe_group(lambda e: e.matmul(ps_ap, lhsT=lhsT, rhs=rhs, start=start, stop=stop), list(reads), pb)

        def tr(ps_ap, in_, n_in_part, reads, pb):
            P.pe_group(lambda e: e.transpose(ps_ap, in_, ident[0:n_in_part, 0:n_in_part]), list(reads) + [b_id], pb)

        def memset(ap, val, writes, eng="pool"):
            P.op(eng, lambda e: e.memset(ap, val), writes=writes)

        def fence(bufs, eng="pool"):
            P.op(eng, lambda e: e.memset(dummy[:, 0:1], 0.0), writes=list(bufs))

        memset(ident[:], 0.0, [b_id])
        P.op("pool", lambda e: e.affine_select(out=ident[:], in_=ident[:], compare_op=ALU.not_equal, fill=1.0,
                                               base=0, pattern=[[-1, 128]], channel_multiplier=1),
             reads=[b_id], writes=[b_id])
        memset(ones[:], 1.0, [b_ones])

        P.dma("sp", rowst[0:NSEQ, :], I["cc"].ap()[:, 0:1024], writes=[b_rowst, b_rowst2])
        for half in range(2):
            if half == 1:
                P.dma("sp", rowst[0:NSEQ, :], I["cc"].ap()[:, 1024:2048], writes=[b_rowst, b_rowst2])
            act(rowst[0:NSEQ, :], rowst[0:NSEQ, :], AF.Silu, [b_rowst, b_rowst2], [b_rowst, b_rowst2])
            ps, pb = nps()
            for j in range(8):
                tr(ps[:, j * NSEQ:(j + 1) * NSEQ], rowst[0:NSEQ, j * 128:(j + 1) * 128], NSEQ, [b_rowst, b_rowst2], pb)
            cp(scT[:, half * 8:(half + 1) * 8, :].rearrange("p a b -> p (a b)"), ps[:, 0:8 * NSEQ], [pb], [b_scT])

        xT_v = xT.rearrange("(kc p) t -> p kc t", p=128)
        for i in range(17):
            src = I["xp"].ap()[i * 128:(i + 1) * 128, :] if i < 16 else I["xs"].ap()[:, :]
            P.dma("sp", slot(0, 2048), src, writes=[b_S[0]])
            for g in range(4):
                ps, pb = nps()
                for j in range(4):
                    kc = g * 4 + j
                    tr(ps[:, j * 128:(j + 1) * 128], slot(0, 128, kc * 128), 128, [b_S[0]], pb)
                P.op("act", lambda e, ps=ps, g=g: e.activation(out=slot(1, 512, g * 512), in_=ps[:, :], func=AF.Identity),
                     reads=[pb], writes=[b_S[1]])
            P.dma("sp", xT_v[:, :, i * 128:(i + 1) * 128], slot(1, 2048).rearrange("p (kc t) -> p kc t", kc=16),
                  reads=[b_S[1]], writes=b_xT)

        def load_params(l):
            for g in range(NPG):
                memset(pst[:], 0.0, [b_pst])
                r = 0
                for name, nrows in PROWS:
                    for rr in range(nrows):
                        grow = PCOL[name] + rr
                        if grow // 128 != g:
                            continue
                    lo = max(PCOL[name], g * 128)
                    hi = min(PCOL[name] + nrows, (g + 1) * 128)
                    if lo >= hi:
                        continue
                    r0 = lo - PCOL[name]
                    n = hi - lo
                    flat = Wt[name].ap()[l]
                    if name in ("lru_conv_w", "ffn_conv_w"):
                        flat = flat.rearrange("j c -> (j c)")
                    if name == "rw_mu":
                        nfull = min(n, max(0, 26 - r0))
                        if nfull > 0:
                            P.dma("act", pst[lo - g * 128:lo - g * 128 + nfull, :],
                                  flat[r0 * 128:(r0 + nfull) * 128].rearrange("(r c) -> r c", c=128), writes=[b_pst])
                        if r0 + n == 27:
                            P.dma("act", pst[hi - 1 - g * 128:hi - g * 128, 0:32],
                                  flat[26 * 128:26 * 128 + 32].rearrange("(r c) -> r c", c=32), writes=[b_pst])
                    else:
                        P.dma("act", pst[lo - g * 128:hi - g * 128, :],
                              flat[r0 * 128:(r0 + n) * 128].rearrange("(r c) -> r c", c=128), writes=[b_pst])
                ps, pb = nps()
                tr(ps[:, 0:128], pst[:, :], 128, [b_pst], pb)
                cp(PT[:, g * 128:(g + 1) * 128], ps[:, 0:128], [pb], [b_PT])

        def pcol(name, row):
            c = PCOL[name] + row
            return PT[:, c:c + 1]

        def loadT(dst3, bdst, src_rows_fn, nrows, nchunks, last_cols=128):
            c = 0
            while c < nchunks:
                g = min(8, nchunks - c)
                width = (g - 1) * 128 + (last_cols if c + g == nchunks else 128)
                P.dma("sp", rowst[0:nrows, 0:width], src_rows_fn(c * 128, width), writes=[b_rowst, b_rowst2])
                per = 512 // nrows
                j = 0
                while j < g:
                    gg = min(per, g - j)
                    ps, pb = nps()
                    for k in range(gg):
                        cc_ = c + j + k
                        ncol = last_cols if cc_ == nchunks - 1 else 128
                        tr(ps[0:ncol, k * nrows:(k + 1) * nrows], rowst[0:nrows, (j + k) * 128:(j + k) * 128 + ncol], nrows, [b_rowst, b_rowst2], pb)
                    cp(dst3[:, c + j:c + j + gg, :].rearrange("p a b -> p (a b)"), ps[:, 0:gg * nrows], [pb], [bdst])
                    j += gg
                c += g

        def flushT(src3, bsrc, nchunks, n, dst_fn, bout, last_cols=128):
            c = 0
            while c < nchunks:
                g = min(4, nchunks - c)
                ps, pb = nps()
                width = 0
                for k in range(g):
                    ncol = last_cols if c + k == nchunks - 1 else 128
                    tr(ps[0:n, k * 128:k * 128 + ncol], src3[0:ncol, c + k, :], ncol, [bsrc], pb)
                    width += ncol
                cp(rowst[0:n, 0:width], ps[0:n, 0:width], [pb], [b_rowst])
                P.dma("sp", dst_fn(c * 128, width), rowst[0:n, 0:width], reads=[b_rowst], writes=[bout])
                c += g

        def wload(dst, src, bw):
            P.dma("pool", dst, src, writes=[bw])

        class WStream:
            def __init__(self, loaders):
                self.loaders = loaders
                self.slots = {}
                self.nxt = 0

            def get(self, i):
                while self.nxt < len(self.loaders) and self.nxt <= i + 1:
                    ws, bw = wslot()
                    self.loaders[self.nxt](ws, bw)
                    self.slots[self.nxt] = (ws, bw)
                    self.nxt += 1
                return self.slots[i]

        def norm_stats(t0, n):
            xb = arena[:, 0:4, :].rearrange("p a b -> p (a b)")[:, 0:16 * n].rearrange("p (kc t) -> p kc t", kc=16)
            P.dma("sp", xb, xT_v[:, :, t0:t0 + n], reads=b_xT, writes=b_S[0:4])
            ps, pb = nps()
            for kc in range(16):
                sqb, bsq = (tmpb, b_tmpb) if kc % 2 == 0 else (tmpc, b_tmpc)
                act(sqb[:, 0:n], xb[:, kc, :], AF.Square, b_S[0:4], [bsq])
                mm(ps[:, 0:n], ones[:], sqb[:, 0:n], kc == 0, kc == 15, [b_ones, bsq], pb)
            rstd = slot(4, n)
            act(rstd, ps[:, 0:n], AF.Sqrt, [pb], [b_S[4]], bias=1e-6, scale=1.0 / D)
            P.op("dve", lambda e: e.reciprocal(out=rstd, in_=rstd), reads=[b_S[4]], writes=[b_S[4]])
            return xb, rstd

        def norm_mod(Aap, Bap, bA, bB):
            for (t0, n) in TB:
                xb, rstd = norm_stats(t0, n)
                for kc in range(16):
                    tt(xb[:, kc, :], xb[:, kc, :], rstd, ALU.mult, b_S[0:5], b_S[0:4])
                    if t0 < TP:
                        ts(actT[:, kc, t0:t0 + n], xb[:, kc, :], Aap[:, kc, 0:1], Bap[:, kc, 0:1], ALU.mult, ALU.add,
                           b_S[0:4] + [bA, bB], [b_act])
                    else:
                        x3 = xb[:, kc, :].rearrange("p (s t) -> p s t", s=NS)
                        tt(x3, x3, Aap[:, kc, 1:NSEQ].unsqueeze(2).to_broadcast([128, NS, 8]), ALU.mult, b_S[0:4] + [bA], b_S[0:4])
                        tt(actT[:, kc, t0:t0 + n].rearrange("p (s t) -> p s t", s=NS), x3,
                           Bap[:, kc, 1:NSEQ].unsqueeze(2).to_broadcast([128, NS, 8]), ALU.add, b_S[0:4] + [bB], [b_act])

        rcnt = [0]

        def resid_update(c, t0, n, ps, pb, gcol0):
            if t0 < TP:
                k_ = rcnt[0] % 2
                rcnt[0] += 1
                xt_, bxt_ = (tmpb, b_tmpb) if k_ == 0 else (tmpc, b_tmpc)
                xs_ = xt_[:, 0:n]
                P.dma("sp", xs_, xT_v[:, c, t0:t0 + n], reads=[b_xT[c]], writes=[bxt_])
                stt(xs_, ps[:, 0:n], modT[:, gcol0 + c, 0:1], xs_, ALU.mult, ALU.add, [pb, b_mod, bxt_], [bxt_])
                P.dma("act", xT_v[:, c, t0:t0 + n], xs_, reads=[bxt_], writes=[b_xT[c]])
                return
            xs_ = tmpb[:, 0:n]
            P.dma("sp", xs_, xT_v[:, c, t0:t0 + n], reads=[b_xT[c]], writes=[b_tmpb])
            if t0 < TP:
                stt(xs_, ps[:, 0:n], modT[:, gcol0 + c, 0:1], xs_, ALU.mult, ALU.add, [pb, b_mod, b_tmpb], [b_tmpb])
            else:
                g3 = modT[:, gcol0 + c, 1:NSEQ].unsqueeze(2).to_broadcast([128, NS, 8])
                t3 = tmpc[:, 0:n].rearrange("p (s t) -> p s t", s=NS)
                tt(t3, ps[:, 0:n].rearrange("p (s t) -> p s t", s=NS), g3, ALU.mult, [pb, b_mod], [b_tmpc])
                tt(xs_, xs_, tmpc[:, 0:n], ALU.add, [b_tmpb, b_tmpc], [b_tmpb])
            P.dma("act", xT_v[:, c, t0:t0 + n], xs_, reads=[b_tmpb], writes=[b_xT[c]])

        tkc = [0]

        def to_tokmajor(src_slot_ap, bsrc, qi, c0):
            tkv = tk[qi].rearrange("(tt p) c -> p tt c", p=128)
            for g0 in range(0, 17, 4):
                g = min(4, 17 - g0)
                ps, pb = nps()
                for k in range(g):
                    tr(ps[:, k * 128:(k + 1) * 128], src_slot_ap[:, (g0 + k) * 128:(g0 + k + 1) * 128], 128, [bsrc], pb)
                hh_ = tkc[0] % 2
                tkc[0] += 1
                rs = rowst[:, hh_ * 512:hh_ * 512 + g * 128]
                brs = b_rowst if hh_ == 0 else b_rowst2
                act(rs, ps[:, 0:g * 128], AF.Identity, [pb], [brs])
                P.dma("sp", tkv[:, g0:g0 + g, c0:c0 + 128], rs.rearrange("p (a b) -> p a b", a=g),
                      reads=[brs], writes=[b_tk[qi]])

        bx = [[P.buf(f"x_{j}_{p}") for p in range(2)] for j in range(5)]
        bv = [P.buf(f"v_{p}") for p in range(2)]
        by = [P.buf(f"y_{p}") for p in range(2)]
        bSS = [P.buf("SA"), P.buf("SB")]
        bT2 = P.buf("T2"); bTt = P.buf("Tt")
        bKV = [P.buf(f"KV_{i}") for i in range(4)]
        b_ab1 = P.buf("ab1")
        for l in range(NLAYERS):
            load_params(l)
            memset(bda[:], 0.0, [b_bda]); memset(bdi[:], 0.0, [b_bdi])
            for (dst, bd_, nm) in ((bda, b_bda, "lru_wa"), (bdi, b_bdi, "lru_wi")):
                wv = Wt[nm].ap()[l].rearrange("(c two) d e -> two d c e", two=2)
                P.dma("pool", dst[0:64, :, 0:64], wv[0], writes=[bd_])
                P.dma("pool", dst[64:128, :, 64:128], wv[1], writes=[bd_])
            P.dma("pool", lw2[0:64, :], Wt["rw_w2"].ap()[l], writes=[b_lw2])
            P.dma("pool", lw2[64:128, :], Wt["rw_a2"].ap()[l], writes=[b_lw2])
            P.dma("pool", g2a[:, :], Wt["rw_g2"].ap()[l, 0:128, :], writes=[b_g2a])
            P.dma("pool", g2b[:, :], Wt["rw_g2"].ap()[l, 128:160, :], writes=[b_g2b])
            lam = PT[:, PCOL["lru_lambda"]:PCOL["lru_lambda"] + 8]
            act(cA[:, 0:8], lam, AF.Exp, [b_PT], [b_cA], scale=-1.0)
            act(cA[:, 0:8], cA[:, 0:8], AF.Ln, [b_cA], [b_cA], bias=1.0)
            ts(cA[:, 8:16], cA[:, 0:8], -16.0, None, ALU.mult, None, [b_cA], [b_cA])
            ts(cA[:, 0:8], cA[:, 0:8], -8.0, None, ALU.mult, None, [b_cA], [b_cA])
            loadT(stlc, b_stlc, lambda c0, w: I["st_lc"].ap()[l].rearrange("s j c -> (s j) c")[:, c0:c0 + w], 48, 8)
            loadT(stlh, b_stlh, lambda c0, w: I["st_lh"].ap()[l][:, c0:c0 + w], 16, 8)
            loadT(stsh, b_stsh, lambda c0, w: I["st_sh"].ap()[l][:, c0:c0 + w], 16, 27, last_cols=32)

            wada = Wt["w_ada"].ap()[l].rearrange("(kc p) c -> p kc c", p=128)
            st1 = WStream([(lambda ws, bw, b=b: wload(ws[:, :, 0:256], wada[:, :, b * 256:(b + 1) * 256], bw)) for b in range(48)])
            for blk_i in range(48):
                ws, bw = st1.get(blk_i)
                ps, pb = nps()
                for j2 in range(2):
                    for kc in range(16):
                        mm(ps[:, j2 * 32:j2 * 32 + NSEQ], ws[:, kc, j2 * 128:(j2 + 1) * 128], scT[:, kc, :], kc == 0, kc == 15,
                           [bw, b_scT], pb)
                for j2 in range(2):
                    j = blk_i * 2 + j2
                    act(modT[:, j, :], ps[:, j2 * 32:j2 * 32 + NSEQ], AF.Identity, [pb, b_PT], [b_mod], bias=pcol("b_ada", j))

            def make_A(sc0, nm_name):
                nmb = PT[:, PCOL[nm_name]:PCOL[nm_name] + 16].unsqueeze(2).to_broadcast([128, 16, NSEQ])
                stt(A1[:, :, :], modT[:, sc0:sc0 + 16, :], 1.0, nmb, ALU.add, ALU.mult, [b_mod, b_PT], [b_A1])

            make_A(16, "norm_mix")
            norm_mod(A1, modT[:, 0:16, :], b_A1, b_mod)

            win = Wt["w_in"].ap()[l].rearrange("(kc p) c -> p kc c", p=128)

            def proj_block(ws, wc0, m, t0, n, bw):
                ps, pb = nps()
                for kc in range(16):
                    mm(ps[0:m, 0:n], ws[:, kc, wc0:wc0 + m], actT[:, kc, t0:t0 + n], kc == 0, kc == 15, [bw, b_act], pb)
                return ps, pb

            def lru_loader(c):
                def f(ws, bw):
                    wload(ws[:, :, 0:128], win[:, :, c * 128:(c + 1) * 128], bw)
                    wload(ws[:, :, 128:256], win[:, :, DL + c * 128:DL + (c + 1) * 128], bw)
                return f
            st3a = WStream([lru_loader(c) for c in range(8)])
            for c in range(8):
                ws, bw = st3a.get(c)
                LX = slot(0); XC = slot(1); AA = slot(3); INP = slot(4)
                LXs = LX[:, 2051:2051 + 176].rearrange("p (s j) -> p s j", s=NS)
                memset(LX[:, 0:3], 0.0, [b_S[0]])
                cp(LXs[:, :, 0:3], stlc[:, c, :].rearrange("p (s j) -> p s j", s=NS), [b_stlc], [b_S[0]], eng="pool")
                for (t0, n) in TB:
                    ps, pb = proj_block(ws, 0, 128, t0, n, bw)
                    if t0 < TP:
                        act(LX[:, 3 + t0:3 + t0 + n], ps[:, 0:n], AF.Identity, [pb], [b_S[0]])
                    else:
                        act(LXs[:, :, 3:11], ps[:, 0:n].rearrange("p (s t) -> p s t", s=NS), AF.Identity, [pb], [b_S[0]])
                cp(olc[:, c, 0:3], LX[:, 2048:2051], [b_S[0]], [b_olc], eng="pool")
                cp(olc[:, c, 3:51].rearrange("p (s j) -> p s j", s=NS), LXs[:, :, 8:11], [b_S[0]], [b_olc], eng="pool")
                XCs = XC[:, TP:T].rearrange("p (s t) -> p s t", s=NS)
                ts(XC[:, 0:TP], LX[:, 0:TP], pcol("lru_conv_w", 0 * 8 + c), pcol("lru_conv_b", c), ALU.mult, ALU.add,
                   [b_S[0], b_PT], [b_S[1]])
                ts(XCs, LXs[:, :, 0:8], pcol("lru_conv_w", 0 * 8 + c), pcol("lru_conv_b", c), ALU.mult, ALU.add,
                   [b_S[0], b_PT], [b_S[1]])
                for j in range(1, 4):
                    stt(XC[:, 0:TP], LX[:, j:j + TP], pcol("lru_conv_w", j * 8 + c), XC[:, 0:TP], ALU.mult, ALU.add,
                        [b_S[0], b_S[1], b_PT], [b_S[1]])
                    stt(XCs, LXs[:, :, j:j + 8], pcol("lru_conv_w", j * 8 + c), XCs, ALU.mult, ALU.add,
                        [b_S[0], b_S[1], b_PT], [b_S[1]])
                XCb = slot(2).bitcast(BF16)[:, 0:T]
                act(XCb, XC[:, 0:T], AF.Identity, [b_S[1]], [b_S[2]])
                for (t0, n) in TB:
                    psr, pbr = nps()
                    mm(psr[:, 0:n], bda[:, c, :], XCb[:, t0:t0 + n], True, True, [b_bda, b_S[2]], pbr)
                    psi, pbi = nps()
                    mm(psi[:, 0:n], bdi[:, c, :], XCb[:, t0:t0 + n], True, True, [b_bdi, b_S[2]], pbi)
                    rr = INP[:, t0:t0 + n]
                    act(rr, psr[:, 0:n], AF.Sigmoid, [pbr, b_PT], [b_S[4]], bias=pcol("lru_ba", c))
                    act(AA[:, t0:t0 + n], rr, AF.Exp, [b_S[4], b_cA], [b_S[3]], scale=cA[:, c:c + 1])
                    act(rr, rr, AF.Exp, [b_S[4], b_cA], [b_S[4]], scale=cA[:, 8 + c:9 + c])
                    act(rr, rr, AF.Sqrt, [b_S[4]], [b_S[4]], scale=-1.0, bias=1.0)
                    act(tmpb[:, 0:n], psi[:, 0:n], AF.Sigmoid, [pbi, b_PT], [b_tmpb], bias=pcol("lru_bi", c))
                    tt(rr, rr, tmpb[:, 0:n], ALU.mult, [b_S[4], b_tmpb], [b_S[4]])
                    tt(rr, rr, XC[:, t0:t0 + n], ALU.mult, [b_S[4], b_S[1]], [b_S[4]])
                AAs = AA[:, TP:T].rearrange("p (s t) -> p s t", s=NS)
                INs = INP[:, TP:T].rearrange("p (s t) -> p s t", s=NS)
                tt(sm[:, 0:NS], AAs[:, :, 0], stlh[:, c, :], ALU.mult, [b_S[3], b_stlh], [b_sm])
                tt(INs[:, :, 0], INs[:, :, 0], sm[:, 0:NS], ALU.add, [b_S[4], b_sm], [b_S[4]])
                memset(AAs[:, :, 0], 0.0, [b_S[3]], eng="dve")
                HL = slot(1)
                P.op("dve", lambda e, HL=HL, AA=AA, INP=INP: e.tensor_tensor_scan(out=HL[:, 0:TP], data0=AA[:, 0:TP], data1=INP[:, 0:TP],
                                                                           initial=0.0, op0=ALU.mult, op1=ALU.add),
                     reads=[b_S[3], b_S[4]], writes=[b_S[1]])
                P.op("dve", lambda e, HL=HL, AA=AA, INP=INP: e.tensor_tensor_scan(out=HL[:, TP:T], data0=AA[:, TP:T], data1=INP[:, TP:T],
                                                                           initial=0.0, op0=ALU.mult, op1=ALU.add),
                     reads=[b_S[3], b_S[4]], writes=[b_S[1]])
                cp(olh[:, c, 0:1], HL[:, TP - 1:TP], [b_S[1]], [b_olh], eng="pool")
                cp(olh[:, c, 1:NSEQ], HL[:, TP:T].rearrange("p (s t) -> p s t", s=NS)[:, :, 7], [b_S[1]], [b_olh], eng="pool")
                GG = slot(0)
                GAb = slot(2).bitcast(BF16)[:, 0:T]
                for (t0, n) in TB:
                    ps, pb = proj_block(ws, 128, 128, t0, n, bw)
                    u = GG[:, t0:t0 + n]
                    act(u, ps[:, 0:n], AF.Identity, [pb], [b_S[0]])
                    tt(tmpb[:, 0:n], u, u, ALU.mult, [b_S[0]], [b_tmpb])
                    ts(tmpb[:, 0:n], tmpb[:, 0:n], 0.044715, 1.0, ALU.mult, ALU.add, [b_tmpb], [b_tmpb])
                    tt(tmpb[:, 0:n], tmpb[:, 0:n], u, ALU.mult, [b_tmpb, b_S[0]], [b_tmpb])
                    act(tmpb[:, 0:n], tmpb[:, 0:n], AF.Sigmoid, [b_tmpb], [b_tmpb], scale=1.5957691216057308)
                    tt(u, u, tmpb[:, 0:n], ALU.mult, [b_S[0], b_tmpb], [b_S[0]])
                    tt(GAb[:, t0:t0 + n], u, HL[:, t0:t0 + n], ALU.mult, [b_S[0], b_S[1]], [b_S[2]])
                P.dma("sp", ga_d[c * 128:(c + 1) * 128, :], GAb, reads=[b_S[2]], writes=[b_ga])
            flushT(olc, b_olc, 8, 51, lambda c0, w: O["o_lc"].ap()[l].rearrange("s j c -> (s j) c")[:, c0:c0 + w], b_out["o_lc"])
            flushT(olh, b_olh, 8, NSEQ, lambda c0, w: O["o_lh"].ap()[l][:, c0:c0 + w], b_out["o_lh"])

            TL = blk[:, :]
            TLa = TL[:, 0:T]
            SG1 = TL[:, T:2 * T]
            SG2 = TL[:, 2 * T:3 * T]
            b_TL = [b_blk]
            qorder = [24, 25, 26] + list(range(24))

            def rw_loader(q):
                m = 32 if q == 26 else 128
                return lambda ws, bw: wload(ws[:, :, 0:m], win[:, :, 2 * DL + q * 128:2 * DL + q * 128 + m], bw)
            st3b = WStream([rw_loader(q) for q in qorder])

            def rw_A(q, par):
                m = 32 if q == 26 else 128
                ws, bw = st3b.get(qorder.index(q))
                RW = slot(2 * par); bRW = b_S[2 * par]
                RWs = RW[:, 2049:2049 + 144].rearrange("p (s j) -> p s j", s=NS)
                memset(RW[0:m, 0:1], 0.0, [bRW])
                cp(RWs[0:m, :, 0], stsh[0:m, q, :], [b_stsh], [bRW], eng="pool")
                for (t0, n) in TB:
                    ps, pb = proj_block(ws, 0, m, t0, n, bw)
                    if t0 < TP:
                        act(RW[0:m, 1 + t0:1 + t0 + n], ps[0:m, 0:n], AF.Identity, [pb], [bRW])
                    else:
                        act(RWs[0:m, :, 1:9], ps[0:m, 0:n].rearrange("p (s t) -> p s t", s=NS), AF.Identity, [pb], [bRW])

            def rw_B(q, par):
                m = 32 if q == 26 else 128
                RW = slot(2 * par); bRW = b_S[2 * par]
                XS = slot(2 * par + 1); bXS = b_S[2 * par + 1]
                RWs = RW[:, 2049:2049 + 144].rearrange("p (s j) -> p s j", s=NS)
                cp(osh[0:m, q, 0:1], RW[0:m, TP:TP + 1], [bRW], [b_osh], eng="pool")
                cp(osh[0:m, q, 1:NSEQ], RWs[0:m, :, 8], [bRW], [b_osh], eng="pool")
                XSs = XS[:, TP:T].rearrange("p (s t) -> p s t", s=NS)
                mu = pcol("rw_mu", q)
                tt(XS[0:m, 0:TP], RW[0:m, 0:TP], RW[0:m, 1:TP + 1], ALU.subtract, [bRW], [bXS])
                tt(XSs[0:m], RWs[0:m, :, 0:8], RWs[0:m, :, 1:9], ALU.subtract, [bRW], [bXS])
                stt(XS[0:m, 0:TP], XS[0:m, 0:TP], mu[0:m], RW[0:m, 1:TP + 1], ALU.mult, ALU.add, [bRW, bXS, b_PT], [bXS])
                stt(XSs[0:m], XSs[0:m], mu[0:m], RWs[0:m, :, 1:9], ALU.mult, ALU.add, [bRW, bXS, b_PT], [bXS])
                if q == 24:
                    act(TLa[0:64, :], XS[0:64, 0:T], AF.Tanh, [bXS], b_TL)
                    act(TLa[64:128, :], XS[64:128, 0:T], AF.Identity, [bXS], b_TL)
                elif q == 25:
                    act(SG1, XS[:, 0:T], AF.Sigmoid, [bXS], b_TL)
                elif q == 26:
                    act(SG2[0:32, :], XS[0:32, 0:T], AF.Sigmoid, [bXS], b_TL)
                else:
                    to_tokmajor(XS, bXS, q // 8, (q % 8) * 128)

            rw_A(qorder[0], 0)
            for qi_, q in enumerate(qorder):
                if qi_ + 1 < len(qorder):
                    rw_A(qorder[qi_ + 1], (qi_ + 1) % 2)
                rw_B(q, qi_ % 2)
            flushT(osh, b_osh, 27, NSEQ, lambda c0, w: O["o_sh"].ap()[l][:, c0:c0 + w], b_out["o_sh"], last_cols=32)
            for c in range(8):
                DC = slot(0); AC = slot(1); GC = slot(2)
                for (t0, n) in TB:
                    ps, pb = nps()
                    mm(ps[:, 0:n], lw2[0:64, c * 128:(c + 1) * 128], TLa[0:64, t0:t0 + n], True, True, [b_lw2] + b_TL, pb)
                    act(DC[:, t0:t0 + n], ps[:, 0:n], AF.Sigmoid, [pb, b_PT], [b_S[0]], bias=pcol("rw_w0", c))
                    act(DC[:, t0:t0 + n], DC[:, t0:t0 + n], AF.Exp, [b_S[0]], [b_S[0]], scale=-math.exp(-0.5))
                    ps, pb = nps()
                    mm(ps[:, 0:n], lw2[64:128, c * 128:(c + 1) * 128], TLa[64:128, t0:t0 + n], True, True, [b_lw2] + b_TL, pb)
                    act(AC[:, t0:t0 + n], ps[:, 0:n], AF.Sigmoid, [pb, b_PT], [b_S[1]], bias=pcol("rw_a0", c))
                    ps, pb = nps()
                    mm(ps[:, 0:n], g2a[:, c * 128:(c + 1) * 128], SG1[:, t0:t0 + n], True, False, [b_g2a] + b_TL, pb)
                    mm(ps[:, 0:n], g2b[0:32, c * 128:(c + 1) * 128], SG2[0:32, t0:t0 + n], False, True, [b_g2b] + b_TL, pb)
                    act(GC[:, t0:t0 + n], ps[:, 0:n], AF.Identity, [pb], [b_S[2]])
                to_tokmajor(DC, b_S[0], 3, c * 128)
                to_tokmajor(AC, b_S[1], 4, c * 128)
                to_tokmajor(GC, b_S[2], 5, c * 128)
            st3c = WStream([(lambda ws, bw, g2=g2: wload(ws[:, :, 0:256], win[:, :, 2 * DL + NRW + g2 * 256:2 * DL + NRW + (g2 + 1) * 256], bw))
                            for g2 in range(16)])
            for gc in range(32):
                if gc % 2 == 0:
                    ws, bw = st3c.get(gc // 2)
                SGb = slot(gc % 2).bitcast(BF16)[:, 0:T]
                bsl = b_S[gc % 2]
                for (t0, n) in TB:
                    ps, pb = proj_block(ws, (gc % 2) * 128, 128, t0, n, bw)
                    act(SGb[:, t0:t0 + n], ps[:, 0:n], AF.Sigmoid, [pb], [bsl])
                P.dma("sp", sg[gc * 128:(gc + 1) * 128, :], SGb, reads=[bsl], writes=[b_sg])

            for i, nm in enumerate(("rw_kk", "rw_ka", "rw_rk")):
                src = Wt[nm].ap()[l]
                if nm == "rw_rk":
                    src = src.rearrange("h k -> (h k)")
                P.dma("act", ct[i][:, :], src.partition_broadcast(128), writes=[b_ct[i]])
            rqv = rq_h.ap()

            def half(i, h):
                return arena[:, i, h * 1024:(h + 1) * 1024]

            def h3(ap):
                return ap.rearrange("p (h k) -> p h k", h=16)

            for i in range(17):
                rows = slice(i * 128, (i + 1) * 128)
                Rt, Kt, Vt, Dt, At = half(0, 0), half(0, 1), half(1, 0), half(1, 1), half(2, 0)
                KKt, TMt, KMt, NBt, BVt = half(2, 1), half(3, 0), half(3, 1), half(4, 0), half(4, 1)
                for (dst, qi, bb) in ((Rt, 0, b_S[0]), (Kt, 1, b_S[0]), (Vt, 2, b_S[1]), (Dt, 3, b_S[1]), (At, 4, b_S[2])):
                    P.dma("sp", dst, tk[qi][rows, :], reads=[b_tk[qi]], writes=[bb])
                tt(KKt, Kt, ct[0][:, :], ALU.mult, [b_S[0], b_ct[0]], [b_S[2]])
                tt(TMt, KKt, KKt, ALU.mult, [b_S[2]], [b_S[3]])
                P.op("dve", lambda e, TMt=TMt: e.tensor_reduce(out=sm[:, 0:16], in_=h3(TMt), axis=AX.X, op=ALU.add),
                     reads=[b_S[3]], writes=[b_sm])
                act(sm[:, 0:16], sm[:, 0:16], AF.Sqrt, [b_sm], [b_sm])
                ts(sm[:, 0:16], sm[:, 0:16], 1e-12, None, ALU.max, None, [b_sm], [b_sm])
                P.op("dve", lambda e: e.reciprocal(out=sm[:, 0:16], in_=sm[:, 0:16]), reads=[b_sm], writes=[b_sm])
                tt(h3(KKt), h3(KKt), sm[:, 0:16].unsqueeze(2).to_broadcast([128, 16, 64]), ALU.mult, [b_S[2], b_sm], [b_S[2]])
                stt(TMt, At, -1.0, ct[1][:, :], ALU.add, ALU.mult, [b_S[2], b_ct[1]], [b_S[3]])
                stt(KMt, TMt, 1.0, Kt, ALU.add, ALU.mult, [b_S[3], b_S[0]], [b_S[3]])
                stt(NBt, KKt, -1.0, At, ALU.mult, ALU.mult, [b_S[2]], [b_S[4]])
                ccv = sm[:, 32:64].rearrange("p (h j) -> p h j", j=2)
                tt(TMt, Rt, KMt, ALU.mult, [b_S[0], b_S[3]], [b_S[3]])
                P.op("dve", lambda e, TMt=TMt, ccv=ccv: e.tensor_reduce(out=ccv[:, :, 1], in_=h3(TMt), axis=AX.X, op=ALU.add),
                     reads=[b_S[3]], writes=[b_sm])
                tt(TMt, TMt, ct[2][:, :], ALU.mult, [b_S[3], b_ct[2]], [b_S[3]])
                P.op("dve", lambda e, TMt=TMt: e.tensor_reduce(out=sm[:, 16:32], in_=h3(TMt), axis=AX.X, op=ALU.add),
                     reads=[b_S[3]], writes=[b_sm])
                tt(h3(BVt), h3(Vt), sm[:, 16:32].unsqueeze(2).to_broadcast([128, 16, 64]), ALU.mult, [b_S[1], b_sm], [b_S[4]])
                for (srct, qi, bb) in ((Rt, 0, b_S[0]), (Dt, 1, b_S[1]), (KMt, 2, b_S[3]), (KKt, 3, b_S[2]), (NBt, 4, b_S[4])):
                    P.dma("sp", rqv[qi].rearrange("h t k -> t h k")[rows], h3(srct), reads=[bb], writes=[b_rq])
                tt(Kt, NBt, Rt, ALU.mult, [b_S[4], b_S[0]], [b_S[0]])
                P.op("dve", lambda e, Kt=Kt, ccv=ccv: e.tensor_reduce(out=ccv[:, :, 0], in_=h3(Kt), axis=AX.X, op=ALU.add),
                     reads=[b_S[0]], writes=[b_sm])
                tt(Kt, Rt, Dt, ALU.mult, [b_S[0], b_S[1]], [b_S[0]])
                P.dma("sp", rqv[5].rearrange("h t k -> t h k")[rows], h3(Kt), reads=[b_S[0]], writes=[b_rq])
                ps, pb = nps()
                tr(ps[0:32, 0:128], sm[:, 32:64], 128, [b_sm], pb)
                cp(rowst[0:32, 0:128], ps[0:32, 0:128], [pb], [b_rowst])
                P.dma("sp", cc_h.ap()[:, rows], rowst[0:32, 0:128], reads=[b_rowst], writes=[b_cc])
                P.dma("sp", tk[6][rows, :], BVt, reads=[b_S[4]], writes=[b_tk[6]])

            def rec_steps(nsq, nsteps, xr_step, v_step, y_step):
                S3 = Sst[:, 0:nsq * 512].rearrange("p (s v k) -> p s v k", s=nsq, v=8)
                T3 = Stmp[:, 0:nsq * 512].rearrange("p (s v k) -> p s v k", s=nsq, v=8)
                K4 = [128, nsq, 8, 64]
                rdx = b_S[0:5]
                for t in range(nsteps):
                    r_, w_, k_, kk_, nb_ = [xr_step(j, t).unsqueeze(2).to_broadcast(K4) for j in range(5)]
                    sk = skk[:, 0:nsq * 8].rearrange("p (s v) -> p s v", s=nsq)
                    tt(T3, S3, kk_, ALU.mult, [b_Sst] + rdx, [b_Stmp])
                    P.op("dve", lambda e, sk=sk, T3=T3: e.tensor_reduce(out=sk, in_=T3, axis=AX.X, op=ALU.add), reads=[b_Stmp], writes=[b_skk])
                    tt(S3, S3, w_, ALU.mult, [b_Sst] + rdx, [b_Sst])
                    tt(T3, nb_, sk.unsqueeze(3).to_broadcast(K4), ALU.mult, [b_skk] + rdx, [b_Stmp])
                    tt(S3, S3, T3, ALU.add, [b_Sst, b_Stmp], [b_Sst])
                    tt(T3, k_, v_step(t).unsqueeze(3).to_broadcast(K4), ALU.mult, [b_vbuf] + rdx, [b_Stmp])
                    tt(S3, S3, T3, ALU.add, [b_Sst, b_Stmp], [b_Sst])
                    tt(T3, S3, r_, ALU.mult, [b_Sst] + rdx, [b_Stmp])
                    yo = y_step(t)
                    P.op("dve", lambda e, yo=yo, T3=T3: e.tensor_reduce(out=yo, in_=T3, axis=AX.X, op=ALU.add), reads=[b_Stmp], writes=[b_ybuf])

            oS = O["o_S"].ap()[l]
            TC = 16
            SLQ = [3, 5, 1, 4, 2]
            allb = b_S[0:5] + [b for bb_ in bx for b in bb_] + bv + by + bSS + [bT2, bTt] + bKV + [b_blk, b_vbuf, b_ybuf]
            fence(allb)
            SS = [blkF[:, 0:512], blkF[:, 512:1024]]
            T2v = blkF[:, 1024:2048].rearrange("p (q v k) -> p q v k", q=2, v=8)
            Ttv = blkF[:, 2048:2560].rearrange("p (v k) -> p v k", v=8)
            KVv = [blkF[:, 2560 + i * 512:2560 + (i + 1) * 512].rearrange("p (v k) -> p v k", v=8) for i in range(4)]
            memset(SS[0], 0.0, [bSS[0]], eng="dve")
            g = 0
            NCH = TP // TC

            def chunk_views(ci):
                par = ci % 2
                vb_ = vbuf[:, par * 128:(par + 1) * 128].rearrange("p (t v) -> p t v", t=TC)
                yb_ = ybuf[:, par * 128:(par + 1) * 128].rearrange("p (t v) -> p t v", t=TC)
                cb_ = ccb[:, par * 32:(par + 1) * 32].rearrange("p (j t) -> p j t", j=2)
                sk_ = skys[:, par * 256:(par + 1) * 256].rearrange("p (q t v) -> p q t v", q=2, t=TC)
                return par, vb_, yb_, cb_, sk_

            def chunk_loads(ci):
                t0 = ci * TC
                par, vb_, yb_, cb_, sk_ = chunk_views(ci)
                xo = par * 1024
                for j in range(5):
                    src = bass.AP(rq_h, SLQ[j] * 16 * T * 64 + t0 * 64, [[T * 64, 16], [0, 8], [1, TC * 64]])
                    P.dma("sp", slot(j, TC * 64, xo), src, reads=[b_rq], writes=[bx[j][par]])
                P.dma("act", vb_, bass.AP(tk_h, 2 * T * DL + t0 * DL, [[8, 128], [DL, TC], [1, 8]]), reads=[b_tk[2]], writes=[bv[par]])
                for jj in range(2):
                    P.dma("act", cb_[:, jj, :], bass.AP(cc_h, jj * T + t0, [[2 * T, 16], [0, 8], [1, TC]]), reads=[b_cc], writes=[b_ccb[par]])

            chunk_loads(0)
            for ci in range(NCH):
                t0 = ci * TC
                par, vb_, yb_, cb_, sk_ = chunk_views(ci)
                xo = par * 1024
                if ci + 1 < NCH:
                    chunk_loads(ci + 1)
                for t in range(TC):
                    cur, nxt = g % 2, (g + 1) % 2
                    Sc3 = SS[cur].rearrange("p (v k) -> p v k", v=8)
                    Sn3 = SS[nxt].rearrange("p (v k) -> p v k", v=8)
                    kv = KVv[g % 4]; bkv = bKV[g % 4]
                    o = xo + t * 64

                    def kv_fn(e, kv=kv, o=o, vb_=vb_, t=t):
                        last = None
                        for v8 in range(8):
                            last = e.activation(out=kv[:, v8, :], in_=slot(4, 64, o), func=AF.Identity, scale=vb_[:, t, v8:v8 + 1])
                        return last
                    P.op("act", kv_fn, reads=[bx[4][par], bv[par]], writes=[bkv])
                    tt(T2v, Sc3.unsqueeze(1).to_broadcast([128, 2, 8, 64]),
                       arena[:, 0:2, o:o + 64].unsqueeze(2).to_broadcast([128, 2, 8, 64]), ALU.mult,
                       [bSS[cur], bx[0][par], bx[1][par]], [bT2])
                    P.op("dve", lambda e, out=sk_[:, :, t, :], T2v=T2v: e.tensor_reduce(out=out, in_=T2v, axis=AX.X, op=ALU.add),
                         reads=[bT2], writes=[b_skys[par]])
                    tt(Sn3, Sc3, slot(2, 64, o).unsqueeze(1).to_broadcast([128, 8, 64]), ALU.mult, [bSS[cur], bx[2][par]], [bSS[nxt]], eng="pool")
                    tt(Sn3, Sn3, kv, ALU.add, [bSS[nxt], bkv], [bSS[nxt]], eng="pool")
                    tt(Ttv, sk_[:, 0, t, :].unsqueeze(2).to_broadcast([128, 8, 64]),
                       slot(3, 64, o).unsqueeze(1).to_broadcast([128, 8, 64]), ALU.mult, [b_skys[par], bx[3][par]], [bTt])
                    tt(Sn3, Sn3, Ttv, ALU.add, [bSS[nxt], bTt], [bSS[nxt]])
                    g += 1
                c1b = cb_[:, 0, :].unsqueeze(2).to_broadcast([128, TC, 8])
                c2b = cb_[:, 1, :].unsqueeze(2).to_broadcast([128, TC, 8])
                tt(yb_, sk_[:, 0, :, :], c1b, ALU.mult, [b_skys[par], b_ccb[par]], [by[par]], eng="pool")
                tt(yb_, yb_, sk_[:, 1, :, :], ALU.add, [by[par], b_skys[par]], [by[par]], eng="pool")
                tt(sk_[:, 0, :, :], vb_, c2b, ALU.mult, [bv[par], b_ccb[par]], [b_skys[par]], eng="pool")
                tt(yb_, yb_, sk_[:, 0, :, :], ALU.add, [by[par], b_skys[par]], [by[par]], eng="pool")
                P.dma("act", bass.AP(tk_h, 7 * T * DL + t0 * DL, [[8, 128], [DL, TC], [1, 8]]), yb_, reads=[by[par]], writes=[b_tk[7]])
            P.dma("sp", oS[0].rearrange("h (vb v8) k -> (h vb) (v8 k)", vb=8), SS[g % 2], reads=[bSS[g % 2]], writes=[b_out["o_S"]])
            fence(allb)
            for sb in range(4):
                s0 = sb * 4
                P.dma("sp", Sst[:, 0:2048].rearrange("p (s f) -> p s f", s=4),
                      bass.AP(I["st_S"], (l * NS + s0) * 65536, [[512, 128], [65536, 4], [1, 512]]), writes=[b_Sst])
                for j in range(5):
                    for s in range(4):
                        tok0 = TP + (s0 + s) * 8
                        src = bass.AP(rq_h, j * 16 * T * 64 + tok0 * 64, [[T * 64, 16], [0, 8], [1, 8 * 64]])
                        P.dma("sp" if s % 2 == 0 else "act", slot(j, 512, s * 512), src, reads=[b_rq], writes=[b_S[j]])
                P.dma("act", vbuf[:, 0:256].rearrange("p (t v) -> p t v", t=32),
                      bass.AP(tk_h, 2 * T * DL + (TP + s0 * 8) * DL, [[8, 128], [DL, 32], [1, 8]]), reads=[b_tk[2]], writes=[b_vbuf])
                rec_steps(4, 8,
                          lambda j, t: slot(j, 2048).rearrange("p (s t k) -> p s t k", s=4, t=8)[:, :, t, :],
                          lambda t: vbuf[:, 0:256].rearrange("p (s t v) -> p s t v", s=4, t=8)[:, :, t, :],
                          lambda t: ybuf[:, 0:256].rearrange("p (s t v) -> p s t v", s=4, t=8)[:, :, t, :])
                P.dma("act", bass.AP(tk_h, 7 * T * DL + (TP + s0 * 8) * DL, [[8, 128], [DL, 32], [1, 8]]),
                      ybuf[:, 0:256].rearrange("p (t v) -> p t v", t=32), reads=[b_ybuf], writes=[b_tk[7]])
                P.dma("sp", bass.AP(O["o_S"], (l * NSEQ + 1 + s0) * 65536, [[512, 128], [65536, 4], [1, 512]]),
                      Sst[:, 0:2048].rearrange("p (s f) -> p s f", s=4), reads=[b_Sst], writes=[b_out["o_S"]])

            for i, nm in enumerate(("rw_gn_g", "rw_gn_b")):
                P.dma("act", ct[i][:, :], Wt[nm].ap()[l].partition_broadcast(128), writes=[b_ct[i]])
            for i in range(17):
                rows = slice(i * 128, (i + 1) * 128)
                pp = i % 2
                bA_, bB_ = b_S[2 * pp], b_S[2 * pp + 1]
                Yt, BVt, GGt, TMt = half(2 * pp, 0), half(2 * pp, 1), half(2 * pp + 1, 0), half(2 * pp + 1, 1)
                P.dma("sp", Yt, tk[7][rows, :], reads=[b_tk[7]], writes=[bA_])
                P.dma("sp", BVt, tk[6][rows, :], reads=[b_tk[6]], writes=[bA_])
                P.dma("sp", GGt, tk[5][rows, :], reads=[b_tk[5]], writes=[bB_])
                P.op("dve", lambda e, Yt=Yt: e.tensor_reduce(out=sm[:, 0:16], in_=h3(Yt), axis=AX.X, op=ALU.add), reads=[bA_], writes=[b_sm])
                ts(sm[:, 0:16], sm[:, 0:16], 1.0 / 64, None, ALU.mult, None, [b_sm], [b_sm])
                tt(h3(Yt), h3(Yt), sm[:, 0:16].unsqueeze(2).to_broadcast([128, 16, 64]), ALU.subtract, [bA_, b_sm], [bA_])
                tt(TMt, Yt, Yt, ALU.mult, [bA_], [bB_])
                P.op("dve", lambda e, TMt=TMt: e.tensor_reduce(out=sm[:, 16:32], in_=h3(TMt), axis=AX.X, op=ALU.add), reads=[bB_], writes=[b_sm])
                act(sm[:, 16:32], sm[:, 16:32], AF.Sqrt, [b_sm], [b_sm], scale=1.0 / 64, bias=64e-5)
                P.op("dve", lambda e: e.reciprocal(out=sm[:, 16:32], in_=sm[:, 16:32]), reads=[b_sm], writes=[b_sm])
                tt(h3(Yt), h3(Yt), sm[:, 16:32].unsqueeze(2).to_broadcast([128, 16, 64]), ALU.mult, [bA_, b_sm], [bA_])
                tt(Yt, Yt, ct[0][:, :], ALU.mult, [bA_, b_ct[0]], [bA_])
                tt(Yt, Yt, ct[1][:, :], ALU.add, [bA_, b_ct[1]], [bA_])
                tt(Yt, Yt, BVt, ALU.add, [bA_], [bA_])
                tt(Yt, Yt, GGt, ALU.mult, [bA_, bB_], [bA_])
                OTb = slot(4).bitcast(BF16)[:, pp * 1024:(pp + 1) * 1024].rearrange("p (c t) -> p c t", c=8)
                for g in range(2):
                    ps, pb = nps()
                    for j in range(4):
                        tr(ps[:, j * 128:(j + 1) * 128], Yt[:, (g * 4 + j) * 128:(g * 4 + j + 1) * 128], 128, [bA_], pb)
                    act(OTb[:, g * 4:(g + 1) * 4, :].rearrange("p a b -> p (a b)"), ps[:, :], AF.Identity, [pb], [b_S[4]])
                P.dma("act", oT_d.rearrange("(c p) t -> p c t", p=128)[:, :, rows], OTb, reads=[b_S[4]], writes=[b_oT])

            wpa = Wt["w_pa"].ap()[l].rearrange("(kc p) c -> p kc c", p=128)
            wpb = Wt["w_pb"].ap()[l].rearrange("(kc p) c -> p kc c", p=128)
            gav = ga_d.rearrange("(kc p) t -> p kc t", p=128)
            otv = oT_d.rearrange("(kc p) t -> p kc t", p=128)
            mcnt = 0

            def mg_loader(c):
                def f(ws, bw):
                    wload(ws[:, 0:8, 0:128], wpa[:, :, c * 128:(c + 1) * 128], bw)
                    wload(ws[:, 8:16, 0:128], wpb[:, :, c * 128:(c + 1) * 128], bw)
                return f
            st7 = WStream([mg_loader(c) for _tb in TB for c in range(16)])
            for (t0, n) in TB:
                gb = blk[:, 0:16 * 512].rearrange("p (kc t) -> p kc t", kc=16)
                P.dma("sp", gb[:, 0:8, 0:n], gav[:, :, t0:t0 + n], reads=[b_ga], writes=[b_blk])
                P.dma("act", gb[:, 8:16, 0:n], otv[:, :, t0:t0 + n], reads=[b_oT], writes=[b_blk])
                for c in range(16):
                    ws, bw = st7.get(mcnt)
                    sg_ = sgt[:, mcnt % 2, :]; bsg = b_sgt[mcnt % 2]
                    mcnt += 1
                    P.dma("sp", sg_[:, 0:n], sg[c * 128:(c + 1) * 128, t0:t0 + n], reads=[b_sg], writes=[bsg])
                    P.dma("sp", sg_[:, 512:512 + n], sg[D + c * 128:D + (c + 1) * 128, t0:t0 + n], reads=[b_sg], writes=[bsg])
                    psa, pba = nps()
                    for kc in range(8):
                        mm(psa[:, 0:n], ws[:, kc, 0:128], gb[:, kc, 0:n], kc == 0, kc == 7, [bw, b_blk], pba)
                    psb, pbb = nps()
                    for kc in range(8):
                        mm(psb[:, 0:n], ws[:, 8 + kc, 0:128], gb[:, 8 + kc, 0:n], kc == 0, kc == 7, [bw, b_blk], pbb)
                    tt(tmpb[:, 0:n], psa[:, 0:n], sg_[:, 0:n], ALU.mult, [pba, bsg], [b_tmpb])
                    tt(tmpc[:, 0:n], psb[:, 0:n], sg_[:, 512:512 + n], ALU.mult, [pbb, bsg], [b_tmpc])
                    tt(actT[:, c, t0:t0 + n], tmpb[:, 0:n], tmpc[:, 0:n], ALU.add, [b_tmpb, b_tmpc], [b_act])
            wo = Wt["w_o"].ap()[l].rearrange("(kc p) c -> p kc c", p=128)
            st8 = WStream([(lambda ws, bw, c2=c2: wload(ws[:, :, 0:256], wo[:, :, c2 * 256:(c2 + 1) * 256], bw)) for c2 in range(8)])
            for c in range(16):
                if c % 2 == 0:
                    ws, bw = st8.get(c // 2)
                for (t0, n) in TB:
                    ps, pb = proj_block(ws, (c % 2) * 128, 128, t0, n, bw)
                    resid_update(c, t0, n, ps, pb, 32)

            make_A(64, "norm_ffn")
            norm_mod(A1, modT[:, 48:64, :], b_A1, b_mod)

            wup = Wt["w_up"].ap()[l].rearrange("(kc p) c -> p kc c", p=128)
            def up_loader(j):
                def f(ws, bw):
                    wload(ws[:, :, 0:128], wup[:, :, j * 128:(j + 1) * 128], bw)
                    wload(ws[:, :, 128:256], wup[:, :, DFF + j * 128:DFF + (j + 1) * 128], bw)
                return f
            st10 = WStream([up_loader(j) for j in range(44)])
            for j in range(44):
                j4 = j % 4
                stv = I["st_fc"].ap()[l].rearrange("s j c -> (s j) c")
                ofv = O["o_fc"].ap()[l].rearrange("s j c -> (s j) c")
                if j4 == 0:
                    for hh in range(2):
                        loadT(stfc[:, hh * 4:hh * 4 + 4, :], b_stfc,
                              lambda c0, w, hh=hh, j=j: stv[:, (hh * 44 + j) * 128 + c0:(hh * 44 + j) * 128 + c0 + w], 32, 4)
                ws, bw = st10.get(j)
                outs = []
                for hh in range(2):
                    ch = hh * 44 + j
                    UU = slot(0 + hh * 2); UC = slot(1 + hh * 2)
                    bU = b_S[0 + hh * 2]; bC = b_S[1 + hh * 2]
                    UUs = UU[:, 2050:2050 + 160].rearrange("p (s j) -> p s j", s=NS)
                    memset(UU[:, 0:2], 0.0, [bU])
                    cp(UUs[:, :, 0:2], stfc[:, hh * 4 + j4, :].rearrange("p (s j) -> p s j", s=NS), [b_stfc], [bU], eng="pool")
                    for (t0, n) in TB:
                        ps, pb = proj_block(ws, hh * 128, 128, t0, n, bw)
                        if t0 < TP:
                            act(UU[:, 2 + t0:2 + t0 + n], ps[:, 0:n], AF.Identity, [pb], [bU])
                        else:
                            act(UUs[:, :, 2:10], ps[:, 0:n].rearrange("p (s t) -> p s t", s=NS), AF.Identity, [pb], [bU])
                    k8 = j % 8
                    if hh == 0 and k8 == 0:
                        pass
                    cp(ofc[:, hh * 4 + j4, 0:2], UU[:, 2048:2050], [bU], [b_ofc], eng="pool")
                    cp(ofc[:, hh * 4 + j4, 2:34].rearrange("p (s j) -> p s j", s=NS), UUs[:, :, 8:10], [bU], [b_ofc], eng="pool")
                    UCs = UC[:, TP:T].rearrange("p (s t) -> p s t", s=NS)
                    ts(UC[:, 0:TP], UU[:, 0:TP], pcol("ffn_conv_w", 0 * 88 + ch), pcol("ffn_conv_b", ch), ALU.mult, ALU.add, [bU, b_PT], [bC])
                    ts(UCs, UUs[:, :, 0:8], pcol("ffn_conv_w", 0 * 88 + ch), pcol("ffn_conv_b", ch), ALU.mult, ALU.add, [bU, b_PT], [bC])
                    for jj in range(1, 3):
                        stt(UC[:, 0:TP], UU[:, jj:jj + TP], pcol("ffn_conv_w", jj * 88 + ch), UC[:, 0:TP], ALU.mult, ALU.add, [bU, bC, b_PT], [bC])
                        stt(UCs, UUs[:, :, jj:jj + 8], pcol("ffn_conv_w", jj * 88 + ch), UCs, ALU.mult, ALU.add, [bU, bC, b_PT], [bC])
                    outs.append((UC, bC))
                (UG, bG), (UV, bV) = outs
                act(UG[:, 0:T], UG[:, 0:T], AF.Silu, [bG], [bG])
                ATb = slot(4).bitcast(BF16)[:, 0:T]
                tt(ATb, UG[:, 0:T], UV[:, 0:T], ALU.mult, [bG, bV], [b_S[4]])
                P.dma("sp", aT_d[j * 128:(j + 1) * 128, :], ATb, reads=[b_S[4]], writes=[b_aT])
                if j4 == 3:
                    for hh in range(2):
                        flushT(ofc[:, hh * 4:hh * 4 + 4, :], b_ofc, 4, 34,
                               lambda c0, w, hh=hh, j=j: ofv[:, (hh * 44 + j - 3) * 128 + c0:(hh * 44 + j - 3) * 128 + c0 + w], b_out["o_fc"])

            wdn = Wt["w_down"].ap()[l].rearrange("(kc p) c -> p kc c", p=128)
            atv = aT_d.rearrange("(kc p) t -> p kc t", p=128)
            fence([b_blk, b_ab1])
            abv = [blk[:, 0:11 * 512].rearrange("p (kc t) -> p kc t", kc=11),
                   blk[:, 11 * 512:22 * 512].rearrange("p (kc t) -> p kc t", kc=11)]
            b_ab = [b_blk, b_ab1]
            qcnt = 0
            dn_items = [(cg, qt, h2) for _tb in TB for cg in range(4) for qt in range(4) for h2 in range(2)]
            st11 = WStream([(lambda ws, bw, cg=cg, qt=qt, h2=h2:
                             wload(ws[:, 0:11, 0:256], wdn[:, qt * 11:(qt + 1) * 11, (cg * 4 + h2 * 2) * 128:(cg * 4 + h2 * 2 + 2) * 128], bw))
                            for (cg, qt, h2) in dn_items])
            dcnt = 0
            for (t0, n) in TB:
                for cg in range(4):
                    banks = [nps() for _ in range(4)]
                    for qt in range(4):
                        ab = abv[qcnt % 2]; bab = b_ab[qcnt % 2]
                        P.dma("sp" if qcnt % 2 == 0 else "act", ab[:, :, 0:n], atv[:, qt * 11:(qt + 1) * 11, t0:t0 + n],
                              reads=[b_aT], writes=[bab])
                        qcnt += 1
                        for ci in range(4):
                            c = cg * 4 + ci
                            if ci % 2 == 0:
                                ws, bw = st11.get(dcnt)
                                dcnt += 1
                            ps, pb = banks[ci]
                            wc = (ci % 2) * 128
                            for k2 in range(11):
                                mm(ps[:, 0:n], ws[:, k2, wc:wc + 128], ab[:, k2, 0:n], qt == 0 and k2 == 0, qt == 3 and k2 == 10, [bw, bab], pb)
                    for ci in range(4):
                        ps, pb = banks[ci]
                        resid_update(cg * 4 + ci, t0, n, ps, pb, 80)
            fence([b_blk, b_ab1])

        nf = Wt["norm_final"].ap().rearrange("(r c) -> r c", c=128)
        memset(pst[:], 0.0, [b_pst])
        P.dma("act", pst[0:16, :], nf, writes=[b_pst])
        ps, pb = nps()
        tr(ps[:, 0:128], pst[:, :], 128, [b_pst], pb)
        cp(PT[:, 0:128], ps[:, 0:128], [pb], [b_PT])
        for (t0, n) in TB:
            xb, rstd = norm_stats(t0, n)
            for kc in range(16):
                stt(xb[:, kc, :], xb[:, kc, :], PT[:, kc:kc + 1], rstd, ALU.mult, ALU.mult, b_S[0:5] + [b_PT], b_S[0:4])
            for tt_i in range(n // 128):
                for g in range(4):
                    ps, pb = nps()
                    for j in range(4):
                        kc = g * 4 + j
                        tr(ps[:, j * 128:(j + 1) * 128], xb[:, kc, tt_i * 128:(tt_i + 1) * 128], 128, b_S[0:4], pb)
                    act(rowst[:, 0:512], ps[:, :], AF.Identity, [pb], [b_rowst])
                    r0 = t0 + tt_i * 128
                    if t0 < TP:
                        P.dma("sp", O["y_p"].ap()[r0:r0 + 128, g * 512:(g + 1) * 512], rowst[:, 0:512], reads=[b_rowst], writes=[b_out["y_p"]])
                    else:
                        P.dma("sp", O["y_s"].ap()[:, g * 512:(g + 1) * 512], rowst[:, 0:512], reads=[b_rowst], writes=[b_out["y_s"]])

        P.final_wait("sp", list(b_out.values()))
        P.emit()
    return nc


_NC_CACHE = {}


def kernel(x_prompt, x_sample, c_prompt, c_sample, state_lru_conv, state_lru_h, state_rwkv_shift, state_rwkv_S,
           state_ffn_conv, **weights):
    f = lambda a: np.ascontiguousarray(np.asarray(a, dtype=np.float32))
    if "nc" not in _NC_CACHE:
        _NC_CACHE["nc"] = build()
    nc = _NC_CACHE["nc"]
    wmap = {k: f(weights[k]) for k in W_SHAPES}
    in_maps = []
    for core in range(8):
        b = core % 4
        ss = slice(core * NS, (core + 1) * NS)
        m = dict(wmap)
        m["xp"] = f(x_prompt[b])
        m["xs"] = f(np.asarray(x_sample)[ss].reshape(TS, D))
        m["cc"] = f(np.concatenate([np.asarray(c_prompt)[b:b + 1], np.asarray(c_sample)[ss]], axis=0))
        m["st_lc"] = f(np.asarray(state_lru_conv)[:, ss])
        m["st_lh"] = f(np.asarray(state_lru_h)[:, ss])
        m["st_sh"] = f(np.asarray(state_rwkv_shift)[:, ss])
        m["st_S"] = f(np.asarray(state_rwkv_S)[:, ss])
        m["st_fc"] = f(np.asarray(state_ffn_conv)[:, ss])
        in_maps.append(m)
    res = run_bass_kernel_spmd(nc, in_maps, core_ids=list(range(8)))
    R = res.results
    y_prompt = np.stack([R[b]["y_p"] for b in range(4)], axis=0)
    y_sample = np.concatenate([R[c]["y_s"].reshape(NS, 8, D) for c in range(8)], axis=0)
    outs = [y_prompt, y_sample]
    names = ["o_lc", "o_lh", "o_sh", "o_S", "o_fc"]
    for nm in names:
        outs.append(np.stack([R[b][nm][:, 0] for b in range(4)], axis=1))
    for nm in names:
        outs.append(np.concatenate([R[c][nm][:, 1:] for c in range(8)], axis=1))
    return tuple(np.ascontiguousarray(o.astype(np.float32, copy=False)) for o in outs)
```
